# Optimizing a Trainium2 kernel written in Bass

```python
import math
import jax
import jax.numpy as jnp
from jax import lax
import numpy as np

D_MODEL = 2048
BATCH = 4
SEQ = 2048
DEPTH = 2


CTX_LEN = 256
GRID_W = 64
HEAD_DIM = 128
MIX_WIDTH = D_MODEL
S5_WIDTH = MIX_WIDTH // 4
S5_GROUP = 16
S5_GROUPS = S5_WIDTH // S5_GROUP
S5_STATE = 64
RET_WIDTH = 3 * MIX_WIDTH // 8
RET_HEADS = RET_WIDTH // HEAD_DIM
MLSTM_WIDTH = MIX_WIDTH - S5_WIDTH - RET_WIDTH
MLSTM_HEADS = MLSTM_WIDTH // HEAD_DIM
IN_WIDTH = S5_WIDTH + 4 * RET_WIDTH + 4 * MLSTM_WIDTH + 4 * MLSTM_HEADS
IN_SPLITS = (S5_WIDTH, S5_WIDTH + 4 * RET_WIDTH, S5_WIDTH + 4 * RET_WIDTH + 4 * MLSTM_WIDTH)
D_FF = 4 * D_MODEL
CHUNK = 128
ROPE_BASE = 10000.0
EPS = 1e-6
S5_DT_MIN = 0.001
S5_DT_MAX = 0.1

kernel_name = 'hybrid_s5_retention_mlstm_dit_block'

F32 = jnp.float32


def _rmsnorm(x, w):
    xf = x.astype(F32)
    y = xf * lax.rsqrt(jnp.mean(xf * xf, axis=-1, keepdims=True) + EPS)
    return (y * w.astype(F32)).astype(x.dtype)


def _modulate(h, shift, scale):
    return h * (1.0 + scale) + shift


def _heads(t, n_heads):
    b, l, _ = t.shape
    return t.reshape(b, l, n_heads, -1).transpose(0, 2, 1, 3).astype(F32)


def _flip(t, rev, axis):
    return jnp.flip(t, axis=axis) if rev else t


def _chunks(t):
    b, hh, l = t.shape[:3]
    return jnp.moveaxis(t.reshape((b, hh, l // CHUNK, CHUNK) + t.shape[3:]), 2, 0)


def _unchunk(t):
    t = jnp.moveaxis(t, 0, 2)
    return t.reshape(t.shape[:2] + (t.shape[2] * t.shape[3],) + t.shape[4:])


def _rope_2d_tables(rows, cols):
    quarter = HEAD_DIM // 4
    inv = ROPE_BASE ** (-jnp.arange(quarter, dtype=F32) / quarter)
    ang = jnp.concatenate([rows[:, None] * inv, cols[:, None] * inv], axis=-1)
    return jnp.cos(ang), jnp.sin(ang)


def _apply_rope(t, cos, sin):
    half = t.shape[-1] // 2
    t1, t2 = t[..., :half], t[..., half:]
    return jnp.concatenate([t1 * cos - t2 * sin, t1 * sin + t2 * cos], axis=-1)


def _head_norm(o, w, center):
    if center:
        o = o - jnp.mean(o, axis=-1, keepdims=True)
    o = o * lax.rsqrt(jnp.mean(o * o, axis=-1, keepdims=True) + EPS)
    b, hh, l, dh = o.shape
    return o.transpose(0, 2, 1, 3).reshape(b, l, hh * dh) * w.astype(F32)


def _cmul(ar, ai, br, bi):
    return ar * br - ai * bi, ar * bi + ai * br


def _s5_discretise(lam_re, lam_im, log_step, b_re, b_im):
    lam_re = jnp.minimum(lam_re.astype(F32), -1e-4)
    lam_im = lam_im.astype(F32)
    step = jnp.exp(log_step.astype(F32))[:, None]
    mag = jnp.exp(lam_re * step)
    ab_re, ab_im = mag * jnp.cos(lam_im * step), mag * jnp.sin(lam_im * step)
    den = lam_re * lam_re + lam_im * lam_im
    nr, ni = _cmul(ab_re - 1.0, ab_im, lam_re / den, -lam_im / den)
    bb_re, bb_im = _cmul(nr[..., None], ni[..., None], b_re.astype(F32), b_im.astype(F32))
    return ab_re, ab_im, bb_re, bb_im


def _s5_scan(u, ab_re, ab_im, bb_re, bb_im, x0_re, x0_im):
    bu_re = jnp.einsum('gpn,blgn->blgp', bb_re, u)
    bu_im = jnp.einsum('gpn,blgn->blgp', bb_im, u)
    ir, ii = _cmul(ab_re, ab_im, x0_re, x0_im)
    bu_re = bu_re.at[:, 0].add(ir)
    bu_im = bu_im.at[:, 0].add(ii)
    a_re = jnp.broadcast_to(ab_re, bu_re.shape)
    a_im = jnp.broadcast_to(ab_im, bu_im.shape)

    def combine(e1, e2):
        a1r, a1i, b1r, b1i = e1
        a2r, a2i, b2r, b2i = e2
        ar, ai = _cmul(a2r, a2i, a1r, a1i)
        br, bi = _cmul(a2r, a2i, b1r, b1i)
        return ar, ai, br + b2r, bi + b2i

    _, _, xr, xi = lax.associative_scan(combine, (a_re, a_im, bu_re, bu_im), axis=1)
    return xr, xi


def _s5_readout(c_re, c_im, xr, xi):
    return jnp.einsum('gnp,blgp->blgn', c_re.astype(F32), xr) - jnp.einsum('gnp,blgp->blgn', c_im.astype(F32), xi)


def _s5_mixer(u_x, u_h, lam_re, lam_im, log_step, b_re, b_im, c_re, c_im, d_skip):
    def groups(u):
        b, l, _ = u.shape
        return u.astype(F32).reshape(b, l, S5_GROUPS, S5_GROUP)

    ux, uh = groups(u_x), groups(u_h)
    dsk = d_skip.astype(F32).reshape(S5_GROUPS, S5_GROUP)
    yx, yh = dsk * ux, dsk * uh
    zero = jnp.zeros((ux.shape[0], S5_GROUPS, S5_STATE), F32)
    for d in range(2):
        rev = d == 1
        ab_re, ab_im, bb_re, bb_im = _s5_discretise(lam_re[d], lam_im[d], log_step[d], b_re[d], b_im[d])
        sr, si = _s5_scan(_flip(uh, rev, 1), ab_re, ab_im, bb_re, bb_im, zero, zero)
        yh = yh + _flip(_s5_readout(c_re[d], c_im[d], sr, si), rev, 1)
        sr, si = _s5_scan(_flip(ux, rev, 1), ab_re, ab_im, bb_re, bb_im, sr[:, -1], si[:, -1])
        yx = yx + _flip(_s5_readout(c_re[d], c_im[d], sr, si), rev, 1)
    return yx, yh


def _s5_glu(y, w_glu, b_glu):
    b, l = y.shape[:2]
    z = jax.nn.gelu(y.reshape(b, l, S5_WIDTH)) @ w_glu.astype(F32) + b_glu.astype(F32)
    return z[..., :S5_WIDTH] * jax.nn.sigmoid(z[..., S5_WIDTH:])


def _retention_dir(q, k, v, log_gamma, r0):
    idx = jnp.arange(CHUNK, dtype=F32)
    rel = idx[:, None] - idx[None, :]
    lg = log_gamma[:, None, None]
    decay = jnp.where(rel >= 0, jnp.exp(lg * jnp.maximum(rel, 0.0)), 0.0)
    q_decay = jnp.exp(log_gamma[:, None] * (idx + 1.0))[..., None]
    k_decay = jnp.exp(log_gamma[:, None] * (CHUNK - 1.0 - idx))[..., None]
    chunk_decay = jnp.exp(log_gamma * CHUNK)[:, None, None]

    def step(r, inp):
        qi, ki, vi = inp
        s = jnp.einsum('bhid,bhjd->bhij', qi, ki) * decay
        o = jnp.einsum('bhij,bhjv->bhiv', s, vi) + jnp.einsum('bhid,bhdv->bhiv', qi * q_decay, r)
        r = chunk_decay * r + jnp.einsum('bhjd,bhjv->bhdv', ki * k_decay, vi)
        return r, o

    r, o = lax.scan(step, r0, (_chunks(q), _chunks(k), _chunks(v)))
    return _unchunk(o), r


def _retention_mixer(lat, ctx, decay_logit):
    qx, kx, vx = lat
    qh, kh, vh = ctx
    scale = HEAD_DIM ** -0.5
    kx, kh = kx * scale, kh * scale
    zero = jnp.zeros(qh.shape[:2] + (HEAD_DIM, HEAD_DIM), F32)
    out_x, out_h = 0.0, 0.0
    for d in range(2):
        rev = d == 1
        lg = jax.nn.log_sigmoid(decay_logit[d].astype(F32))
        oh, rh = _retention_dir(_flip(qh, rev, 2), _flip(kh, rev, 2), _flip(vh, rev, 2), lg, zero)
        ox, _ = _retention_dir(_flip(qx, rev, 2), _flip(kx, rev, 2), _flip(vx, rev, 2), lg, rh)
        out_h = out_h + _flip(oh, rev, 2)
        out_x = out_x + _flip(ox, rev, 2)
    return out_x, out_h


def _mlstm_dir(q, k, v, i_pre, f_pre, state):
    tril = jnp.tril(jnp.ones((CHUNK, CHUNK), dtype=bool))
    log_f = jax.nn.log_sigmoid(f_pre)

    def step(carry, inp):
        c_mem, n_mem, m_prev = carry
        qi, ki, vi, ii, lfi = inp
        b = jnp.cumsum(lfi, axis=-1)
        log_w = jnp.where(tril, b[..., :, None] - b[..., None, :] + ii[..., None, :], -jnp.inf)
        log_a = b + m_prev[..., None]
        m_t = jnp.maximum(log_a, jnp.max(log_w, axis=-1))
        w = jnp.exp(log_w - m_t[..., None])
        a = jnp.exp(log_a - m_t)
        s = jnp.einsum('bhid,bhjd->bhij', qi, ki) * w
        num = jnp.einsum('bhij,bhjv->bhiv', s, vi) + a[..., None] * jnp.einsum('bhid,bhdv->bhiv', qi, c_mem)
        den = jnp.sum(s, axis=-1) + a * jnp.einsum('bhid,bhd->bhi', qi, n_mem)
        h = num / jnp.maximum(jnp.abs(den), jnp.exp(-m_t))[..., None]
        b_end = b[..., -1:]
        log_w_end = b_end - b + ii
        m_new = jnp.maximum(b_end[..., 0] + m_prev, jnp.max(log_w_end, axis=-1))
        a_end = jnp.exp(b_end[..., 0] + m_prev - m_new)
        w_end = jnp.exp(log_w_end - m_new[..., None])
        c_mem = a_end[..., None, None] * c_mem + jnp.einsum('bhj,bhjd,bhjv->bhdv', w_end, ki, vi)
        n_mem = a_end[..., None] * n_mem + jnp.einsum('bhj,bhjd->bhd', w_end, ki)
        return (c_mem, n_mem, m_new), h

    state, h = lax.scan(step, state, (_chunks(q), _chunks(k), _chunks(v), _chunks(i_pre), _chunks(log_f)))
    return _unchunk(h), state


def _mlstm_mixer(lat, ctx, igate_b, fgate_b):
    qx, kx, vx, gx = lat
    qh, kh, vh, gh = ctx
    scale = HEAD_DIM ** -0.5
    kx, kh = kx * scale, kh * scale
    b, hh = qh.shape[:2]
    init = (jnp.zeros((b, hh, HEAD_DIM, HEAD_DIM), F32), jnp.zeros((b, hh, HEAD_DIM), F32), jnp.zeros((b, hh), F32))
    out_x, out_h = 0.0, 0.0
    for d in range(2):
        rev = d == 1
        ib = igate_b[d].astype(F32)[:, None]
        fb = fgate_b[d].astype(F32)[:, None]
        oh, st = _mlstm_dir(_flip(qh, rev, 2), _flip(kh, rev, 2), _flip(vh, rev, 2),
                            _flip(gh[d, 0] + ib, rev, 2), _flip(gh[d, 1] + fb, rev, 2), init)
        ox, _ = _mlstm_dir(_flip(qx, rev, 2), _flip(kx, rev, 2), _flip(vx, rev, 2),
                           _flip(gx[d, 0] + ib, rev, 2), _flip(gx[d, 1] + fb, rev, 2), st)
        out_h = out_h + _flip(oh, rev, 2)
        out_x = out_x + _flip(ox, rev, 2)
    return out_x, out_h


def _gates(t):
    b, l, _ = t.shape
    return t.reshape(b, l, 2, 2, MLSTM_HEADS).transpose(2, 3, 0, 4, 1).astype(F32)


def _token_mix(a_x, a_h, cos, sin, ctx_out, s5_lam_re, s5_lam_im, s5_log_step, s5_b_re, s5_b_im,
               s5_c_re, s5_c_im, s5_d, s5_w_glu, s5_b_glu, ret_decay_logit, ret_norm_w,
               mlstm_igate_b, mlstm_fgate_b, mlstm_norm_w):
    px = jnp.split(a_x, IN_SPLITS, axis=-1)
    ph = jnp.split(a_h, IN_SPLITS, axis=-1)
    s5x, s5h = _s5_mixer(px[0], ph[0], s5_lam_re, s5_lam_im, s5_log_step, s5_b_re, s5_b_im, s5_c_re, s5_c_im, s5_d)
    rx = jnp.split(px[1], 4, axis=-1)
    rh = jnp.split(ph[1], 4, axis=-1)
    lat_r = (_apply_rope(_heads(rx[0], RET_HEADS), cos, sin), _apply_rope(_heads(rx[1], RET_HEADS), cos, sin),
             _heads(rx[2], RET_HEADS))
    ctx_r = (_heads(rh[0], RET_HEADS), _heads(rh[1], RET_HEADS), _heads(rh[2], RET_HEADS))
    retx, reth = _retention_mixer(lat_r, ctx_r, ret_decay_logit)
    mx = jnp.split(px[2], 4, axis=-1)
    mh = jnp.split(ph[2], 4, axis=-1)
    lat_m = (_heads(mx[0], MLSTM_HEADS), _heads(mx[1], MLSTM_HEADS), _heads(mx[2], MLSTM_HEADS), _gates(px[3]))
    ctx_m = (_heads(mh[0], MLSTM_HEADS), _heads(mh[1], MLSTM_HEADS), _heads(mh[2], MLSTM_HEADS), _gates(ph[3]))
    mlx, mlh = _mlstm_mixer(lat_m, ctx_m, mlstm_igate_b, mlstm_fgate_b)

    def merge(s5y, rety, rgate, mly, ogate):
        return jnp.concatenate([
            _s5_glu(s5y, s5_w_glu, s5_b_glu),
            _head_norm(rety, ret_norm_w, True) * jax.nn.silu(rgate.astype(F32)),
            _head_norm(mly, mlstm_norm_w, False) * jax.nn.sigmoid(ogate.astype(F32)),
        ], axis=-1)

    y_x = merge(s5x, retx, rx[3], mlx, mx[3])
    y_h = merge(s5h, reth, rh[3], mlh, mh[3]) if ctx_out else None
    return y_x, y_h


def _sq_relu_mlp(h, w1, w2):
    return jnp.square(jax.nn.relu(h @ w1)) @ w2


def setup_inputs(seed: int = 0) -> dict:
    key = jax.random.key(seed)
    keys = iter(jax.random.split(key, 32))

    def normal(shape, scale):
        return jax.random.normal(next(keys), shape, F32) * scale

    g, p, n, h = S5_GROUPS, S5_STATE, S5_GROUP, RET_HEADS
    x = normal((BATCH, SEQ, D_MODEL), 1.0)
    c = normal((BATCH, D_MODEL), 1.0)
    ctx = normal((BATCH, CTX_LEN, D_MODEL), 1.0)
    c_ctx = normal((D_MODEL,), 1.0)
    w_mod = normal((DEPTH, D_MODEL, 6 * D_MODEL), 0.5 * D_MODEL ** -0.5)
    b_mod = normal((DEPTH, 6 * D_MODEL), 0.02)
    norm1_w = 1.0 + normal((DEPTH, D_MODEL), 0.02)
    norm2_w = 1.0 + normal((DEPTH, D_MODEL), 0.02)
    w_in = normal((DEPTH, D_MODEL, IN_WIDTH), D_MODEL ** -0.5)
    w_out = normal((DEPTH, MIX_WIDTH, D_MODEL), MIX_WIDTH ** -0.5)
    s5_lam_re = -0.5 + normal((DEPTH, 2, g, p), 0.01)
    s5_lam_im = jnp.pi * jnp.arange(p, dtype=F32) + normal((DEPTH, 2, g, p), 0.01)
    s5_log_step = jax.random.uniform(next(keys), (DEPTH, 2, g), F32, math.log(S5_DT_MIN), math.log(S5_DT_MAX))
    s5_b_re = normal((DEPTH, 2, g, p, n), (2.0 * n) ** -0.5)
    s5_b_im = normal((DEPTH, 2, g, p, n), (2.0 * n) ** -0.5)
    s5_c_re = normal((DEPTH, 2, g, n, p), p ** -0.5)
    s5_c_im = normal((DEPTH, 2, g, n, p), p ** -0.5)
    s5_d = normal((DEPTH, S5_WIDTH), 1.0)
    s5_w_glu = normal((DEPTH, S5_WIDTH, 2 * S5_WIDTH), S5_WIDTH ** -0.5)
    s5_b_glu = normal((DEPTH, 2 * S5_WIDTH), 0.02)
    expo = 5.0 + jnp.arange(h, dtype=F32)
    ret_decay_logit = jnp.log(2.0 ** expo - 1.0) + normal((DEPTH, 2, h), 0.01)
    ret_norm_w = 1.0 + normal((DEPTH, RET_WIDTH), 0.02)
    mlstm_igate_b = normal((DEPTH, 2, MLSTM_HEADS), 0.1)
    mlstm_fgate_b = jnp.linspace(3.0, 6.0, MLSTM_HEADS, dtype=F32) + normal((DEPTH, 2, MLSTM_HEADS), 0.1)
    mlstm_norm_w = 1.0 + normal((DEPTH, MLSTM_WIDTH), 0.02)
    w_ff1 = normal((DEPTH, D_MODEL, D_FF), D_MODEL ** -0.5)
    w_ff2 = normal((DEPTH, D_FF, D_MODEL), D_FF ** -0.5)
    norm_f_w = 1.0 + normal((D_MODEL,), 0.02)
    return {'x': x, 'c': c, 'ctx': ctx, 'c_ctx': c_ctx, 'w_mod': w_mod, 'b_mod': b_mod,
            'norm1_w': norm1_w, 'norm2_w': norm2_w, 'w_in': w_in, 'w_out': w_out,
            's5_lam_re': s5_lam_re, 's5_lam_im': s5_lam_im, 's5_log_step': s5_log_step,
            's5_b_re': s5_b_re, 's5_b_im': s5_b_im, 's5_c_re': s5_c_re, 's5_c_im': s5_c_im,
            's5_d': s5_d, 's5_w_glu': s5_w_glu, 's5_b_glu': s5_b_glu,
            'ret_decay_logit': ret_decay_logit, 'ret_norm_w': ret_norm_w,
            'mlstm_igate_b': mlstm_igate_b, 'mlstm_fgate_b': mlstm_fgate_b, 'mlstm_norm_w': mlstm_norm_w,
            'w_ff1': w_ff1, 'w_ff2': w_ff2, 'norm_f_w': norm_f_w}


def reference(x, c, ctx, c_ctx, w_mod, b_mod, norm1_w, norm2_w, w_in, w_out,
              s5_lam_re, s5_lam_im, s5_log_step, s5_b_re, s5_b_im, s5_c_re, s5_c_im,
              s5_d, s5_w_glu, s5_b_glu, ret_decay_logit, ret_norm_w,
              mlstm_igate_b, mlstm_fgate_b, mlstm_norm_w, w_ff1, w_ff2, norm_f_w):
    dt = x.dtype
    n_lat = x.shape[1]
    rows_count = n_lat // GRID_W
    rows = jnp.repeat(jnp.arange(rows_count, dtype=F32), GRID_W)
    cols = jnp.tile(jnp.arange(GRID_W, dtype=F32), rows_count)
    cos, sin = _rope_2d_tables(rows, cols)
    silu_c = jax.nn.silu(c)[:, None, :]
    silu_cc = jax.nn.silu(c_ctx)[None, None, :]
    h = ctx
    for l in range(DEPTH):
        last = l == DEPTH - 1
        mod_x = jnp.split(silu_c @ w_mod[l] + b_mod[l], 6, axis=-1)
        mod_h = jnp.split(silu_cc @ w_mod[l] + b_mod[l], 6, axis=-1)
        a_x = _modulate(_rmsnorm(x, norm1_w[l]), mod_x[0], mod_x[1]) @ w_in[l]
        a_h = _modulate(_rmsnorm(h, norm1_w[l]), mod_h[0], mod_h[1]) @ w_in[l]
        y_x, y_h = _token_mix(a_x, a_h, cos, sin, not last,
                              s5_lam_re[l], s5_lam_im[l], s5_log_step[l], s5_b_re[l], s5_b_im[l],
                              s5_c_re[l], s5_c_im[l], s5_d[l], s5_w_glu[l], s5_b_glu[l],
                              ret_decay_logit[l], ret_norm_w[l],
                              mlstm_igate_b[l], mlstm_fgate_b[l], mlstm_norm_w[l])
        x = x + mod_x[2] * (y_x.astype(dt) @ w_out[l])
        x = x + mod_x[5] * _sq_relu_mlp(_modulate(_rmsnorm(x, norm2_w[l]), mod_x[3], mod_x[4]), w_ff1[l], w_ff2[l])
        if not last:
            h = h + mod_h[2] * (y_h.astype(dt) @ w_out[l])
            h = h + mod_h[5] * _sq_relu_mlp(_modulate(_rmsnorm(h, norm2_w[l]), mod_h[3], mod_h[4]), w_ff1[l], w_ff2[l])
    return _rmsnorm(x, norm_f_w)
```

```python
import math
from contextlib import ExitStack
import numpy as np
import ml_dtypes
import concourse.bass as bass
import concourse.mybir as mybir
from concourse.bass_utils import run_bass_kernel_spmd

F32 = mybir.dt.float32
BF16 = mybir.dt.bfloat16
I32 = mybir.dt.int32
AF = mybir.ActivationFunctionType
ALU = mybir.AluOpType
AX = mybir.AxisListType

D = 2048
NB = 4
SEQ = 2048
CTX = 256
T = SEQ + CTX
NCH = T // 128
DEPTH = 2
INW = 6680
DFF = 8192
EPS = 1e-6
NDMASEM = 12
STRICT = True


class Tok:
    __slots__ = ("w", "r")

    def __init__(self):
        self.w = None
        self.r = []


class _Rec:
    def __init__(self):
        self.call = None

    def __getattr__(self, name):
        def f(*a, **k):
            self.call = (name, a, k)
            return self
        return f


class Builder:
    def __init__(self, nc, es):
        self.nc = nc
        self.es = es
        self.engs = ["pe", "dve", "act", "pool", "sp"]
        self.ops = {e: [] for e in self.engs}
        self.seq = {e: 0 for e in self.engs}
        self.waited = {e: {} for e in self.engs}
        self.sems = {}
        for e in ["pe", "dve", "act", "pool"]:
            self.sems[("e", e)] = es.enter_context(nc.semaphore("p_" + e))
        self.dcount = {}
        self.drr = {"sp": 0, "pool": 0, "act": 0}
        for q in ["sp", "pool", "act"]:
            for i in range(NDMASEM):
                self.sems[("d", q, i)] = es.enter_context(nc.semaphore("d_%s%d" % (q, i)))
                self.dcount[("d", q, i)] = 0
        self.final = []

    def _need(self, eng, deps):
        out = []
        for (k, v) in deps:
            if k == ("e", eng) and not (STRICT or eng == "pool"):
                continue
            if k == ("e", eng) and eng == "pe":
                continue
            if self.waited[eng].get(k, 0) >= v:
                continue
            self.waited[eng][k] = v
            out.append((k, v))
        return out

    def _deps(self, reads, writes):
        deps = []
        for t in reads:
            if t.w is not None:
                deps.append(t.w)
        for t in writes:
            if t.w is not None:
                deps.append(t.w)
            deps.extend(t.r)
        return deps

    def op(self, eng, fn, reads=(), writes=()):
        deps = self._deps(reads, writes)
        waits = self._need(eng, deps)
        self.seq[eng] += 1
        done = (("e", eng), self.seq[eng])
        rec = _Rec()
        fn(rec)
        call = rec.call
        fn = lambda e, call=call: getattr(e, call[0])(*call[1], **call[2])
        self.ops[eng].append((waits, fn, done))
        for t in reads:
            t.r.append(done)
        for t in writes:
            t.w = done
            t.r = []
        return done

    def dma(self, q, out, in_, reads=(), writes=(), **kw):
        i = self.drr[q]
        self.drr[q] = (i + 1) % NDMASEM
        k = ("d", q, i)
        deps = self._deps(reads, writes)
        if self.dcount[k] > 0:
            deps.append((k, self.dcount[k]))
        waits = self._need(q, deps)
        self.dcount[k] += 16
        done = (k, self.dcount[k])
        fn = lambda e, out=out, in_=in_, kw=kw: e.dma_start(out=out, in_=in_, **kw)
        self.ops[q].append((waits, fn, done))
        for t in reads:
            t.r.append(done)
        for t in writes:
            t.w = done
            t.r = []
        return done

    def barrier(self):
        allk = [(("e", e), self.seq[e]) for e in ["pe", "dve", "act", "pool"] if self.seq[e] > 0]
        allk += [(k, v) for k, v in self.dcount.items() if v > 0]
        for e in self.engs:
            waits = self._need(e, allk)
            if waits:
                self.ops[e].append((waits, None, None))

    def replay(self):
        nc = self.nc
        block = self.es.enter_context(nc.Block())
        hw = {"pe": block.tensor, "dve": block.vector, "act": block.scalar, "pool": block.gpsimd, "sp": block.sync}
        for e in self.engs:
            ops = self.ops[e]

            def body(eng, ops=ops):
                for (waits, fn, done) in ops:
                    for (k, v) in waits:
                        eng.wait_ge(self.sems[k], v)
                    if fn is None:
                        continue
                    inst = fn(eng)
                    if done[0][0] == "e":
                        inst.then_inc(self.sems[done[0]], 1)
                    else:
                        inst.then_inc(self.sems[done[0]], 16)

            hw[e](body)


class _Stop(Exception):
    pass


def build_program(debug=(), upto=None):
    nc = bass.Bass("TRN2", target_bir_lowering=False)
    es = ExitStack()
    B = Builder(nc, es)
    dbg = {}

    def din(name, shape, dt=F32):
        return nc.dram_tensor(name, list(shape), dt, kind="ExternalInput").ap()

    def dscr(name, shape, dt=F32):
        kind = "ExternalOutput" if name in debug else "Internal"
        ap = nc.dram_tensor(name, list(shape), dt, kind=kind).ap()
        if name in debug:
            dbg[name] = ap
        return ap

    def dump(name, ap, toks):
        if name not in debug or name in dbg:
            return
        t = nc.dram_tensor(name, list(ap.shape), ap.dtype, kind="ExternalOutput").ap()
        dbg[name] = t
        B.dma("sp", t, ap, reads=toks, writes=[Tok()])

    def sb(name, shape, dt=F32):
        return es.enter_context(nc.sbuf_tensor(name, list(shape), dt))

    xh = din("xh", [T, D])
    cc = din("cc", [2, D])
    w_mod = din("w_mod", [DEPTH, D, 6 * D])
    b_mod = din("b_mod", [DEPTH, 6 * D])
    norm1_w = din("norm1_w", [DEPTH, D])
    norm2_w = din("norm2_w", [DEPTH, D])
    w_in = din("w_in", [DEPTH, D, INW])
    w_out = din("w_out", [DEPTH, D, D])
    lam_re = din("s5_lam_re", [DEPTH, 2, 32, 64])
    lam_im = din("s5_lam_im", [DEPTH, 2, 32, 64])
    log_step = din("s5_log_step", [DEPTH, 2, 32])
    s5b_re = din("s5_b_re", [DEPTH, 2, 32, 64, 16])
    s5b_im = din("s5_b_im", [DEPTH, 2, 32, 64, 16])
    s5c_re = din("s5_c_re", [DEPTH, 2, 32, 16, 64])
    s5c_im = din("s5_c_im", [DEPTH, 2, 32, 16, 64])
    s5_d = din("s5_d", [DEPTH, 512])
    w_glu = din("s5_w_glu", [DEPTH, 512, 1024])
    b_glu = din("s5_b_glu", [DEPTH, 1024])
    ret_logit = din("ret_decay_logit", [DEPTH, 2, 6])
    ret_nw = din("ret_norm_w", [DEPTH, 768])
    ig_b = din("mlstm_igate_b", [DEPTH, 2, 6])
    fg_b = din("mlstm_fgate_b", [DEPTH, 2, 6])
    ml_nw = din("mlstm_norm_w", [DEPTH, 768])
    w_ff1 = din("w_ff1", [DEPTH, D, DFF])
    w_ff2 = din("w_ff2", [DEPTH, DFF, D])
    norm_f = din("norm_f_w", [D])
    k_ident = din("k_ident", [128, 128])
    k_ufwd = din("k_ufwd", [128, 128])
    k_ubwd = din("k_ubwd", [128, 128])
    k_self = din("k_self", [128, 128])
    k_selb = din("k_selb", [128, 128])
    k_cos = din("k_cos", [128, NCH, 64])
    k_sin = din("k_sin", [128, NCH, 64])
    k_mask8 = din("k_mask8", [128, 8])
    k_tpos = din("k_tpos", [128, 2])
    out = nc.dram_tensor("out", [SEQ, D], F32, kind="ExternalOutput").ap()

    modd = dscr("modd", [DEPTH, 2, 6 * D])
    a16 = dscr("a16", [T, 6656], BF16)
    ymix = dscr("ymix", [T, D], BF16)
    resA = dscr("resA", [T, D])
    resB = dscr("resB", [T, D])
    h1d = dscr("h1d", [NCH, 128, 64, 128], BF16)

    PS = [es.enter_context(nc.psum_tensor("ps%d" % i, [128, 512], F32)) for i in range(8)]
    PST = [Tok() for _ in range(8)]

    ident_f = sb("ident_f", [128, 128]); ident_b = sb("ident_b", [128, 128], BF16)
    nident_b = sb("nident_b", [128, 128], BF16)
    ufwd_f = sb("ufwd_f", [128, 128]); ubwd_f = sb("ubwd_f", [128, 128])
    ufwd_b = sb("ufwd_b", [128, 128], BF16); ubwd_b = sb("ubwd_b", [128, 128], BF16)
    nufwd_b = sb("nufwd_b", [128, 128], BF16); nubwd_b = sb("nubwd_b", [128, 128], BF16)
    self_b = sb("self_b", [128, 128], BF16); selb_b = sb("selb_b", [128, 128], BF16)
    nself_b = sb("nself_b", [128, 128], BF16); nselb_b = sb("nselb_b", [128, 128], BF16)
    ones_f = sb("ones_f", [128, 128])
    mask8 = sb("mask8", [128, 8]); tpos = sb("tpos", [128, 2])
    cos_t = sb("cos_t", [128, NCH, 64]); sin_t = sb("sin_t", [128, NCH, 64])
    ctok = Tok()
    actT = sb("actT", [128, 16, T], BF16)
    actT_tok = [Tok() for _ in range(NCH)]
    arenaB = sb("arenaB", [128, 20480])
    arenaB_bf = arenaB[:].bitcast(BF16)
    gates_sb = sb("gates_sb", [128, NCH, 24]); gates_tok = Tok()
    small = sb("small", [128, 256]);

    def setup():
        stg = arenaB
        loads = [(k_ident, 0), (k_ufwd, 128), (k_ubwd, 256), (k_self, 384), (k_selb, 512)]
        for ap, o in loads:
            B.dma("sp", stg[:, o:o + 128], ap, writes=[ctok])
        B.dma("sp", mask8[:], k_mask8, writes=[ctok])
        B.dma("sp", tpos[:], k_tpos, writes=[ctok])
        B.dma("sp", cos_t[:], k_cos, writes=[ctok])
        B.dma("sp", sin_t[:], k_sin, writes=[ctok])
        cp = lambda o, i: B.op("dve", lambda e, o=o, i=i: e.tensor_copy(out=o, in_=i), reads=[ctok], writes=[ctok])
        ng = lambda o, i: B.op("dve", lambda e, o=o, i=i: e.tensor_scalar(out=o, in0=i, scalar1=-1.0, scalar2=None, op0=ALU.mult), reads=[ctok], writes=[ctok])
        cp(ident_f[:], stg[:, 0:128]); cp(ident_b[:], stg[:, 0:128]); ng(nident_b[:], stg[:, 0:128])
        cp(ufwd_f[:], stg[:, 128:256]); cp(ufwd_b[:], stg[:, 128:256]); ng(nufwd_b[:], stg[:, 128:256])
        cp(ubwd_f[:], stg[:, 256:384]); cp(ubwd_b[:], stg[:, 256:384]); ng(nubwd_b[:], stg[:, 256:384])
        cp(self_b[:], stg[:, 384:512]); ng(nself_b[:], stg[:, 384:512])
        cp(selb_b[:], stg[:, 512:640]); ng(nselb_b[:], stg[:, 512:640])
        B.op("dve", lambda e: e.memset(ones_f[:], 1.0), writes=[ctok])
        B.barrier()

    def phase_mod():
        cT = sb("cT", [128, 16, 2]); sT = sb("sT", [128, 16, 2]); t_c = Tok()
        for s in range(2):
            B.dma("sp", cT[:, :, s], cc[s].rearrange("(c p) -> p c", p=128), writes=[t_c], allow_slow_non_contiguous=True)
        B.op("act", lambda e: e.activation(out=sT[:], in_=cT[:], func=AF.Silu), reads=[t_c], writes=[t_c])
        wst = [arenaB[:, i * 2048:(i + 1) * 2048] for i in range(4)]
        wtok = [Tok() for _ in range(4)]
        bst = [arenaB[0:2, 8192 + i * 512: 8192 + (i + 1) * 512] for i in range(2)]
        btok = [Tok() for _ in range(2)]
        ost = [arenaB[0:2, 9216 + i * 512: 9216 + (i + 1) * 512] for i in range(2)]
        otok = [Tok() for _ in range(2)]
        n = 0
        for l in range(DEPTH):
            for cb in range(24):
                bi = cb % 2
                B.dma("sp", bst[bi], b_mod[l, cb * 512:(cb + 1) * 512].partition_broadcast(2), writes=[btok[bi]])
                pt = PST[cb % 2]; ps = PS[cb % 2]
                for kg in range(4):
                    wi = n % 4; n += 1
                    src = w_mod[l, kg * 512:(kg + 1) * 512, cb * 512:(cb + 1) * 512].rearrange("(k p) n -> p k n", p=128)
                    B.dma("act" if kg % 2 else "sp", wst[wi].rearrange("p (k n) -> p k n", k=4), src, writes=[wtok[wi]])
                    for k4 in range(4):
                        kc = kg * 4 + k4
                        B.op("pe", lambda e, ps=ps, kc=kc, wi=wi, k4=k4: e.matmul(ps[0:2, :], lhsT=sT[:, kc, :], rhs=wst[wi][:, k4 * 512:(k4 + 1) * 512], start=(kc == 0), stop=(kc == 15)),
                             reads=[t_c, wtok[wi]], writes=[pt])
                B.op("dve", lambda e, ps=ps, bi=bi: e.tensor_tensor(out=ost[bi], in0=ps[0:2, :], in1=bst[bi], op=ALU.add), reads=[pt, btok[bi]], writes=[otok[bi]])
                B.dma("sp", modd[l, :, cb * 512:(cb + 1) * 512], ost[bi], reads=[otok[bi]], writes=[modtok])
        B.barrier()

    modtok = Tok()

    def load_vec16(dst, src, tok):
        B.dma("sp", dst, src.rearrange("(c p) -> p c", p=128), reads=[modtok], writes=[tok], allow_slow_non_contiguous=True)

    gsh = sb("gsh", [128, 2, 2, 2, 16]); gsh_tok = Tok()
    gate_bc = sb("gate_bc", [128, 2, D]); gate_tok = Tok()

    def prep_norm_mod(l):
        nw = small[:, 0:32].rearrange("p (a c) -> p a c", a=2); ntok = Tok()
        load_vec16(nw[:, 0, :], norm1_w[l], ntok); load_vec16(nw[:, 1, :], norm2_w[l], ntok)
        tmp = small[:, 32:160].rearrange("p (a s g c) -> p a s g c", a=2, s=2, g=2)
        for a in range(2):
            for s in range(2):
                load_vec16(tmp[:, a, s, 1, :], modd[l, s, (3 * a) * D:(3 * a + 1) * D], ntok)
                load_vec16(tmp[:, a, s, 0, :], modd[l, s, (3 * a + 1) * D:(3 * a + 2) * D], ntok)
        for a in range(2):
            for s in range(2):
                B.op("dve", lambda e, a=a, s=s: e.scalar_tensor_tensor(out=gsh[:, a, s, 0, :], in0=tmp[:, a, s, 0, :], scalar=1.0, in1=nw[:, a, :], op0=ALU.add, op1=ALU.mult),
                     reads=[ntok], writes=[gsh_tok])
                B.op("dve", lambda e, a=a, s=s: e.tensor_copy(out=gsh[:, a, s, 1, :], in_=tmp[:, a, s, 1, :]), reads=[ntok], writes=[gsh_tok])

    def load_gate(l, which):
        for s in range(2):
            B.dma("sp", gate_bc[:, s, :], modd[l, s, which * D:(which + 1) * D].partition_broadcast(128), reads=[modtok], writes=[gate_tok])

    def phase_norm_T(src, a_idx, chunks):
        xin = [arenaB[:, i * 2048:(i + 1) * 2048] for i in range(2)]; xtok = [Tok() for _ in range(2)]
        xs = [arenaB_bf[:, 8192 + i * 2048: 8192 + (i + 1) * 2048] for i in range(2)]; xstok = [Tok() for _ in range(2)]
        junk = arenaB_bf[:, 12288:14336]; jtok = Tok()
        st = small[:, 160:176]; sttok = [Tok() for _ in range(2)]
        for n, tc in enumerate(chunks):
            bi = n % 2
            s = 1 if tc < 2 else 0
            B.dma("sp", xin[bi], src[tc * 128:(tc + 1) * 128, :], writes=[xtok[bi]])
            ss = st[:, bi * 4:bi * 4 + 1]; rs = st[:, bi * 4 + 1:bi * 4 + 2]
            B.op("act", lambda e, bi=bi, ss=ss: e.activation(out=junk, in_=xin[bi], func=AF.Square, accum_out=ss), reads=[xtok[bi]], writes=[jtok, sttok[bi]])
            B.op("dve", lambda e, ss=ss, rs=rs: e.tensor_scalar(out=rs, in0=ss, scalar1=1.0 / D, scalar2=EPS, op0=ALU.mult, op1=ALU.add), reads=[sttok[bi]], writes=[sttok[bi]])
            B.op("act", lambda e, rs=rs: e.activation(out=rs, in_=rs, func=AF.Sqrt), reads=[sttok[bi]], writes=[sttok[bi]])
            B.op("dve", lambda e, rs=rs: e.reciprocal(out=rs, in_=rs), reads=[sttok[bi]], writes=[sttok[bi]])
            B.op("act", lambda e, bi=bi, rs=rs: e.activation(out=xs[bi], in_=xin[bi], func=AF.Copy, scale=rs), reads=[xtok[bi], sttok[bi]], writes=[xstok[bi]])
            for q in range(4):
                pi = 4 + (n * 4 + q) % 4
                psb = PS[pi][:].bitcast(BF16)
                for j in range(4):
                    kc = q * 4 + j
                    B.op("pe", lambda e, psb=psb, j=j, kc=kc, bi=bi: e.transpose(out=psb[:, j * 128:(j + 1) * 128], in_=xs[bi][:, kc * 128:(kc + 1) * 128], identity=ident_b[:]),
                         reads=[xstok[bi], ctok], writes=[PST[pi]])
                for j in range(4):
                    kc = q * 4 + j
                    B.op("dve", lambda e, psb=psb, j=j, kc=kc, tc=tc, s=s: e.tensor_scalar(out=actT[:, kc, tc * 128:(tc + 1) * 128], in0=psb[:, j * 128:(j + 1) * 128],
                                                                                 scalar1=gsh[:, a_idx, s, 0, kc:kc + 1], scalar2=gsh[:, a_idx, s, 1, kc:kc + 1], op0=ALU.mult, op1=ALU.add),
                         reads=[PST[pi], gsh_tok], writes=[actT_tok[tc]])

    def phase_plain_T(src, chunks):
        xs = [arenaB_bf[:, i * 2048:(i + 1) * 2048] for i in range(2)]; xstok = [Tok() for _ in range(2)]
        for n, tc in enumerate(chunks):
            bi = n % 2
            B.dma("sp", xs[bi], src[tc * 128:(tc + 1) * 128, :], writes=[xstok[bi]])
            for q in range(4):
                pi = 4 + (n * 4 + q) % 4
                psb = PS[pi][:].bitcast(BF16)
                for j in range(4):
                    kc = q * 4 + j
                    B.op("pe", lambda e, psb=psb, j=j, kc=kc, bi=bi: e.transpose(out=psb[:, j * 128:(j + 1) * 128], in_=xs[bi][:, kc * 128:(kc + 1) * 128], identity=ident_b[:]),
                         reads=[xstok[bi], ctok], writes=[PST[pi]])
                B.op("act", lambda e, psb=psb, q=q, tc=tc: e.activation(out=actT[:, q * 4:(q + 1) * 4, tc * 128:(tc + 1) * 128], in_=psb[:, 0:512].rearrange("p (j t) -> p j t", j=4), func=AF.Copy),
                     reads=[PST[pi]], writes=[actT_tok[tc]])

    WB_OFF = 16384
    wstage = [arenaB[:, 4096 + i * 2048: 4096 + (i + 1) * 2048] for i in range(2)]; wstok = [Tok() for _ in range(2)]
    wcnt = [0]

    def load_wblock(wsrc, c0, w, KC, dst, dtok):
        g = max(1, 2048 // w)
        for k0 in range(0, KC, g):
            kk = min(g, KC - k0)
            si = wcnt[0] % 2; wcnt[0] += 1
            src = wsrc[k0 * 128:(k0 + kk) * 128, c0:c0 + w].rearrange("(k p) n -> p k n", p=128)
            B.dma("sp" if si else "act", wstage[si][:, 0:kk * w].rearrange("p (k n) -> p k n", k=kk), src, writes=[wstok[si]])
            B.op("pool", lambda e, si=si, k0=k0, kk=kk: e.tensor_copy(out=dst[:, k0 * w:(k0 + kk) * w], in_=wstage[si][:, 0:kk * w]), reads=[wstok[si]], writes=[dtok])


    wbuf = [arenaB_bf[:, 16384 + i * 8192: 16384 + (i + 1) * 8192] for i in range(2)]; wbtok = [Tok() for _ in range(2)]
    ostg_f = [arenaB[:, 16384 + i * 512: 16384 + (i + 1) * 512] for i in range(4)]; ostok = [Tok() for _ in range(4)]
    ostg2_f = [arenaB[:, 18432 + i * 512: 18432 + (i + 1) * 512] for i in range(4)]; os2tok = [Tok() for _ in range(4)]
    cnt = {"ps": 0, "o": 0}

    def proj_tok(wsrc, col_blocks, chunks, epilogue):
        for cbi, (c0, w) in enumerate(col_blocks):
            bi = cbi % 2
            load_wblock(wsrc, c0, w, 16, wbuf[bi], wbtok[bi])
            for tc in chunks:
                pi = cnt["ps"] % 4; cnt["ps"] += 1
                for kc in range(16):
                    B.op("pe", lambda e, pi=pi, kc=kc, tc=tc, bi=bi, w=w: e.matmul(PS[pi][:, 0:w], lhsT=actT[:, kc, tc * 128:(tc + 1) * 128], rhs=wbuf[bi][:, kc * w:(kc + 1) * w], start=(kc == 0), stop=(kc == 15)),
                         reads=[actT_tok[tc], wbtok[bi]], writes=[PST[pi]])
                epilogue(tc, c0, w, pi)

    def ep_inproj(tc, c0, w, pi):
        if c0 >= 6656:
            B.op("act", lambda e: e.activation(out=gates_sb[:, tc, :], in_=PS[pi][:, 0:24], func=AF.Copy), reads=[PST[pi]], writes=[gates_tok])
            return
        oi = cnt["o"] % 4; cnt["o"] += 1
        ob = ostg_f[oi].bitcast(BF16)[:, 0:512]
        B.op("act", lambda e: e.activation(out=ob, in_=PS[pi][:, 0:512], func=AF.Copy), reads=[PST[pi]], writes=[ostok[oi]])
        B.dma("sp", a16[tc * 128:(tc + 1) * 128, c0:c0 + 512], ob, reads=[ostok[oi]], writes=[a16_tok])

    a16_tok = Tok(); ymix_tok = Tok(); res_tok = {"A": Tok(), "B": Tok()}; h1_tok = Tok()

    def make_ep_resid(rsrc, rsrc_tok, rdst, rdst_tok):
        def ep(tc, c0, w, pi):
            s = 1 if tc < 2 else 0
            oi = cnt["o"] % 4; cnt["o"] += 1
            xo = ostg2_f[oi][:, 0:w]; tm = ostg_f[oi][:, 0:w]
            B.dma("act", xo, rsrc[tc * 128:(tc + 1) * 128, c0:c0 + w], reads=[rsrc_tok] if rsrc_tok else [], writes=[os2tok[oi]])
            B.op("dve", lambda e: e.tensor_tensor(out=tm, in0=PS[pi][:, 0:w], in1=gate_bc[:, s, c0:c0 + w], op=ALU.mult), reads=[PST[pi], gate_tok], writes=[ostok[oi]])
            B.op("pool", lambda e: e.tensor_tensor(out=tm, in0=tm, in1=xo, op=ALU.add), reads=[ostok[oi], os2tok[oi]], writes=[ostok[oi]])
            B.dma("sp", rdst[tc * 128:(tc + 1) * 128, c0:c0 + w], tm, reads=[ostok[oi]], writes=[rdst_tok])
        return ep

    def phase_ffn1(wsrc, chunks):
        blocks = []
        cl = list(chunks)
        for i in range(0, len(cl), 4):
            blocks.append(cl[i:i + 4])
        for cb in range(16):
            bi = cb % 2
            load_wblock(wsrc, cb * 512, 512, 16, wbuf[bi], wbtok[bi])
            for fs in range(4):
                kcf = cb * 4 + fs
                for blk in blocks:
                    t0 = blk[0] * 128; N = len(blk) * 128
                    pi = cnt["ps"] % 4; cnt["ps"] += 1
                    for kc in range(16):
                        B.op("pe", lambda e, pi=pi, kc=kc, bi=bi, fs=fs, t0=t0, N=N: e.matmul(PS[pi][:, 0:N], lhsT=wbuf[bi][:, kc * 512 + fs * 128: kc * 512 + (fs + 1) * 128], rhs=actT[:, kc, t0:t0 + N], start=(kc == 0), stop=(kc == 15)),
                             reads=[actT_tok[t] for t in blk] + [wbtok[bi]], writes=[PST[pi]])
                    oi = cnt["o"] % 4; cnt["o"] += 1
                    r = ostg2_f[oi][:, 0:N]; hb = ostg_f[oi].bitcast(BF16)[:, 0:N]
                    B.op("act", lambda e, pi=pi, r=r, N=N: e.activation(out=r, in_=PS[pi][:, 0:N], func=AF.Relu), reads=[PST[pi]], writes=[os2tok[oi]])
                    B.op("dve", lambda e, r=r, hb=hb: e.tensor_tensor(out=hb, in0=r, in1=r, op=ALU.mult), reads=[os2tok[oi]], writes=[ostok[oi]])
                    B.dma("sp", h1d[blk[0]:blk[0] + len(blk), :, kcf, :].rearrange("t p j -> p t j"), hb.rearrange("p (t j) -> p t j", j=128), reads=[ostok[oi]], writes=[h1_tok])

    def phase_ffn2(wsrc, chunks, epilogue):
        actv = actT[:].rearrange("p a b -> p (a b)")
        wb2 = [actv[:, i * 16384:(i + 1) * 16384] for i in range(2)]; wb2tok = [Tok() for _ in range(2)]
        hst = [arenaB_bf[:, 16384 + i * 8192: 16384 + (i + 1) * 8192] for i in range(2)]; hstok = [Tok() for _ in range(2)]
        n = 0
        for cb in range(8):
            bi = cb % 2
            load_wblock(wsrc, cb * 256, 256, 64, wb2[bi], wb2tok[bi])
            for tc in chunks:
                hi = n % 2; n += 1
                B.dma("sp", hst[hi], h1d[tc].rearrange("p k j -> p (k j)"), reads=[h1_tok], writes=[hstok[hi]])
                pi = cnt["ps"] % 4; cnt["ps"] += 1
                for kc in range(64):
                    B.op("pe", lambda e, pi=pi, kc=kc, hi=hi, bi=bi: e.matmul(PS[pi][:, 0:256], lhsT=hst[hi][:, kc * 128:(kc + 1) * 128], rhs=wb2[bi][:, kc * 256:(kc + 1) * 256], start=(kc == 0), stop=(kc == 63)),
                         reads=[hstok[hi], wb2tok[bi]], writes=[PST[pi]])
                epilogue(tc, cb * 256, 256, pi)

    def phase_final(src, src_tok):
        nf = gate_bc[:, 0, :]
        B.dma("sp", nf, norm_f.partition_broadcast(128), writes=[gate_tok])
        xin = [arenaB[:, i * 2048:(i + 1) * 2048] for i in range(2)]; xtok = [Tok() for _ in range(2)]
        yo = [arenaB[:, 4096 + i * 2048: 4096 + (i + 1) * 2048] for i in range(2)]; ytok = [Tok() for _ in range(2)]
        junk = arenaB_bf[:, 16384:18432]; jtok = Tok()
        st = small[:, 160:176]; sttok = [Tok() for _ in range(2)]
        outtok = Tok()
        for n, tc in enumerate(range(2, NCH)):
            bi = n % 2
            B.dma("sp", xin[bi], src[tc * 128:(tc + 1) * 128, :], reads=[src_tok], writes=[xtok[bi]])
            ss = st[:, bi * 4:bi * 4 + 1]; rs = st[:, bi * 4 + 1:bi * 4 + 2]
            B.op("act", lambda e, bi=bi, ss=ss: e.activation(out=junk, in_=xin[bi], func=AF.Square, accum_out=ss), reads=[xtok[bi]], writes=[jtok, sttok[bi]])
            B.op("dve", lambda e, ss=ss, rs=rs: e.tensor_scalar(out=rs, in0=ss, scalar1=1.0 / D, scalar2=EPS, op0=ALU.mult, op1=ALU.add), reads=[sttok[bi]], writes=[sttok[bi]])
            B.op("act", lambda e, rs=rs: e.activation(out=rs, in_=rs, func=AF.Sqrt), reads=[sttok[bi]], writes=[sttok[bi]])
            B.op("dve", lambda e, rs=rs: e.reciprocal(out=rs, in_=rs), reads=[sttok[bi]], writes=[sttok[bi]])
            B.op("act", lambda e, bi=bi, rs=rs: e.activation(out=yo[bi], in_=xin[bi], func=AF.Copy, scale=rs), reads=[xtok[bi], sttok[bi]], writes=[ytok[bi]])
            B.op("dve", lambda e, bi=bi: e.tensor_tensor(out=yo[bi], in0=yo[bi], in1=nf, op=ALU.mult), reads=[ytok[bi], gate_tok], writes=[ytok[bi]])
            d = B.dma("sp", out[(tc - 2) * 128:(tc - 1) * 128, :], yo[bi], reads=[ytok[bi]], writes=[outtok])

    A32 = actT[:].rearrange("p a b -> p (a b)").bitcast(F32)
    A16 = actT[:].rearrange("p a b -> p (a b)")
    B32 = arenaB; B16 = arenaB_bf
    ORD = [list(range(NCH)), [1, 0] + list(range(NCH - 1, 1, -1))]
    TWO_PI = 2.0 * math.pi

    def v3(ap, a, b):
        return ap.rearrange("p (a b) -> p a b", a=a, b=b)

    def dve(fn, reads, writes):
        return B.op("dve", fn, reads=reads, writes=writes)

    def act(fn, reads, writes):
        return B.op("act", fn, reads=reads, writes=writes)

    def pool(fn, reads, writes):
        return B.op("pool", fn, reads=reads, writes=writes)

    def pe(fn, reads, writes):
        return B.op("pe", fn, reads=reads, writes=writes)

    def range_reduce_sincos(ph, kint, sinv, cosv, tk, shape_note=None):
        dve(lambda e: e.tensor_copy(out=kint, in_=ph), [tk], [tk])
        dve(lambda e: e.tensor_copy(out=cosv, in_=kint), [tk], [tk])
        dve(lambda e: e.tensor_tensor(out=ph, in0=ph, in1=cosv, op=ALU.subtract), [tk], [tk])
        act(lambda e: e.activation(out=sinv, in_=ph, func=AF.Sin, scale=TWO_PI), [tk], [tk])
        dve(lambda e: e.tensor_scalar(out=ph, in0=ph, scalar1=0.25, scalar2=None, op0=ALU.add), [tk], [tk])
        dve(lambda e: e.tensor_scalar(out=cosv, in0=ph, scalar1=0.5, scalar2=None, op0=ALU.is_gt), [tk], [tk])
        dve(lambda e: e.tensor_tensor(out=ph, in0=ph, in1=cosv, op=ALU.subtract), [tk], [tk])
        act(lambda e: e.activation(out=cosv, in_=ph, func=AF.Sin, scale=TWO_PI), [tk], [tk])

    def phase_s5(l):
        yacc = A32[:, 0:9216]; ytok = [Tok() for _ in range(NCH)]
        uT = A16[:, 18432:27648]; uTtok = Tok()
        Bblk = A16[:, 27648:31744]; Btok = Tok()
        Cmat = A16[:, 31744:35840]; Ctok = Tok()
        misc = A32[:, 17920:18432]
        tab = [B32[:, i * 2048:(i + 1) * 2048] for i in range(4)]; tabtok = Tok()
        Pp = [[B16[:, 16384 + (blk * 4 + k) * 512: 16384 + (blk * 4 + k + 1) * 512] for k in range(4)] for blk in range(4)]
        Ptok = [Tok() for _ in range(4)]
        Zp = [[B16[:, 24576 + (i * 4 + k) * 512: 24576 + (i * 4 + k + 1) * 512] for k in range(4)] for i in range(2)]
        Ztok = [Tok() for _ in range(2)]
        xTt = [B16[:, 28672 + i * 128: 28672 + (i + 1) * 128] for i in range(8)]; xTtok = [Tok() for _ in range(8)]
        u_sb = B16[:, 29696:38912]; utok = Tok()
        dsk = B32[:, 8192:8704]; dtok = Tok()
        B.dma("sp", v3(u_sb, NCH, 512), a16[:, 0:512].rearrange("(c p) n -> p c n", p=128), reads=[a16_tok], writes=[utok])
        B.dma("sp", dsk, s5_d[l].partition_broadcast(128), writes=[dtok])
        for c in range(NCH):
            dve(lambda e, c=c: e.tensor_tensor(out=yacc[:, c * 512:(c + 1) * 512], in0=u_sb[:, c * 512:(c + 1) * 512], in1=dsk, op=ALU.mult), [utok, dtok], [ytok[c]])
            pi = 6 + c % 2
            psb = PS[pi][:].bitcast(BF16)
            for blk in range(4):
                pe(lambda e, psb=psb, blk=blk, c=c: e.transpose(out=psb[:, blk * 128:(blk + 1) * 128], in_=u_sb[:, c * 512 + blk * 128: c * 512 + (blk + 1) * 128], identity=ident_b[:]), [utok, ctok], [PST[pi]])
            act(lambda e, psb=psb, c=c: e.activation(out=v3(uT, 4, T)[:, :, c * 128:(c + 1) * 128], in_=v3(psb[:, 0:512], 4, 128), func=AF.Copy), [PST[pi]], [uTtok])
        B.barrier()
        for d in range(2):
            S = [B32[:, 8192 + i * 2048: 8192 + (i + 1) * 2048] for i in range(5)]
            stok = Tok()
            lrd, th, ph, sinv, cosv = S
            kint = B32[:, 18432:20480].bitcast(I32)
            dtb = misc[:, 0:32]
            ntp = misc[:, 32:33]
            tp = tpos[:, d:d + 1]
            B.dma("sp", lrd, lam_re[l, d].rearrange("g p -> (g p)").partition_broadcast(128), writes=[stok])
            B.dma("sp", th, lam_im[l, d].rearrange("g p -> (g p)").partition_broadcast(128), writes=[stok])
            B.dma("sp", dtb, log_step[l, d].partition_broadcast(128), writes=[stok])
            act(lambda e: e.activation(out=dtb, in_=dtb, func=AF.Exp), [stok], [stok])
            dve(lambda e: e.tensor_scalar(out=ntp, in0=tp, scalar1=-1.0, scalar2=None, op0=ALU.mult), [ctok, stok], [stok])
            dtb3 = dtb[:, :, None].broadcast_to([128, 32, 64])
            dve(lambda e: e.tensor_scalar(out=lrd, in0=lrd, scalar1=-1e-4, scalar2=None, op0=ALU.min), [stok], [stok])
            dve(lambda e: e.tensor_tensor(out=v3(lrd, 32, 64), in0=v3(lrd, 32, 64), in1=dtb3, op=ALU.mult), [stok], [stok])
            dve(lambda e: e.tensor_tensor(out=v3(th, 32, 64), in0=v3(th, 32, 64), in1=dtb3, op=ALU.mult), [stok], [stok])
            dve(lambda e: e.tensor_scalar(out=ph, in0=th, scalar1=tp, scalar2=1.0 / TWO_PI, op0=ALU.mult, op1=ALU.mult), [stok, ctok], [stok])
            range_reduce_sincos(ph, kint, sinv, cosv, stok)
            act(lambda e: e.activation(out=th, in_=lrd, func=AF.Exp, scale=tp), [stok, ctok], [stok])
            dve(lambda e: e.reciprocal(out=ph, in_=th), [stok], [stok])
            dve(lambda e: e.tensor_tensor(out=tab[2], in0=th, in1=cosv, op=ALU.mult), [stok], [tabtok])
            dve(lambda e: e.tensor_tensor(out=tab[3], in0=th, in1=sinv, op=ALU.mult), [stok], [tabtok])
            dve(lambda e: e.tensor_tensor(out=tab[0], in0=ph, in1=cosv, op=ALU.mult), [stok], [tabtok])
            dve(lambda e: e.scalar_tensor_tensor(out=tab[1], in0=ph, scalar=-1.0, in1=sinv, op0=ALU.mult, op1=ALU.mult), [stok], [tabtok])
            B.barrier()
            bre = B32[0:64, 8192:8704]; bim = B32[0:64, 8704:9216]; bbr = B32[0:64, 9216:9728]; bbi = B32[0:64, 9728:10240]
            t1 = B32[0:64, 10240:10752]; t2 = B32[0:64, 10752:11264]
            sm = [B32[0:64, 11264 + i * 32: 11264 + (i + 1) * 32] for i in range(12)]
            smi = B32[0:64, 11776:11808].bitcast(I32)
            btk = Tok()
            B.dma("sp", v3(bre, 32, 16), s5b_re[l, d].rearrange("g p n -> p g n"), writes=[btk])
            B.dma("sp", v3(bim, 32, 16), s5b_im[l, d].rearrange("g p n -> p g n"), writes=[btk])
            lr, li, dt2, mag, phs, sn, cs, are, aim, rden, cr, ci = sm
            B.dma("sp", lr, lam_re[l, d].rearrange("g p -> p g"), writes=[btk], allow_slow_non_contiguous=True)
            B.dma("sp", li, lam_im[l, d].rearrange("g p -> p g"), writes=[btk], allow_slow_non_contiguous=True)
            B.dma("sp", dt2, log_step[l, d].partition_broadcast(64), writes=[btk])
            act(lambda e: e.activation(out=dt2, in_=dt2, func=AF.Exp), [btk], [btk])
            dve(lambda e: e.tensor_scalar(out=lr, in0=lr, scalar1=-1e-4, scalar2=None, op0=ALU.min), [btk], [btk])
            dve(lambda e: e.tensor_tensor(out=mag, in0=lr, in1=dt2, op=ALU.mult), [btk], [btk])
            act(lambda e: e.activation(out=mag, in_=mag, func=AF.Exp), [btk], [btk])
            dve(lambda e: e.scalar_tensor_tensor(out=phs, in0=li, scalar=1.0 / TWO_PI, in1=dt2, op0=ALU.mult, op1=ALU.mult), [btk], [btk])
            range_reduce_sincos(phs, smi, sn, cs, btk)
            dve(lambda e: e.tensor_tensor(out=are, in0=mag, in1=cs, op=ALU.mult), [btk], [btk])
            dve(lambda e: e.tensor_scalar(out=are, in0=are, scalar1=-1.0, scalar2=None, op0=ALU.add), [btk], [btk])
            dve(lambda e: e.tensor_tensor(out=aim, in0=mag, in1=sn, op=ALU.mult), [btk], [btk])
            dve(lambda e: e.tensor_tensor(out=rden, in0=lr, in1=lr, op=ALU.mult), [btk], [btk])
            dve(lambda e: e.tensor_tensor(out=cr, in0=li, in1=li, op=ALU.mult), [btk], [btk])
            dve(lambda e: e.tensor_tensor(out=rden, in0=rden, in1=cr, op=ALU.add), [btk], [btk])
            dve(lambda e: e.reciprocal(out=rden, in_=rden), [btk], [btk])
            dve(lambda e: e.tensor_tensor(out=cr, in0=lr, in1=rden, op=ALU.mult), [btk], [btk])
            dve(lambda e: e.scalar_tensor_tensor(out=ci, in0=li, scalar=-1.0, in1=rden, op0=ALU.mult, op1=ALU.mult), [btk], [btk])
            dve(lambda e: e.tensor_tensor(out=mag, in0=are, in1=cr, op=ALU.mult), [btk], [btk])
            dve(lambda e: e.tensor_tensor(out=sn, in0=aim, in1=ci, op=ALU.mult), [btk], [btk])
            dve(lambda e: e.tensor_tensor(out=mag, in0=mag, in1=sn, op=ALU.subtract), [btk], [btk])
            dve(lambda e: e.tensor_tensor(out=phs, in0=are, in1=ci, op=ALU.mult), [btk], [btk])
            dve(lambda e: e.tensor_tensor(out=sn, in0=aim, in1=cr, op=ALU.mult), [btk], [btk])
            dve(lambda e: e.tensor_tensor(out=phs, in0=phs, in1=sn, op=ALU.add), [btk], [btk])
            nr3 = mag[:, :, None].broadcast_to([64, 32, 16]); ni3 = phs[:, :, None].broadcast_to([64, 32, 16])
            dve(lambda e: e.tensor_tensor(out=v3(t1, 32, 16), in0=v3(bre, 32, 16), in1=nr3, op=ALU.mult), [btk], [btk])
            dve(lambda e: e.tensor_tensor(out=v3(t2, 32, 16), in0=v3(bim, 32, 16), in1=ni3, op=ALU.mult), [btk], [btk])
            dve(lambda e: e.tensor_tensor(out=bbr, in0=t1, in1=t2, op=ALU.subtract), [btk], [btk])
            dve(lambda e: e.tensor_tensor(out=v3(t1, 32, 16), in0=v3(bim, 32, 16), in1=nr3, op=ALU.mult), [btk], [btk])
            dve(lambda e: e.tensor_tensor(out=v3(t2, 32, 16), in0=v3(bre, 32, 16), in1=ni3, op=ALU.mult), [btk], [btk])
            dve(lambda e: e.tensor_tensor(out=bbi, in0=t1, in1=t2, op=ALU.add), [btk], [btk])
            for nm_, ap_ in (("lr", lr), ("li", li), ("dt2", dt2), ("cs", cs), ("are", are), ("aim", aim), ("rden", rden), ("cr", cr), ("ci", ci)):
                dump("s_%s%d" % (nm_, d), ap_, [btk])
            dump("s_bbr%d" % d, bbr, [btk]); dump("s_bbi%d" % d, bbi, [btk]); dump("s_nr%d" % d, mag, [btk]); dump("s_ni%d" % d, phs, [btk])
            m8 = mask8[:, :, None].broadcast_to([128, 8, 64])
            n = 0
            for blk in range(4):
                for ri, bb in enumerate((bbr, bbi)):
                    pi = 6 + n % 2; n += 1
                    pe(lambda e, pi=pi, bb=bb, blk=blk: e.transpose(out=PS[pi][:, 0:64], in_=bb[:, blk * 128:(blk + 1) * 128], identity=ident_f[0:64, 0:64]), [btk, ctok], [PST[pi]])
                    dst = v3(Bblk, 4, 1024)[:, blk, ri * 512:(ri + 1) * 512].rearrange("p (g q) -> p g q", g=8)
                    dve(lambda e, pi=pi, dst=dst: e.tensor_tensor(out=dst, in0=PS[pi][:, None, 0:64].broadcast_to([128, 8, 64]), in1=m8, op=ALU.mult), [PST[pi], ctok], [Btok])
            cnat = [B32[:, 12288 + i * 64: 12288 + (i + 1) * 64] for i in range(8)]
            cntk = Tok()
            for ri, csrc in enumerate((s5c_re, s5c_im)):
                for blk in range(4):
                    B.dma("sp", cnat[ri * 4 + blk], csrc[l, d, blk * 8:(blk + 1) * 8].rearrange("g n p -> (g n) p"), writes=[cntk])
            xm = [B16[:, 26624 + i * 128: 26624 + (i + 1) * 128] for i in range(4)]; xmtok = [Tok() for _ in range(4)]
            n = 0
            for blk in range(4):
                for q in range(4):
                    for ri in range(2):
                        xi = n % 4; pi = 6 + n % 2; n += 1
                        for g2 in range(2):
                            mk = mask8[:, 2 * q + g2: 2 * q + g2 + 1]
                            dve(lambda e, xi=xi, g2=g2, ri=ri, blk=blk, mk=mk: e.tensor_scalar(out=xm[xi][:, g2 * 64:(g2 + 1) * 64], in0=cnat[ri * 4 + blk], scalar1=mk, scalar2=(-1.0 if ri else 1.0), op0=ALU.mult, op1=ALU.mult),
                                [cntk, ctok], [xmtok[xi]])
                        psb = PS[pi][:].bitcast(BF16)
                        pe(lambda e, psb=psb, xi=xi: e.transpose(out=psb[:, 0:128], in_=xm[xi], identity=ident_b[:]), [xmtok[xi], ctok], [PST[pi]])
                        ci_ = (blk * 4 + q) * 2 + ri
                        act(lambda e, psb=psb, ci_=ci_: e.activation(out=Cmat[:, ci_ * 128:(ci_ + 1) * 128], in_=psb[:, 0:128], func=AF.Copy), [PST[pi]], [Ctok])
            B.barrier()
            for i_ in range(4):
                dump("s_tab%d_%d" % (i_, d), tab[i_], [tabtok])
            dump("s_Bblk%d" % d, Bblk, [Btok])
            dump("s_Cmat%d" % d, Cmat, [Ctok])
            U = ufwd_b if d == 0 else ubwd_b; NU = nufwd_b if d == 0 else nubwd_b
            SEL = self_b if d == 0 else selb_b; NSEL = nself_b if d == 0 else nselb_b
            xn = 0
            X6 = [Tok() for _ in range(4)]; Y7 = [Tok() for _ in range(4)]
            for step, c in enumerate(ORD[d]):
                for blk in range(4):
                    zi = blk % 2
                    pr, pim = (0, 1) if zi == 0 else (2, 3)
                    cols = slice(blk * 512, (blk + 1) * 512)
                    lhs_u = v3(uT, 4, T)[:, blk, c * 128:(c + 1) * 128]
                    pe(lambda e, pr=pr, lhs_u=lhs_u, blk=blk: e.matmul(PS[pr][:], lhsT=lhs_u, rhs=v3(Bblk, 4, 1024)[:, blk, 0:512], start=True, stop=True), [uTtok, Btok], [PST[pr]])
                    pe(lambda e, pim=pim, lhs_u=lhs_u, blk=blk: e.matmul(PS[pim][:], lhsT=lhs_u, rhs=v3(Bblk, 4, 1024)[:, blk, 512:1024], start=True, stop=True), [uTtok, Btok], [PST[pim]])
                    Z = Zp[zi]
                    dve(lambda e, Z=Z, pr=pr, cols=cols: e.tensor_tensor(out=Z[0], in0=PS[pr][:], in1=tab[0][:, cols], op=ALU.mult), [PST[pr], tabtok], [Ztok[zi]])
                    dve(lambda e, Z=Z, pim=pim, cols=cols: e.tensor_tensor(out=Z[1], in0=PS[pim][:], in1=tab[1][:, cols], op=ALU.mult), [PST[pim], tabtok], [Ztok[zi]])
                    dve(lambda e, Z=Z, pim=pim, cols=cols: e.tensor_tensor(out=Z[2], in0=PS[pim][:], in1=tab[0][:, cols], op=ALU.mult), [PST[pim], tabtok], [Ztok[zi]])
                    dve(lambda e, Z=Z, pr=pr, cols=cols: e.tensor_tensor(out=Z[3], in0=PS[pr][:], in1=tab[1][:, cols], op=ALU.mult), [PST[pr], tabtok], [Ztok[zi]])
                    P = Pp[blk]
                    first = (step == 0)
                    pe(lambda e, Z=Z: e.matmul(PS[4][:], lhsT=U[:], rhs=Z[0], start=True, stop=False), [Ztok[zi], ctok], [PST[4]])
                    pe(lambda e, Z=Z, first=first: e.matmul(PS[4][:], lhsT=NU[:], rhs=Z[1], start=False, stop=first), [Ztok[zi], ctok], [PST[4]])
                    if not first:
                        pe(lambda e, P=P: e.matmul(PS[4][:], lhsT=SEL[:], rhs=P[0], start=False, stop=False), [Ptok[blk], ctok], [PST[4]])
                        pe(lambda e, P=P: e.matmul(PS[4][:], lhsT=NSEL[:], rhs=P[1], start=False, stop=True), [Ptok[blk], ctok], [PST[4]])
                    pe(lambda e, Z=Z: e.matmul(PS[5][:], lhsT=U[:], rhs=Z[2], start=True, stop=False), [Ztok[zi], ctok], [PST[5]])
                    pe(lambda e, Z=Z, first=first: e.matmul(PS[5][:], lhsT=U[:], rhs=Z[3], start=False, stop=first), [Ztok[zi], ctok], [PST[5]])
                    if not first:
                        pe(lambda e, P=P: e.matmul(PS[5][:], lhsT=SEL[:], rhs=P[2], start=False, stop=False), [Ptok[blk], ctok], [PST[5]])
                        pe(lambda e, P=P: e.matmul(PS[5][:], lhsT=SEL[:], rhs=P[3], start=False, stop=True), [Ptok[blk], ctok], [PST[5]])
                    dve(lambda e, P=P, cols=cols: e.tensor_tensor(out=P[0], in0=PS[4][:], in1=tab[2][:, cols], op=ALU.mult), [PST[4], tabtok], [Ptok[blk]])
                    dve(lambda e, P=P, cols=cols: e.tensor_tensor(out=P[1], in0=PS[5][:], in1=tab[3][:, cols], op=ALU.mult), [PST[5], tabtok], [Ptok[blk]])
                    dve(lambda e, P=P, cols=cols: e.tensor_tensor(out=P[2], in0=PS[5][:], in1=tab[2][:, cols], op=ALU.mult), [PST[5], tabtok], [Ptok[blk]])
                    dve(lambda e, P=P, cols=cols: e.tensor_tensor(out=P[3], in0=PS[4][:], in1=tab[3][:, cols], op=ALU.mult), [PST[4], tabtok], [Ptok[blk]])
                    for q in range(4):
                        qs = slice(q * 128, (q + 1) * 128)
                        xs_ = []
                        for ri in range(2):
                            xi = xn % 8; xn += 1
                            pslot = PS[6][:, (xi % 4) * 128:((xi % 4) + 1) * 128]
                            a0, a1 = (P[0], P[1]) if ri == 0 else (P[2], P[3])
                            idn = nident_b if ri == 0 else ident_b
                            pe(lambda e, pslot=pslot, a0=a0, qs=qs: e.matmul(pslot, lhsT=a0[:, qs], rhs=ident_b[:], start=True, stop=False), [Ptok[blk], ctok], [X6[xi % 4]])
                            pe(lambda e, pslot=pslot, a1=a1, qs=qs, idn=idn: e.matmul(pslot, lhsT=a1[:, qs], rhs=idn[:], start=False, stop=True), [Ptok[blk], ctok], [X6[xi % 4]])
                            act(lambda e, pslot=pslot, xi=xi: e.activation(out=xTt[xi], in_=pslot, func=AF.Copy), [X6[xi % 4]], [xTtok[xi]])
                            xs_.append(xi)
                        yslot = PS[7][:, (blk % 4) * 128:((blk % 4) + 1) * 128]
                        for ri in range(2):
                            ci_ = (blk * 4 + q) * 2 + ri
                            xi = xs_[ri]
                            pe(lambda e, yslot=yslot, xi=xi, ci_=ci_, q=q, ri=ri: e.matmul(yslot, lhsT=xTt[xi], rhs=Cmat[:, ci_ * 128:(ci_ + 1) * 128], start=(q == 0 and ri == 0), stop=(q == 3 and ri == 1)),
                               [xTtok[xi], Ctok], [Y7[blk]])
                    ysl = yacc[:, c * 512 + blk * 128: c * 512 + (blk + 1) * 128]
                    dve(lambda e, ysl=ysl, yslot=yslot: e.tensor_tensor(out=ysl, in0=yslot, in1=ysl, op=ALU.add), [Y7[blk], ytok[c]], [ytok[c]])
            B.barrier()
        dump("s_yacc", yacc, ytok)
        gyT = uT; gtok = Tok()
        wg = B16[:, 0:4096]; wgtok = Tok()
        bgl = B32[:, 2048:3072]; bgtok = Tok()
        B.dma("sp", bgl, b_glu[l].partition_broadcast(128), writes=[bgtok])
        load_wblock(w_glu[l], 0, 1024, 4, wg, wgtok)
        gs = [B32[:, 8192 + i * 512: 8192 + (i + 1) * 512] for i in range(4)]; gstok = [Tok() for _ in range(2)]
        gb = [B16[:, 24576 + i * 512: 24576 + (i + 1) * 512] for i in range(2)]; gbtok = [Tok() for _ in range(2)]
        GC = 2.0 * math.sqrt(2.0 / math.pi)
        for c in range(NCH):
            bi = c % 2
            y = yacc[:, c * 512:(c + 1) * 512]; t = gs[bi * 2]; sg = gs[bi * 2 + 1]
            dve(lambda e, y=y, t=t: e.tensor_tensor(out=t, in0=y, in1=y, op=ALU.mult), [ytok[c]], [gstok[bi]])
            dve(lambda e, t=t: e.tensor_scalar(out=t, in0=t, scalar1=0.044715, scalar2=1.0, op0=ALU.mult, op1=ALU.add), [gstok[bi]], [gstok[bi]])
            dve(lambda e, y=y, t=t: e.tensor_tensor(out=t, in0=t, in1=y, op=ALU.mult), [gstok[bi], ytok[c]], [gstok[bi]])
            act(lambda e, t=t, sg=sg: e.activation(out=sg, in_=t, func=AF.Sigmoid, scale=GC), [gstok[bi]], [gstok[bi]])
            dve(lambda e, y=y, sg=sg, bi=bi: e.tensor_tensor(out=gb[bi], in0=y, in1=sg, op=ALU.mult), [gstok[bi], ytok[c]], [gbtok[bi]])
            pi = 6 + c % 2
            psb = PS[pi][:].bitcast(BF16)
            for kc in range(4):
                pe(lambda e, psb=psb, kc=kc, bi=bi: e.transpose(out=psb[:, kc * 128:(kc + 1) * 128], in_=gb[bi][:, kc * 128:(kc + 1) * 128], identity=ident_b[:]), [gbtok[bi], ctok], [PST[pi]])
            act(lambda e, psb=psb, c=c: e.activation(out=v3(gyT, 4, T)[:, :, c * 128:(c + 1) * 128], in_=v3(psb[:, 0:512], 4, 128), func=AF.Copy), [PST[pi]], [gtok])
        zs = [B32[:, 10240 + i * 512: 10240 + (i + 1) * 512] for i in range(4)]; zstok = [Tok() for _ in range(2)]
        zo = [B16[:, 25600 + i * 512: 25600 + (i + 1) * 512] for i in range(2)]; zotok = [Tok() for _ in range(2)]
        for c in range(NCH):
            bi = c % 2
            for half in range(2):
                pi = half + 2 * bi
                for kc in range(4):
                    pe(lambda e, pi=pi, kc=kc, c=c, half=half: e.matmul(PS[pi][:], lhsT=v3(gyT, 4, T)[:, kc, c * 128:(c + 1) * 128], rhs=wg[:, kc * 1024 + half * 512: kc * 1024 + (half + 1) * 512], start=(kc == 0), stop=(kc == 3)),
                       [gtok, wgtok], [PST[pi]])
            va = zs[bi * 2]; gt = zs[bi * 2 + 1]
            dve(lambda e, va=va, bi=bi: e.tensor_tensor(out=va, in0=PS[2 * bi][:], in1=bgl[:, 0:512], op=ALU.add), [PST[2 * bi], bgtok], [zstok[bi]])
            dve(lambda e, gt=gt, bi=bi: e.tensor_tensor(out=gt, in0=PS[2 * bi + 1][:], in1=bgl[:, 512:1024], op=ALU.add), [PST[2 * bi + 1], bgtok], [zstok[bi]])
            act(lambda e, gt=gt: e.activation(out=gt, in_=gt, func=AF.Sigmoid), [zstok[bi]], [zstok[bi]])
            dve(lambda e, va=va, gt=gt, bi=bi: e.tensor_tensor(out=zo[bi], in0=va, in1=gt, op=ALU.mult), [zstok[bi]], [zotok[bi]])
            B.dma("sp", ymix[c * 128:(c + 1) * 128, 0:512], zo[bi], reads=[zotok[bi]], writes=[ymix_tok])
        B.barrier()

    def phase_gla(l):
        gt = [A32[:, i * 432:(i + 1) * 432] for i in range(6)]
        LF, II, Bc, colfac, rowfac, cdec = gt
        biasF = A32[:, 2592:2616]; biasI = A32[:, 2616:2640]
        gk = Tok()
        v24 = lambda ap: v3(ap, NCH, 24)
        dve(lambda e: e.memset(biasI, 0.0), [], [gk])
        dve(lambda e: e.memset(LF, 0.0), [], [gk])
        dve(lambda e: e.memset(II, 0.0), [], [gk])
        B.dma("sp", v3(biasF, 2, 12)[:, :, 0:6], ret_logit[l].partition_broadcast(128), writes=[gk])
        B.dma("sp", v3(biasF, 2, 12)[:, :, 6:12], fg_b[l].partition_broadcast(128), writes=[gk])
        B.dma("sp", v3(biasI, 2, 12)[:, :, 6:12], ig_b[l].partition_broadcast(128), writes=[gk])
        for d in range(2):
            dve(lambda e, d=d: e.tensor_copy(out=v24(LF)[:, :, d * 12 + 6: d * 12 + 12], in_=gates_sb[:, :, d * 12 + 6: d * 12 + 12]), [gates_tok, gk], [gk])
            dve(lambda e, d=d: e.tensor_copy(out=v24(II)[:, :, d * 12 + 6: d * 12 + 12], in_=gates_sb[:, :, d * 12: d * 12 + 6]), [gates_tok, gk], [gk])
        dve(lambda e: e.tensor_tensor(out=v24(LF), in0=v24(LF), in1=biasF[:, None, :].broadcast_to([128, NCH, 24]), op=ALU.add), [gk], [gk])
        dve(lambda e: e.tensor_tensor(out=v24(II), in0=v24(II), in1=biasI[:, None, :].broadcast_to([128, NCH, 24]), op=ALU.add), [gk], [gk])
        act(lambda e: e.activation(out=LF, in_=LF, func=AF.Exp, scale=-1.0), [gk], [gk])
        act(lambda e: e.activation(out=LF, in_=LF, func=AF.Ln, bias=1.0), [gk], [gk])
        dve(lambda e: e.tensor_scalar(out=LF, in0=LF, scalar1=-1.0, scalar2=None, op0=ALU.mult), [gk], [gk])
        for d in range(2):
            Uf = ufwd_f if d == 0 else ubwd_f
            rhs = v24(LF)[:, :, d * 12:(d + 1) * 12]
            pe(lambda e, Uf=Uf, rhs=rhs: e.matmul(PS[0][:, 0:216], lhsT=Uf[:], rhs=rhs, start=True, stop=True), [gk, ctok], [PST[0]])
            pe(lambda e, rhs=rhs: e.matmul(PS[1][:, 0:216], lhsT=ones_f[:], rhs=rhs, start=True, stop=True), [gk, ctok], [PST[1]])
            act(lambda e, d=d: e.activation(out=v24(Bc)[:, :, d * 12:(d + 1) * 12], in_=v3(PS[0][:, 0:216], NCH, 12), func=AF.Copy), [PST[0]], [gk])
            act(lambda e, d=d: e.activation(out=v24(cdec)[:, :, d * 12:(d + 1) * 12], in_=v3(PS[1][:, 0:216], NCH, 12), func=AF.Exp), [PST[1]], [gk])
        act(lambda e: e.activation(out=rowfac, in_=Bc, func=AF.Exp), [gk], [gk])
        dve(lambda e: e.tensor_tensor(out=colfac, in0=II, in1=Bc, op=ALU.subtract), [gk], [gk])
        act(lambda e: e.activation(out=colfac, in_=colfac, func=AF.Exp, bias=float(math.log(128.0 ** -0.5))), [gk], [gk])
        for nm, ap_ in (("g_LF", LF), ("g_II", II), ("g_Bc", Bc), ("g_colfac", colfac), ("g_rowfac", rowfac), ("g_cdec", cdec)):
            dump(nm, ap_, [gk])
        o16 = 5376
        def a16v(i):
            return A16[:, o16 + i * 2304: o16 + (i + 1) * 2304]
        qs, ks, vs, gs_, qr, kr, kc0, kc1, qT, kT0, kT1 = [a16v(i) for i in range(11)]
        vaug = A16[:, 30720:33060]
        sTm = [A16[:, 33060 + i * 128: 33060 + (i + 1) * 128] for i in range(4)]; sTtok = [Tok() for _ in range(4)]
        Cbf = [A16[:, 33572 + i * 130: 33572 + (i + 1) * 130] for i in range(2)]
        Oacc = B32[:, 0:2304]; F1 = B32[:, 2304:4608]; F2 = B32[:, 4608:6912]; F3 = B32[:, 6912:9216]
        ybf = B16[:, 18432:20736]
        Cst = [B32[:, 10368 + i * 130: 10368 + (i + 1) * 130] for i in range(2)]
        wn = B32[:, 10752:10880]
        st = B32[:, 10880:10880 + 128]
        tiny = B32[:, 11008:11008 + 64]
        for hh in range(12):
            is_ml = hh >= 6; h = hh % 6
            base = 3584 if is_ml else 512
            hk = Tok(); ck = [Tok(), Tok()]; ok_ = Tok(); tk = [Tok(), Tok()]; fk = Tok()
            for i, (t_, off) in enumerate(((qs, 0), (ks, 768), (vs, 1536), (gs_, 2304))):
                c0 = base + off + h * 128
                B.dma("sp" if i % 2 else "act", v3(t_, NCH, 128), a16[:, c0:c0 + 128].rearrange("(c p) n -> p c n", p=128), reads=[a16_tok], writes=[hk])
            B.dma("sp", wn, (ml_nw if is_ml else ret_nw)[l, h * 128:(h + 1) * 128].partition_broadcast(128), writes=[fk])
            if not is_ml:
                for (src, dst, eng) in ((qs, qr, "dve"), (ks, kr, "pool")):
                    s1 = v3(src, NCH, 128)[:, :, 0:64]; s2 = v3(src, NCH, 128)[:, :, 64:128]
                    d1 = v3(dst, NCH, 128)[:, :, 0:64]; d2 = v3(dst, NCH, 128)[:, :, 64:128]
                    f1 = v3(F1[:, 0:1152], NCH, 64) if eng == "dve" else v3(F2[:, 0:1152], NCH, 64)
                    f2 = v3(F1[:, 1152:2304], NCH, 64) if eng == "dve" else v3(F2[:, 1152:2304], NCH, 64)
                    rk = Tok()
                    opf = lambda fn, rd, wr, eng=eng: B.op(eng, fn, reads=rd, writes=wr)
                    opf(lambda e, s1=s1, f1=f1: e.tensor_tensor(out=f1, in0=s1, in1=cos_t[:], op=ALU.mult), [hk, ctok], [rk])
                    opf(lambda e, s2=s2, f2=f2: e.tensor_tensor(out=f2, in0=s2, in1=sin_t[:], op=ALU.mult), [hk, ctok], [rk])
                    opf(lambda e, d1=d1, f1=f1, f2=f2: e.tensor_tensor(out=d1, in0=f1, in1=f2, op=ALU.subtract), [rk], [hk])
                    opf(lambda e, s1=s1, f1=f1: e.tensor_tensor(out=f1, in0=s1, in1=sin_t[:], op=ALU.mult), [hk, ctok], [rk])
                    opf(lambda e, s2=s2, f2=f2: e.tensor_tensor(out=f2, in0=s2, in1=cos_t[:], op=ALU.mult), [hk, ctok], [rk])
                    opf(lambda e, d2=d2, f1=f1, f2=f2: e.tensor_tensor(out=d2, in0=f1, in1=f2, op=ALU.add), [rk], [hk])
                qq, kk = qr, kr
            else:
                qq, kk = qs, ks
            for d, kcd in enumerate((kc0, kc1)):
                col = d * 12 + hh
                dve(lambda e, kcd=kcd, kk=kk, col=col: e.tensor_tensor(out=v3(kcd, NCH, 128), in0=v3(kk, NCH, 128), in1=v24(colfac)[:, :, col:col + 1].broadcast_to([128, NCH, 128]), op=ALU.mult), [hk, gk], [hk])
            act(lambda e: e.activation(out=v3(vaug, NCH, 130)[:, :, 0:128], in_=v3(vs, NCH, 128), func=AF.Copy), [hk], [hk])
            pool(lambda e: e.memset(v3(vaug, NCH, 130)[:, :, 128:130], 1.0), [hk], [hk])
            n = 0
            for (src, dst) in ((qq, qT), (kc0, kT0), (kc1, kT1)):
                for g4 in range(0, NCH, 4):
                    cnt4 = min(4, NCH - g4)
                    pi = 6 + n % 2; n += 1
                    psb = PS[pi][:].bitcast(BF16)
                    for j in range(cnt4):
                        c = g4 + j
                        pe(lambda e, psb=psb, j=j, c=c, src=src: e.transpose(out=psb[:, j * 128:(j + 1) * 128], in_=src[:, c * 128:(c + 1) * 128], identity=ident_b[:]), [hk, ctok], [PST[pi]])
                    act(lambda e, psb=psb, g4=g4, cnt4=cnt4, dst=dst: e.activation(out=dst[:, g4 * 128:(g4 + cnt4) * 128], in_=psb[:, 0:cnt4 * 128], func=AF.Copy), [PST[pi]], [hk])
            dve(lambda e: e.memset(Oacc, 0.0), [], [ok_])
            for d in range(2):
                dve(lambda e, d=d: e.memset(Cst[d], 0.0), [], [ck[d]])
                pool(lambda e, d=d: e.memset(Cbf[d], 0.0), [], [ck[d]])
            sn_ = 0
            for step in range(NCH):
                for d in range(2):
                    c = ORD[d][step]
                    col = d * 12 + hh
                    kT = kT0 if d == 0 else kT1; kcd = kc0 if d == 0 else kc1
                    msk = ufwd_f if d == 0 else ubwd_f
                    cs_ = slice(c * 128, (c + 1) * 128)
                    pS, pO, pC = d, 2 + d, 4 + d
                    pe(lambda e, pS=pS, kT=kT, cs_=cs_: e.matmul(PS[pS][:, 0:128], lhsT=kT[:, cs_], rhs=qT[:, cs_], start=True, stop=True), [hk], [PST[pS]])
                    si = sn_ % 4; sn_ += 1
                    dve(lambda e, pS=pS, si=si, msk=msk: e.tensor_tensor(out=sTm[si], in0=PS[pS][:, 0:128], in1=msk[:], op=ALU.mult), [PST[pS], ctok], [sTtok[si]])
                    va = v3(vaug, NCH, 130)[:, c, :]
                    pe(lambda e, pO=pO, si=si, va=va: e.matmul(PS[pO][:, 0:130], lhsT=sTm[si], rhs=va, start=True, stop=False), [sTtok[si], hk], [PST[pO]])
                    pe(lambda e, pO=pO, cs_=cs_, d=d: e.matmul(PS[pO][:, 0:130], lhsT=qT[:, cs_], rhs=Cbf[d], start=False, stop=True), [hk, ck[d]], [PST[pO]])
                    rf = v24(rowfac)[:, c, col:col + 1]
                    oslice = Oacc[:, c * 128:(c + 1) * 128]
                    if is_ml:
                        t1_ = tiny[:, d * 4:d * 4 + 1]; t2_ = tiny[:, d * 4 + 1:d * 4 + 2]
                        act(lambda e, pO=pO, t1_=t1_, rf=rf: e.activation(out=t1_, in_=PS[pO][:, 128:129], func=AF.Abs, scale=rf), [PST[pO], gk], [tk[d]])
                        dve(lambda e, t1_=t1_: e.tensor_scalar(out=t1_, in0=t1_, scalar1=1.0, scalar2=None, op0=ALU.max), [tk[d]], [tk[d]])
                        dve(lambda e, t1_=t1_: e.reciprocal(out=t1_, in_=t1_), [tk[d]], [tk[d]])
                        dve(lambda e, t1_=t1_, t2_=t2_, rf=rf: e.tensor_tensor(out=t2_, in0=t1_, in1=rf, op=ALU.mult), [tk[d], gk], [tk[d]])
                        rr = t2_
                    else:
                        rr = rf
                    dve(lambda e, pO=pO, rr=rr, oslice=oslice: e.scalar_tensor_tensor(out=oslice, in0=PS[pO][:, 0:128], scalar=rr, in1=oslice, op0=ALU.mult, op1=ALU.add), [PST[pO], tk[d], gk, ok_], [ok_])
                    if step < NCH - 1:
                        pe(lambda e, pC=pC, kcd=kcd, cs_=cs_, va=va: e.matmul(PS[pC][:, 0:130], lhsT=kcd[:, cs_], rhs=va, start=True, stop=True), [hk], [PST[pC]])
                        cd = v24(cdec)[:, c, col:col + 1]
                        dve(lambda e, d=d, cd=cd: e.tensor_scalar(out=Cst[d], in0=Cst[d], scalar1=cd, scalar2=None, op0=ALU.mult), [ck[d], gk], [ck[d]])
                        dve(lambda e, d=d, cd=cd, pC=pC: e.scalar_tensor_tensor(out=Cst[d], in0=PS[pC][:, 0:130], scalar=cd, in1=Cst[d], op0=ALU.mult, op1=ALU.add), [PST[pC], ck[d], gk], [ck[d]])
                        act(lambda e, d=d: e.activation(out=Cbf[d], in_=Cst[d], func=AF.Copy), [ck[d]], [ck[d]])
            dump("g_qr%d" % hh, qq, [hk])
            dump("g_kr%d" % hh, kk, [hk])
            dump("g_O%d" % hh, Oacc, [ok_])
            dump("g_vaug%d" % hh, vaug, [hk])
            dump("g_qT%d" % hh, qT, [hk])
            dump("g_kc0_%d" % hh, kc0, [hk])
            O3 = v3(Oacc, NCH, 128)
            mean = st[:, 0:18]; ssq = st[:, 18:36]
            if not is_ml:
                dve(lambda e: e.tensor_reduce(out=mean, in_=O3, axis=AX.X, op=ALU.add), [ok_], [fk])
                dve(lambda e: e.scalar_tensor_tensor(out=O3, in0=mean[:, :, None].broadcast_to([128, NCH, 128]), scalar=-1.0 / 128.0, in1=O3, op0=ALU.mult, op1=ALU.add), [fk, ok_], [ok_])
            act(lambda e: e.activation(out=F1, in_=Oacc, func=AF.Square), [ok_], [fk])
            dve(lambda e: e.tensor_reduce(out=ssq, in_=v3(F1, NCH, 128), axis=AX.X, op=ALU.add), [fk], [fk])
            var_ = st[:, 36:54]; sd_ = st[:, 54:72]; rstd_ = st[:, 72:90]
            dve(lambda e: e.tensor_scalar(out=var_, in0=ssq, scalar1=1.0 / 128.0, scalar2=EPS, op0=ALU.mult, op1=ALU.add), [fk], [fk])
            act(lambda e: e.activation(out=sd_, in_=var_, func=AF.Sqrt), [fk], [fk])
            dve(lambda e: e.reciprocal(out=rstd_, in_=sd_), [fk], [fk])
            act(lambda e: e.activation(out=F2, in_=gs_, func=(AF.Sigmoid if is_ml else AF.Silu)), [hk], [fk])
            pool(lambda e: e.tensor_tensor(out=v3(F2, NCH, 128), in0=v3(F2, NCH, 128), in1=wn[:, None, :].broadcast_to([128, NCH, 128]), op=ALU.mult), [fk], [fk])
            dve(lambda e: e.tensor_tensor(out=v3(F3, NCH, 128), in0=O3, in1=rstd_[:, :, None].broadcast_to([128, NCH, 128]), op=ALU.mult), [fk, ok_], [fk])
            dve(lambda e: e.tensor_tensor(out=ybf, in0=F3, in1=F2, op=ALU.mult), [fk], [fk])
            dump("g_F1_%d" % hh, F1, [fk]); dump("g_F2_%d" % hh, F2, [fk]); dump("g_F3_%d" % hh, F3, [fk]); dump("g_st%d" % hh, st, [fk]); dump("g_ybf%d" % hh, ybf, [fk])
            yc0 = (1280 if is_ml else 512) + h * 128
            B.dma("sp", ymix[:, yc0:yc0 + 128].rearrange("(c p) n -> p c n", p=128), v3(ybf, NCH, 128), reads=[fk], writes=[ymix_tok])
            B.barrier()

    def mark(name):
        if upto == name:
            raise _Stop()

    gates_d = dscr("gates_d", [128, NCH * 24])
    IN_BLOCKS = [(cb * 512, 512) for cb in range(13)] + [(6656, 24)]
    OUT_BLOCKS = [(cb * 512, 512) for cb in range(4)]
    try:
        setup()
        phase_mod()
        mark("mod")
        for l in range(DEPTH):
            last = (l == DEPTH - 1)
            src = xh if l == 0 else resB
            stok = None if l == 0 else res_tok["B"]
            prep_norm_mod(l)
            phase_norm_T(src, 0, range(NCH))
            B.barrier()
            proj_tok(w_in[l], IN_BLOCKS, range(NCH), ep_inproj)
            B.barrier()
            if "gates_d" in debug and l == 0:
                B.dma("sp", gates_d, gates_sb[:].rearrange("p a b -> p (a b)"), reads=[gates_tok], writes=[Tok()])
            mark("inproj%d" % l)
            phase_s5(l)
            mark("s5%d" % l)
            phase_gla(l)
            B.barrier()
            mark("mix%d" % l)
            chunks = range(2, NCH) if last else range(NCH)
            load_gate(l, 2)
            phase_plain_T(ymix, chunks)
            B.barrier()
            proj_tok(w_out[l], OUT_BLOCKS, chunks, make_ep_resid(src, stok, resA, res_tok["A"]))
            B.barrier()
            mark("outproj%d" % l)
            phase_norm_T(resA, 1, chunks)
            B.barrier()
            phase_ffn1(w_ff1[l], chunks)
            B.barrier()
            mark("ffn1%d" % l)
            load_gate(l, 5)
            phase_ffn2(w_ff2[l], chunks, make_ep_resid(resA, res_tok["A"], resB, res_tok["B"]))
            B.barrier()
            mark("layer%d" % l)
        phase_final(resB, res_tok["B"])
    except _Stop:
        pass
    B.barrier()
    print("instr counts", {e: len(B.ops[e]) for e in B.engs}, flush=True)
    B.replay()
    return nc, es, dbg


def _consts():
    idx = np.arange(128)
    c = {}
    c["k_ident"] = np.eye(128, dtype=np.float32)
    c["k_ufwd"] = (idx[:, None] <= idx[None, :]).astype(np.float32)
    c["k_ubwd"] = (idx[:, None] >= idx[None, :]).astype(np.float32)
    c["k_self"] = np.zeros((128, 128), np.float32); c["k_self"][127, :] = 1.0
    c["k_selb"] = np.zeros((128, 128), np.float32); c["k_selb"][0, :] = 1.0
    t = np.arange(SEQ)
    rows = (t // 64).astype(np.float32); cols = (t % 64).astype(np.float32)
    inv = (10000.0 ** (-np.arange(32, dtype=np.float32) / 32.0)).astype(np.float32)
    ang = np.concatenate([rows[:, None] * inv, cols[:, None] * inv], axis=-1).astype(np.float32)
    cos = np.ones((T, 64), np.float32); sin = np.zeros((T, 64), np.float32)
    cos[CTX:] = np.cos(ang); sin[CTX:] = np.sin(ang)
    c["k_cos"] = np.ascontiguousarray(cos.reshape(NCH, 128, 64).transpose(1, 0, 2))
    c["k_sin"] = np.ascontiguousarray(sin.reshape(NCH, 128, 64).transpose(1, 0, 2))
    c["k_mask8"] = (idx[:, None] // 16 == np.arange(8)[None, :]).astype(np.float32)
    c["k_tpos"] = np.stack([idx + 1.0, 128.0 - idx], axis=1).astype(np.float32)
    return c


_PROG = {}


def kernel(**inputs):
    if "p" not in _PROG:
        _PROG["p"] = build_program()
    nc, es, _ = _PROG["p"]
    f32 = lambda a: np.ascontiguousarray(np.asarray(a, dtype=np.float32))
    x = f32(inputs["x"]); ctx = f32(inputs["ctx"]); c = f32(inputs["c"]); c_ctx = f32(inputs["c_ctx"])
    shared = {k: f32(inputs[k]) for k in ("w_mod", "b_mod", "norm1_w", "norm2_w", "w_in", "w_out", "s5_lam_re", "s5_lam_im", "s5_log_step",
                                          "s5_b_re", "s5_b_im", "s5_c_re", "s5_c_im", "s5_d", "s5_w_glu", "s5_b_glu", "ret_decay_logit",
                                          "ret_norm_w", "mlstm_igate_b", "mlstm_fgate_b", "mlstm_norm_w", "w_ff1", "w_ff2", "norm_f_w")}
    shared.update(_consts())
    in_maps = []
    for core in range(8):
        b = core % NB
        m = dict(shared)
        m["xh"] = np.ascontiguousarray(np.concatenate([ctx[b], x[b]], axis=0))
        m["cc"] = np.ascontiguousarray(np.stack([c[b], c_ctx], axis=0))
        in_maps.append(m)
    res = run_bass_kernel_spmd(nc, in_maps, core_ids=list(range(8)))
    outs = [np.asarray(res.results[b]["out"], dtype=np.float32) for b in range(NB)]
    return np.stack(outs, axis=0)
```

```python
import math
from contextlib import ExitStack
import numpy as np
import ml_dtypes
import concourse.bass as bass
import concourse.mybir as mybir
from concourse.bass_utils import run_bass_kernel_spmd

F32 = mybir.dt.float32
BF16 = mybir.dt.bfloat16
I32 = mybir.dt.int32
AF = mybir.ActivationFunctionType
ALU = mybir.AluOpType
AX = mybir.AxisListType

D = 2048
NB = 4
SEQ = 2048
CTX = 256
T = SEQ + CTX
NCH = T // 128
DEPTH = 2
INW = 6680
DFF = 8192
EPS = 1e-6
NDMASEM = 12
STRICT = True


class Tok:
    __slots__ = ("w", "r")

    def __init__(self):
        self.w = None
        self.r = []


class _Rec:
    def __init__(self):
        self.call = None

    def __getattr__(self, name):
        def f(*a, **k):
            self.call = (name, a, k)
            return self
        return f


class Builder:
    def __init__(self, nc, es):
        self.nc = nc
        self.es = es
        self.engs = ["pe", "dve", "act", "pool", "sp"]
        self.ops = {e: [] for e in self.engs}
        self.seq = {e: 0 for e in self.engs}
        self.waited = {e: {} for e in self.engs}
        self.sems = {}
        for e in ["pe", "dve", "act", "pool"]:
            self.sems[("e", e)] = es.enter_context(nc.semaphore("p_" + e))
        self.dcount = {}
        self.drr = {"sp": 0, "pool": 0, "act": 0}
        for q in ["sp", "pool", "act"]:
            for i in range(NDMASEM):
                self.sems[("d", q, i)] = es.enter_context(nc.semaphore("d_%s%d" % (q, i)))
                self.dcount[("d", q, i)] = 0
        self.final = []

    def _need(self, eng, deps):
        out = []
        for (k, v) in deps:
            if k == ("e", eng) and not (STRICT or eng == "pool"):
                continue
            if k == ("e", eng) and eng == "pe":
                continue
            if self.waited[eng].get(k, 0) >= v:
                continue
            self.waited[eng][k] = v
            out.append((k, v))
        return out

    def _deps(self, reads, writes):
        deps = []
        for t in reads:
            if t.w is not None:
                deps.append(t.w)
        for t in writes:
            if t.w is not None:
                deps.append(t.w)
            deps.extend(t.r)
        return deps

    def op(self, eng, fn, reads=(), writes=()):
        deps = self._deps(reads, writes)
        waits = self._need(eng, deps)
        self.seq[eng] += 1
        done = (("e", eng), self.seq[eng])
        rec = _Rec()
        fn(rec)
        call = rec.call
        fn = lambda e, call=call: getattr(e, call[0])(*call[1], **call[2])
        self.ops[eng].append((waits, fn, done))
        for t in reads:
            t.r.append(done)
        for t in writes:
            t.w = done
            t.r = []
        return done

    def dma(self, q, out, in_, reads=(), writes=(), **kw):
        i = self.drr[q]
        self.drr[q] = (i + 1) % NDMASEM
        k = ("d", q, i)
        deps = self._deps(reads, writes)
        if self.dcount[k] > 0:
            deps.append((k, self.dcount[k]))
        waits = self._need(q, deps)
        self.dcount[k] += 16
        done = (k, self.dcount[k])
        fn = lambda e, out=out, in_=in_, kw=kw: e.dma_start(out=out, in_=in_, **kw)
        self.ops[q].append((waits, fn, done))
        for t in reads:
            t.r.append(done)
        for t in writes:
            t.w = done
            t.r = []
        return done

    def barrier(self):
        allk = [(("e", e), self.seq[e]) for e in ["pe", "dve", "act", "pool"] if self.seq[e] > 0]
        allk += [(k, v) for k, v in self.dcount.items() if v > 0]
        for e in self.engs:
            waits = self._need(e, allk)
            if waits:
                self.ops[e].append((waits, None, None))

    def check_deadlock(self):
        vals = {k: 0 for k in self.sems}
        pos = {e: 0 for e in self.engs}
        progress = True
        while progress:
            progress = False
            for e in self.engs:
                ops = self.ops[e]
                while pos[e] < len(ops):
                    waits, fn, done = ops[pos[e]]
                    if any(vals[k] < v for (k, v) in waits):
                        break
                    if done is not None:
                        vals[done[0]] += 1 if done[0][0] == "e" else 16
                        assert vals[done[0]] == done[1], (e, pos[e], done, vals[done[0]])
                    pos[e] += 1
                    progress = True
        stuck = {e: (pos[e], len(self.ops[e])) for e in self.engs if pos[e] < len(self.ops[e])}
        if stuck:
            for e in stuck:
                waits, fn, done = self.ops[e][pos[e]]
                print("DEADLOCK", e, pos[e], [(k, v, vals[k]) for (k, v) in waits], flush=True)
            raise RuntimeError("deadlock in semaphore protocol: %s" % stuck)

    def replay(self):
        self.check_deadlock()
        nc = self.nc
        block = self.es.enter_context(nc.Block())
        hw = {"pe": block.tensor, "dve": block.vector, "act": block.scalar, "pool": block.gpsimd, "sp": block.sync}
        for e in self.engs:
            ops = self.ops[e]

            def body(eng, ops=ops):
                for (waits, fn, done) in ops:
                    for (k, v) in waits:
                        eng.wait_ge(self.sems[k], v)
                    if fn is None:
                        continue
                    inst = fn(eng)
                    if done[0][0] == "e":
                        inst.then_inc(self.sems[done[0]], 1)
                    else:
                        inst.then_inc(self.sems[done[0]], 16)

            hw[e](body)


class _Stop(Exception):
    pass


def build_program(debug=(), upto=None):
    nc = bass.Bass("TRN2", target_bir_lowering=False)
    es = ExitStack()
    B = Builder(nc, es)
    dbg = {}

    def din(name, shape, dt=F32):
        return nc.dram_tensor(name, list(shape), dt, kind="ExternalInput").ap()

    def dscr(name, shape, dt=F32):
        kind = "ExternalOutput" if name in debug else "Internal"
        ap = nc.dram_tensor(name, list(shape), dt, kind=kind).ap()
        if name in debug:
            dbg[name] = ap
        return ap

    def mark(name):
        if upto == name:
            raise _Stop()

    def dump(name, ap, toks):
        if name not in debug or name in dbg:
            return
        t = nc.dram_tensor(name, list(ap.shape), ap.dtype, kind="ExternalOutput").ap()
        dbg[name] = t
        B.dma("sp", t, ap, reads=toks, writes=[Tok()])

    def sb(name, shape, dt=F32):
        return es.enter_context(nc.sbuf_tensor(name, list(shape), dt))

    xh = din("xh", [T, D])
    cc = din("cc", [2, D])
    w_mod = din("w_mod", [DEPTH, D, 6 * D])
    b_mod = din("b_mod", [DEPTH, 6 * D])
    norm1_w = din("norm1_w", [DEPTH, D])
    norm2_w = din("norm2_w", [DEPTH, D])
    w_in = din("w_in", [DEPTH, D, INW])
    w_out = din("w_out", [DEPTH, D, D])
    lam_re = din("s5_lam_re", [DEPTH, 2, 32, 64])
    lam_im = din("s5_lam_im", [DEPTH, 2, 32, 64])
    log_step = din("s5_log_step", [DEPTH, 2, 32])
    s5b_re = din("s5_b_re", [DEPTH, 2, 32, 64, 16])
    s5b_im = din("s5_b_im", [DEPTH, 2, 32, 64, 16])
    s5c_re = din("s5_c_re", [DEPTH, 2, 32, 16, 64])
    s5c_im = din("s5_c_im", [DEPTH, 2, 32, 16, 64])
    s5_d = din("s5_d", [DEPTH, 512])
    w_glu = din("s5_w_glu", [DEPTH, 512, 1024])
    b_glu = din("s5_b_glu", [DEPTH, 1024])
    ret_logit = din("ret_decay_logit", [DEPTH, 2, 6])
    ret_nw = din("ret_norm_w", [DEPTH, 768])
    ig_b = din("mlstm_igate_b", [DEPTH, 2, 6])
    fg_b = din("mlstm_fgate_b", [DEPTH, 2, 6])
    ml_nw = din("mlstm_norm_w", [DEPTH, 768])
    w_ff1 = din("w_ff1", [DEPTH, D, DFF])
    w_ff2 = din("w_ff2", [DEPTH, DFF, D])
    norm_f = din("norm_f_w", [D])
    k_ident = din("k_ident", [128, 128])
    k_ufwd = din("k_ufwd", [128, 128])
    k_ubwd = din("k_ubwd", [128, 128])
    k_self = din("k_self", [128, 128])
    k_selb = din("k_selb", [128, 128])
    k_cos = din("k_cos", [128, NCH, 64])
    k_sin = din("k_sin", [128, NCH, 64])
    k_mask8 = din("k_mask8", [128, 8])
    k_tpos = din("k_tpos", [128, 2])
    out = nc.dram_tensor("out", [SEQ, D], F32, kind="ExternalOutput").ap()

    modd = dscr("modd", [DEPTH, 2, 6 * D])
    a16 = dscr("a16", [T, 6656], BF16)
    ymix = dscr("ymix", [T, D], BF16)
    resA = dscr("resA", [T, D])
    resB = dscr("resB", [T, D])
    h1d = dscr("h1d", [NCH, 128, 64, 128], BF16)

    PS = [es.enter_context(nc.psum_tensor("ps%d" % i, [128, 512], F32)) for i in range(8)]
    PST = [Tok() for _ in range(8)]

    ident_f = sb("ident_f", [128, 128]); ident_b = sb("ident_b", [128, 128], BF16)
    nident_b = sb("nident_b", [128, 128], BF16)
    ufwd_f = sb("ufwd_f", [128, 128]); ubwd_f = sb("ubwd_f", [128, 128])
    ufwd_b = sb("ufwd_b", [128, 128], BF16); ubwd_b = sb("ubwd_b", [128, 128], BF16)
    nufwd_b = sb("nufwd_b", [128, 128], BF16); nubwd_b = sb("nubwd_b", [128, 128], BF16)
    self_b = sb("self_b", [128, 128], BF16); selb_b = sb("selb_b", [128, 128], BF16)
    nself_b = sb("nself_b", [128, 128], BF16); nselb_b = sb("nselb_b", [128, 128], BF16)
    ones_f = sb("ones_f", [128, 128])
    mask8 = sb("mask8", [128, 8]); tpos = sb("tpos", [128, 2])
    cos_t = sb("cos_t", [128, NCH, 64]); sin_t = sb("sin_t", [128, NCH, 64])
    ctok = Tok()
    actT = sb("actT", [128, 16, T], BF16)
    actT_tok = [Tok() for _ in range(NCH)]
    arenaB = sb("arenaB", [128, 20480])
    arenaB_bf = arenaB[:].bitcast(BF16)
    gates_sb = sb("gates_sb", [128, NCH, 24]); gates_tok = Tok()
    small = sb("small", [128, 256]);

    def setup():
        stg = arenaB
        loads = [(k_ident, 0), (k_ufwd, 128), (k_ubwd, 256), (k_self, 384), (k_selb, 512)]
        for ap, o in loads:
            B.dma("sp", stg[:, o:o + 128], ap, writes=[ctok])
        B.dma("sp", mask8[:], k_mask8, writes=[ctok])
        B.dma("sp", tpos[:], k_tpos, writes=[ctok])
        B.dma("sp", cos_t[:], k_cos, writes=[ctok])
        B.dma("sp", sin_t[:], k_sin, writes=[ctok])
        cp = lambda o, i: B.op("dve", lambda e, o=o, i=i: e.tensor_copy(out=o, in_=i), reads=[ctok], writes=[ctok])
        ng = lambda o, i: B.op("dve", lambda e, o=o, i=i: e.tensor_scalar(out=o, in0=i, scalar1=-1.0, scalar2=None, op0=ALU.mult), reads=[ctok], writes=[ctok])
        cp(ident_f[:], stg[:, 0:128]); cp(ident_b[:], stg[:, 0:128]); ng(nident_b[:], stg[:, 0:128])
        cp(ufwd_f[:], stg[:, 128:256]); cp(ufwd_b[:], stg[:, 128:256]); ng(nufwd_b[:], stg[:, 128:256])
        cp(ubwd_f[:], stg[:, 256:384]); cp(ubwd_b[:], stg[:, 256:384]); ng(nubwd_b[:], stg[:, 256:384])
        cp(self_b[:], stg[:, 384:512]); ng(nself_b[:], stg[:, 384:512])
        cp(selb_b[:], stg[:, 512:640]); ng(nselb_b[:], stg[:, 512:640])
        B.op("dve", lambda e: e.memset(ones_f[:], 1.0), writes=[ctok])
        B.barrier()

    def phase_mod():
        cT = sb("cT", [128, 16, 2]); sT = sb("sT", [128, 16, 2]); t_c = Tok()
        for s in range(2):
            B.dma("sp", cT[:, :, s], cc[s].rearrange("(c p) -> p c", p=128), writes=[t_c], allow_slow_non_contiguous=True)
        B.op("act", lambda e: e.activation(out=sT[:], in_=cT[:], func=AF.Silu), reads=[t_c], writes=[t_c])
        wst = [arenaB[:, i * 2048:(i + 1) * 2048] for i in range(4)]
        wtok = [Tok() for _ in range(4)]
        bst = [arenaB[0:2, 8192 + i * 512: 8192 + (i + 1) * 512] for i in range(2)]
        btok = [Tok() for _ in range(2)]
        ost = [arenaB[0:2, 9216 + i * 512: 9216 + (i + 1) * 512] for i in range(2)]
        otok = [Tok() for _ in range(2)]
        n = 0
        for l in range(DEPTH):
            for cb in range(24):
                bi = cb % 2
                B.dma("sp", bst[bi], b_mod[l, cb * 512:(cb + 1) * 512].partition_broadcast(2), writes=[btok[bi]])
                pt = PST[cb % 2]; ps = PS[cb % 2]
                for kg in range(4):
                    wi = n % 4; n += 1
                    src = w_mod[l, kg * 512:(kg + 1) * 512, cb * 512:(cb + 1) * 512].rearrange("(k p) n -> p k n", p=128)
                    B.dma("act" if kg % 2 else "sp", wst[wi].rearrange("p (k n) -> p k n", k=4), src, writes=[wtok[wi]])
                    for k4 in range(4):
                        kc = kg * 4 + k4
                        B.op("pe", lambda e, ps=ps, kc=kc, wi=wi, k4=k4: e.matmul(ps[0:2, :], lhsT=sT[:, kc, :], rhs=wst[wi][:, k4 * 512:(k4 + 1) * 512], start=(kc == 0), stop=(kc == 15)),
                             reads=[t_c, wtok[wi]], writes=[pt])
                B.op("dve", lambda e, ps=ps, bi=bi: e.tensor_tensor(out=ost[bi], in0=ps[0:2, :], in1=bst[bi], op=ALU.add), reads=[pt, btok[bi]], writes=[otok[bi]])
                B.dma("sp", modd[l, :, cb * 512:(cb + 1) * 512], ost[bi], reads=[otok[bi]], writes=[modtok])
        B.barrier()

    modtok = Tok()

    def load_vec16(dst, src, tok):
        B.dma("sp", dst, src.rearrange("(c p) -> p c", p=128), reads=[modtok], writes=[tok], allow_slow_non_contiguous=True)

    gsh = sb("gsh", [128, 2, 2, 2, 16]); gsh_tok = Tok()
    gate_bc = sb("gate_bc", [128, 2, D]); gate_tok = Tok()

    def prep_norm_mod(l):
        nw = small[:, 0:32].rearrange("p (a c) -> p a c", a=2); ntok = Tok()
        load_vec16(nw[:, 0, :], norm1_w[l], ntok); load_vec16(nw[:, 1, :], norm2_w[l], ntok)
        tmp = small[:, 32:160].rearrange("p (a s g c) -> p a s g c", a=2, s=2, g=2)
        for a in range(2):
            for s in range(2):
                load_vec16(tmp[:, a, s, 1, :], modd[l, s, (3 * a) * D:(3 * a + 1) * D], ntok)
                load_vec16(tmp[:, a, s, 0, :], modd[l, s, (3 * a + 1) * D:(3 * a + 2) * D], ntok)
        for a in range(2):
            for s in range(2):
                B.op("dve", lambda e, a=a, s=s: e.scalar_tensor_tensor(out=gsh[:, a, s, 0, :], in0=tmp[:, a, s, 0, :], scalar=1.0, in1=nw[:, a, :], op0=ALU.add, op1=ALU.mult),
                     reads=[ntok], writes=[gsh_tok])
                B.op("dve", lambda e, a=a, s=s: e.tensor_copy(out=gsh[:, a, s, 1, :], in_=tmp[:, a, s, 1, :]), reads=[ntok], writes=[gsh_tok])

    def load_gate(l, which):
        for s in range(2):
            B.dma("sp", gate_bc[:, s, :], modd[l, s, which * D:(which + 1) * D].partition_broadcast(128), reads=[modtok], writes=[gate_tok])

    def phase_norm_T(src, a_idx, chunks):
        xin = [arenaB[:, i * 2048:(i + 1) * 2048] for i in range(2)]; xtok = [Tok() for _ in range(2)]
        xs = [arenaB_bf[:, 8192 + i * 2048: 8192 + (i + 1) * 2048] for i in range(2)]; xstok = [Tok() for _ in range(2)]
        junk = arenaB_bf[:, 12288:14336]; jtok = Tok()
        st = small[:, 160:176]; sttok = [Tok() for _ in range(2)]
        for n, tc in enumerate(chunks):
            bi = n % 2
            s = 1 if tc < 2 else 0
            B.dma("sp", xin[bi], src[tc * 128:(tc + 1) * 128, :], writes=[xtok[bi]])
            ss = st[:, bi * 4:bi * 4 + 1]; rs = st[:, bi * 4 + 1:bi * 4 + 2]
            B.op("act", lambda e, bi=bi, ss=ss: e.activation(out=junk, in_=xin[bi], func=AF.Square, accum_out=ss), reads=[xtok[bi]], writes=[jtok, sttok[bi]])
            B.op("dve", lambda e, ss=ss, rs=rs: e.tensor_scalar(out=rs, in0=ss, scalar1=1.0 / D, scalar2=EPS, op0=ALU.mult, op1=ALU.add), reads=[sttok[bi]], writes=[sttok[bi]])
            B.op("act", lambda e, rs=rs: e.activation(out=rs, in_=rs, func=AF.Sqrt), reads=[sttok[bi]], writes=[sttok[bi]])
            B.op("dve", lambda e, rs=rs: e.reciprocal(out=rs, in_=rs), reads=[sttok[bi]], writes=[sttok[bi]])
            B.op("act", lambda e, bi=bi, rs=rs: e.activation(out=xs[bi], in_=xin[bi], func=AF.Copy, scale=rs), reads=[xtok[bi], sttok[bi]], writes=[xstok[bi]])
            for q in range(4):
                pi = 4 + (n * 4 + q) % 4
                psb = PS[pi][:].bitcast(BF16)
                for j in range(4):
                    kc = q * 4 + j
                    B.op("pe", lambda e, psb=psb, j=j, kc=kc, bi=bi: e.transpose(out=psb[:, j * 128:(j + 1) * 128], in_=xs[bi][:, kc * 128:(kc + 1) * 128], identity=ident_b[:]),
                         reads=[xstok[bi], ctok], writes=[PST[pi]])
                for j in range(4):
                    kc = q * 4 + j
                    B.op("dve", lambda e, psb=psb, j=j, kc=kc, tc=tc, s=s: e.tensor_scalar(out=actT[:, kc, tc * 128:(tc + 1) * 128], in0=psb[:, j * 128:(j + 1) * 128],
                                                                                 scalar1=gsh[:, a_idx, s, 0, kc:kc + 1], scalar2=gsh[:, a_idx, s, 1, kc:kc + 1], op0=ALU.mult, op1=ALU.add),
                         reads=[PST[pi], gsh_tok], writes=[actT_tok[tc]])

    def phase_plain_T(src, chunks):
        xs = [arenaB_bf[:, i * 2048:(i + 1) * 2048] for i in range(2)]; xstok = [Tok() for _ in range(2)]
        for n, tc in enumerate(chunks):
            bi = n % 2
            B.dma("sp", xs[bi], src[tc * 128:(tc + 1) * 128, :], writes=[xstok[bi]])
            for q in range(4):
                pi = 4 + (n * 4 + q) % 4
                psb = PS[pi][:].bitcast(BF16)
                for j in range(4):
                    kc = q * 4 + j
                    B.op("pe", lambda e, psb=psb, j=j, kc=kc, bi=bi: e.transpose(out=psb[:, j * 128:(j + 1) * 128], in_=xs[bi][:, kc * 128:(kc + 1) * 128], identity=ident_b[:]),
                         reads=[xstok[bi], ctok], writes=[PST[pi]])
                B.op("act", lambda e, psb=psb, q=q, tc=tc: e.activation(out=actT[:, q * 4:(q + 1) * 4, tc * 128:(tc + 1) * 128], in_=psb[:, 0:512].rearrange("p (j t) -> p j t", j=4), func=AF.Copy),
                     reads=[PST[pi]], writes=[actT_tok[tc]])

    WB_OFF = 16384
    wstage = [arenaB[:, 4096 + i * 2048: 4096 + (i + 1) * 2048] for i in range(2)]; wstok = [Tok() for _ in range(2)]
    wcnt = [0]

    def load_wblock(wsrc, c0, w, KC, dst, dtok, stage=None, stok_=None):
        stage = stage or wstage; stok_ = stok_ or wstok
        g = max(1, 2048 // w)
        for k0 in range(0, KC, g):
            kk = min(g, KC - k0)
            si = wcnt[0] % 2; wcnt[0] += 1
            src = wsrc[k0 * 128:(k0 + kk) * 128, c0:c0 + w].rearrange("(k p) n -> p k n", p=128)
            B.dma("sp" if si else "act", stage[si][:, 0:kk * w].rearrange("p (k n) -> p k n", k=kk), src, writes=[stok_[si]])
            B.op("pool", lambda e, si=si, k0=k0, kk=kk: e.tensor_copy(out=dst[:, k0 * w:(k0 + kk) * w], in_=stage[si][:, 0:kk * w]), reads=[stok_[si]], writes=[dtok])

    wbuf = [arenaB_bf[:, 16384 + i * 8192: 16384 + (i + 1) * 8192] for i in range(2)]; wbtok = [Tok() for _ in range(2)]
    ostg_f = [arenaB[:, 16384 + i * 512: 16384 + (i + 1) * 512] for i in range(4)]; ostok = [Tok() for _ in range(4)]
    ostg2_f = [arenaB[:, 18432 + i * 512: 18432 + (i + 1) * 512] for i in range(4)]; os2tok = [Tok() for _ in range(4)]
    cnt = {"ps": 0, "o": 0}

    def proj_tok(wsrc, col_blocks, chunks, epilogue):
        load_wblock(wsrc, col_blocks[0][0], col_blocks[0][1], 16, wbuf[0], wbtok[0])
        for cbi, (c0, w) in enumerate(col_blocks):
            bi = cbi % 2
            if cbi + 1 < len(col_blocks):
                load_wblock(wsrc, col_blocks[cbi + 1][0], col_blocks[cbi + 1][1], 16, wbuf[1 - bi], wbtok[1 - bi])
            for tc in chunks:
                pi = cnt["ps"] % 4; cnt["ps"] += 1
                for kc in range(16):
                    B.op("pe", lambda e, pi=pi, kc=kc, tc=tc, bi=bi, w=w: e.matmul(PS[pi][:, 0:w], lhsT=actT[:, kc, tc * 128:(tc + 1) * 128], rhs=wbuf[bi][:, kc * w:(kc + 1) * w], start=(kc == 0), stop=(kc == 15)),
                         reads=[actT_tok[tc], wbtok[bi]], writes=[PST[pi]])
                epilogue(tc, c0, w, pi)

    def ep_inproj(tc, c0, w, pi):
        if c0 >= 6656:
            B.op("act", lambda e: e.activation(out=gates_sb[:, tc, :], in_=PS[pi][:, 0:24], func=AF.Copy), reads=[PST[pi]], writes=[gates_tok])
            return
        oi = cnt["o"] % 4; cnt["o"] += 1
        ob = ostg_f[oi].bitcast(BF16)[:, 0:512]
        B.op("act", lambda e: e.activation(out=ob, in_=PS[pi][:, 0:512], func=AF.Copy), reads=[PST[pi]], writes=[ostok[oi]])
        B.dma("sp", a16[tc * 128:(tc + 1) * 128, c0:c0 + 512], ob, reads=[ostok[oi]], writes=[a16_tok])

    a16_tok = Tok(); ymix_tok = Tok(); res_tok = {"A": Tok(), "B": Tok()}; h1_tok = Tok()

    def make_ep_resid(rsrc, rsrc_tok, rdst, rdst_tok):
        def ep(tc, c0, w, pi):
            s = 1 if tc < 2 else 0
            oi = cnt["o"] % 4; cnt["o"] += 1
            xo = ostg2_f[oi][:, 0:w]; tm = ostg_f[oi][:, 0:w]
            B.dma("act", xo, rsrc[tc * 128:(tc + 1) * 128, c0:c0 + w], reads=[rsrc_tok] if rsrc_tok else [], writes=[os2tok[oi]])
            B.op("dve", lambda e: e.tensor_tensor(out=tm, in0=PS[pi][:, 0:w], in1=gate_bc[:, s, c0:c0 + w], op=ALU.mult), reads=[PST[pi], gate_tok], writes=[ostok[oi]])
            B.op("pool", lambda e: e.tensor_tensor(out=tm, in0=tm, in1=xo, op=ALU.add), reads=[ostok[oi], os2tok[oi]], writes=[ostok[oi]])
            B.dma("sp", rdst[tc * 128:(tc + 1) * 128, c0:c0 + w], tm, reads=[ostok[oi]], writes=[rdst_tok])
        return ep

    def phase_ffn1(wsrc, chunks):
        blocks = []
        cl = list(chunks)
        for i in range(0, len(cl), 4):
            blocks.append(cl[i:i + 4])
        load_wblock(wsrc, 0, 512, 16, wbuf[0], wbtok[0])
        for cb in range(16):
            bi = cb % 2
            if cb + 1 < 16:
                load_wblock(wsrc, (cb + 1) * 512, 512, 16, wbuf[1 - bi], wbtok[1 - bi])
            for fs in range(4):
                kcf = cb * 4 + fs
                for blk in blocks:
                    t0 = blk[0] * 128; N = len(blk) * 128
                    pi = cnt["ps"] % 4; cnt["ps"] += 1
                    for kc in range(16):
                        B.op("pe", lambda e, pi=pi, kc=kc, bi=bi, fs=fs, t0=t0, N=N: e.matmul(PS[pi][:, 0:N], lhsT=wbuf[bi][:, kc * 512 + fs * 128: kc * 512 + (fs + 1) * 128], rhs=actT[:, kc, t0:t0 + N], start=(kc == 0), stop=(kc == 15)),
                             reads=[actT_tok[t] for t in blk] + [wbtok[bi]], writes=[PST[pi]])
                    oi = cnt["o"] % 4; cnt["o"] += 1
                    r = ostg2_f[oi][:, 0:N]; hb = ostg_f[oi].bitcast(BF16)[:, 0:N]
                    B.op("act", lambda e, pi=pi, r=r, N=N: e.activation(out=r, in_=PS[pi][:, 0:N], func=AF.Relu), reads=[PST[pi]], writes=[os2tok[oi]])
                    B.op("dve", lambda e, r=r, hb=hb: e.tensor_tensor(out=hb, in0=r, in1=r, op=ALU.mult), reads=[os2tok[oi]], writes=[ostok[oi]])
                    B.dma("sp", h1d[blk[0]:blk[0] + len(blk), :, kcf, :].rearrange("t p j -> p t j"), hb.rearrange("p (t j) -> p t j", j=128), reads=[ostok[oi]], writes=[h1_tok])

    def phase_ffn2(wsrc, chunks, rsrc, rsrc_tok, rdst, rdst_tok):
        actv = actT[:].rearrange("p a b -> p (a b)")
        wb2 = [actv[:, i * 16384:(i + 1) * 16384] for i in range(2)]; wb2tok = [Tok() for _ in range(2)]
        acc = arenaB[:, 0:9216]; acctok = [Tok() for _ in range(NCH)]
        stg = [arenaB[:, 9216 + i * 2048: 9216 + (i + 1) * 2048] for i in range(2)]; stgtok = [Tok() for _ in range(2)]
        hst = [arenaB_bf[:, 26624 + i * 4096: 26624 + (i + 1) * 4096] for i in range(2)]; hstok = [Tok() for _ in range(2)]
        o1 = [arenaB[:, 17408 + i * 512: 17408 + (i + 1) * 512] for i in range(2)]; o1tok = [Tok() for _ in range(2)]
        o2 = [arenaB[:, 18432 + i * 512: 18432 + (i + 1) * 512] for i in range(2)]; o2tok = [Tok() for _ in range(2)]
        units = [(cb, half) for cb in range(4) for half in range(2)]
        def ld(u):
            cb, half = units[u]
            load_wblock(wsrc[half * 4096:(half + 1) * 4096], cb * 512, 512, 32, wb2[u % 2], wb2tok[u % 2], stage=stg, stok_=stgtok)
        ld(0)
        n = 0; on = 0
        for u, (cb, half) in enumerate(units):
            bi = u % 2
            if u + 1 < len(units):
                ld(u + 1)
            for tc in chunks:
                hi = n % 2; n += 1
                B.dma("sp", hst[hi], h1d[tc][:, half * 32:(half + 1) * 32, :].rearrange("p k j -> p (k j)"), reads=[h1_tok], writes=[hstok[hi]])
                pi = cnt["ps"] % 4; cnt["ps"] += 1
                for kc in range(32):
                    B.op("pe", lambda e, pi=pi, kc=kc, hi=hi, bi=bi: e.matmul(PS[pi][:], lhsT=hst[hi][:, kc * 128:(kc + 1) * 128], rhs=wb2[bi][:, kc * 512:(kc + 1) * 512], start=(kc == 0), stop=(kc == 31)),
                         reads=[hstok[hi], wb2tok[bi]], writes=[PST[pi]])
                asl = acc[:, tc * 512:(tc + 1) * 512]
                if half == 0:
                    B.op("act", lambda e, pi=pi, asl=asl: e.activation(out=asl, in_=PS[pi][:], func=AF.Copy), reads=[PST[pi]], writes=[acctok[tc]])
                else:
                    s_ = 1 if tc < 2 else 0
                    oi = on % 2; on += 1
                    c0 = cb * 512
                    B.dma("act", o2[oi], rsrc[tc * 128:(tc + 1) * 128, c0:c0 + 512], reads=[rsrc_tok] if rsrc_tok else [], writes=[o2tok[oi]])
                    B.op("dve", lambda e, pi=pi, asl=asl, oi=oi: e.tensor_tensor(out=o1[oi], in0=PS[pi][:], in1=asl, op=ALU.add), reads=[PST[pi], acctok[tc]], writes=[o1tok[oi]])
                    B.op("dve", lambda e, oi=oi, s_=s_, c0=c0: e.tensor_tensor(out=o1[oi], in0=o1[oi], in1=gate_bc[:, s_, c0:c0 + 512], op=ALU.mult), reads=[o1tok[oi], gate_tok], writes=[o1tok[oi]])
                    B.op("pool", lambda e, oi=oi: e.tensor_tensor(out=o1[oi], in0=o1[oi], in1=o2[oi], op=ALU.add), reads=[o1tok[oi], o2tok[oi]], writes=[o1tok[oi]])
                    B.dma("sp", rdst[tc * 128:(tc + 1) * 128, c0:c0 + 512], o1[oi], reads=[o1tok[oi]], writes=[rdst_tok])

    def phase_final(src, src_tok):
        nf = gate_bc[:, 0, :]
        B.dma("sp", nf, norm_f.partition_broadcast(128), writes=[gate_tok])
        xin = [arenaB[:, i * 2048:(i + 1) * 2048] for i in range(2)]; xtok = [Tok() for _ in range(2)]
        yo = [arenaB[:, 4096 + i * 2048: 4096 + (i + 1) * 2048] for i in range(2)]; ytok = [Tok() for _ in range(2)]
        junk = arenaB_bf[:, 16384:18432]; jtok = Tok()
        st = small[:, 160:176]; sttok = [Tok() for _ in range(2)]
        outtok = Tok()
        for n, tc in enumerate(range(2, NCH)):
            bi = n % 2
            B.dma("sp", xin[bi], src[tc * 128:(tc + 1) * 128, :], reads=[src_tok], writes=[xtok[bi]])
            ss = st[:, bi * 4:bi * 4 + 1]; rs = st[:, bi * 4 + 1:bi * 4 + 2]
            B.op("act", lambda e, bi=bi, ss=ss: e.activation(out=junk, in_=xin[bi], func=AF.Square, accum_out=ss), reads=[xtok[bi]], writes=[jtok, sttok[bi]])
            B.op("dve", lambda e, ss=ss, rs=rs: e.tensor_scalar(out=rs, in0=ss, scalar1=1.0 / D, scalar2=EPS, op0=ALU.mult, op1=ALU.add), reads=[sttok[bi]], writes=[sttok[bi]])
            B.op("act", lambda e, rs=rs: e.activation(out=rs, in_=rs, func=AF.Sqrt), reads=[sttok[bi]], writes=[sttok[bi]])
            B.op("dve", lambda e, rs=rs: e.reciprocal(out=rs, in_=rs), reads=[sttok[bi]], writes=[sttok[bi]])
            B.op("act", lambda e, bi=bi, rs=rs: e.activation(out=yo[bi], in_=xin[bi], func=AF.Copy, scale=rs), reads=[xtok[bi], sttok[bi]], writes=[ytok[bi]])
            B.op("dve", lambda e, bi=bi: e.tensor_tensor(out=yo[bi], in0=yo[bi], in1=nf, op=ALU.mult), reads=[ytok[bi], gate_tok], writes=[ytok[bi]])
            d = B.dma("sp", out[(tc - 2) * 128:(tc - 1) * 128, :], yo[bi], reads=[ytok[bi]], writes=[outtok])

    A32 = actT[:].rearrange("p a b -> p (a b)").bitcast(F32)
    A16 = actT[:].rearrange("p a b -> p (a b)")
    B32 = arenaB; B16 = arenaB_bf
    ORD = [list(range(NCH)), [1, 0] + list(range(NCH - 1, 1, -1))]
    TWO_PI = 2.0 * math.pi

    def v3(ap, a, b):
        return ap.rearrange("p (a b) -> p a b", a=a, b=b)

    def dve(fn, reads, writes):
        return B.op("dve", fn, reads=reads, writes=writes)

    def act(fn, reads, writes):
        return B.op("act", fn, reads=reads, writes=writes)

    def pool(fn, reads, writes):
        return B.op("pool", fn, reads=reads, writes=writes)

    def pe(fn, reads, writes):
        return B.op("pe", fn, reads=reads, writes=writes)

    def range_reduce_sincos(ph, kint, sinv, cosv, tk, shape_note=None):
        dve(lambda e: e.tensor_copy(out=kint, in_=ph), [tk], [tk])
        dve(lambda e: e.tensor_copy(out=cosv, in_=kint), [tk], [tk])
        dve(lambda e: e.tensor_tensor(out=ph, in0=ph, in1=cosv, op=ALU.subtract), [tk], [tk])
        act(lambda e: e.activation(out=sinv, in_=ph, func=AF.Sin, scale=TWO_PI), [tk], [tk])
        dve(lambda e: e.tensor_scalar(out=ph, in0=ph, scalar1=0.25, scalar2=None, op0=ALU.add), [tk], [tk])
        dve(lambda e: e.tensor_scalar(out=cosv, in0=ph, scalar1=0.5, scalar2=None, op0=ALU.is_gt), [tk], [tk])
        dve(lambda e: e.tensor_tensor(out=ph, in0=ph, in1=cosv, op=ALU.subtract), [tk], [tk])
        act(lambda e: e.activation(out=cosv, in_=ph, func=AF.Sin, scale=TWO_PI), [tk], [tk])

    def phase_s5(l):
        yacc = A32[:, 0:9216]; ytok = [[Tok() for _ in range(4)] for _ in range(NCH)]
        uT = A16[:, 18432:27648]; uTtok = Tok()
        Bblk = A16[:, 27648:31744]; Btok = Tok()
        Cmat = A16[:, 31744:35840]; Ctok = Tok()
        misc = A32[:, 17920:18432]
        tab = [B32[:, i * 2048:(i + 1) * 2048] for i in range(4)]; tabtok = Tok()
        Pp = [[B16[:, 16384 + (blk * 4 + k) * 512: 16384 + (blk * 4 + k + 1) * 512] for k in range(4)] for blk in range(4)]
        Ptk = [[Tok() for _ in range(4)] for _ in range(4)]
        Zp = [[B16[:, 24576 + (i * 4 + k) * 512: 24576 + (i * 4 + k + 1) * 512] for k in range(4)] for i in range(2)]
        Ztok = [Tok() for _ in range(2)]
        xTt = [B16[:, 28672 + i * 128: 28672 + (i + 1) * 128] for i in range(8)]; xTtok = [Tok() for _ in range(8)]
        u_sb = B16[:, 29696:38912]; utok = Tok()
        dsk = B32[:, 8192:8704]; dtok = Tok()
        B.dma("sp", v3(u_sb, NCH, 512), a16[:, 0:512].rearrange("(c p) n -> p c n", p=128), reads=[a16_tok], writes=[utok])
        B.dma("sp", dsk, s5_d[l].partition_broadcast(128), writes=[dtok])
        for c in range(NCH):
            dve(lambda e, c=c: e.tensor_tensor(out=yacc[:, c * 512:(c + 1) * 512], in0=u_sb[:, c * 512:(c + 1) * 512], in1=dsk, op=ALU.mult), [utok, dtok], ytok[c])
            pi = 6 + c % 2
            psb = PS[pi][:].bitcast(BF16)
            for blk in range(4):
                pe(lambda e, psb=psb, blk=blk, c=c: e.transpose(out=psb[:, blk * 128:(blk + 1) * 128], in_=u_sb[:, c * 512 + blk * 128: c * 512 + (blk + 1) * 128], identity=ident_b[:]), [utok, ctok], [PST[pi]])
            act(lambda e, psb=psb, c=c: e.activation(out=v3(uT, 4, T)[:, :, c * 128:(c + 1) * 128], in_=v3(psb[:, 0:512], 4, 128), func=AF.Copy), [PST[pi]], [uTtok])
        B.barrier()
        for d in range(2):
            S = [B32[:, 8192 + i * 2048: 8192 + (i + 1) * 2048] for i in range(5)]
            stok = Tok()
            lrd, th, ph, sinv, cosv = S
            kint = B32[:, 18432:20480].bitcast(I32)
            dtb = misc[:, 0:32]
            ntp = misc[:, 32:33]
            tp = tpos[:, d:d + 1]
            B.dma("sp", lrd, lam_re[l, d].rearrange("g p -> (g p)").partition_broadcast(128), writes=[stok])
            B.dma("sp", th, lam_im[l, d].rearrange("g p -> (g p)").partition_broadcast(128), writes=[stok])
            B.dma("sp", dtb, log_step[l, d].partition_broadcast(128), writes=[stok])
            act(lambda e: e.activation(out=dtb, in_=dtb, func=AF.Exp), [stok], [stok])
            dve(lambda e: e.tensor_scalar(out=ntp, in0=tp, scalar1=-1.0, scalar2=None, op0=ALU.mult), [ctok, stok], [stok])
            dtb3 = dtb[:, :, None].broadcast_to([128, 32, 64])
            dve(lambda e: e.tensor_scalar(out=lrd, in0=lrd, scalar1=-1e-4, scalar2=None, op0=ALU.min), [stok], [stok])
            dve(lambda e: e.tensor_tensor(out=v3(lrd, 32, 64), in0=v3(lrd, 32, 64), in1=dtb3, op=ALU.mult), [stok], [stok])
            dve(lambda e: e.tensor_tensor(out=v3(th, 32, 64), in0=v3(th, 32, 64), in1=dtb3, op=ALU.mult), [stok], [stok])
            dve(lambda e: e.tensor_scalar(out=ph, in0=th, scalar1=tp, scalar2=1.0 / TWO_PI, op0=ALU.mult, op1=ALU.mult), [stok, ctok], [stok])
            range_reduce_sincos(ph, kint, sinv, cosv, stok)
            act(lambda e: e.activation(out=th, in_=lrd, func=AF.Exp, scale=tp), [stok, ctok], [stok])
            dve(lambda e: e.reciprocal(out=ph, in_=th), [stok], [stok])
            dve(lambda e: e.tensor_tensor(out=tab[2], in0=th, in1=cosv, op=ALU.mult), [stok], [tabtok])
            dve(lambda e: e.tensor_tensor(out=tab[3], in0=th, in1=sinv, op=ALU.mult), [stok], [tabtok])
            dve(lambda e: e.tensor_tensor(out=tab[0], in0=ph, in1=cosv, op=ALU.mult), [stok], [tabtok])
            dve(lambda e: e.scalar_tensor_tensor(out=tab[1], in0=ph, scalar=-1.0, in1=sinv, op0=ALU.mult, op1=ALU.mult), [stok], [tabtok])
            B.barrier()
            bre = B32[0:64, 8192:8704]; bim = B32[0:64, 8704:9216]; bbr = B32[0:64, 9216:9728]; bbi = B32[0:64, 9728:10240]
            t1 = B32[0:64, 10240:10752]; t2 = B32[0:64, 10752:11264]
            sm = [B32[0:64, 11264 + i * 32: 11264 + (i + 1) * 32] for i in range(12)]
            smi = B32[0:64, 11776:11808].bitcast(I32)
            btk = Tok()
            B.dma("sp", v3(bre, 32, 16), s5b_re[l, d].rearrange("g p n -> p g n"), writes=[btk])
            B.dma("sp", v3(bim, 32, 16), s5b_im[l, d].rearrange("g p n -> p g n"), writes=[btk])
            lr, li, dt2, mag, phs, sn, cs, are, aim, rden, cr, ci = sm
            B.dma("sp", lr, lam_re[l, d].rearrange("g p -> p g"), writes=[btk], allow_slow_non_contiguous=True)
            B.dma("sp", li, lam_im[l, d].rearrange("g p -> p g"), writes=[btk], allow_slow_non_contiguous=True)
            B.dma("sp", dt2, log_step[l, d].partition_broadcast(64), writes=[btk])
            act(lambda e: e.activation(out=dt2, in_=dt2, func=AF.Exp), [btk], [btk])
            dve(lambda e: e.tensor_scalar(out=lr, in0=lr, scalar1=-1e-4, scalar2=None, op0=ALU.min), [btk], [btk])
            dve(lambda e: e.tensor_tensor(out=mag, in0=lr, in1=dt2, op=ALU.mult), [btk], [btk])
            act(lambda e: e.activation(out=mag, in_=mag, func=AF.Exp), [btk], [btk])
            dve(lambda e: e.scalar_tensor_tensor(out=phs, in0=li, scalar=1.0 / TWO_PI, in1=dt2, op0=ALU.mult, op1=ALU.mult), [btk], [btk])
            range_reduce_sincos(phs, smi, sn, cs, btk)
            dve(lambda e: e.tensor_tensor(out=are, in0=mag, in1=cs, op=ALU.mult), [btk], [btk])
            dve(lambda e: e.tensor_scalar(out=are, in0=are, scalar1=-1.0, scalar2=None, op0=ALU.add), [btk], [btk])
            dve(lambda e: e.tensor_tensor(out=aim, in0=mag, in1=sn, op=ALU.mult), [btk], [btk])
            dve(lambda e: e.tensor_tensor(out=rden, in0=lr, in1=lr, op=ALU.mult), [btk], [btk])
            dve(lambda e: e.tensor_tensor(out=cr, in0=li, in1=li, op=ALU.mult), [btk], [btk])
            dve(lambda e: e.tensor_tensor(out=rden, in0=rden, in1=cr, op=ALU.add), [btk], [btk])
            dve(lambda e: e.reciprocal(out=rden, in_=rden), [btk], [btk])
            dve(lambda e: e.tensor_tensor(out=cr, in0=lr, in1=rden, op=ALU.mult), [btk], [btk])
            dve(lambda e: e.scalar_tensor_tensor(out=ci, in0=li, scalar=-1.0, in1=rden, op0=ALU.mult, op1=ALU.mult), [btk], [btk])
            dve(lambda e: e.tensor_tensor(out=mag, in0=are, in1=cr, op=ALU.mult), [btk], [btk])
            dve(lambda e: e.tensor_tensor(out=sn, in0=aim, in1=ci, op=ALU.mult), [btk], [btk])
            dve(lambda e: e.tensor_tensor(out=mag, in0=mag, in1=sn, op=ALU.subtract), [btk], [btk])
            dve(lambda e: e.tensor_tensor(out=phs, in0=are, in1=ci, op=ALU.mult), [btk], [btk])
            dve(lambda e: e.tensor_tensor(out=sn, in0=aim, in1=cr, op=ALU.mult), [btk], [btk])
            dve(lambda e: e.tensor_tensor(out=phs, in0=phs, in1=sn, op=ALU.add), [btk], [btk])
            nr3 = mag[:, :, None].broadcast_to([64, 32, 16]); ni3 = phs[:, :, None].broadcast_to([64, 32, 16])
            dve(lambda e: e.tensor_tensor(out=v3(t1, 32, 16), in0=v3(bre, 32, 16), in1=nr3, op=ALU.mult), [btk], [btk])
            dve(lambda e: e.tensor_tensor(out=v3(t2, 32, 16), in0=v3(bim, 32, 16), in1=ni3, op=ALU.mult), [btk], [btk])
            dve(lambda e: e.tensor_tensor(out=bbr, in0=t1, in1=t2, op=ALU.subtract), [btk], [btk])
            dve(lambda e: e.tensor_tensor(out=v3(t1, 32, 16), in0=v3(bim, 32, 16), in1=nr3, op=ALU.mult), [btk], [btk])
            dve(lambda e: e.tensor_tensor(out=v3(t2, 32, 16), in0=v3(bre, 32, 16), in1=ni3, op=ALU.mult), [btk], [btk])
            dve(lambda e: e.tensor_tensor(out=bbi, in0=t1, in1=t2, op=ALU.add), [btk], [btk])
            for nm_, ap_ in (("lr", lr), ("li", li), ("dt2", dt2), ("cs", cs), ("are", are), ("aim", aim), ("rden", rden), ("cr", cr), ("ci", ci)):
                dump("s_%s%d" % (nm_, d), ap_, [btk])
            dump("s_bbr%d" % d, bbr, [btk]); dump("s_bbi%d" % d, bbi, [btk]); dump("s_nr%d" % d, mag, [btk]); dump("s_ni%d" % d, phs, [btk])
            m8 = mask8[:, :, None].broadcast_to([128, 8, 64])
            n = 0
            for blk in range(4):
                for ri, bb in enumerate((bbr, bbi)):
                    pi = 6 + n % 2; n += 1
                    pe(lambda e, pi=pi, bb=bb, blk=blk: e.transpose(out=PS[pi][:, 0:64], in_=bb[:, blk * 128:(blk + 1) * 128], identity=ident_f[0:64, 0:64]), [btk, ctok], [PST[pi]])
                    dst = v3(Bblk, 4, 1024)[:, blk, ri * 512:(ri + 1) * 512].rearrange("p (g q) -> p g q", g=8)
                    dve(lambda e, pi=pi, dst=dst: e.tensor_tensor(out=dst, in0=PS[pi][:, None, 0:64].broadcast_to([128, 8, 64]), in1=m8, op=ALU.mult), [PST[pi], ctok], [Btok])
            cnat = [B32[:, 12288 + i * 64: 12288 + (i + 1) * 64] for i in range(8)]
            cntk = Tok()
            for ri, csrc in enumerate((s5c_re, s5c_im)):
                for blk in range(4):
                    B.dma("sp", cnat[ri * 4 + blk], csrc[l, d, blk * 8:(blk + 1) * 8].rearrange("g n p -> (g n) p"), writes=[cntk])
            xm = [B16[:, 26624 + i * 128: 26624 + (i + 1) * 128] for i in range(4)]; xmtok = [Tok() for _ in range(4)]
            n = 0
            for blk in range(4):
                for q in range(4):
                    for ri in range(2):
                        xi = n % 4; pi = 6 + n % 2; n += 1
                        for g2 in range(2):
                            mk = mask8[:, 2 * q + g2: 2 * q + g2 + 1]
                            dve(lambda e, xi=xi, g2=g2, ri=ri, blk=blk, mk=mk: e.tensor_scalar(out=xm[xi][:, g2 * 64:(g2 + 1) * 64], in0=cnat[ri * 4 + blk], scalar1=mk, scalar2=(-1.0 if ri else 1.0), op0=ALU.mult, op1=ALU.mult),
                                [cntk, ctok], [xmtok[xi]])
                        psb = PS[pi][:].bitcast(BF16)
                        pe(lambda e, psb=psb, xi=xi: e.transpose(out=psb[:, 0:128], in_=xm[xi], identity=ident_b[:]), [xmtok[xi], ctok], [PST[pi]])
                        ci_ = (blk * 4 + q) * 2 + ri
                        act(lambda e, psb=psb, ci_=ci_: e.activation(out=Cmat[:, ci_ * 128:(ci_ + 1) * 128], in_=psb[:, 0:128], func=AF.Copy), [PST[pi]], [Ctok])
            B.barrier()
            for i_ in range(4):
                dump("s_tab%d_%d" % (i_, d), tab[i_], [tabtok])
            dump("s_Bblk%d" % d, Bblk, [Btok])
            dump("s_Cmat%d" % d, Cmat, [Ctok])
            mark("s5c%d" % d)
            U = ufwd_b if d == 0 else ubwd_b; NU = nufwd_b if d == 0 else nubwd_b
            SEL = self_b if d == 0 else selb_b; NSEL = nself_b if d == 0 else nselb_b
            xn = 0
            X6 = [Tok() for _ in range(4)]; Y7 = [Tok() for _ in range(4)]
            for step, c in enumerate(ORD[d]):
                for blk in range(4):
                    zi = blk % 2
                    pr, pim = (0, 1) if zi == 0 else (2, 3)
                    cols = slice(blk * 512, (blk + 1) * 512)
                    lhs_u = v3(uT, 4, T)[:, blk, c * 128:(c + 1) * 128]
                    pe(lambda e, pr=pr, lhs_u=lhs_u, blk=blk: e.matmul(PS[pr][:], lhsT=lhs_u, rhs=v3(Bblk, 4, 1024)[:, blk, 0:512], start=True, stop=True), [uTtok, Btok], [PST[pr]])
                    pe(lambda e, pim=pim, lhs_u=lhs_u, blk=blk: e.matmul(PS[pim][:], lhsT=lhs_u, rhs=v3(Bblk, 4, 1024)[:, blk, 512:1024], start=True, stop=True), [uTtok, Btok], [PST[pim]])
                    Z = Zp[zi]
                    dve(lambda e, Z=Z, pr=pr, cols=cols: e.tensor_tensor(out=Z[0], in0=PS[pr][:], in1=tab[0][:, cols], op=ALU.mult), [PST[pr], tabtok], [Ztok[zi]])
                    dve(lambda e, Z=Z, pim=pim, cols=cols: e.tensor_tensor(out=Z[1], in0=PS[pim][:], in1=tab[1][:, cols], op=ALU.mult), [PST[pim], tabtok], [Ztok[zi]])
                    dve(lambda e, Z=Z, pim=pim, cols=cols: e.tensor_tensor(out=Z[2], in0=PS[pim][:], in1=tab[0][:, cols], op=ALU.mult), [PST[pim], tabtok], [Ztok[zi]])
                    dve(lambda e, Z=Z, pr=pr, cols=cols: e.tensor_tensor(out=Z[3], in0=PS[pr][:], in1=tab[1][:, cols], op=ALU.mult), [PST[pr], tabtok], [Ztok[zi]])
                    P = Pp[blk]
                    first = (step == 0)
                    pe(lambda e, Z=Z: e.matmul(PS[4][:], lhsT=U[:], rhs=Z[0], start=True, stop=False), [Ztok[zi], ctok], [PST[4]])
                    pe(lambda e, Z=Z, first=first: e.matmul(PS[4][:], lhsT=NU[:], rhs=Z[1], start=False, stop=first), [Ztok[zi], ctok], [PST[4]])
                    if not first:
                        pe(lambda e, P=P: e.matmul(PS[4][:], lhsT=SEL[:], rhs=P[0], start=False, stop=False), Ptk[blk] + [ctok], [PST[4]])
                        pe(lambda e, P=P: e.matmul(PS[4][:], lhsT=NSEL[:], rhs=P[1], start=False, stop=True), Ptk[blk] + [ctok], [PST[4]])
                    pe(lambda e, Z=Z: e.matmul(PS[5][:], lhsT=U[:], rhs=Z[2], start=True, stop=False), [Ztok[zi], ctok], [PST[5]])
                    pe(lambda e, Z=Z, first=first: e.matmul(PS[5][:], lhsT=U[:], rhs=Z[3], start=False, stop=first), [Ztok[zi], ctok], [PST[5]])
                    if not first:
                        pe(lambda e, P=P: e.matmul(PS[5][:], lhsT=SEL[:], rhs=P[2], start=False, stop=False), Ptk[blk] + [ctok], [PST[5]])
                        pe(lambda e, P=P: e.matmul(PS[5][:], lhsT=SEL[:], rhs=P[3], start=False, stop=True), Ptk[blk] + [ctok], [PST[5]])
                    dve(lambda e, P=P, cols=cols: e.tensor_tensor(out=P[0], in0=PS[4][:], in1=tab[2][:, cols], op=ALU.mult), [PST[4], tabtok], Ptk[blk])
                    dve(lambda e, P=P, cols=cols: e.tensor_tensor(out=P[1], in0=PS[5][:], in1=tab[3][:, cols], op=ALU.mult), [PST[5], tabtok], Ptk[blk])
                    dve(lambda e, P=P, cols=cols: e.tensor_tensor(out=P[2], in0=PS[5][:], in1=tab[2][:, cols], op=ALU.mult), [PST[5], tabtok], Ptk[blk])
                    dve(lambda e, P=P, cols=cols: e.tensor_tensor(out=P[3], in0=PS[4][:], in1=tab[3][:, cols], op=ALU.mult), [PST[4], tabtok], Ptk[blk])
                    for q in range(4):
                        qs = slice(q * 128, (q + 1) * 128)
                        xs_ = []
                        for ri in range(2):
                            xi = xn % 8; xn += 1
                            pslot = PS[6][:, (xi % 4) * 128:((xi % 4) + 1) * 128]
                            a0, a1 = (P[0], P[1]) if ri == 0 else (P[2], P[3])
                            idn = nident_b if ri == 0 else ident_b
                            pe(lambda e, pslot=pslot, a0=a0, qs=qs: e.matmul(pslot, lhsT=a0[:, qs], rhs=ident_b[:], start=True, stop=False), Ptk[blk] + [ctok], [X6[xi % 4]])
                            pe(lambda e, pslot=pslot, a1=a1, qs=qs, idn=idn: e.matmul(pslot, lhsT=a1[:, qs], rhs=idn[:], start=False, stop=True), Ptk[blk] + [ctok], [X6[xi % 4]])
                            act(lambda e, pslot=pslot, xi=xi: e.activation(out=xTt[xi], in_=pslot, func=AF.Copy), [X6[xi % 4]], [xTtok[xi]])
                            xs_.append(xi)
                        yslot = PS[7][:, (blk % 4) * 128:((blk % 4) + 1) * 128]
                        for ri in range(2):
                            ci_ = (blk * 4 + q) * 2 + ri
                            xi = xs_[ri]
                            pe(lambda e, yslot=yslot, xi=xi, ci_=ci_, q=q, ri=ri: e.matmul(yslot, lhsT=xTt[xi], rhs=Cmat[:, ci_ * 128:(ci_ + 1) * 128], start=(q == 0 and ri == 0), stop=(q == 3 and ri == 1)),
                               [xTtok[xi], Ctok], [Y7[blk]])
                    ysl = yacc[:, c * 512 + blk * 128: c * 512 + (blk + 1) * 128]
                    dve(lambda e, ysl=ysl, yslot=yslot: e.tensor_tensor(out=ysl, in0=yslot, in1=ysl, op=ALU.add), [Y7[blk], ytok[c][blk]], [ytok[c][blk]])
            B.barrier()
        mark("s5d")
        dump("s_yacc", yacc, [t_ for r_ in ytok for t_ in r_])
        mark("s5d")
        gyT = uT; gtok = Tok()
        wg = B16[:, 0:4096]; wgtok = Tok()
        bgl = B32[:, 2048:3072]; bgtok = Tok()
        B.dma("sp", bgl, b_glu[l].partition_broadcast(128), writes=[bgtok])
        load_wblock(w_glu[l], 0, 1024, 4, wg, wgtok)
        gs = [B32[:, 8192 + i * 512: 8192 + (i + 1) * 512] for i in range(4)]; gstok = [Tok() for _ in range(2)]
        gb = [B16[:, 24576 + i * 512: 24576 + (i + 1) * 512] for i in range(2)]; gbtok = [Tok() for _ in range(2)]
        GC = 2.0 * math.sqrt(2.0 / math.pi)
        for c in range(NCH):
            bi = c % 2
            y = yacc[:, c * 512:(c + 1) * 512]; t = gs[bi * 2]; sg = gs[bi * 2 + 1]
            dve(lambda e, y=y, t=t: e.tensor_tensor(out=t, in0=y, in1=y, op=ALU.mult), ytok[c], [gstok[bi]])
            dve(lambda e, t=t: e.tensor_scalar(out=t, in0=t, scalar1=0.044715, scalar2=1.0, op0=ALU.mult, op1=ALU.add), [gstok[bi]], [gstok[bi]])
            dve(lambda e, y=y, t=t: e.tensor_tensor(out=t, in0=t, in1=y, op=ALU.mult), [gstok[bi]] + ytok[c], [gstok[bi]])
            act(lambda e, t=t, sg=sg: e.activation(out=sg, in_=t, func=AF.Sigmoid, scale=GC), [gstok[bi]], [gstok[bi]])
            dve(lambda e, y=y, sg=sg, bi=bi: e.tensor_tensor(out=gb[bi], in0=y, in1=sg, op=ALU.mult), [gstok[bi]] + ytok[c], [gbtok[bi]])
            pi = 6 + c % 2
            psb = PS[pi][:].bitcast(BF16)
            for kc in range(4):
                pe(lambda e, psb=psb, kc=kc, bi=bi: e.transpose(out=psb[:, kc * 128:(kc + 1) * 128], in_=gb[bi][:, kc * 128:(kc + 1) * 128], identity=ident_b[:]), [gbtok[bi], ctok], [PST[pi]])
            act(lambda e, psb=psb, c=c: e.activation(out=v3(gyT, 4, T)[:, :, c * 128:(c + 1) * 128], in_=v3(psb[:, 0:512], 4, 128), func=AF.Copy), [PST[pi]], [gtok])
        zs = [B32[:, 10240 + i * 512: 10240 + (i + 1) * 512] for i in range(4)]; zstok = [Tok() for _ in range(2)]
        zo = [B16[:, 25600 + i * 512: 25600 + (i + 1) * 512] for i in range(2)]; zotok = [Tok() for _ in range(2)]
        for c in range(NCH):
            bi = c % 2
            for half in range(2):
                pi = half + 2 * bi
                for kc in range(4):
                    pe(lambda e, pi=pi, kc=kc, c=c, half=half: e.matmul(PS[pi][:], lhsT=v3(gyT, 4, T)[:, kc, c * 128:(c + 1) * 128], rhs=wg[:, kc * 1024 + half * 512: kc * 1024 + (half + 1) * 512], start=(kc == 0), stop=(kc == 3)),
                       [gtok, wgtok], [PST[pi]])
            va = zs[bi * 2]; gt = zs[bi * 2 + 1]
            dve(lambda e, va=va, bi=bi: e.tensor_tensor(out=va, in0=PS[2 * bi][:], in1=bgl[:, 0:512], op=ALU.add), [PST[2 * bi], bgtok], [zstok[bi]])
            dve(lambda e, gt=gt, bi=bi: e.tensor_tensor(out=gt, in0=PS[2 * bi + 1][:], in1=bgl[:, 512:1024], op=ALU.add), [PST[2 * bi + 1], bgtok], [zstok[bi]])
            act(lambda e, gt=gt: e.activation(out=gt, in_=gt, func=AF.Sigmoid), [zstok[bi]], [zstok[bi]])
            dve(lambda e, va=va, gt=gt, bi=bi: e.tensor_tensor(out=zo[bi], in0=va, in1=gt, op=ALU.mult), [zstok[bi]], [zotok[bi]])
            B.dma("sp", ymix[c * 128:(c + 1) * 128, 0:512], zo[bi], reads=[zotok[bi]], writes=[ymix_tok])
        B.barrier()

    def phase_gla(l):
        gt = [A32[:, i * 432:(i + 1) * 432] for i in range(6)]
        LF, II, Bc, colfac, rowfac, cdec = gt
        biasF = A32[:, 2592:2616]; biasI = A32[:, 2616:2640]
        gk = Tok()
        v24 = lambda ap: v3(ap, NCH, 24)
        dve(lambda e: e.memset(biasI, 0.0), [], [gk])
        dve(lambda e: e.memset(LF, 0.0), [], [gk])
        dve(lambda e: e.memset(II, 0.0), [], [gk])
        B.dma("sp", v3(biasF, 2, 12)[:, :, 0:6], ret_logit[l].partition_broadcast(128), writes=[gk])
        B.dma("sp", v3(biasF, 2, 12)[:, :, 6:12], fg_b[l].partition_broadcast(128), writes=[gk])
        B.dma("sp", v3(biasI, 2, 12)[:, :, 6:12], ig_b[l].partition_broadcast(128), writes=[gk])
        for d in range(2):
            dve(lambda e, d=d: e.tensor_copy(out=v24(LF)[:, :, d * 12 + 6: d * 12 + 12], in_=gates_sb[:, :, d * 12 + 6: d * 12 + 12]), [gates_tok, gk], [gk])
            dve(lambda e, d=d: e.tensor_copy(out=v24(II)[:, :, d * 12 + 6: d * 12 + 12], in_=gates_sb[:, :, d * 12: d * 12 + 6]), [gates_tok, gk], [gk])
        dve(lambda e: e.tensor_tensor(out=v24(LF), in0=v24(LF), in1=biasF[:, None, :].broadcast_to([128, NCH, 24]), op=ALU.add), [gk], [gk])
        dve(lambda e: e.tensor_tensor(out=v24(II), in0=v24(II), in1=biasI[:, None, :].broadcast_to([128, NCH, 24]), op=ALU.add), [gk], [gk])
        act(lambda e: e.activation(out=LF, in_=LF, func=AF.Exp, scale=-1.0), [gk], [gk])
        act(lambda e: e.activation(out=LF, in_=LF, func=AF.Ln, bias=1.0), [gk], [gk])
        dve(lambda e: e.tensor_scalar(out=LF, in0=LF, scalar1=-1.0, scalar2=None, op0=ALU.mult), [gk], [gk])
        for d in range(2):
            Uf = ufwd_f if d == 0 else ubwd_f
            rhs = v24(LF)[:, :, d * 12:(d + 1) * 12]
            pe(lambda e, Uf=Uf, rhs=rhs: e.matmul(PS[0][:, 0:216], lhsT=Uf[:], rhs=rhs, start=True, stop=True), [gk, ctok], [PST[0]])
            pe(lambda e, rhs=rhs: e.matmul(PS[1][:, 0:216], lhsT=ones_f[:], rhs=rhs, start=True, stop=True), [gk, ctok], [PST[1]])
            act(lambda e, d=d: e.activation(out=v24(Bc)[:, :, d * 12:(d + 1) * 12], in_=v3(PS[0][:, 0:216], NCH, 12), func=AF.Copy), [PST[0]], [gk])
            act(lambda e, d=d: e.activation(out=v24(cdec)[:, :, d * 12:(d + 1) * 12], in_=v3(PS[1][:, 0:216], NCH, 12), func=AF.Exp), [PST[1]], [gk])
        act(lambda e: e.activation(out=rowfac, in_=Bc, func=AF.Exp), [gk], [gk])
        dve(lambda e: e.tensor_tensor(out=colfac, in0=II, in1=Bc, op=ALU.subtract), [gk], [gk])
        act(lambda e: e.activation(out=colfac, in_=colfac, func=AF.Exp, bias=float(math.log(128.0 ** -0.5))), [gk], [gk])
        for nm, ap_ in (("g_LF", LF), ("g_II", II), ("g_Bc", Bc), ("g_colfac", colfac), ("g_rowfac", rowfac), ("g_cdec", cdec)):
            dump(nm, ap_, [gk])
        o16 = 5376
        def a16v(i):
            return A16[:, o16 + i * 2304: o16 + (i + 1) * 2304]
        qs, ks, vs, gs_, qr, kr, kc0, kc1, qT, kT0, kT1 = [a16v(i) for i in range(11)]
        vaug = A16[:, 30720:33060]
        sTm = [A16[:, 33060 + i * 128: 33060 + (i + 1) * 128] for i in range(4)]; sTtok = [Tok() for _ in range(4)]
        Cbf = [A16[:, 33572 + i * 130: 33572 + (i + 1) * 130] for i in range(2)]
        Oacc = B32[:, 0:2304]; F1 = B32[:, 2304:4608]; F2 = B32[:, 4608:6912]; F3 = B32[:, 6912:9216]
        ybf = B16[:, 18432:20736]
        Cst = [B32[:, 10368 + i * 130: 10368 + (i + 1) * 130] for i in range(2)]
        wn = B32[:, 10752:10880]
        st = B32[:, 10880:10880 + 128]
        tiny = B32[:, 11008:11008 + 64]
        for hh in range(12):
            is_ml = hh >= 6; h = hh % 6
            base = 3584 if is_ml else 512
            hk = Tok(); ck = [Tok(), Tok()]; ok_ = Tok(); tk = [Tok(), Tok()]; fk = Tok()
            for i, (t_, off) in enumerate(((qs, 0), (ks, 768), (vs, 1536), (gs_, 2304))):
                c0 = base + off + h * 128
                B.dma("sp" if i % 2 else "act", v3(t_, NCH, 128), a16[:, c0:c0 + 128].rearrange("(c p) n -> p c n", p=128), reads=[a16_tok], writes=[hk])
            B.dma("sp", wn, (ml_nw if is_ml else ret_nw)[l, h * 128:(h + 1) * 128].partition_broadcast(128), writes=[fk])
            if not is_ml:
                for (src, dst, eng) in ((qs, qr, "dve"), (ks, kr, "pool")):
                    s1 = v3(src, NCH, 128)[:, :, 0:64]; s2 = v3(src, NCH, 128)[:, :, 64:128]
                    d1 = v3(dst, NCH, 128)[:, :, 0:64]; d2 = v3(dst, NCH, 128)[:, :, 64:128]
                    f1 = v3(F1[:, 0:1152], NCH, 64) if eng == "dve" else v3(F2[:, 0:1152], NCH, 64)
                    f2 = v3(F1[:, 1152:2304], NCH, 64) if eng == "dve" else v3(F2[:, 1152:2304], NCH, 64)
                    rk = Tok()
                    opf = lambda fn, rd, wr, eng=eng: B.op(eng, fn, reads=rd, writes=wr)
                    opf(lambda e, s1=s1, f1=f1: e.tensor_tensor(out=f1, in0=s1, in1=cos_t[:], op=ALU.mult), [hk, ctok], [rk])
                    opf(lambda e, s2=s2, f2=f2: e.tensor_tensor(out=f2, in0=s2, in1=sin_t[:], op=ALU.mult), [hk, ctok], [rk])
                    opf(lambda e, d1=d1, f1=f1, f2=f2: e.tensor_tensor(out=d1, in0=f1, in1=f2, op=ALU.subtract), [rk], [hk])
                    opf(lambda e, s1=s1, f1=f1: e.tensor_tensor(out=f1, in0=s1, in1=sin_t[:], op=ALU.mult), [hk, ctok], [rk])
                    opf(lambda e, s2=s2, f2=f2: e.tensor_tensor(out=f2, in0=s2, in1=cos_t[:], op=ALU.mult), [hk, ctok], [rk])
                    opf(lambda e, d2=d2, f1=f1, f2=f2: e.tensor_tensor(out=d2, in0=f1, in1=f2, op=ALU.add), [rk], [hk])
                qq, kk = qr, kr
            else:
                qq, kk = qs, ks
            for d, kcd in enumerate((kc0, kc1)):
                col = d * 12 + hh
                dve(lambda e, kcd=kcd, kk=kk, col=col: e.tensor_tensor(out=v3(kcd, NCH, 128), in0=v3(kk, NCH, 128), in1=v24(colfac)[:, :, col:col + 1].broadcast_to([128, NCH, 128]), op=ALU.mult), [hk, gk], [hk])
            act(lambda e: e.activation(out=v3(vaug, NCH, 130)[:, :, 0:128], in_=v3(vs, NCH, 128), func=AF.Copy), [hk], [hk])
            pool(lambda e: e.memset(v3(vaug, NCH, 130)[:, :, 128:130], 1.0), [hk], [hk])
            n = 0
            for (src, dst) in ((qq, qT), (kc0, kT0), (kc1, kT1)):
                for g4 in range(0, NCH, 4):
                    cnt4 = min(4, NCH - g4)
                    pi = 6 + n % 2; n += 1
                    psb = PS[pi][:].bitcast(BF16)
                    for j in range(cnt4):
                        c = g4 + j
                        pe(lambda e, psb=psb, j=j, c=c, src=src: e.transpose(out=psb[:, j * 128:(j + 1) * 128], in_=src[:, c * 128:(c + 1) * 128], identity=ident_b[:]), [hk, ctok], [PST[pi]])
                    act(lambda e, psb=psb, g4=g4, cnt4=cnt4, dst=dst: e.activation(out=dst[:, g4 * 128:(g4 + cnt4) * 128], in_=psb[:, 0:cnt4 * 128], func=AF.Copy), [PST[pi]], [hk])
            dve(lambda e: e.memset(Oacc, 0.0), [], [ok_])
            for d in range(2):
                dve(lambda e, d=d: e.memset(Cst[d], 0.0), [], [ck[d]])
                pool(lambda e, d=d: e.memset(Cbf[d], 0.0), [], [ck[d]])
            sn_ = 0
            for step in range(NCH):
                for d in range(2):
                    c = ORD[d][step]
                    col = d * 12 + hh
                    kT = kT0 if d == 0 else kT1; kcd = kc0 if d == 0 else kc1
                    msk = ufwd_f if d == 0 else ubwd_f
                    cs_ = slice(c * 128, (c + 1) * 128)
                    pS, pO, pC = d, 2 + d, 4 + d
                    pe(lambda e, pS=pS, kT=kT, cs_=cs_: e.matmul(PS[pS][:, 0:128], lhsT=kT[:, cs_], rhs=qT[:, cs_], start=True, stop=True), [hk], [PST[pS]])
                    si = sn_ % 4; sn_ += 1
                    dve(lambda e, pS=pS, si=si, msk=msk: e.tensor_tensor(out=sTm[si], in0=PS[pS][:, 0:128], in1=msk[:], op=ALU.mult), [PST[pS], ctok], [sTtok[si]])
                    va = v3(vaug, NCH, 130)[:, c, :]
                    pe(lambda e, pO=pO, si=si, va=va: e.matmul(PS[pO][:, 0:130], lhsT=sTm[si], rhs=va, start=True, stop=False), [sTtok[si], hk], [PST[pO]])
                    pe(lambda e, pO=pO, cs_=cs_, d=d: e.matmul(PS[pO][:, 0:130], lhsT=qT[:, cs_], rhs=Cbf[d], start=False, stop=True), [hk, ck[d]], [PST[pO]])
                    rf = v24(rowfac)[:, c, col:col + 1]
                    oslice = Oacc[:, c * 128:(c + 1) * 128]
                    if is_ml:
                        t1_ = tiny[:, d * 4:d * 4 + 1]; t2_ = tiny[:, d * 4 + 1:d * 4 + 2]
                        act(lambda e, pO=pO, t1_=t1_, rf=rf: e.activation(out=t1_, in_=PS[pO][:, 128:129], func=AF.Abs, scale=rf), [PST[pO], gk], [tk[d]])
                        dve(lambda e, t1_=t1_: e.tensor_scalar(out=t1_, in0=t1_, scalar1=1.0, scalar2=None, op0=ALU.max), [tk[d]], [tk[d]])
                        dve(lambda e, t1_=t1_: e.reciprocal(out=t1_, in_=t1_), [tk[d]], [tk[d]])
                        dve(lambda e, t1_=t1_, t2_=t2_, rf=rf: e.tensor_tensor(out=t2_, in0=t1_, in1=rf, op=ALU.mult), [tk[d], gk], [tk[d]])
                        rr = t2_
                    else:
                        rr = rf
                    dve(lambda e, pO=pO, rr=rr, oslice=oslice: e.scalar_tensor_tensor(out=oslice, in0=PS[pO][:, 0:128], scalar=rr, in1=oslice, op0=ALU.mult, op1=ALU.add), [PST[pO], tk[d], gk, ok_], [ok_])
                    if step < NCH - 1:
                        pe(lambda e, pC=pC, kcd=kcd, cs_=cs_, va=va: e.matmul(PS[pC][:, 0:130], lhsT=kcd[:, cs_], rhs=va, start=True, stop=True), [hk], [PST[pC]])
                        cd = v24(cdec)[:, c, col:col + 1]
                        dve(lambda e, d=d, cd=cd: e.tensor_scalar(out=Cst[d], in0=Cst[d], scalar1=cd, scalar2=None, op0=ALU.mult), [ck[d], gk], [ck[d]])
                        dve(lambda e, d=d, cd=cd, pC=pC: e.scalar_tensor_tensor(out=Cst[d], in0=PS[pC][:, 0:130], scalar=cd, in1=Cst[d], op0=ALU.mult, op1=ALU.add), [PST[pC], ck[d], gk], [ck[d]])
                        act(lambda e, d=d: e.activation(out=Cbf[d], in_=Cst[d], func=AF.Copy), [ck[d]], [ck[d]])
            dump("g_qr%d" % hh, qq, [hk])
            dump("g_kr%d" % hh, kk, [hk])
            dump("g_O%d" % hh, Oacc, [ok_])
            dump("g_vaug%d" % hh, vaug, [hk])
            dump("g_qT%d" % hh, qT, [hk])
            dump("g_kc0_%d" % hh, kc0, [hk])
            O3 = v3(Oacc, NCH, 128)
            mean = st[:, 0:18]; ssq = st[:, 18:36]
            if not is_ml:
                dve(lambda e: e.tensor_reduce(out=mean, in_=O3, axis=AX.X, op=ALU.add), [ok_], [fk])
                dve(lambda e: e.scalar_tensor_tensor(out=O3, in0=mean[:, :, None].broadcast_to([128, NCH, 128]), scalar=-1.0 / 128.0, in1=O3, op0=ALU.mult, op1=ALU.add), [fk, ok_], [ok_])
            act(lambda e: e.activation(out=F1, in_=Oacc, func=AF.Square), [ok_], [fk])
            dve(lambda e: e.tensor_reduce(out=ssq, in_=v3(F1, NCH, 128), axis=AX.X, op=ALU.add), [fk], [fk])
            var_ = st[:, 36:54]; sd_ = st[:, 54:72]; rstd_ = st[:, 72:90]
            dve(lambda e: e.tensor_scalar(out=var_, in0=ssq, scalar1=1.0 / 128.0, scalar2=EPS, op0=ALU.mult, op1=ALU.add), [fk], [fk])
            act(lambda e: e.activation(out=sd_, in_=var_, func=AF.Sqrt), [fk], [fk])
            dve(lambda e: e.reciprocal(out=rstd_, in_=sd_), [fk], [fk])
            act(lambda e: e.activation(out=F2, in_=gs_, func=(AF.Sigmoid if is_ml else AF.Silu)), [hk], [fk])
            pool(lambda e: e.tensor_tensor(out=v3(F2, NCH, 128), in0=v3(F2, NCH, 128), in1=wn[:, None, :].broadcast_to([128, NCH, 128]), op=ALU.mult), [fk], [fk])
            dve(lambda e: e.tensor_tensor(out=v3(F3, NCH, 128), in0=O3, in1=rstd_[:, :, None].broadcast_to([128, NCH, 128]), op=ALU.mult), [fk, ok_], [fk])
            dve(lambda e: e.tensor_tensor(out=ybf, in0=F3, in1=F2, op=ALU.mult), [fk], [fk])
            dump("g_F1_%d" % hh, F1, [fk]); dump("g_F2_%d" % hh, F2, [fk]); dump("g_F3_%d" % hh, F3, [fk]); dump("g_st%d" % hh, st, [fk]); dump("g_ybf%d" % hh, ybf, [fk])
            yc0 = (1280 if is_ml else 512) + h * 128
            B.dma("sp", ymix[:, yc0:yc0 + 128].rearrange("(c p) n -> p c n", p=128), v3(ybf, NCH, 128), reads=[fk], writes=[ymix_tok])
            B.barrier()


    gates_d = dscr("gates_d", [128, NCH * 24])
    IN_BLOCKS = [(cb * 512, 512) for cb in range(13)] + [(6656, 24)]
    OUT_BLOCKS = [(cb * 512, 512) for cb in range(4)]
    try:
        setup()
        phase_mod()
        mark("mod")
        for l in range(DEPTH):
            last = (l == DEPTH - 1)
            src = xh if l == 0 else resB
            stok = None if l == 0 else res_tok["B"]
            prep_norm_mod(l)
            phase_norm_T(src, 0, range(NCH))
            B.barrier()
            proj_tok(w_in[l], IN_BLOCKS, range(NCH), ep_inproj)
            B.barrier()
            if "gates_d" in debug and l == 0:
                B.dma("sp", gates_d, gates_sb[:].rearrange("p a b -> p (a b)"), reads=[gates_tok], writes=[Tok()])
            mark("inproj%d" % l)
            phase_s5(l)
            mark("s5%d" % l)
            phase_gla(l)
            B.barrier()
            mark("mix%d" % l)
            chunks = range(2, NCH) if last else range(NCH)
            load_gate(l, 2)
            phase_plain_T(ymix, chunks)
            B.barrier()
            proj_tok(w_out[l], OUT_BLOCKS, chunks, make_ep_resid(src, stok, resA, res_tok["A"]))
            B.barrier()
            mark("outproj%d" % l)
            phase_norm_T(resA, 1, chunks)
            B.barrier()
            phase_ffn1(w_ff1[l], chunks)
            B.barrier()
            mark("ffn1%d" % l)
            load_gate(l, 5)
            phase_ffn2(w_ff2[l], chunks, resA, res_tok["A"], resB, res_tok["B"])
            B.barrier()
            mark("layer%d" % l)
        phase_final(resB, res_tok["B"])
    except _Stop:
        pass
    B.barrier()
    print("instr counts", {e: len(B.ops[e]) for e in B.engs}, flush=True)
    B.replay()
    return nc, es, dbg


def _consts():
    idx = np.arange(128)
    c = {}
    c["k_ident"] = np.eye(128, dtype=np.float32)
    c["k_ufwd"] = (idx[:, None] <= idx[None, :]).astype(np.float32)
    c["k_ubwd"] = (idx[:, None] >= idx[None, :]).astype(np.float32)
    c["k_self"] = np.zeros((128, 128), np.float32); c["k_self"][127, :] = 1.0
    c["k_selb"] = np.zeros((128, 128), np.float32); c["k_selb"][0, :] = 1.0
    t = np.arange(SEQ)
    rows = (t // 64).astype(np.float32); cols = (t % 64).astype(np.float32)
    inv = (10000.0 ** (-np.arange(32, dtype=np.float32) / 32.0)).astype(np.float32)
    ang = np.concatenate([rows[:, None] * inv, cols[:, None] * inv], axis=-1).astype(np.float32)
    cos = np.ones((T, 64), np.float32); sin = np.zeros((T, 64), np.float32)
    cos[CTX:] = np.cos(ang); sin[CTX:] = np.sin(ang)
    c["k_cos"] = np.ascontiguousarray(cos.reshape(NCH, 128, 64).transpose(1, 0, 2))
    c["k_sin"] = np.ascontiguousarray(sin.reshape(NCH, 128, 64).transpose(1, 0, 2))
    c["k_mask8"] = (idx[:, None] // 16 == np.arange(8)[None, :]).astype(np.float32)
    c["k_tpos"] = np.stack([idx + 1.0, 128.0 - idx], axis=1).astype(np.float32)
    return c


_PROG = {}


def kernel(**inputs):
    if "p" not in _PROG:
        _PROG["p"] = build_program()
    nc, es, _ = _PROG["p"]
    f32 = lambda a: np.ascontiguousarray(np.asarray(a, dtype=np.float32))
    x = f32(inputs["x"]); ctx = f32(inputs["ctx"]); c = f32(inputs["c"]); c_ctx = f32(inputs["c_ctx"])
    shared = {k: f32(inputs[k]) for k in ("w_mod", "b_mod", "norm1_w", "norm2_w", "w_in", "w_out", "s5_lam_re", "s5_lam_im", "s5_log_step",
                                          "s5_b_re", "s5_b_im", "s5_c_re", "s5_c_im", "s5_d", "s5_w_glu", "s5_b_glu", "ret_decay_logit",
                                          "ret_norm_w", "mlstm_igate_b", "mlstm_fgate_b", "mlstm_norm_w", "w_ff1", "w_ff2", "norm_f_w")}
    shared.update(_consts())
    in_maps = []
    for core in range(8):
        b = core % NB
        m = dict(shared)
        m["xh"] = np.ascontiguousarray(np.concatenate([ctx[b], x[b]], axis=0))
        m["cc"] = np.ascontiguousarray(np.stack([c[b], c_ctx], axis=0))
        in_maps.append(m)
    res = run_bass_kernel_spmd(nc, in_maps, core_ids=list(range(8)))
    outs = [np.asarray(res.results[b]["out"], dtype=np.float32) for b in range(NB)]
    return np.stack(outs, axis=0)
```

```python
import math
from contextlib import ExitStack
import numpy as np
import ml_dtypes
import concourse.bass as bass
import concourse.mybir as mybir
from concourse.bass_utils import run_bass_kernel_spmd

F32 = mybir.dt.float32
BF16 = mybir.dt.bfloat16
I32 = mybir.dt.int32
AF = mybir.ActivationFunctionType
ALU = mybir.AluOpType
AX = mybir.AxisListType

D = 2048
NB = 4
SEQ = 2048
CTX = 256
T = SEQ + CTX
NCH = T // 128
DEPTH = 2
INW = 6680
DFF = 8192
EPS = 1e-6
NDMASEM = 12
STRICT = True


class Tok:
    __slots__ = ("w", "r")

    def __init__(self):
        self.w = None
        self.r = []


class _Rec:
    def __init__(self):
        self.call = None

    def __getattr__(self, name):
        def f(*a, **k):
            self.call = (name, a, k)
            return self
        return f


class Builder:
    def __init__(self, nc, es):
        self.nc = nc
        self.es = es
        self.engs = ["pe", "dve", "act", "pool", "sp"]
        self.ops = {e: [] for e in self.engs}
        self.seq = {e: 0 for e in self.engs}
        self.waited = {e: {} for e in self.engs}
        self.sems = {}
        for e in ["pe", "dve", "act", "pool"]:
            self.sems[("e", e)] = es.enter_context(nc.semaphore("p_" + e))
        self.dcount = {}
        self.drr = {"sp": 0, "pool": 0, "act": 0}
        for q in ["sp", "pool", "act"]:
            for i in range(NDMASEM):
                self.sems[("d", q, i)] = es.enter_context(nc.semaphore("d_%s%d" % (q, i)))
                self.dcount[("d", q, i)] = 0
        self.final = []

    def _need(self, eng, deps):
        out = []
        for (k, v) in deps:
            if k == ("e", eng) and not (STRICT or eng == "pool"):
                continue
            if k == ("e", eng) and eng == "pe":
                continue
            if self.waited[eng].get(k, 0) >= v:
                continue
            self.waited[eng][k] = v
            out.append((k, v))
        return out

    def _deps(self, reads, writes):
        deps = []
        for t in reads:
            if t.w is not None:
                deps.append(t.w)
        for t in writes:
            if t.w is not None:
                deps.append(t.w)
            deps.extend(t.r)
        return deps

    def op(self, eng, fn, reads=(), writes=()):
        deps = self._deps(reads, writes)
        waits = self._need(eng, deps)
        self.seq[eng] += 1
        done = (("e", eng), self.seq[eng])
        rec = _Rec()
        fn(rec)
        call = rec.call
        fn = lambda e, call=call: getattr(e, call[0])(*call[1], **call[2])
        self.ops[eng].append((waits, fn, done))
        for t in reads:
            t.r.append(done)
        for t in writes:
            t.w = done
            t.r = []
        return done

    def dma(self, q, out, in_, reads=(), writes=(), **kw):
        i = self.drr[q]
        self.drr[q] = (i + 1) % NDMASEM
        k = ("d", q, i)
        deps = self._deps(reads, writes)
        if self.dcount[k] > 0:
            deps.append((k, self.dcount[k]))
        waits = self._need(q, deps)
        self.dcount[k] += 16
        done = (k, self.dcount[k])
        fn = lambda e, out=out, in_=in_, kw=kw: e.dma_start(out=out, in_=in_, **kw)
        self.ops[q].append((waits, fn, done))
        for t in reads:
            t.r.append(done)
        for t in writes:
            t.w = done
            t.r = []
        return done

    def barrier(self):
        allk = [(("e", e), self.seq[e]) for e in ["pe", "dve", "act", "pool"] if self.seq[e] > 0]
        allk += [(k, v) for k, v in self.dcount.items() if v > 0]
        for e in self.engs:
            waits = self._need(e, allk)
            if waits:
                self.ops[e].append((waits, None, None))

    def check_deadlock(self):
        vals = {k: 0 for k in self.sems}
        pos = {e: 0 for e in self.engs}
        progress = True
        while progress:
            progress = False
            for e in self.engs:
                ops = self.ops[e]
                while pos[e] < len(ops):
                    waits, fn, done = ops[pos[e]]
                    if any(vals[k] < v for (k, v) in waits):
                        break
                    if done is not None:
                        vals[done[0]] += 1 if done[0][0] == "e" else 16
                        assert vals[done[0]] == done[1], (e, pos[e], done, vals[done[0]])
                    pos[e] += 1
                    progress = True
        stuck = {e: (pos[e], len(self.ops[e])) for e in self.engs if pos[e] < len(self.ops[e])}
        if stuck:
            for e in stuck:
                waits, fn, done = self.ops[e][pos[e]]
                print("DEADLOCK", e, pos[e], [(k, v, vals[k]) for (k, v) in waits], flush=True)
            raise RuntimeError("deadlock in semaphore protocol: %s" % stuck)

    def replay(self):
        self.check_deadlock()
        nc = self.nc
        block = self.es.enter_context(nc.Block())
        hw = {"pe": block.tensor, "dve": block.vector, "act": block.scalar, "pool": block.gpsimd, "sp": block.sync}
        for e in self.engs:
            ops = self.ops[e]

            def body(eng, ops=ops):
                for (waits, fn, done) in ops:
                    for (k, v) in waits:
                        eng.wait_ge(self.sems[k], v)
                    if fn is None:
                        continue
                    inst = fn(eng)
                    if done[0][0] == "e":
                        inst.then_inc(self.sems[done[0]], 1)
                    else:
                        inst.then_inc(self.sems[done[0]], 16)

            hw[e](body)


class _Stop(Exception):
    pass


def build_program(debug=(), upto=None):
    nc = bass.Bass("TRN2", target_bir_lowering=False)
    es = ExitStack()
    B = Builder(nc, es)
    dbg = {}

    def din(name, shape, dt=F32):
        return nc.dram_tensor(name, list(shape), dt, kind="ExternalInput").ap()

    def dscr(name, shape, dt=F32):
        kind = "ExternalOutput" if name in debug else "Internal"
        ap = nc.dram_tensor(name, list(shape), dt, kind=kind).ap()
        if name in debug:
            dbg[name] = ap
        return ap

    def mark(name):
        if upto == name:
            raise _Stop()

    def dump(name, ap, toks):
        if name not in debug or name in dbg:
            return
        t = nc.dram_tensor(name, list(ap.shape), ap.dtype, kind="ExternalOutput").ap()
        dbg[name] = t
        B.dma("sp", t, ap, reads=toks, writes=[Tok()])

    def sb(name, shape, dt=F32):
        return es.enter_context(nc.sbuf_tensor(name, list(shape), dt))

    xh = din("xh", [T, D])
    cc = din("cc", [2, D])
    w_mod = din("w_mod", [DEPTH, D, 6 * D])
    b_mod = din("b_mod", [DEPTH, 6 * D])
    norm1_w = din("norm1_w", [DEPTH, D])
    norm2_w = din("norm2_w", [DEPTH, D])
    w_in = din("w_in", [DEPTH, D, INW])
    w_out = din("w_out", [DEPTH, D, D])
    lam_re = din("s5_lam_re", [DEPTH, 2, 32, 64])
    lam_im = din("s5_lam_im", [DEPTH, 2, 32, 64])
    log_step = din("s5_log_step", [DEPTH, 2, 32])
    s5b_re = din("s5_b_re", [DEPTH, 2, 32, 64, 16])
    s5b_im = din("s5_b_im", [DEPTH, 2, 32, 64, 16])
    s5c_re = din("s5_c_re", [DEPTH, 2, 32, 16, 64])
    s5c_im = din("s5_c_im", [DEPTH, 2, 32, 16, 64])
    s5_d = din("s5_d", [DEPTH, 512])
    w_glu = din("s5_w_glu", [DEPTH, 512, 1024])
    b_glu = din("s5_b_glu", [DEPTH, 1024])
    ret_logit = din("ret_decay_logit", [DEPTH, 2, 6])
    ret_nw = din("ret_norm_w", [DEPTH, 768])
    ig_b = din("mlstm_igate_b", [DEPTH, 2, 6])
    fg_b = din("mlstm_fgate_b", [DEPTH, 2, 6])
    ml_nw = din("mlstm_norm_w", [DEPTH, 768])
    w_ff1 = din("w_ff1", [DEPTH, D, DFF])
    w_ff2 = din("w_ff2", [DEPTH, DFF, D])
    norm_f = din("norm_f_w", [D])
    k_ident = din("k_ident", [128, 128])
    k_ufwd = din("k_ufwd", [128, 128])
    k_ubwd = din("k_ubwd", [128, 128])
    k_self = din("k_self", [128, 128])
    k_selb = din("k_selb", [128, 128])
    k_cos = din("k_cos", [128, NCH, 64])
    k_sin = din("k_sin", [128, NCH, 64])
    k_mask8 = din("k_mask8", [128, 8])
    k_tpos = din("k_tpos", [128, 2])
    out = nc.dram_tensor("out", [SEQ, D], F32, kind="ExternalOutput").ap()

    modd = dscr("modd", [DEPTH, 2, 6 * D])
    a16 = dscr("a16", [T, 6656], BF16)
    ymix = dscr("ymix", [T, D], BF16)
    resA = dscr("resA", [T, D])
    resB = dscr("resB", [T, D])
    h1d = dscr("h1d", [NCH, 128, 64, 128], BF16)

    PS = [es.enter_context(nc.psum_tensor("ps%d" % i, [128, 512], F32)) for i in range(8)]
    PST = [Tok() for _ in range(8)]

    ident_f = sb("ident_f", [128, 128]); ident_b = sb("ident_b", [128, 128], BF16)
    nident_b = sb("nident_b", [128, 128], BF16)
    ufwd_f = sb("ufwd_f", [128, 128]); ubwd_f = sb("ubwd_f", [128, 128])
    ufwd_b = sb("ufwd_b", [128, 128], BF16); ubwd_b = sb("ubwd_b", [128, 128], BF16)
    nufwd_b = sb("nufwd_b", [128, 128], BF16); nubwd_b = sb("nubwd_b", [128, 128], BF16)
    self_b = sb("self_b", [128, 128], BF16); selb_b = sb("selb_b", [128, 128], BF16)
    nself_b = sb("nself_b", [128, 128], BF16); nselb_b = sb("nselb_b", [128, 128], BF16)
    ones_f = sb("ones_f", [128, 128])
    mask8 = sb("mask8", [128, 8]); tpos = sb("tpos", [128, 2])
    cos_t = sb("cos_t", [128, NCH, 64]); sin_t = sb("sin_t", [128, NCH, 64])
    ctok = Tok()
    actT = sb("actT", [128, 16, T], BF16)
    actT_tok = [Tok() for _ in range(NCH)]
    arenaB = sb("arenaB", [128, 20480])
    arenaB_bf = arenaB[:].bitcast(BF16)
    gates_sb = sb("gates_sb", [128, NCH, 24]); gates_tok = Tok()
    small = sb("small", [128, 256]);

    def setup():
        stg = arenaB
        loads = [(k_ident, 0), (k_ufwd, 128), (k_ubwd, 256), (k_self, 384), (k_selb, 512)]
        for ap, o in loads:
            B.dma("sp", stg[:, o:o + 128], ap, writes=[ctok])
        B.dma("sp", mask8[:], k_mask8, writes=[ctok])
        B.dma("sp", tpos[:], k_tpos, writes=[ctok])
        B.dma("sp", cos_t[:], k_cos, writes=[ctok])
        B.dma("sp", sin_t[:], k_sin, writes=[ctok])
        cp = lambda o, i: B.op("dve", lambda e, o=o, i=i: e.tensor_copy(out=o, in_=i), reads=[ctok], writes=[ctok])
        ng = lambda o, i: B.op("dve", lambda e, o=o, i=i: e.tensor_scalar(out=o, in0=i, scalar1=-1.0, scalar2=None, op0=ALU.mult), reads=[ctok], writes=[ctok])
        cp(ident_f[:], stg[:, 0:128]); cp(ident_b[:], stg[:, 0:128]); ng(nident_b[:], stg[:, 0:128])
        cp(ufwd_f[:], stg[:, 128:256]); cp(ufwd_b[:], stg[:, 128:256]); ng(nufwd_b[:], stg[:, 128:256])
        cp(ubwd_f[:], stg[:, 256:384]); cp(ubwd_b[:], stg[:, 256:384]); ng(nubwd_b[:], stg[:, 256:384])
        cp(self_b[:], stg[:, 384:512]); ng(nself_b[:], stg[:, 384:512])
        cp(selb_b[:], stg[:, 512:640]); ng(nselb_b[:], stg[:, 512:640])
        B.op("dve", lambda e: e.memset(ones_f[:], 1.0), writes=[ctok])
        B.barrier()

    def phase_mod():
        cT = sb("cT", [128, 16, 2]); sT = sb("sT", [128, 16, 2]); t_c = Tok()
        for s in range(2):
            B.dma("sp", cT[:, :, s], cc[s].rearrange("(c p) -> p c", p=128), writes=[t_c], allow_slow_non_contiguous=True)
        B.op("act", lambda e: e.activation(out=sT[:], in_=cT[:], func=AF.Silu), reads=[t_c], writes=[t_c])
        wst = [arenaB[:, i * 2048:(i + 1) * 2048] for i in range(4)]
        wtok = [Tok() for _ in range(4)]
        bst = [arenaB[0:2, 8192 + i * 512: 8192 + (i + 1) * 512] for i in range(2)]
        btok = [Tok() for _ in range(2)]
        ost = [arenaB[0:2, 9216 + i * 512: 9216 + (i + 1) * 512] for i in range(2)]
        otok = [Tok() for _ in range(2)]
        n = 0
        for l in range(DEPTH):
            for cb in range(24):
                bi = cb % 2
                B.dma("sp", bst[bi], b_mod[l, cb * 512:(cb + 1) * 512].partition_broadcast(2), writes=[btok[bi]])
                pt = PST[cb % 2]; ps = PS[cb % 2]
                for kg in range(4):
                    wi = n % 4; n += 1
                    src = w_mod[l, kg * 512:(kg + 1) * 512, cb * 512:(cb + 1) * 512].rearrange("(k p) n -> p k n", p=128)
                    B.dma("act" if kg % 2 else "sp", wst[wi].rearrange("p (k n) -> p k n", k=4), src, writes=[wtok[wi]])
                    for k4 in range(4):
                        kc = kg * 4 + k4
                        B.op("pe", lambda e, ps=ps, kc=kc, wi=wi, k4=k4: e.matmul(ps[0:2, :], lhsT=sT[:, kc, :], rhs=wst[wi][:, k4 * 512:(k4 + 1) * 512], start=(kc == 0), stop=(kc == 15)),
                             reads=[t_c, wtok[wi]], writes=[pt])
                B.op("dve", lambda e, ps=ps, bi=bi: e.tensor_tensor(out=ost[bi], in0=ps[0:2, :], in1=bst[bi], op=ALU.add), reads=[pt, btok[bi]], writes=[otok[bi]])
                B.dma("sp", modd[l, :, cb * 512:(cb + 1) * 512], ost[bi], reads=[otok[bi]], writes=[modtok])
        B.barrier()

    modtok = Tok()

    def load_vec16(dst, src, tok):
        B.dma("sp", dst, src.rearrange("(c p) -> p c", p=128), reads=[modtok], writes=[tok], allow_slow_non_contiguous=True)

    gsh = sb("gsh", [128, 2, 2, 2, 16]); gsh_tok = Tok()
    gate_bc = sb("gate_bc", [128, 2, D]); gate_tok = Tok()

    def prep_norm_mod(l):
        nw = small[:, 0:32].rearrange("p (a c) -> p a c", a=2); ntok = Tok()
        load_vec16(nw[:, 0, :], norm1_w[l], ntok); load_vec16(nw[:, 1, :], norm2_w[l], ntok)
        tmp = small[:, 32:160].rearrange("p (a s g c) -> p a s g c", a=2, s=2, g=2)
        for a in range(2):
            for s in range(2):
                load_vec16(tmp[:, a, s, 1, :], modd[l, s, (3 * a) * D:(3 * a + 1) * D], ntok)
                load_vec16(tmp[:, a, s, 0, :], modd[l, s, (3 * a + 1) * D:(3 * a + 2) * D], ntok)
        for a in range(2):
            for s in range(2):
                B.op("dve", lambda e, a=a, s=s: e.scalar_tensor_tensor(out=gsh[:, a, s, 0, :], in0=tmp[:, a, s, 0, :], scalar=1.0, in1=nw[:, a, :], op0=ALU.add, op1=ALU.mult),
                     reads=[ntok], writes=[gsh_tok])
                B.op("dve", lambda e, a=a, s=s: e.tensor_copy(out=gsh[:, a, s, 1, :], in_=tmp[:, a, s, 1, :]), reads=[ntok], writes=[gsh_tok])

    def load_gate(l, which):
        for s in range(2):
            B.dma("sp", gate_bc[:, s, :], modd[l, s, which * D:(which + 1) * D].partition_broadcast(128), reads=[modtok], writes=[gate_tok])

    def phase_norm_T(src, a_idx, chunks):
        xin = [arenaB[:, i * 2048:(i + 1) * 2048] for i in range(2)]; xtok = [Tok() for _ in range(2)]
        xs = [arenaB_bf[:, 8192 + i * 2048: 8192 + (i + 1) * 2048] for i in range(2)]; xstok = [Tok() for _ in range(2)]
        junk = arenaB_bf[:, 12288:14336]; jtok = Tok()
        st = small[:, 160:176]; sttok = [Tok() for _ in range(2)]
        for n, tc in enumerate(chunks):
            bi = n % 2
            s = 1 if tc < 2 else 0
            B.dma("sp", xin[bi], src[tc * 128:(tc + 1) * 128, :], writes=[xtok[bi]])
            ss = st[:, bi * 4:bi * 4 + 1]; rs = st[:, bi * 4 + 1:bi * 4 + 2]
            B.op("act", lambda e, bi=bi, ss=ss: e.activation(out=junk, in_=xin[bi], func=AF.Square, accum_out=ss), reads=[xtok[bi]], writes=[jtok, sttok[bi]])
            B.op("dve", lambda e, ss=ss, rs=rs: e.tensor_scalar(out=rs, in0=ss, scalar1=1.0 / D, scalar2=EPS, op0=ALU.mult, op1=ALU.add), reads=[sttok[bi]], writes=[sttok[bi]])
            B.op("act", lambda e, rs=rs: e.activation(out=rs, in_=rs, func=AF.Sqrt), reads=[sttok[bi]], writes=[sttok[bi]])
            B.op("dve", lambda e, rs=rs: e.reciprocal(out=rs, in_=rs), reads=[sttok[bi]], writes=[sttok[bi]])
            B.op("act", lambda e, bi=bi, rs=rs: e.activation(out=xs[bi], in_=xin[bi], func=AF.Copy, scale=rs), reads=[xtok[bi], sttok[bi]], writes=[xstok[bi]])
            for q in range(4):
                pi = 4 + (n * 4 + q) % 4
                psb = PS[pi][:].bitcast(BF16)
                for j in range(4):
                    kc = q * 4 + j
                    B.op("pe", lambda e, psb=psb, j=j, kc=kc, bi=bi: e.transpose(out=psb[:, j * 128:(j + 1) * 128], in_=xs[bi][:, kc * 128:(kc + 1) * 128], identity=ident_b[:]),
                         reads=[xstok[bi], ctok], writes=[PST[pi]])
                for j in range(4):
                    kc = q * 4 + j
                    B.op("dve", lambda e, psb=psb, j=j, kc=kc, tc=tc, s=s: e.tensor_scalar(out=actT[:, kc, tc * 128:(tc + 1) * 128], in0=psb[:, j * 128:(j + 1) * 128],
                                                                                 scalar1=gsh[:, a_idx, s, 0, kc:kc + 1], scalar2=gsh[:, a_idx, s, 1, kc:kc + 1], op0=ALU.mult, op1=ALU.add),
                         reads=[PST[pi], gsh_tok], writes=[actT_tok[tc]])

    def phase_plain_T(src, chunks):
        xs = [arenaB_bf[:, i * 2048:(i + 1) * 2048] for i in range(2)]; xstok = [Tok() for _ in range(2)]
        for n, tc in enumerate(chunks):
            bi = n % 2
            B.dma("sp", xs[bi], src[tc * 128:(tc + 1) * 128, :], writes=[xstok[bi]])
            for q in range(4):
                pi = 4 + (n * 4 + q) % 4
                psb = PS[pi][:].bitcast(BF16)
                for j in range(4):
                    kc = q * 4 + j
                    B.op("pe", lambda e, psb=psb, j=j, kc=kc, bi=bi: e.transpose(out=psb[:, j * 128:(j + 1) * 128], in_=xs[bi][:, kc * 128:(kc + 1) * 128], identity=ident_b[:]),
                         reads=[xstok[bi], ctok], writes=[PST[pi]])
                B.op("act", lambda e, psb=psb, q=q, tc=tc: e.activation(out=actT[:, q * 4:(q + 1) * 4, tc * 128:(tc + 1) * 128], in_=psb[:, 0:512].rearrange("p (j t) -> p j t", j=4), func=AF.Copy),
                     reads=[PST[pi]], writes=[actT_tok[tc]])

    WB_OFF = 16384
    wstage = [arenaB[:, 4096 + i * 2048: 4096 + (i + 1) * 2048] for i in range(2)]; wstok = [Tok() for _ in range(2)]
    wcnt = [0]

    def load_wblock(wsrc, c0, w, KC, dst, dtok, stage=None, stok_=None, queues=("act", "sp")):
        stage = stage or wstage; stok_ = stok_ or wstok
        g = max(1, 2048 // w)
        for k0 in range(0, KC, g):
            kk = min(g, KC - k0)
            si = wcnt[0] % 2; wcnt[0] += 1
            src = wsrc[k0 * 128:(k0 + kk) * 128, c0:c0 + w].rearrange("(k p) n -> p k n", p=128)
            B.dma(queues[si % len(queues)], stage[si][:, 0:kk * w].rearrange("p (k n) -> p k n", k=kk), src, writes=[stok_[si]])
            B.op("pool", lambda e, si=si, k0=k0, kk=kk: e.tensor_copy(out=dst[:, k0 * w:(k0 + kk) * w], in_=stage[si][:, 0:kk * w]), reads=[stok_[si]], writes=[dtok])

    wbuf = [arenaB_bf[:, 16384 + i * 8192: 16384 + (i + 1) * 8192] for i in range(2)]; wbtok = [Tok() for _ in range(2)]
    ostg_f = [arenaB[:, 16384 + i * 512: 16384 + (i + 1) * 512] for i in range(4)]; ostok = [Tok() for _ in range(4)]
    ostg2_f = [arenaB[:, 18432 + i * 512: 18432 + (i + 1) * 512] for i in range(4)]; os2tok = [Tok() for _ in range(4)]
    cnt = {"ps": 0, "o": 0}

    def proj_tok(wsrc, col_blocks, chunks, epilogue):
        load_wblock(wsrc, col_blocks[0][0], col_blocks[0][1], 16, wbuf[0], wbtok[0])
        for cbi, (c0, w) in enumerate(col_blocks):
            bi = cbi % 2
            if cbi + 1 < len(col_blocks):
                load_wblock(wsrc, col_blocks[cbi + 1][0], col_blocks[cbi + 1][1], 16, wbuf[1 - bi], wbtok[1 - bi])
            for tc in chunks:
                pi = cnt["ps"] % 4; cnt["ps"] += 1
                for kc in range(16):
                    B.op("pe", lambda e, pi=pi, kc=kc, tc=tc, bi=bi, w=w: e.matmul(PS[pi][:, 0:w], lhsT=actT[:, kc, tc * 128:(tc + 1) * 128], rhs=wbuf[bi][:, kc * w:(kc + 1) * w], start=(kc == 0), stop=(kc == 15)),
                         reads=[actT_tok[tc], wbtok[bi]], writes=[PST[pi]])
                epilogue(tc, c0, w, pi)

    def ep_inproj(tc, c0, w, pi):
        if c0 >= 6656:
            B.op("act", lambda e: e.activation(out=gates_sb[:, tc, :], in_=PS[pi][:, 0:24], func=AF.Copy), reads=[PST[pi]], writes=[gates_tok])
            return
        oi = cnt["o"] % 4; cnt["o"] += 1
        ob = ostg_f[oi].bitcast(BF16)[:, 0:512]
        B.op("act", lambda e: e.activation(out=ob, in_=PS[pi][:, 0:512], func=AF.Copy), reads=[PST[pi]], writes=[ostok[oi]])
        B.dma("sp", a16[tc * 128:(tc + 1) * 128, c0:c0 + 512], ob, reads=[ostok[oi]], writes=[a16_tok])

    a16_tok = Tok(); ymix_tok = Tok(); res_tok = {"A": Tok(), "B": Tok()}; h1_tok = Tok()

    def make_ep_resid(rsrc, rsrc_tok, rdst, rdst_tok):
        def ep(tc, c0, w, pi):
            s = 1 if tc < 2 else 0
            oi = cnt["o"] % 4; cnt["o"] += 1
            xo = ostg2_f[oi][:, 0:w]; tm = ostg_f[oi][:, 0:w]
            B.dma("act", xo, rsrc[tc * 128:(tc + 1) * 128, c0:c0 + w], reads=[rsrc_tok] if rsrc_tok else [], writes=[os2tok[oi]])
            B.op("dve", lambda e: e.tensor_tensor(out=tm, in0=PS[pi][:, 0:w], in1=gate_bc[:, s, c0:c0 + w], op=ALU.mult), reads=[PST[pi], gate_tok], writes=[ostok[oi]])
            B.op("pool", lambda e: e.tensor_tensor(out=tm, in0=tm, in1=xo, op=ALU.add), reads=[ostok[oi], os2tok[oi]], writes=[ostok[oi]])
            B.dma("sp", rdst[tc * 128:(tc + 1) * 128, c0:c0 + w], tm, reads=[ostok[oi]], writes=[rdst_tok])
        return ep

    def phase_ffn1(wsrc, chunks):
        blocks = []
        cl = list(chunks)
        for i in range(0, len(cl), 4):
            blocks.append(cl[i:i + 4])
        load_wblock(wsrc, 0, 512, 16, wbuf[0], wbtok[0])
        for cb in range(16):
            bi = cb % 2
            if cb + 1 < 16:
                load_wblock(wsrc, (cb + 1) * 512, 512, 16, wbuf[1 - bi], wbtok[1 - bi])
            for fs in range(4):
                kcf = cb * 4 + fs
                for blk in blocks:
                    t0 = blk[0] * 128; N = len(blk) * 128
                    pi = cnt["ps"] % 4; cnt["ps"] += 1
                    for kc in range(16):
                        B.op("pe", lambda e, pi=pi, kc=kc, bi=bi, fs=fs, t0=t0, N=N: e.matmul(PS[pi][:, 0:N], lhsT=wbuf[bi][:, kc * 512 + fs * 128: kc * 512 + (fs + 1) * 128], rhs=actT[:, kc, t0:t0 + N], start=(kc == 0), stop=(kc == 15)),
                             reads=[actT_tok[t] for t in blk] + [wbtok[bi]], writes=[PST[pi]])
                    oi = cnt["o"] % 4; cnt["o"] += 1
                    r = ostg2_f[oi][:, 0:N]; hb = ostg_f[oi].bitcast(BF16)[:, 0:N]
                    B.op("act", lambda e, pi=pi, r=r, N=N: e.activation(out=r, in_=PS[pi][:, 0:N], func=AF.Relu), reads=[PST[pi]], writes=[os2tok[oi]])
                    B.op("dve", lambda e, r=r, hb=hb: e.tensor_tensor(out=hb, in0=r, in1=r, op=ALU.mult), reads=[os2tok[oi]], writes=[ostok[oi]])
                    B.dma("sp", h1d[blk[0]:blk[0] + len(blk), :, kcf, :].rearrange("t p j -> p t j"), hb.rearrange("p (t j) -> p t j", j=128), reads=[ostok[oi]], writes=[h1_tok])

    def phase_ffn2(wsrc, chunks, rsrc, rsrc_tok, rdst, rdst_tok):
        actv = actT[:].rearrange("p a b -> p (a b)")
        wb2 = [actv[:, i * 16384:(i + 1) * 16384] for i in range(2)]; wb2tok = [Tok() for _ in range(2)]
        acc = arenaB[:, 0:9216]; acctok = [Tok() for _ in range(NCH)]
        stg = [arenaB[:, 9216 + i * 2048: 9216 + (i + 1) * 2048] for i in range(2)]; stgtok = [Tok() for _ in range(2)]
        hst = [arenaB_bf[:, 26624 + i * 4096: 26624 + (i + 1) * 4096] for i in range(2)]; hstok = [Tok() for _ in range(2)]
        o1 = [arenaB[:, 17408 + i * 512: 17408 + (i + 1) * 512] for i in range(2)]; o1tok = [Tok() for _ in range(2)]
        o2 = [arenaB[:, 18432 + i * 512: 18432 + (i + 1) * 512] for i in range(2)]; o2tok = [Tok() for _ in range(2)]
        units = [(cb, half) for cb in range(4) for half in range(2)]
        def ld(u):
            cb, half = units[u]
            load_wblock(wsrc[half * 4096:(half + 1) * 4096], cb * 512, 512, 32, wb2[u % 2], wb2tok[u % 2], stage=stg, stok_=stgtok, queues=("act",))
        ld(0)
        n = 0; on = 0
        for u, (cb, half) in enumerate(units):
            bi = u % 2
            if u + 1 < len(units):
                ld(u + 1)
            for tc in chunks:
                hi = n % 2; n += 1
                B.dma("sp", hst[hi], h1d[tc][:, half * 32:(half + 1) * 32, :].rearrange("p k j -> p (k j)"), reads=[h1_tok], writes=[hstok[hi]])
                pi = cnt["ps"] % 4; cnt["ps"] += 1
                for kc in range(32):
                    B.op("pe", lambda e, pi=pi, kc=kc, hi=hi, bi=bi: e.matmul(PS[pi][:], lhsT=hst[hi][:, kc * 128:(kc + 1) * 128], rhs=wb2[bi][:, kc * 512:(kc + 1) * 512], start=(kc == 0), stop=(kc == 31)),
                         reads=[hstok[hi], wb2tok[bi]], writes=[PST[pi]])
                asl = acc[:, tc * 512:(tc + 1) * 512]
                if half == 0:
                    B.op("dve", lambda e, pi=pi, asl=asl: e.tensor_copy(out=asl, in_=PS[pi][:]), reads=[PST[pi]], writes=[acctok[tc]])
                else:
                    s_ = 1 if tc < 2 else 0
                    oi = on % 2; on += 1
                    c0 = cb * 512
                    B.dma("sp", o2[oi], rsrc[tc * 128:(tc + 1) * 128, c0:c0 + 512], reads=[rsrc_tok] if rsrc_tok else [], writes=[o2tok[oi]])
                    B.op("dve", lambda e, pi=pi, asl=asl, oi=oi: e.tensor_tensor(out=o1[oi], in0=PS[pi][:], in1=asl, op=ALU.add), reads=[PST[pi], acctok[tc]], writes=[o1tok[oi]])
                    B.op("dve", lambda e, oi=oi, s_=s_, c0=c0: e.tensor_tensor(out=o1[oi], in0=o1[oi], in1=gate_bc[:, s_, c0:c0 + 512], op=ALU.mult), reads=[o1tok[oi], gate_tok], writes=[o1tok[oi]])
                    B.op("pool", lambda e, oi=oi: e.tensor_tensor(out=o1[oi], in0=o1[oi], in1=o2[oi], op=ALU.add), reads=[o1tok[oi], o2tok[oi]], writes=[o1tok[oi]])
                    B.dma("sp", rdst[tc * 128:(tc + 1) * 128, c0:c0 + 512], o1[oi], reads=[o1tok[oi]], writes=[rdst_tok])

    def phase_final(src, src_tok):
        nf = gate_bc[:, 0, :]
        B.dma("sp", nf, norm_f.partition_broadcast(128), writes=[gate_tok])
        xin = [arenaB[:, i * 2048:(i + 1) * 2048] for i in range(2)]; xtok = [Tok() for _ in range(2)]
        yo = [arenaB[:, 4096 + i * 2048: 4096 + (i + 1) * 2048] for i in range(2)]; ytok = [Tok() for _ in range(2)]
        junk = arenaB_bf[:, 16384:18432]; jtok = Tok()
        st = small[:, 160:176]; sttok = [Tok() for _ in range(2)]
        outtok = Tok()
        for n, tc in enumerate(range(2, NCH)):
            bi = n % 2
            B.dma("sp", xin[bi], src[tc * 128:(tc + 1) * 128, :], reads=[src_tok], writes=[xtok[bi]])
            ss = st[:, bi * 4:bi * 4 + 1]; rs = st[:, bi * 4 + 1:bi * 4 + 2]
            B.op("act", lambda e, bi=bi, ss=ss: e.activation(out=junk, in_=xin[bi], func=AF.Square, accum_out=ss), reads=[xtok[bi]], writes=[jtok, sttok[bi]])
            B.op("dve", lambda e, ss=ss, rs=rs: e.tensor_scalar(out=rs, in0=ss, scalar1=1.0 / D, scalar2=EPS, op0=ALU.mult, op1=ALU.add), reads=[sttok[bi]], writes=[sttok[bi]])
            B.op("act", lambda e, rs=rs: e.activation(out=rs, in_=rs, func=AF.Sqrt), reads=[sttok[bi]], writes=[sttok[bi]])
            B.op("dve", lambda e, rs=rs: e.reciprocal(out=rs, in_=rs), reads=[sttok[bi]], writes=[sttok[bi]])
            B.op("act", lambda e, bi=bi, rs=rs: e.activation(out=yo[bi], in_=xin[bi], func=AF.Copy, scale=rs), reads=[xtok[bi], sttok[bi]], writes=[ytok[bi]])
            B.op("dve", lambda e, bi=bi: e.tensor_tensor(out=yo[bi], in0=yo[bi], in1=nf, op=ALU.mult), reads=[ytok[bi], gate_tok], writes=[ytok[bi]])
            d = B.dma("sp", out[(tc - 2) * 128:(tc - 1) * 128, :], yo[bi], reads=[ytok[bi]], writes=[outtok])

    A32 = actT[:].rearrange("p a b -> p (a b)").bitcast(F32)
    A16 = actT[:].rearrange("p a b -> p (a b)")
    B32 = arenaB; B16 = arenaB_bf
    ORD = [list(range(NCH)), [1, 0] + list(range(NCH - 1, 1, -1))]
    TWO_PI = 2.0 * math.pi

    def v3(ap, a, b):
        return ap.rearrange("p (a b) -> p a b", a=a, b=b)

    def dve(fn, reads, writes):
        return B.op("dve", fn, reads=reads, writes=writes)

    def act(fn, reads, writes):
        return B.op("act", fn, reads=reads, writes=writes)

    def pool(fn, reads, writes):
        return B.op("pool", fn, reads=reads, writes=writes)

    def pe(fn, reads, writes):
        return B.op("pe", fn, reads=reads, writes=writes)

    def range_reduce_sincos(ph, kint, sinv, cosv, tk, shape_note=None):
        dve(lambda e: e.tensor_copy(out=kint, in_=ph), [tk], [tk])
        dve(lambda e: e.tensor_copy(out=cosv, in_=kint), [tk], [tk])
        dve(lambda e: e.tensor_tensor(out=ph, in0=ph, in1=cosv, op=ALU.subtract), [tk], [tk])
        act(lambda e: e.activation(out=sinv, in_=ph, func=AF.Sin, scale=TWO_PI), [tk], [tk])
        dve(lambda e: e.tensor_scalar(out=ph, in0=ph, scalar1=0.25, scalar2=None, op0=ALU.add), [tk], [tk])
        dve(lambda e: e.tensor_scalar(out=cosv, in0=ph, scalar1=0.5, scalar2=None, op0=ALU.is_gt), [tk], [tk])
        dve(lambda e: e.tensor_tensor(out=ph, in0=ph, in1=cosv, op=ALU.subtract), [tk], [tk])
        act(lambda e: e.activation(out=cosv, in_=ph, func=AF.Sin, scale=TWO_PI), [tk], [tk])

    def phase_s5(l):
        yacc = A32[:, 0:9216]; ytok = [[Tok() for _ in range(4)] for _ in range(NCH)]
        uT = A16[:, 18432:27648]; uTtok = Tok()
        Bblk = A16[:, 27648:31744]; Btok = Tok()
        Cmat = A16[:, 31744:35840]; Ctok = Tok()
        misc = A32[:, 17920:18432]
        tab = [B32[:, i * 2048:(i + 1) * 2048] for i in range(4)]; tabtok = Tok()
        Pp = [[B16[:, 16384 + (blk * 4 + k) * 512: 16384 + (blk * 4 + k + 1) * 512] for k in range(4)] for blk in range(4)]
        Ptk = [[Tok() for _ in range(4)] for _ in range(4)]
        Zp = [[B16[:, 24576 + (i * 4 + k) * 512: 24576 + (i * 4 + k + 1) * 512] for k in range(4)] for i in range(2)]
        Ztok = [Tok() for _ in range(2)]
        xTt = [B16[:, 28672 + i * 128: 28672 + (i + 1) * 128] for i in range(8)]; xTtok = [Tok() for _ in range(8)]
        u_sb = B16[:, 29696:38912]; utok = Tok()
        dsk = B32[:, 8192:8704]; dtok = Tok()
        B.dma("sp", v3(u_sb, NCH, 512), a16[:, 0:512].rearrange("(c p) n -> p c n", p=128), reads=[a16_tok], writes=[utok])
        B.dma("sp", dsk, s5_d[l].partition_broadcast(128), writes=[dtok])
        for c in range(NCH):
            dve(lambda e, c=c: e.tensor_tensor(out=yacc[:, c * 512:(c + 1) * 512], in0=u_sb[:, c * 512:(c + 1) * 512], in1=dsk, op=ALU.mult), [utok, dtok], ytok[c])
            pi = 6 + c % 2
            psb = PS[pi][:].bitcast(BF16)
            for blk in range(4):
                pe(lambda e, psb=psb, blk=blk, c=c: e.transpose(out=psb[:, blk * 128:(blk + 1) * 128], in_=u_sb[:, c * 512 + blk * 128: c * 512 + (blk + 1) * 128], identity=ident_b[:]), [utok, ctok], [PST[pi]])
            act(lambda e, psb=psb, c=c: e.activation(out=v3(uT, 4, T)[:, :, c * 128:(c + 1) * 128], in_=v3(psb[:, 0:512], 4, 128), func=AF.Copy), [PST[pi]], [uTtok])
        B.barrier()
        for d in range(2):
            S = [B32[:, 8192 + i * 2048: 8192 + (i + 1) * 2048] for i in range(5)]
            stok = Tok()
            lrd, th, ph, sinv, cosv = S
            kint = B32[:, 18432:20480].bitcast(I32)
            dtb = misc[:, 0:32]
            ntp = misc[:, 32:33]
            tp = tpos[:, d:d + 1]
            B.dma("sp", lrd, lam_re[l, d].rearrange("g p -> (g p)").partition_broadcast(128), writes=[stok])
            B.dma("sp", th, lam_im[l, d].rearrange("g p -> (g p)").partition_broadcast(128), writes=[stok])
            B.dma("sp", dtb, log_step[l, d].partition_broadcast(128), writes=[stok])
            act(lambda e: e.activation(out=dtb, in_=dtb, func=AF.Exp), [stok], [stok])
            dve(lambda e: e.tensor_scalar(out=ntp, in0=tp, scalar1=-1.0, scalar2=None, op0=ALU.mult), [ctok, stok], [stok])
            dtb3 = dtb[:, :, None].broadcast_to([128, 32, 64])
            dve(lambda e: e.tensor_scalar(out=lrd, in0=lrd, scalar1=-1e-4, scalar2=None, op0=ALU.min), [stok], [stok])
            dve(lambda e: e.tensor_tensor(out=v3(lrd, 32, 64), in0=v3(lrd, 32, 64), in1=dtb3, op=ALU.mult), [stok], [stok])
            dve(lambda e: e.tensor_tensor(out=v3(th, 32, 64), in0=v3(th, 32, 64), in1=dtb3, op=ALU.mult), [stok], [stok])
            dve(lambda e: e.tensor_scalar(out=ph, in0=th, scalar1=tp, scalar2=1.0 / TWO_PI, op0=ALU.mult, op1=ALU.mult), [stok, ctok], [stok])
            range_reduce_sincos(ph, kint, sinv, cosv, stok)
            act(lambda e: e.activation(out=th, in_=lrd, func=AF.Exp, scale=tp), [stok, ctok], [stok])
            dve(lambda e: e.reciprocal(out=ph, in_=th), [stok], [stok])
            dve(lambda e: e.tensor_tensor(out=tab[2], in0=th, in1=cosv, op=ALU.mult), [stok], [tabtok])
            dve(lambda e: e.tensor_tensor(out=tab[3], in0=th, in1=sinv, op=ALU.mult), [stok], [tabtok])
            dve(lambda e: e.tensor_tensor(out=tab[0], in0=ph, in1=cosv, op=ALU.mult), [stok], [tabtok])
            dve(lambda e: e.scalar_tensor_tensor(out=tab[1], in0=ph, scalar=-1.0, in1=sinv, op0=ALU.mult, op1=ALU.mult), [stok], [tabtok])
            B.barrier()
            bre = B32[0:64, 8192:8704]; bim = B32[0:64, 8704:9216]; bbr = B32[0:64, 9216:9728]; bbi = B32[0:64, 9728:10240]
            t1 = B32[0:64, 10240:10752]; t2 = B32[0:64, 10752:11264]
            sm = [B32[0:64, 11264 + i * 32: 11264 + (i + 1) * 32] for i in range(12)]
            smi = B32[0:64, 11776:11808].bitcast(I32)
            btk = Tok()
            B.dma("sp", v3(bre, 32, 16), s5b_re[l, d].rearrange("g p n -> p g n"), writes=[btk])
            B.dma("sp", v3(bim, 32, 16), s5b_im[l, d].rearrange("g p n -> p g n"), writes=[btk])
            lr, li, dt2, mag, phs, sn, cs, are, aim, rden, cr, ci = sm
            B.dma("sp", lr, lam_re[l, d].rearrange("g p -> p g"), writes=[btk], allow_slow_non_contiguous=True)
            B.dma("sp", li, lam_im[l, d].rearrange("g p -> p g"), writes=[btk], allow_slow_non_contiguous=True)
            B.dma("sp", dt2, log_step[l, d].partition_broadcast(64), writes=[btk])
            act(lambda e: e.activation(out=dt2, in_=dt2, func=AF.Exp), [btk], [btk])
            dve(lambda e: e.tensor_scalar(out=lr, in0=lr, scalar1=-1e-4, scalar2=None, op0=ALU.min), [btk], [btk])
            dve(lambda e: e.tensor_tensor(out=mag, in0=lr, in1=dt2, op=ALU.mult), [btk], [btk])
            act(lambda e: e.activation(out=mag, in_=mag, func=AF.Exp), [btk], [btk])
            dve(lambda e: e.scalar_tensor_tensor(out=phs, in0=li, scalar=1.0 / TWO_PI, in1=dt2, op0=ALU.mult, op1=ALU.mult), [btk], [btk])
            range_reduce_sincos(phs, smi, sn, cs, btk)
            dve(lambda e: e.tensor_tensor(out=are, in0=mag, in1=cs, op=ALU.mult), [btk], [btk])
            dve(lambda e: e.tensor_scalar(out=are, in0=are, scalar1=-1.0, scalar2=None, op0=ALU.add), [btk], [btk])
            dve(lambda e: e.tensor_tensor(out=aim, in0=mag, in1=sn, op=ALU.mult), [btk], [btk])
            dve(lambda e: e.tensor_tensor(out=rden, in0=lr, in1=lr, op=ALU.mult), [btk], [btk])
            dve(lambda e: e.tensor_tensor(out=cr, in0=li, in1=li, op=ALU.mult), [btk], [btk])
            dve(lambda e: e.tensor_tensor(out=rden, in0=rden, in1=cr, op=ALU.add), [btk], [btk])
            dve(lambda e: e.reciprocal(out=rden, in_=rden), [btk], [btk])
            dve(lambda e: e.tensor_tensor(out=cr, in0=lr, in1=rden, op=ALU.mult), [btk], [btk])
            dve(lambda e: e.scalar_tensor_tensor(out=ci, in0=li, scalar=-1.0, in1=rden, op0=ALU.mult, op1=ALU.mult), [btk], [btk])
            dve(lambda e: e.tensor_tensor(out=mag, in0=are, in1=cr, op=ALU.mult), [btk], [btk])
            dve(lambda e: e.tensor_tensor(out=sn, in0=aim, in1=ci, op=ALU.mult), [btk], [btk])
            dve(lambda e: e.tensor_tensor(out=mag, in0=mag, in1=sn, op=ALU.subtract), [btk], [btk])
            dve(lambda e: e.tensor_tensor(out=phs, in0=are, in1=ci, op=ALU.mult), [btk], [btk])
            dve(lambda e: e.tensor_tensor(out=sn, in0=aim, in1=cr, op=ALU.mult), [btk], [btk])
            dve(lambda e: e.tensor_tensor(out=phs, in0=phs, in1=sn, op=ALU.add), [btk], [btk])
            nr3 = mag[:, :, None].broadcast_to([64, 32, 16]); ni3 = phs[:, :, None].broadcast_to([64, 32, 16])
            dve(lambda e: e.tensor_tensor(out=v3(t1, 32, 16), in0=v3(bre, 32, 16), in1=nr3, op=ALU.mult), [btk], [btk])
            dve(lambda e: e.tensor_tensor(out=v3(t2, 32, 16), in0=v3(bim, 32, 16), in1=ni3, op=ALU.mult), [btk], [btk])
            dve(lambda e: e.tensor_tensor(out=bbr, in0=t1, in1=t2, op=ALU.subtract), [btk], [btk])
            dve(lambda e: e.tensor_tensor(out=v3(t1, 32, 16), in0=v3(bim, 32, 16), in1=nr3, op=ALU.mult), [btk], [btk])
            dve(lambda e: e.tensor_tensor(out=v3(t2, 32, 16), in0=v3(bre, 32, 16), in1=ni3, op=ALU.mult), [btk], [btk])
            dve(lambda e: e.tensor_tensor(out=bbi, in0=t1, in1=t2, op=ALU.add), [btk], [btk])
            for nm_, ap_ in (("lr", lr), ("li", li), ("dt2", dt2), ("cs", cs), ("are", are), ("aim", aim), ("rden", rden), ("cr", cr), ("ci", ci)):
                dump("s_%s%d" % (nm_, d), ap_, [btk])
            dump("s_bbr%d" % d, bbr, [btk]); dump("s_bbi%d" % d, bbi, [btk]); dump("s_nr%d" % d, mag, [btk]); dump("s_ni%d" % d, phs, [btk])
            m8 = mask8[:, :, None].broadcast_to([128, 8, 64])
            n = 0
            for blk in range(4):
                for ri, bb in enumerate((bbr, bbi)):
                    pi = 6 + n % 2; n += 1
                    pe(lambda e, pi=pi, bb=bb, blk=blk: e.transpose(out=PS[pi][:, 0:64], in_=bb[:, blk * 128:(blk + 1) * 128], identity=ident_f[0:64, 0:64]), [btk, ctok], [PST[pi]])
                    dst = v3(Bblk, 4, 1024)[:, blk, ri * 512:(ri + 1) * 512].rearrange("p (g q) -> p g q", g=8)
                    dve(lambda e, pi=pi, dst=dst: e.tensor_tensor(out=dst, in0=PS[pi][:, None, 0:64].broadcast_to([128, 8, 64]), in1=m8, op=ALU.mult), [PST[pi], ctok], [Btok])
            cnat = [B32[:, 12288 + i * 64: 12288 + (i + 1) * 64] for i in range(8)]
            cntk = Tok()
            for ri, csrc in enumerate((s5c_re, s5c_im)):
                for blk in range(4):
                    B.dma("sp", cnat[ri * 4 + blk], csrc[l, d, blk * 8:(blk + 1) * 8].rearrange("g n p -> (g n) p"), writes=[cntk])
            xm = [B16[:, 26624 + i * 128: 26624 + (i + 1) * 128] for i in range(4)]; xmtok = [Tok() for _ in range(4)]
            n = 0
            for blk in range(4):
                for q in range(4):
                    for ri in range(2):
                        xi = n % 4; pi = 6 + n % 2; n += 1
                        for g2 in range(2):
                            mk = mask8[:, 2 * q + g2: 2 * q + g2 + 1]
                            dve(lambda e, xi=xi, g2=g2, ri=ri, blk=blk, mk=mk: e.tensor_scalar(out=xm[xi][:, g2 * 64:(g2 + 1) * 64], in0=cnat[ri * 4 + blk], scalar1=mk, scalar2=(-1.0 if ri else 1.0), op0=ALU.mult, op1=ALU.mult),
                                [cntk, ctok], [xmtok[xi]])
                        psb = PS[pi][:].bitcast(BF16)
                        pe(lambda e, psb=psb, xi=xi: e.transpose(out=psb[:, 0:128], in_=xm[xi], identity=ident_b[:]), [xmtok[xi], ctok], [PST[pi]])
                        ci_ = (blk * 4 + q) * 2 + ri
                        act(lambda e, psb=psb, ci_=ci_: e.activation(out=Cmat[:, ci_ * 128:(ci_ + 1) * 128], in_=psb[:, 0:128], func=AF.Copy), [PST[pi]], [Ctok])
            B.barrier()
            for i_ in range(4):
                dump("s_tab%d_%d" % (i_, d), tab[i_], [tabtok])
            dump("s_Bblk%d" % d, Bblk, [Btok])
            dump("s_Cmat%d" % d, Cmat, [Ctok])
            mark("s5c%d" % d)
            U = ufwd_b if d == 0 else ubwd_b; NU = nufwd_b if d == 0 else nubwd_b
            SEL = self_b if d == 0 else selb_b; NSEL = nself_b if d == 0 else nselb_b
            xn = 0
            X6 = [Tok() for _ in range(4)]; Y7 = [Tok() for _ in range(4)]
            Zt = [[Tok() for _ in range(4)] for _ in range(2)]
            for step, c in enumerate(ORD[d]):
                for blk in range(4):
                    zi = blk % 2
                    pr, pim = (0, 1) if zi == 0 else (2, 3)
                    cols = slice(blk * 512, (blk + 1) * 512)
                    lhs_u = v3(uT, 4, T)[:, blk, c * 128:(c + 1) * 128]
                    pe(lambda e, pr=pr, lhs_u=lhs_u, blk=blk: e.matmul(PS[pr][:], lhsT=lhs_u, rhs=v3(Bblk, 4, 1024)[:, blk, 0:512], start=True, stop=True), [uTtok, Btok], [PST[pr]])
                    pe(lambda e, pim=pim, lhs_u=lhs_u, blk=blk: e.matmul(PS[pim][:], lhsT=lhs_u, rhs=v3(Bblk, 4, 1024)[:, blk, 512:1024], start=True, stop=True), [uTtok, Btok], [PST[pim]])
                    Z = Zp[zi]
                    dve(lambda e, Z=Z, pr=pr, cols=cols: e.tensor_tensor(out=Z[0], in0=PS[pr][:], in1=tab[0][:, cols], op=ALU.mult), [PST[pr], tabtok], [Zt[zi][0]])
                    dve(lambda e, Z=Z, pim=pim, cols=cols: e.tensor_tensor(out=Z[1], in0=PS[pim][:], in1=tab[1][:, cols], op=ALU.mult), [PST[pim], tabtok], [Zt[zi][1]])
                    dve(lambda e, Z=Z, pim=pim, cols=cols: e.tensor_tensor(out=Z[2], in0=PS[pim][:], in1=tab[0][:, cols], op=ALU.mult), [PST[pim], tabtok], [Zt[zi][2]])
                    dve(lambda e, Z=Z, pr=pr, cols=cols: e.tensor_tensor(out=Z[3], in0=PS[pr][:], in1=tab[1][:, cols], op=ALU.mult), [PST[pr], tabtok], [Zt[zi][3]])
                    P = Pp[blk]
                    first = (step == 0)
                    pe(lambda e, Z=Z: e.matmul(PS[4][:], lhsT=U[:], rhs=Z[0], start=True, stop=False), [Zt[zi][0], ctok], [PST[4]])
                    pe(lambda e, Z=Z, first=first: e.matmul(PS[4][:], lhsT=NU[:], rhs=Z[1], start=False, stop=first), [Zt[zi][1], ctok], [PST[4]])
                    if not first:
                        pe(lambda e, P=P: e.matmul(PS[4][:], lhsT=SEL[:], rhs=P[0], start=False, stop=False), [Ptk[blk][0], ctok], [PST[4]])
                        pe(lambda e, P=P: e.matmul(PS[4][:], lhsT=NSEL[:], rhs=P[1], start=False, stop=True), [Ptk[blk][1], ctok], [PST[4]])
                    pe(lambda e, Z=Z: e.matmul(PS[5][:], lhsT=U[:], rhs=Z[2], start=True, stop=False), [Zt[zi][2], ctok], [PST[5]])
                    pe(lambda e, Z=Z, first=first: e.matmul(PS[5][:], lhsT=U[:], rhs=Z[3], start=False, stop=first), [Zt[zi][3], ctok], [PST[5]])
                    if not first:
                        pe(lambda e, P=P: e.matmul(PS[5][:], lhsT=SEL[:], rhs=P[2], start=False, stop=False), [Ptk[blk][2], ctok], [PST[5]])
                        pe(lambda e, P=P: e.matmul(PS[5][:], lhsT=SEL[:], rhs=P[3], start=False, stop=True), [Ptk[blk][3], ctok], [PST[5]])
                    dve(lambda e, P=P, cols=cols: e.tensor_tensor(out=P[0], in0=PS[4][:], in1=tab[2][:, cols], op=ALU.mult), [PST[4], tabtok], [Ptk[blk][0]])
                    dve(lambda e, P=P, cols=cols: e.tensor_tensor(out=P[1], in0=PS[5][:], in1=tab[3][:, cols], op=ALU.mult), [PST[5], tabtok], [Ptk[blk][1]])
                    dve(lambda e, P=P, cols=cols: e.tensor_tensor(out=P[2], in0=PS[5][:], in1=tab[2][:, cols], op=ALU.mult), [PST[5], tabtok], [Ptk[blk][2]])
                    dve(lambda e, P=P, cols=cols: e.tensor_tensor(out=P[3], in0=PS[4][:], in1=tab[3][:, cols], op=ALU.mult), [PST[4], tabtok], [Ptk[blk][3]])
                    for q in range(4):
                        qs = slice(q * 128, (q + 1) * 128)
                        xs_ = []
                        for ri in range(2):
                            xi = xn % 8; xn += 1
                            pslot = PS[6][:, (xi % 4) * 128:((xi % 4) + 1) * 128]
                            a0, a1 = (P[0], P[1]) if ri == 0 else (P[2], P[3])
                            idn = nident_b if ri == 0 else ident_b
                            pe(lambda e, pslot=pslot, a0=a0, qs=qs: e.matmul(pslot, lhsT=a0[:, qs], rhs=ident_b[:], start=True, stop=False), Ptk[blk] + [ctok], [X6[xi % 4]])
                            pe(lambda e, pslot=pslot, a1=a1, qs=qs, idn=idn: e.matmul(pslot, lhsT=a1[:, qs], rhs=idn[:], start=False, stop=True), Ptk[blk] + [ctok], [X6[xi % 4]])
                            act(lambda e, pslot=pslot, xi=xi: e.activation(out=xTt[xi], in_=pslot, func=AF.Copy), [X6[xi % 4]], [xTtok[xi]])
                            xs_.append(xi)
                        yslot = PS[7][:, (blk % 4) * 128:((blk % 4) + 1) * 128]
                        for ri in range(2):
                            ci_ = (blk * 4 + q) * 2 + ri
                            xi = xs_[ri]
                            pe(lambda e, yslot=yslot, xi=xi, ci_=ci_, q=q, ri=ri: e.matmul(yslot, lhsT=xTt[xi], rhs=Cmat[:, ci_ * 128:(ci_ + 1) * 128], start=(q == 0 and ri == 0), stop=(q == 3 and ri == 1)),
                               [xTtok[xi], Ctok], [Y7[blk]])
                    ysl = yacc[:, c * 512 + blk * 128: c * 512 + (blk + 1) * 128]
                    dve(lambda e, ysl=ysl, yslot=yslot: e.tensor_tensor(out=ysl, in0=yslot, in1=ysl, op=ALU.add), [Y7[blk], ytok[c][blk]], [ytok[c][blk]])
            B.barrier()
        mark("s5d")
        dump("s_yacc", yacc, [t_ for r_ in ytok for t_ in r_])
        mark("s5d")
        gyT = uT; gtok = Tok()
        wg = B16[:, 0:4096]; wgtok = Tok()
        bgl = B32[:, 2048:3072]; bgtok = Tok()
        B.dma("sp", bgl, b_glu[l].partition_broadcast(128), writes=[bgtok])
        load_wblock(w_glu[l], 0, 1024, 4, wg, wgtok)
        gs = [B32[:, 8192 + i * 512: 8192 + (i + 1) * 512] for i in range(4)]; gstok = [Tok() for _ in range(2)]
        gb = [B16[:, 24576 + i * 512: 24576 + (i + 1) * 512] for i in range(2)]; gbtok = [Tok() for _ in range(2)]
        GC = 2.0 * math.sqrt(2.0 / math.pi)
        for c in range(NCH):
            bi = c % 2
            y = yacc[:, c * 512:(c + 1) * 512]; t = gs[bi * 2]; sg = gs[bi * 2 + 1]
            dve(lambda e, y=y, t=t: e.tensor_tensor(out=t, in0=y, in1=y, op=ALU.mult), ytok[c], [gstok[bi]])
            dve(lambda e, t=t: e.tensor_scalar(out=t, in0=t, scalar1=0.044715, scalar2=1.0, op0=ALU.mult, op1=ALU.add), [gstok[bi]], [gstok[bi]])
            dve(lambda e, y=y, t=t: e.tensor_tensor(out=t, in0=t, in1=y, op=ALU.mult), [gstok[bi]] + ytok[c], [gstok[bi]])
            act(lambda e, t=t, sg=sg: e.activation(out=sg, in_=t, func=AF.Sigmoid, scale=GC), [gstok[bi]], [gstok[bi]])
            dve(lambda e, y=y, sg=sg, bi=bi: e.tensor_tensor(out=gb[bi], in0=y, in1=sg, op=ALU.mult), [gstok[bi]] + ytok[c], [gbtok[bi]])
            pi = 6 + c % 2
            psb = PS[pi][:].bitcast(BF16)
            for kc in range(4):
                pe(lambda e, psb=psb, kc=kc, bi=bi: e.transpose(out=psb[:, kc * 128:(kc + 1) * 128], in_=gb[bi][:, kc * 128:(kc + 1) * 128], identity=ident_b[:]), [gbtok[bi], ctok], [PST[pi]])
            act(lambda e, psb=psb, c=c: e.activation(out=v3(gyT, 4, T)[:, :, c * 128:(c + 1) * 128], in_=v3(psb[:, 0:512], 4, 128), func=AF.Copy), [PST[pi]], [gtok])
        zs = [B32[:, 10240 + i * 512: 10240 + (i + 1) * 512] for i in range(4)]; zstok = [Tok() for _ in range(2)]
        zo = [B16[:, 25600 + i * 512: 25600 + (i + 1) * 512] for i in range(2)]; zotok = [Tok() for _ in range(2)]
        for c in range(NCH):
            bi = c % 2
            for half in range(2):
                pi = half + 2 * bi
                for kc in range(4):
                    pe(lambda e, pi=pi, kc=kc, c=c, half=half: e.matmul(PS[pi][:], lhsT=v3(gyT, 4, T)[:, kc, c * 128:(c + 1) * 128], rhs=wg[:, kc * 1024 + half * 512: kc * 1024 + (half + 1) * 512], start=(kc == 0), stop=(kc == 3)),
                       [gtok, wgtok], [PST[pi]])
            va = zs[bi * 2]; gt = zs[bi * 2 + 1]
            dve(lambda e, va=va, bi=bi: e.tensor_tensor(out=va, in0=PS[2 * bi][:], in1=bgl[:, 0:512], op=ALU.add), [PST[2 * bi], bgtok], [zstok[bi]])
            dve(lambda e, gt=gt, bi=bi: e.tensor_tensor(out=gt, in0=PS[2 * bi + 1][:], in1=bgl[:, 512:1024], op=ALU.add), [PST[2 * bi + 1], bgtok], [zstok[bi]])
            act(lambda e, gt=gt: e.activation(out=gt, in_=gt, func=AF.Sigmoid), [zstok[bi]], [zstok[bi]])
            dve(lambda e, va=va, gt=gt, bi=bi: e.tensor_tensor(out=zo[bi], in0=va, in1=gt, op=ALU.mult), [zstok[bi]], [zotok[bi]])
            B.dma("sp", ymix[c * 128:(c + 1) * 128, 0:512], zo[bi], reads=[zotok[bi]], writes=[ymix_tok])
        B.barrier()

    def phase_gla(l):
        gt = [A32[:, i * 432:(i + 1) * 432] for i in range(6)]
        LF, II, Bc, colfac, rowfac, cdec = gt
        biasF = A32[:, 2592:2616]; biasI = A32[:, 2616:2640]
        gk = Tok()
        v24 = lambda ap: v3(ap, NCH, 24)
        dve(lambda e: e.memset(biasI, 0.0), [], [gk])
        dve(lambda e: e.memset(LF, 0.0), [], [gk])
        dve(lambda e: e.memset(II, 0.0), [], [gk])
        B.dma("sp", v3(biasF, 2, 12)[:, :, 0:6], ret_logit[l].partition_broadcast(128), writes=[gk])
        B.dma("sp", v3(biasF, 2, 12)[:, :, 6:12], fg_b[l].partition_broadcast(128), writes=[gk])
        B.dma("sp", v3(biasI, 2, 12)[:, :, 6:12], ig_b[l].partition_broadcast(128), writes=[gk])
        for d in range(2):
            dve(lambda e, d=d: e.tensor_copy(out=v24(LF)[:, :, d * 12 + 6: d * 12 + 12], in_=gates_sb[:, :, d * 12 + 6: d * 12 + 12]), [gates_tok, gk], [gk])
            dve(lambda e, d=d: e.tensor_copy(out=v24(II)[:, :, d * 12 + 6: d * 12 + 12], in_=gates_sb[:, :, d * 12: d * 12 + 6]), [gates_tok, gk], [gk])
        dve(lambda e: e.tensor_tensor(out=v24(LF), in0=v24(LF), in1=biasF[:, None, :].broadcast_to([128, NCH, 24]), op=ALU.add), [gk], [gk])
        dve(lambda e: e.tensor_tensor(out=v24(II), in0=v24(II), in1=biasI[:, None, :].broadcast_to([128, NCH, 24]), op=ALU.add), [gk], [gk])
        act(lambda e: e.activation(out=LF, in_=LF, func=AF.Exp, scale=-1.0), [gk], [gk])
        act(lambda e: e.activation(out=LF, in_=LF, func=AF.Ln, bias=1.0), [gk], [gk])
        dve(lambda e: e.tensor_scalar(out=LF, in0=LF, scalar1=-1.0, scalar2=None, op0=ALU.mult), [gk], [gk])
        for d in range(2):
            Uf = ufwd_f if d == 0 else ubwd_f
            rhs = v24(LF)[:, :, d * 12:(d + 1) * 12]
            pe(lambda e, Uf=Uf, rhs=rhs: e.matmul(PS[0][:, 0:216], lhsT=Uf[:], rhs=rhs, start=True, stop=True), [gk, ctok], [PST[0]])
            pe(lambda e, rhs=rhs: e.matmul(PS[1][:, 0:216], lhsT=ones_f[:], rhs=rhs, start=True, stop=True), [gk, ctok], [PST[1]])
            act(lambda e, d=d: e.activation(out=v24(Bc)[:, :, d * 12:(d + 1) * 12], in_=v3(PS[0][:, 0:216], NCH, 12), func=AF.Copy), [PST[0]], [gk])
            act(lambda e, d=d: e.activation(out=v24(cdec)[:, :, d * 12:(d + 1) * 12], in_=v3(PS[1][:, 0:216], NCH, 12), func=AF.Exp), [PST[1]], [gk])
        act(lambda e: e.activation(out=rowfac, in_=Bc, func=AF.Exp), [gk], [gk])
        dve(lambda e: e.tensor_tensor(out=colfac, in0=II, in1=Bc, op=ALU.subtract), [gk], [gk])
        act(lambda e: e.activation(out=colfac, in_=colfac, func=AF.Exp, bias=float(math.log(128.0 ** -0.5))), [gk], [gk])
        for nm, ap_ in (("g_LF", LF), ("g_II", II), ("g_Bc", Bc), ("g_colfac", colfac), ("g_rowfac", rowfac), ("g_cdec", cdec)):
            dump(nm, ap_, [gk])
        o16 = 5376
        def a16v(i):
            return A16[:, o16 + i * 2304: o16 + (i + 1) * 2304]
        qs, ks, vs, gs_, qr, kr, kc0, kc1, qT, kT0, kT1 = [a16v(i) for i in range(11)]
        vaug = A16[:, 30720:33060]
        sTm = [A16[:, 33060 + i * 128: 33060 + (i + 1) * 128] for i in range(4)]; sTtok = [Tok() for _ in range(4)]
        Cbf = [A16[:, 33572 + i * 130: 33572 + (i + 1) * 130] for i in range(2)]
        Oacc = B32[:, 0:2304]; F1 = B32[:, 2304:4608]; F2 = B32[:, 4608:6912]; F3 = B32[:, 6912:9216]
        ybf = B16[:, 18432:20736]
        Cst = [B32[:, 10368 + i * 130: 10368 + (i + 1) * 130] for i in range(2)]
        wn = B32[:, 10752:10880]
        st = B32[:, 10880:10880 + 128]
        tiny = B32[:, 11008:11008 + 64]
        for hh in range(12):
            is_ml = hh >= 6; h = hh % 6
            base = 3584 if is_ml else 512
            hk = Tok(); ck = [Tok(), Tok()]; okc = [Tok() for _ in range(NCH)]; tk = [Tok(), Tok()]; fk = Tok()
            for i, (t_, off) in enumerate(((qs, 0), (ks, 768), (vs, 1536), (gs_, 2304))):
                c0 = base + off + h * 128
                B.dma("sp" if i % 2 else "act", v3(t_, NCH, 128), a16[:, c0:c0 + 128].rearrange("(c p) n -> p c n", p=128), reads=[a16_tok], writes=[hk])
            B.dma("sp", wn, (ml_nw if is_ml else ret_nw)[l, h * 128:(h + 1) * 128].partition_broadcast(128), writes=[fk])
            if not is_ml:
                for (src, dst, eng) in ((qs, qr, "dve"), (ks, kr, "pool")):
                    s1 = v3(src, NCH, 128)[:, :, 0:64]; s2 = v3(src, NCH, 128)[:, :, 64:128]
                    d1 = v3(dst, NCH, 128)[:, :, 0:64]; d2 = v3(dst, NCH, 128)[:, :, 64:128]
                    f1 = v3(F1[:, 0:1152], NCH, 64) if eng == "dve" else v3(F2[:, 0:1152], NCH, 64)
                    f2 = v3(F1[:, 1152:2304], NCH, 64) if eng == "dve" else v3(F2[:, 1152:2304], NCH, 64)
                    rk = Tok()
                    opf = lambda fn, rd, wr, eng=eng: B.op(eng, fn, reads=rd, writes=wr)
                    opf(lambda e, s1=s1, f1=f1: e.tensor_tensor(out=f1, in0=s1, in1=cos_t[:], op=ALU.mult), [hk, ctok], [rk])
                    opf(lambda e, s2=s2, f2=f2: e.tensor_tensor(out=f2, in0=s2, in1=sin_t[:], op=ALU.mult), [hk, ctok], [rk])
                    opf(lambda e, d1=d1, f1=f1, f2=f2: e.tensor_tensor(out=d1, in0=f1, in1=f2, op=ALU.subtract), [rk], [hk])
                    opf(lambda e, s1=s1, f1=f1: e.tensor_tensor(out=f1, in0=s1, in1=sin_t[:], op=ALU.mult), [hk, ctok], [rk])
                    opf(lambda e, s2=s2, f2=f2: e.tensor_tensor(out=f2, in0=s2, in1=cos_t[:], op=ALU.mult), [hk, ctok], [rk])
                    opf(lambda e, d2=d2, f1=f1, f2=f2: e.tensor_tensor(out=d2, in0=f1, in1=f2, op=ALU.add), [rk], [hk])
                qq, kk = qr, kr
            else:
                qq, kk = qs, ks
            for d, kcd in enumerate((kc0, kc1)):
                col = d * 12 + hh
                dve(lambda e, kcd=kcd, kk=kk, col=col: e.tensor_tensor(out=v3(kcd, NCH, 128), in0=v3(kk, NCH, 128), in1=v24(colfac)[:, :, col:col + 1].broadcast_to([128, NCH, 128]), op=ALU.mult), [hk, gk], [hk])
            act(lambda e: e.activation(out=v3(vaug, NCH, 130)[:, :, 0:128], in_=v3(vs, NCH, 128), func=AF.Copy), [hk], [hk])
            pool(lambda e: e.memset(v3(vaug, NCH, 130)[:, :, 128:130], 1.0), [hk], [hk])
            n = 0
            for (src, dst) in ((qq, qT), (kc0, kT0), (kc1, kT1)):
                for g4 in range(0, NCH, 4):
                    cnt4 = min(4, NCH - g4)
                    pi = 6 + n % 2; n += 1
                    psb = PS[pi][:].bitcast(BF16)
                    for j in range(cnt4):
                        c = g4 + j
                        pe(lambda e, psb=psb, j=j, c=c, src=src: e.transpose(out=psb[:, j * 128:(j + 1) * 128], in_=src[:, c * 128:(c + 1) * 128], identity=ident_b[:]), [hk, ctok], [PST[pi]])
                    act(lambda e, psb=psb, g4=g4, cnt4=cnt4, dst=dst: e.activation(out=dst[:, g4 * 128:(g4 + cnt4) * 128], in_=psb[:, 0:cnt4 * 128], func=AF.Copy), [PST[pi]], [hk])
            dve(lambda e: e.memset(Oacc, 0.0), [], okc)
            for d in range(2):
                dve(lambda e, d=d: e.memset(Cst[d], 0.0), [], [ck[d]])
                pool(lambda e, d=d: e.memset(Cbf[d], 0.0), [], [ck[d]])
            sn_ = 0
            for step in range(NCH):
                for d in range(2):
                    c = ORD[d][step]
                    col = d * 12 + hh
                    kT = kT0 if d == 0 else kT1; kcd = kc0 if d == 0 else kc1
                    msk = ufwd_f if d == 0 else ubwd_f
                    cs_ = slice(c * 128, (c + 1) * 128)
                    pS, pO, pC = d, 2 + d, 4 + d
                    pe(lambda e, pS=pS, kT=kT, cs_=cs_: e.matmul(PS[pS][:, 0:128], lhsT=kT[:, cs_], rhs=qT[:, cs_], start=True, stop=True), [hk], [PST[pS]])
                    si = sn_ % 4; sn_ += 1
                    dve(lambda e, pS=pS, si=si, msk=msk: e.tensor_tensor(out=sTm[si], in0=PS[pS][:, 0:128], in1=msk[:], op=ALU.mult), [PST[pS], ctok], [sTtok[si]])
                    va = v3(vaug, NCH, 130)[:, c, :]
                    pe(lambda e, pO=pO, si=si, va=va: e.matmul(PS[pO][:, 0:130], lhsT=sTm[si], rhs=va, start=True, stop=False), [sTtok[si], hk], [PST[pO]])
                    pe(lambda e, pO=pO, cs_=cs_, d=d: e.matmul(PS[pO][:, 0:130], lhsT=qT[:, cs_], rhs=Cbf[d], start=False, stop=True), [hk, ck[d]], [PST[pO]])
                    rf = v24(rowfac)[:, c, col:col + 1]
                    oslice = Oacc[:, c * 128:(c + 1) * 128]
                    if is_ml:
                        t1_ = tiny[:, d * 4:d * 4 + 1]; t2_ = tiny[:, d * 4 + 1:d * 4 + 2]
                        act(lambda e, pO=pO, t1_=t1_, rf=rf: e.activation(out=t1_, in_=PS[pO][:, 128:129], func=AF.Abs, scale=rf), [PST[pO], gk], [tk[d]])
                        dve(lambda e, t1_=t1_: e.tensor_scalar(out=t1_, in0=t1_, scalar1=1.0, scalar2=None, op0=ALU.max), [tk[d]], [tk[d]])
                        dve(lambda e, t1_=t1_: e.reciprocal(out=t1_, in_=t1_), [tk[d]], [tk[d]])
                        dve(lambda e, t1_=t1_, t2_=t2_, rf=rf: e.tensor_tensor(out=t2_, in0=t1_, in1=rf, op=ALU.mult), [tk[d], gk], [tk[d]])
                        rr = t2_
                    else:
                        rr = rf
                    dve(lambda e, pO=pO, rr=rr, oslice=oslice: e.scalar_tensor_tensor(out=oslice, in0=PS[pO][:, 0:128], scalar=rr, in1=oslice, op0=ALU.mult, op1=ALU.add), [PST[pO], tk[d], gk, okc[c]], [okc[c]])
                    if step < NCH - 1:
                        pe(lambda e, pC=pC, kcd=kcd, cs_=cs_, va=va: e.matmul(PS[pC][:, 0:130], lhsT=kcd[:, cs_], rhs=va, start=True, stop=True), [hk], [PST[pC]])
                        cd = v24(cdec)[:, c, col:col + 1]
                        if step == 0:
                            dve(lambda e: e.tensor_copy(out=Cst[d], in_=PS[pC][:, 0:130]), [PST[pC], ck[d]], [ck[d]])
                        else:
                            cprev = v24(cdec)[:, ORD[d][step - 1], col:col + 1]
                            dve(lambda e: e.scalar_tensor_tensor(out=Cst[d], in0=Cst[d], scalar=cprev, in1=PS[pC][:, 0:130], op0=ALU.mult, op1=ALU.add), [PST[pC], ck[d], gk], [ck[d]])
                        act(lambda e: e.activation(out=Cbf[d], in_=Cst[d], func=AF.Copy, scale=cd), [ck[d], gk], [ck[d]])
            dump("g_qr%d" % hh, qq, [hk])
            dump("g_kr%d" % hh, kk, [hk])
            dump("g_O%d" % hh, Oacc, okc)
            dump("g_vaug%d" % hh, vaug, [hk])
            dump("g_qT%d" % hh, qT, [hk])
            dump("g_kc0_%d" % hh, kc0, [hk])
            O3 = v3(Oacc, NCH, 128)
            mean = st[:, 0:18]; ssq = st[:, 18:36]
            if not is_ml:
                dve(lambda e: e.tensor_reduce(out=mean, in_=O3, axis=AX.X, op=ALU.add), okc, [fk])
                dve(lambda e: e.scalar_tensor_tensor(out=O3, in0=mean[:, :, None].broadcast_to([128, NCH, 128]), scalar=-1.0 / 128.0, in1=O3, op0=ALU.mult, op1=ALU.add), [fk] + okc, okc)
            act(lambda e: e.activation(out=F1, in_=Oacc, func=AF.Square), okc, [fk])
            dve(lambda e: e.tensor_reduce(out=ssq, in_=v3(F1, NCH, 128), axis=AX.X, op=ALU.add), [fk], [fk])
            var_ = st[:, 36:54]; sd_ = st[:, 54:72]; rstd_ = st[:, 72:90]
            dve(lambda e: e.tensor_scalar(out=var_, in0=ssq, scalar1=1.0 / 128.0, scalar2=EPS, op0=ALU.mult, op1=ALU.add), [fk], [fk])
            act(lambda e: e.activation(out=sd_, in_=var_, func=AF.Sqrt), [fk], [fk])
            dve(lambda e: e.reciprocal(out=rstd_, in_=sd_), [fk], [fk])
            act(lambda e: e.activation(out=F2, in_=gs_, func=(AF.Sigmoid if is_ml else AF.Silu)), [hk], [fk])
            pool(lambda e: e.tensor_tensor(out=v3(F2, NCH, 128), in0=v3(F2, NCH, 128), in1=wn[:, None, :].broadcast_to([128, NCH, 128]), op=ALU.mult), [fk], [fk])
            dve(lambda e: e.tensor_tensor(out=v3(F3, NCH, 128), in0=O3, in1=rstd_[:, :, None].broadcast_to([128, NCH, 128]), op=ALU.mult), [fk] + okc, [fk])
            dve(lambda e: e.tensor_tensor(out=ybf, in0=F3, in1=F2, op=ALU.mult), [fk], [fk])
            dump("g_F1_%d" % hh, F1, [fk]); dump("g_F2_%d" % hh, F2, [fk]); dump("g_F3_%d" % hh, F3, [fk]); dump("g_st%d" % hh, st, [fk]); dump("g_ybf%d" % hh, ybf, [fk])
            yc0 = (1280 if is_ml else 512) + h * 128
            B.dma("sp", ymix[:, yc0:yc0 + 128].rearrange("(c p) n -> p c n", p=128), v3(ybf, NCH, 128), reads=[fk], writes=[ymix_tok])
            B.barrier()


    gates_d = dscr("gates_d", [128, NCH * 24])
    IN_BLOCKS = [(cb * 512, 512) for cb in range(13)] + [(6656, 24)]
    OUT_BLOCKS = [(cb * 512, 512) for cb in range(4)]
    try:
        setup()
        phase_mod()
        mark("mod")
        for l in range(DEPTH):
            last = (l == DEPTH - 1)
            src = xh if l == 0 else resB
            stok = None if l == 0 else res_tok["B"]
            prep_norm_mod(l)
            phase_norm_T(src, 0, range(NCH))
            B.barrier()
            proj_tok(w_in[l], IN_BLOCKS, range(NCH), ep_inproj)
            B.barrier()
            if "gates_d" in debug and l == 0:
                B.dma("sp", gates_d, gates_sb[:].rearrange("p a b -> p (a b)"), reads=[gates_tok], writes=[Tok()])
            mark("inproj%d" % l)
            phase_s5(l)
            mark("s5%d" % l)
            phase_gla(l)
            B.barrier()
            mark("mix%d" % l)
            chunks = range(2, NCH) if last else range(NCH)
            load_gate(l, 2)
            phase_plain_T(ymix, chunks)
            B.barrier()
            proj_tok(w_out[l], OUT_BLOCKS, chunks, make_ep_resid(src, stok, resA, res_tok["A"]))
            B.barrier()
            mark("outproj%d" % l)
            phase_norm_T(resA, 1, chunks)
            B.barrier()
            phase_ffn1(w_ff1[l], chunks)
            B.barrier()
            mark("ffn1%d" % l)
            load_gate(l, 5)
            phase_ffn2(w_ff2[l], chunks, resA, res_tok["A"], resB, res_tok["B"])
            B.barrier()
            mark("layer%d" % l)
        phase_final(resB, res_tok["B"])
    except _Stop:
        pass
    B.barrier()
    print("instr counts", {e: len(B.ops[e]) for e in B.engs}, flush=True)
    B.replay()
    return nc, es, dbg


def _consts():
    idx = np.arange(128)
    c = {}
    c["k_ident"] = np.eye(128, dtype=np.float32)
    c["k_ufwd"] = (idx[:, None] <= idx[None, :]).astype(np.float32)
    c["k_ubwd"] = (idx[:, None] >= idx[None, :]).astype(np.float32)
    c["k_self"] = np.zeros((128, 128), np.float32); c["k_self"][127, :] = 1.0
    c["k_selb"] = np.zeros((128, 128), np.float32); c["k_selb"][0, :] = 1.0
    t = np.arange(SEQ)
    rows = (t // 64).astype(np.float32); cols = (t % 64).astype(np.float32)
    inv = (10000.0 ** (-np.arange(32, dtype=np.float32) / 32.0)).astype(np.float32)
    ang = np.concatenate([rows[:, None] * inv, cols[:, None] * inv], axis=-1).astype(np.float32)
    cos = np.ones((T, 64), np.float32); sin = np.zeros((T, 64), np.float32)
    cos[CTX:] = np.cos(ang); sin[CTX:] = np.sin(ang)
    c["k_cos"] = np.ascontiguousarray(cos.reshape(NCH, 128, 64).transpose(1, 0, 2))
    c["k_sin"] = np.ascontiguousarray(sin.reshape(NCH, 128, 64).transpose(1, 0, 2))
    c["k_mask8"] = (idx[:, None] // 16 == np.arange(8)[None, :]).astype(np.float32)
    c["k_tpos"] = np.stack([idx + 1.0, 128.0 - idx], axis=1).astype(np.float32)
    return c


_PROG = {}


def kernel(**inputs):
    if "p" not in _PROG:
        _PROG["p"] = build_program()
    nc, es, _ = _PROG["p"]
    f32 = lambda a: np.ascontiguousarray(np.asarray(a, dtype=np.float32))
    x = f32(inputs["x"]); ctx = f32(inputs["ctx"]); c = f32(inputs["c"]); c_ctx = f32(inputs["c_ctx"])
    shared = {k: f32(inputs[k]) for k in ("w_mod", "b_mod", "norm1_w", "norm2_w", "w_in", "w_out", "s5_lam_re", "s5_lam_im", "s5_log_step",
                                          "s5_b_re", "s5_b_im", "s5_c_re", "s5_c_im", "s5_d", "s5_w_glu", "s5_b_glu", "ret_decay_logit",
                                          "ret_norm_w", "mlstm_igate_b", "mlstm_fgate_b", "mlstm_norm_w", "w_ff1", "w_ff2", "norm_f_w")}
    shared.update(_consts())
    in_maps = []
    for core in range(8):
        b = core % NB
        m = dict(shared)
        m["xh"] = np.ascontiguousarray(np.concatenate([ctx[b], x[b]], axis=0))
        m["cc"] = np.ascontiguousarray(np.stack([c[b], c_ctx], axis=0))
        in_maps.append(m)
    res = run_bass_kernel_spmd(nc, in_maps, core_ids=list(range(8)))
    outs = [np.asarray(res.results[b]["out"], dtype=np.float32) for b in range(NB)]
    return np.stack(outs, axis=0)
```

```python
import math
from contextlib import ExitStack
import numpy as np
import ml_dtypes
import concourse.bass as bass
import concourse.mybir as mybir
from concourse.bass_utils import run_bass_kernel_spmd

F32 = mybir.dt.float32
BF16 = mybir.dt.bfloat16
I32 = mybir.dt.int32
AF = mybir.ActivationFunctionType
ALU = mybir.AluOpType
AX = mybir.AxisListType

D = 2048
NB = 4
SEQ = 2048
CTX = 256
T = SEQ + CTX
NCH = T // 128
DEPTH = 2
INW = 6680
DFF = 8192
EPS = 1e-6
NDMASEM = 12
STRICT = True


class Tok:
    __slots__ = ("w", "r")

    def __init__(self):
        self.w = None
        self.r = []


class _Rec:
    def __init__(self):
        self.call = None

    def __getattr__(self, name):
        def f(*a, **k):
            self.call = (name, a, k)
            return self
        return f


class Builder:
    def __init__(self, nc, es):
        self.nc = nc
        self.es = es
        self.engs = ["pe", "dve", "act", "pool", "sp"]
        self.ops = {e: [] for e in self.engs}
        self.seq = {e: 0 for e in self.engs}
        self.waited = {e: {} for e in self.engs}
        self.sems = {}
        for e in ["pe", "dve", "act", "pool"]:
            self.sems[("e", e)] = es.enter_context(nc.semaphore("p_" + e))
        self.dcount = {}
        self.drr = {"sp": 0, "pool": 0, "act": 0}
        for q in ["sp", "pool", "act"]:
            for i in range(NDMASEM):
                self.sems[("d", q, i)] = es.enter_context(nc.semaphore("d_%s%d" % (q, i)))
                self.dcount[("d", q, i)] = 0
        self.final = []

    def _need(self, eng, deps):
        out = []
        for (k, v) in deps:
            if k == ("e", eng) and not (STRICT or eng == "pool"):
                continue
            if k == ("e", eng) and eng == "pe":
                continue
            if self.waited[eng].get(k, 0) >= v:
                continue
            self.waited[eng][k] = v
            out.append((k, v))
        return out

    def _deps(self, reads, writes):
        deps = []
        for t in reads:
            if t.w is not None:
                deps.append(t.w)
        for t in writes:
            if t.w is not None:
                deps.append(t.w)
            deps.extend(t.r)
        return deps

    def op(self, eng, fn, reads=(), writes=()):
        deps = self._deps(reads, writes)
        waits = self._need(eng, deps)
        self.seq[eng] += 1
        done = (("e", eng), self.seq[eng])
        rec = _Rec()
        fn(rec)
        call = rec.call
        fn = lambda e, call=call: getattr(e, call[0])(*call[1], **call[2])
        self.ops[eng].append((waits, fn, done))
        for t in reads:
            t.r.append(done)
        for t in writes:
            t.w = done
            t.r = []
        return done

    def dma(self, q, out, in_, reads=(), writes=(), **kw):
        i = self.drr[q]
        self.drr[q] = (i + 1) % NDMASEM
        k = ("d", q, i)
        deps = self._deps(reads, writes)
        if self.dcount[k] > 0:
            deps.append((k, self.dcount[k]))
        waits = self._need(q, deps)
        self.dcount[k] += 16
        done = (k, self.dcount[k])
        fn = lambda e, out=out, in_=in_, kw=kw: e.dma_start(out=out, in_=in_, **kw)
        self.ops[q].append((waits, fn, done))
        for t in reads:
            t.r.append(done)
        for t in writes:
            t.w = done
            t.r = []
        return done

    def barrier(self):
        allk = [(("e", e), self.seq[e]) for e in ["pe", "dve", "act", "pool"] if self.seq[e] > 0]
        allk += [(k, v) for k, v in self.dcount.items() if v > 0]
        for e in self.engs:
            waits = self._need(e, allk)
            if waits:
                self.ops[e].append((waits, None, None))

    def check_deadlock(self):
        vals = {k: 0 for k in self.sems}
        pos = {e: 0 for e in self.engs}
        progress = True
        while progress:
            progress = False
            for e in self.engs:
                ops = self.ops[e]
                while pos[e] < len(ops):
                    waits, fn, done = ops[pos[e]]
                    if any(vals[k] < v for (k, v) in waits):
                        break
                    if done is not None:
                        vals[done[0]] += 1 if done[0][0] == "e" else 16
                        assert vals[done[0]] == done[1], (e, pos[e], done, vals[done[0]])
                    pos[e] += 1
                    progress = True
        stuck = {e: (pos[e], len(self.ops[e])) for e in self.engs if pos[e] < len(self.ops[e])}
        if stuck:
            for e in stuck:
                waits, fn, done = self.ops[e][pos[e]]
                print("DEADLOCK", e, pos[e], [(k, v, vals[k]) for (k, v) in waits], flush=True)
            raise RuntimeError("deadlock in semaphore protocol: %s" % stuck)

    def replay(self):
        self.check_deadlock()
        nc = self.nc
        block = self.es.enter_context(nc.Block())
        hw = {"pe": block.tensor, "dve": block.vector, "act": block.scalar, "pool": block.gpsimd, "sp": block.sync}
        for e in self.engs:
            ops = self.ops[e]

            def body(eng, ops=ops):
                for (waits, fn, done) in ops:
                    for (k, v) in waits:
                        eng.wait_ge(self.sems[k], v)
                    if fn is None:
                        continue
                    inst = fn(eng)
                    if done[0][0] == "e":
                        inst.then_inc(self.sems[done[0]], 1)
                    else:
                        inst.then_inc(self.sems[done[0]], 16)

            hw[e](body)


class _Stop(Exception):
    pass


def build_program(debug=(), upto=None):
    nc = bass.Bass("TRN2", target_bir_lowering=False)
    es = ExitStack()
    B = Builder(nc, es)
    dbg = {}

    def din(name, shape, dt=F32):
        return nc.dram_tensor(name, list(shape), dt, kind="ExternalInput").ap()

    def dscr(name, shape, dt=F32):
        kind = "ExternalOutput" if name in debug else "Internal"
        ap = nc.dram_tensor(name, list(shape), dt, kind=kind).ap()
        if name in debug:
            dbg[name] = ap
        return ap

    def mark(name):
        if upto == name:
            raise _Stop()

    def dump(name, ap, toks):
        if name not in debug or name in dbg:
            return
        t = nc.dram_tensor(name, list(ap.shape), ap.dtype, kind="ExternalOutput").ap()
        dbg[name] = t
        B.dma("sp", t, ap, reads=toks, writes=[Tok()])

    def sb(name, shape, dt=F32):
        return es.enter_context(nc.sbuf_tensor(name, list(shape), dt))

    xh = din("xh", [T, D])
    cc = din("cc", [2, D])
    w_mod = din("w_mod", [DEPTH, D, 6 * D])
    b_mod = din("b_mod", [DEPTH, 6 * D])
    norm1_w = din("norm1_w", [DEPTH, D])
    norm2_w = din("norm2_w", [DEPTH, D])
    w_in = din("w_in", [DEPTH, D, INW])
    w_out = din("w_out", [DEPTH, D, D])
    lam_re = din("s5_lam_re", [DEPTH, 2, 32, 64])
    lam_im = din("s5_lam_im", [DEPTH, 2, 32, 64])
    log_step = din("s5_log_step", [DEPTH, 2, 32])
    s5b_re = din("s5_b_re", [DEPTH, 2, 32, 64, 16])
    s5b_im = din("s5_b_im", [DEPTH, 2, 32, 64, 16])
    s5c_re = din("s5_c_re", [DEPTH, 2, 32, 16, 64])
    s5c_im = din("s5_c_im", [DEPTH, 2, 32, 16, 64])
    s5_d = din("s5_d", [DEPTH, 512])
    w_glu = din("s5_w_glu", [DEPTH, 512, 1024])
    b_glu = din("s5_b_glu", [DEPTH, 1024])
    ret_logit = din("ret_decay_logit", [DEPTH, 2, 6])
    ret_nw = din("ret_norm_w", [DEPTH, 768])
    ig_b = din("mlstm_igate_b", [DEPTH, 2, 6])
    fg_b = din("mlstm_fgate_b", [DEPTH, 2, 6])
    ml_nw = din("mlstm_norm_w", [DEPTH, 768])
    w_ff1 = din("w_ff1", [DEPTH, D, DFF])
    w_ff2 = din("w_ff2", [DEPTH, DFF, D])
    norm_f = din("norm_f_w", [D])
    k_ident = din("k_ident", [128, 128])
    k_ufwd = din("k_ufwd", [128, 128])
    k_ubwd = din("k_ubwd", [128, 128])
    k_self = din("k_self", [128, 128])
    k_selb = din("k_selb", [128, 128])
    k_cos = din("k_cos", [128, NCH, 64])
    k_sin = din("k_sin", [128, NCH, 64])
    k_mask8 = din("k_mask8", [128, 8])
    k_tpos = din("k_tpos", [128, 2])
    out = nc.dram_tensor("out", [SEQ, D], F32, kind="ExternalOutput").ap()

    modd = dscr("modd", [DEPTH, 2, 6 * D])
    a16 = dscr("a16", [T, 6656], BF16)
    ymix = dscr("ymix", [T, D], BF16)
    resA = dscr("resA", [T, D])
    resB = dscr("resB", [T, D])
    h1d = dscr("h1d", [NCH, 128, 64, 128], BF16)

    PS = [es.enter_context(nc.psum_tensor("ps%d" % i, [128, 512], F32)) for i in range(8)]
    PST = [Tok() for _ in range(8)]

    ident_f = sb("ident_f", [128, 128]); ident_b = sb("ident_b", [128, 128], BF16)
    nident_b = sb("nident_b", [128, 128], BF16)
    ufwd_f = sb("ufwd_f", [128, 128]); ubwd_f = sb("ubwd_f", [128, 128])
    ufwd_b = sb("ufwd_b", [128, 128], BF16); ubwd_b = sb("ubwd_b", [128, 128], BF16)
    nufwd_b = sb("nufwd_b", [128, 128], BF16); nubwd_b = sb("nubwd_b", [128, 128], BF16)
    self_b = sb("self_b", [128, 128], BF16); selb_b = sb("selb_b", [128, 128], BF16)
    nself_b = sb("nself_b", [128, 128], BF16); nselb_b = sb("nselb_b", [128, 128], BF16)
    ones_f = sb("ones_f", [128, 128])
    mask8 = sb("mask8", [128, 8]); tpos = sb("tpos", [128, 2])
    cos_t = sb("cos_t", [128, NCH, 64]); sin_t = sb("sin_t", [128, NCH, 64])
    ctok = Tok()
    actT = sb("actT", [128, 16, T], BF16)
    actT_tok = [Tok() for _ in range(NCH)]
    arenaB = sb("arenaB", [128, 20480])
    arenaB_bf = arenaB[:].bitcast(BF16)
    gates_sb = sb("gates_sb", [128, NCH, 24]); gates_tok = Tok()
    small = sb("small", [128, 256]);

    def setup():
        stg = arenaB
        loads = [(k_ident, 0), (k_ufwd, 128), (k_ubwd, 256), (k_self, 384), (k_selb, 512)]
        for ap, o in loads:
            B.dma("sp", stg[:, o:o + 128], ap, writes=[ctok])
        B.dma("sp", mask8[:], k_mask8, writes=[ctok])
        B.dma("sp", tpos[:], k_tpos, writes=[ctok])
        B.dma("sp", cos_t[:], k_cos, writes=[ctok])
        B.dma("sp", sin_t[:], k_sin, writes=[ctok])
        cp = lambda o, i: B.op("dve", lambda e, o=o, i=i: e.tensor_copy(out=o, in_=i), reads=[ctok], writes=[ctok])
        ng = lambda o, i: B.op("dve", lambda e, o=o, i=i: e.tensor_scalar(out=o, in0=i, scalar1=-1.0, scalar2=None, op0=ALU.mult), reads=[ctok], writes=[ctok])
        cp(ident_f[:], stg[:, 0:128]); cp(ident_b[:], stg[:, 0:128]); ng(nident_b[:], stg[:, 0:128])
        cp(ufwd_f[:], stg[:, 128:256]); cp(ufwd_b[:], stg[:, 128:256]); ng(nufwd_b[:], stg[:, 128:256])
        cp(ubwd_f[:], stg[:, 256:384]); cp(ubwd_b[:], stg[:, 256:384]); ng(nubwd_b[:], stg[:, 256:384])
        cp(self_b[:], stg[:, 384:512]); ng(nself_b[:], stg[:, 384:512])
        cp(selb_b[:], stg[:, 512:640]); ng(nselb_b[:], stg[:, 512:640])
        B.op("dve", lambda e: e.memset(ones_f[:], 1.0), writes=[ctok])
        B.barrier()

    def phase_mod():
        cT = sb("cT", [128, 16, 2]); sT = sb("sT", [128, 16, 2]); t_c = Tok()
        for s in range(2):
            B.dma("sp", cT[:, :, s], cc[s].rearrange("(c p) -> p c", p=128), writes=[t_c], allow_slow_non_contiguous=True)
        B.op("act", lambda e: e.activation(out=sT[:], in_=cT[:], func=AF.Silu), reads=[t_c], writes=[t_c])
        NWS = 8
        wst = [arenaB[:, i * 2048:(i + 1) * 2048] for i in range(NWS)]
        wtok = [Tok() for _ in range(NWS)]
        bst = [arenaB[0:2, 18432 + i * 512: 18432 + (i + 1) * 512] for i in range(2)]
        btok = [Tok() for _ in range(2)]
        ost = [arenaB[0:2, 19456 + i * 512: 19456 + (i + 1) * 512] for i in range(2)]
        otok = [Tok() for _ in range(2)]
        n = 0
        for l in range(DEPTH):
            for cb in range(24):
                bi = cb % 2
                B.dma("sp", bst[bi], b_mod[l, cb * 512:(cb + 1) * 512].partition_broadcast(2), writes=[btok[bi]])
                pt = PST[cb % 2]; ps = PS[cb % 2]
                for kg in range(4):
                    wi = n % NWS; n += 1
                    src = w_mod[l, kg * 512:(kg + 1) * 512, cb * 512:(cb + 1) * 512].rearrange("(k p) n -> p k n", p=128)
                    B.dma("act" if kg % 2 else "sp", wst[wi].rearrange("p (k n) -> p k n", k=4), src, writes=[wtok[wi]])
                    for k4 in range(4):
                        kc = kg * 4 + k4
                        B.op("pe", lambda e, ps=ps, kc=kc, wi=wi, k4=k4: e.matmul(ps[0:2, :], lhsT=sT[:, kc, :], rhs=wst[wi][:, k4 * 512:(k4 + 1) * 512], start=(kc == 0), stop=(kc == 15)),
                             reads=[t_c, wtok[wi]], writes=[pt])
                B.op("dve", lambda e, ps=ps, bi=bi: e.tensor_tensor(out=ost[bi], in0=ps[0:2, :], in1=bst[bi], op=ALU.add), reads=[pt, btok[bi]], writes=[otok[bi]])
                B.dma("sp", modd[l, :, cb * 512:(cb + 1) * 512], ost[bi], reads=[otok[bi]], writes=[modtok])
        B.barrier()

    modtok = Tok()

    def load_vec16(dst, src, tok):
        B.dma("sp", dst, src.rearrange("(c p) -> p c", p=128), reads=[modtok], writes=[tok], allow_slow_non_contiguous=True)

    gsh = sb("gsh", [128, 2, 2, 2, 16]); gsh_tok = Tok()
    gate_bc = sb("gate_bc", [128, 2, D]); gate_tok = Tok()

    def prep_norm_mod(l):
        nw = small[:, 0:32].rearrange("p (a c) -> p a c", a=2); ntok = Tok()
        load_vec16(nw[:, 0, :], norm1_w[l], ntok); load_vec16(nw[:, 1, :], norm2_w[l], ntok)
        tmp = small[:, 32:160].rearrange("p (a s g c) -> p a s g c", a=2, s=2, g=2)
        for a in range(2):
            for s in range(2):
                load_vec16(tmp[:, a, s, 1, :], modd[l, s, (3 * a) * D:(3 * a + 1) * D], ntok)
                load_vec16(tmp[:, a, s, 0, :], modd[l, s, (3 * a + 1) * D:(3 * a + 2) * D], ntok)
        for a in range(2):
            for s in range(2):
                B.op("dve", lambda e, a=a, s=s: e.scalar_tensor_tensor(out=gsh[:, a, s, 0, :], in0=tmp[:, a, s, 0, :], scalar=1.0, in1=nw[:, a, :], op0=ALU.add, op1=ALU.mult),
                     reads=[ntok], writes=[gsh_tok])
                B.op("dve", lambda e, a=a, s=s: e.tensor_copy(out=gsh[:, a, s, 1, :], in_=tmp[:, a, s, 1, :]), reads=[ntok], writes=[gsh_tok])

    def load_gate(l, which):
        for s in range(2):
            B.dma("sp", gate_bc[:, s, :], modd[l, s, which * D:(which + 1) * D].partition_broadcast(128), reads=[modtok], writes=[gate_tok])

    def phase_norm_T(src, a_idx, chunks):
        xin = [arenaB[:, i * 2048:(i + 1) * 2048] for i in range(2)]; xtok = [Tok() for _ in range(2)]
        xs = [arenaB_bf[:, 8192 + i * 2048: 8192 + (i + 1) * 2048] for i in range(2)]; xstok = [Tok() for _ in range(2)]
        junk = arenaB_bf[:, 12288:14336]; jtok = Tok()
        st = small[:, 160:176]; sttok = [Tok() for _ in range(2)]
        for n, tc in enumerate(chunks):
            bi = n % 2
            s = 1 if tc < 2 else 0
            B.dma("sp", xin[bi], src[tc * 128:(tc + 1) * 128, :], writes=[xtok[bi]])
            ss = st[:, bi * 4:bi * 4 + 1]; rs = st[:, bi * 4 + 1:bi * 4 + 2]
            B.op("act", lambda e, bi=bi, ss=ss: e.activation(out=junk, in_=xin[bi], func=AF.Square, accum_out=ss), reads=[xtok[bi]], writes=[jtok, sttok[bi]])
            B.op("dve", lambda e, ss=ss, rs=rs: e.tensor_scalar(out=rs, in0=ss, scalar1=1.0 / D, scalar2=EPS, op0=ALU.mult, op1=ALU.add), reads=[sttok[bi]], writes=[sttok[bi]])
            B.op("act", lambda e, rs=rs: e.activation(out=rs, in_=rs, func=AF.Sqrt), reads=[sttok[bi]], writes=[sttok[bi]])
            B.op("dve", lambda e, rs=rs: e.reciprocal(out=rs, in_=rs), reads=[sttok[bi]], writes=[sttok[bi]])
            B.op("act", lambda e, bi=bi, rs=rs: e.activation(out=xs[bi], in_=xin[bi], func=AF.Copy, scale=rs), reads=[xtok[bi], sttok[bi]], writes=[xstok[bi]])
            for q in range(4):
                pi = 4 + (n * 4 + q) % 4
                psb = PS[pi][:].bitcast(BF16)
                for j in range(4):
                    kc = q * 4 + j
                    B.op("pe", lambda e, psb=psb, j=j, kc=kc, bi=bi: e.transpose(out=psb[:, j * 128:(j + 1) * 128], in_=xs[bi][:, kc * 128:(kc + 1) * 128], identity=ident_b[:]),
                         reads=[xstok[bi], ctok], writes=[PST[pi]])
                for j in range(4):
                    kc = q * 4 + j
                    B.op("dve", lambda e, psb=psb, j=j, kc=kc, tc=tc, s=s: e.tensor_scalar(out=actT[:, kc, tc * 128:(tc + 1) * 128], in0=psb[:, j * 128:(j + 1) * 128],
                                                                                 scalar1=gsh[:, a_idx, s, 0, kc:kc + 1], scalar2=gsh[:, a_idx, s, 1, kc:kc + 1], op0=ALU.mult, op1=ALU.add),
                         reads=[PST[pi], gsh_tok], writes=[actT_tok[tc]])

    def phase_plain_T(src, chunks):
        xs = [arenaB_bf[:, i * 2048:(i + 1) * 2048] for i in range(2)]; xstok = [Tok() for _ in range(2)]
        for n, tc in enumerate(chunks):
            bi = n % 2
            B.dma("sp", xs[bi], src[tc * 128:(tc + 1) * 128, :], writes=[xstok[bi]])
            for q in range(4):
                pi = 4 + (n * 4 + q) % 4
                psb = PS[pi][:].bitcast(BF16)
                for j in range(4):
                    kc = q * 4 + j
                    B.op("pe", lambda e, psb=psb, j=j, kc=kc, bi=bi: e.transpose(out=psb[:, j * 128:(j + 1) * 128], in_=xs[bi][:, kc * 128:(kc + 1) * 128], identity=ident_b[:]),
                         reads=[xstok[bi], ctok], writes=[PST[pi]])
                B.op("act", lambda e, psb=psb, q=q, tc=tc: e.activation(out=actT[:, q * 4:(q + 1) * 4, tc * 128:(tc + 1) * 128], in_=psb[:, 0:512].rearrange("p (j t) -> p j t", j=4), func=AF.Copy),
                     reads=[PST[pi]], writes=[actT_tok[tc]])

    WB_OFF = 16384
    wstage = [arenaB[:, 4096 + i * 2048: 4096 + (i + 1) * 2048] for i in range(2)]; wstok = [Tok() for _ in range(2)]
    wcnt = [0]

    def load_wblock(wsrc, c0, w, KC, dst, dtok, stage=None, stok_=None, queues=("act", "sp")):
        stage = stage or wstage; stok_ = stok_ or wstok
        g = max(1, 2048 // w)
        for k0 in range(0, KC, g):
            kk = min(g, KC - k0)
            si = wcnt[0] % 2; wcnt[0] += 1
            src = wsrc[k0 * 128:(k0 + kk) * 128, c0:c0 + w].rearrange("(k p) n -> p k n", p=128)
            B.dma(queues[si % len(queues)], stage[si][:, 0:kk * w].rearrange("p (k n) -> p k n", k=kk), src, writes=[stok_[si]])
            B.op("pool", lambda e, si=si, k0=k0, kk=kk: e.tensor_copy(out=dst[:, k0 * w:(k0 + kk) * w], in_=stage[si][:, 0:kk * w]), reads=[stok_[si]], writes=[dtok])

    wbuf = [arenaB_bf[:, 16384 + i * 8192: 16384 + (i + 1) * 8192] for i in range(2)]; wbtok = [Tok() for _ in range(2)]
    ostg_f = [arenaB[:, 16384 + i * 512: 16384 + (i + 1) * 512] for i in range(4)]; ostok = [Tok() for _ in range(4)]
    ostg2_f = [arenaB[:, 18432 + i * 512: 18432 + (i + 1) * 512] for i in range(4)]; os2tok = [Tok() for _ in range(4)]
    cnt = {"ps": 0, "o": 0}

    def proj_tok(wsrc, col_blocks, chunks, epilogue):
        load_wblock(wsrc, col_blocks[0][0], col_blocks[0][1], 16, wbuf[0], wbtok[0])
        for cbi, (c0, w) in enumerate(col_blocks):
            bi = cbi % 2
            if cbi + 1 < len(col_blocks):
                load_wblock(wsrc, col_blocks[cbi + 1][0], col_blocks[cbi + 1][1], 16, wbuf[1 - bi], wbtok[1 - bi])
            for tc in chunks:
                pi = cnt["ps"] % 4; cnt["ps"] += 1
                for kc in range(16):
                    B.op("pe", lambda e, pi=pi, kc=kc, tc=tc, bi=bi, w=w: e.matmul(PS[pi][:, 0:w], lhsT=actT[:, kc, tc * 128:(tc + 1) * 128], rhs=wbuf[bi][:, kc * w:(kc + 1) * w], start=(kc == 0), stop=(kc == 15)),
                         reads=[actT_tok[tc], wbtok[bi]], writes=[PST[pi]])
                epilogue(tc, c0, w, pi)

    def ep_inproj(tc, c0, w, pi):
        if c0 >= 6656:
            B.op("act", lambda e: e.activation(out=gates_sb[:, tc, :], in_=PS[pi][:, 0:24], func=AF.Copy), reads=[PST[pi]], writes=[gates_tok])
            return
        oi = cnt["o"] % 4; cnt["o"] += 1
        ob = ostg_f[oi].bitcast(BF16)[:, 0:512]
        B.op("act", lambda e: e.activation(out=ob, in_=PS[pi][:, 0:512], func=AF.Copy), reads=[PST[pi]], writes=[ostok[oi]])
        B.dma("sp", a16[tc * 128:(tc + 1) * 128, c0:c0 + 512], ob, reads=[ostok[oi]], writes=[a16_tok])

    a16_tok = Tok(); ymix_tok = Tok(); res_tok = {"A": Tok(), "B": Tok()}; h1_tok = Tok()

    def make_ep_resid(rsrc, rsrc_tok, rdst, rdst_tok):
        def ep(tc, c0, w, pi):
            s = 1 if tc < 2 else 0
            oi = cnt["o"] % 4; cnt["o"] += 1
            xo = ostg2_f[oi][:, 0:w]; tm = ostg_f[oi][:, 0:w]
            B.dma("act", xo, rsrc[tc * 128:(tc + 1) * 128, c0:c0 + w], reads=[rsrc_tok] if rsrc_tok else [], writes=[os2tok[oi]])
            B.op("dve", lambda e: e.tensor_tensor(out=tm, in0=PS[pi][:, 0:w], in1=gate_bc[:, s, c0:c0 + w], op=ALU.mult), reads=[PST[pi], gate_tok], writes=[ostok[oi]])
            B.op("pool", lambda e: e.tensor_tensor(out=tm, in0=tm, in1=xo, op=ALU.add), reads=[ostok[oi], os2tok[oi]], writes=[ostok[oi]])
            B.dma("sp", rdst[tc * 128:(tc + 1) * 128, c0:c0 + w], tm, reads=[ostok[oi]], writes=[rdst_tok])
        return ep

    def phase_ffn1(wsrc, chunks):
        blocks = []
        cl = list(chunks)
        for i in range(0, len(cl), 4):
            blocks.append(cl[i:i + 4])
        load_wblock(wsrc, 0, 512, 16, wbuf[0], wbtok[0])
        for cb in range(16):
            bi = cb % 2
            if cb + 1 < 16:
                load_wblock(wsrc, (cb + 1) * 512, 512, 16, wbuf[1 - bi], wbtok[1 - bi])
            for fs in range(4):
                kcf = cb * 4 + fs
                for blk in blocks:
                    t0 = blk[0] * 128; N = len(blk) * 128
                    pi = cnt["ps"] % 4; cnt["ps"] += 1
                    for kc in range(16):
                        B.op("pe", lambda e, pi=pi, kc=kc, bi=bi, fs=fs, t0=t0, N=N: e.matmul(PS[pi][:, 0:N], lhsT=wbuf[bi][:, kc * 512 + fs * 128: kc * 512 + (fs + 1) * 128], rhs=actT[:, kc, t0:t0 + N], start=(kc == 0), stop=(kc == 15)),
                             reads=[actT_tok[t] for t in blk] + [wbtok[bi]], writes=[PST[pi]])
                    oi = cnt["o"] % 4; cnt["o"] += 1
                    r = ostg2_f[oi][:, 0:N]; hb = ostg_f[oi].bitcast(BF16)[:, 0:N]
                    B.op("act", lambda e, pi=pi, r=r, N=N: e.activation(out=r, in_=PS[pi][:, 0:N], func=AF.Relu), reads=[PST[pi]], writes=[os2tok[oi]])
                    B.op("dve", lambda e, r=r, hb=hb: e.tensor_tensor(out=hb, in0=r, in1=r, op=ALU.mult), reads=[os2tok[oi]], writes=[ostok[oi]])
                    B.dma("sp", h1d[blk[0]:blk[0] + len(blk), :, kcf, :].rearrange("t p j -> p t j"), hb.rearrange("p (t j) -> p t j", j=128), reads=[ostok[oi]], writes=[h1_tok])

    def phase_ffn2(wsrc, chunks, rsrc, rsrc_tok, rdst, rdst_tok):
        actv = actT[:].rearrange("p a b -> p (a b)")
        wb2 = [actv[:, i * 16384:(i + 1) * 16384] for i in range(2)]; wb2tok = [Tok() for _ in range(2)]
        acc = arenaB[:, 0:9216]; acctok = [Tok() for _ in range(NCH)]
        stg = [arenaB[:, 9216 + i * 2048: 9216 + (i + 1) * 2048] for i in range(2)]; stgtok = [Tok() for _ in range(2)]
        hst = [arenaB_bf[:, 26624 + i * 4096: 26624 + (i + 1) * 4096] for i in range(2)] + [actv[:, 32768:36864]]; hstok = [Tok() for _ in range(3)]
        o1 = [arenaB[:, 17408 + i * 512: 17408 + (i + 1) * 512] for i in range(2)]; o1tok = [Tok() for _ in range(2)]
        o2 = [arenaB[:, 18432 + i * 512: 18432 + (i + 1) * 512] for i in range(2)]; o2tok = [Tok() for _ in range(2)]
        units = [(cb, half) for cb in range(4) for half in range(2)]
        def ld(u):
            cb, half = units[u]
            load_wblock(wsrc[half * 4096:(half + 1) * 4096], cb * 512, 512, 32, wb2[u % 2], wb2tok[u % 2], stage=stg, stok_=stgtok, queues=("act",))
        ld(0)
        n = 0; on = 0
        for u, (cb, half) in enumerate(units):
            bi = u % 2
            if u + 1 < len(units):
                ld(u + 1)
            for tc in chunks:
                hi = n % 3; n += 1
                B.dma("sp", hst[hi], h1d[tc][:, half * 32:(half + 1) * 32, :].rearrange("p k j -> p (k j)"), reads=[h1_tok], writes=[hstok[hi]])
                pi = cnt["ps"] % 4; cnt["ps"] += 1
                for kc in range(32):
                    B.op("pe", lambda e, pi=pi, kc=kc, hi=hi, bi=bi: e.matmul(PS[pi][:], lhsT=hst[hi][:, kc * 128:(kc + 1) * 128], rhs=wb2[bi][:, kc * 512:(kc + 1) * 512], start=(kc == 0), stop=(kc == 31)),
                         reads=[hstok[hi], wb2tok[bi]], writes=[PST[pi]])
                asl = acc[:, tc * 512:(tc + 1) * 512]
                if half == 0:
                    B.op("dve", lambda e, pi=pi, asl=asl: e.tensor_copy(out=asl, in_=PS[pi][:]), reads=[PST[pi]], writes=[acctok[tc]])
                else:
                    s_ = 1 if tc < 2 else 0
                    oi = on % 2; on += 1
                    c0 = cb * 512
                    B.dma("sp", o2[oi], rsrc[tc * 128:(tc + 1) * 128, c0:c0 + 512], reads=[rsrc_tok] if rsrc_tok else [], writes=[o2tok[oi]])
                    B.op("dve", lambda e, pi=pi, asl=asl, oi=oi: e.tensor_tensor(out=o1[oi], in0=PS[pi][:], in1=asl, op=ALU.add), reads=[PST[pi], acctok[tc]], writes=[o1tok[oi]])
                    B.op("dve", lambda e, oi=oi, s_=s_, c0=c0: e.tensor_tensor(out=o1[oi], in0=o1[oi], in1=gate_bc[:, s_, c0:c0 + 512], op=ALU.mult), reads=[o1tok[oi], gate_tok], writes=[o1tok[oi]])
                    B.op("pool", lambda e, oi=oi: e.tensor_tensor(out=o1[oi], in0=o1[oi], in1=o2[oi], op=ALU.add), reads=[o1tok[oi], o2tok[oi]], writes=[o1tok[oi]])
                    B.dma("sp", rdst[tc * 128:(tc + 1) * 128, c0:c0 + 512], o1[oi], reads=[o1tok[oi]], writes=[rdst_tok])

    def phase_final(src, src_tok):
        nf = gate_bc[:, 0, :]
        B.dma("sp", nf, norm_f.partition_broadcast(128), writes=[gate_tok])
        xin = [arenaB[:, i * 2048:(i + 1) * 2048] for i in range(2)]; xtok = [Tok() for _ in range(2)]
        yo = [arenaB[:, 4096 + i * 2048: 4096 + (i + 1) * 2048] for i in range(2)]; ytok = [Tok() for _ in range(2)]
        junk = arenaB_bf[:, 16384:18432]; jtok = Tok()
        st = small[:, 160:176]; sttok = [Tok() for _ in range(2)]
        outtok = Tok()
        for n, tc in enumerate(range(2, NCH)):
            bi = n % 2
            B.dma("sp", xin[bi], src[tc * 128:(tc + 1) * 128, :], reads=[src_tok], writes=[xtok[bi]])
            ss = st[:, bi * 4:bi * 4 + 1]; rs = st[:, bi * 4 + 1:bi * 4 + 2]
            B.op("act", lambda e, bi=bi, ss=ss: e.activation(out=junk, in_=xin[bi], func=AF.Square, accum_out=ss), reads=[xtok[bi]], writes=[jtok, sttok[bi]])
            B.op("dve", lambda e, ss=ss, rs=rs: e.tensor_scalar(out=rs, in0=ss, scalar1=1.0 / D, scalar2=EPS, op0=ALU.mult, op1=ALU.add), reads=[sttok[bi]], writes=[sttok[bi]])
            B.op("act", lambda e, rs=rs: e.activation(out=rs, in_=rs, func=AF.Sqrt), reads=[sttok[bi]], writes=[sttok[bi]])
            B.op("dve", lambda e, rs=rs: e.reciprocal(out=rs, in_=rs), reads=[sttok[bi]], writes=[sttok[bi]])
            B.op("act", lambda e, bi=bi, rs=rs: e.activation(out=yo[bi], in_=xin[bi], func=AF.Copy, scale=rs), reads=[xtok[bi], sttok[bi]], writes=[ytok[bi]])
            B.op("dve", lambda e, bi=bi: e.tensor_tensor(out=yo[bi], in0=yo[bi], in1=nf, op=ALU.mult), reads=[ytok[bi], gate_tok], writes=[ytok[bi]])
            d = B.dma("sp", out[(tc - 2) * 128:(tc - 1) * 128, :], yo[bi], reads=[ytok[bi]], writes=[outtok])

    A32 = actT[:].rearrange("p a b -> p (a b)").bitcast(F32)
    A16 = actT[:].rearrange("p a b -> p (a b)")
    B32 = arenaB; B16 = arenaB_bf
    ORD = [list(range(NCH)), [1, 0] + list(range(NCH - 1, 1, -1))]
    TWO_PI = 2.0 * math.pi

    def v3(ap, a, b):
        return ap.rearrange("p (a b) -> p a b", a=a, b=b)

    def dve(fn, reads, writes):
        return B.op("dve", fn, reads=reads, writes=writes)

    def act(fn, reads, writes):
        return B.op("act", fn, reads=reads, writes=writes)

    def pool(fn, reads, writes):
        return B.op("pool", fn, reads=reads, writes=writes)

    def pe(fn, reads, writes):
        return B.op("pe", fn, reads=reads, writes=writes)

    def range_reduce_sincos(ph, kint, sinv, cosv, tk, shape_note=None):
        dve(lambda e: e.tensor_copy(out=kint, in_=ph), [tk], [tk])
        dve(lambda e: e.tensor_copy(out=cosv, in_=kint), [tk], [tk])
        dve(lambda e: e.tensor_tensor(out=ph, in0=ph, in1=cosv, op=ALU.subtract), [tk], [tk])
        act(lambda e: e.activation(out=sinv, in_=ph, func=AF.Sin, scale=TWO_PI), [tk], [tk])
        dve(lambda e: e.tensor_scalar(out=ph, in0=ph, scalar1=0.25, scalar2=None, op0=ALU.add), [tk], [tk])
        dve(lambda e: e.tensor_scalar(out=cosv, in0=ph, scalar1=0.5, scalar2=None, op0=ALU.is_gt), [tk], [tk])
        dve(lambda e: e.tensor_tensor(out=ph, in0=ph, in1=cosv, op=ALU.subtract), [tk], [tk])
        act(lambda e: e.activation(out=cosv, in_=ph, func=AF.Sin, scale=TWO_PI), [tk], [tk])

    def phase_s5(l):
        yacc = A32[:, 0:9216]; ytok = [[Tok() for _ in range(4)] for _ in range(NCH)]
        uT = A16[:, 18432:27648]; uTtok = Tok()
        Bblk = A16[:, 27648:31744]; Btok = Tok()
        Cmat = A16[:, 31744:35840]; Ctok = Tok()
        misc = A32[:, 17920:18432]
        tab = [B32[:, i * 2048:(i + 1) * 2048] for i in range(4)]; tabtok = Tok()
        Pp = [[B16[:, 16384 + (blk * 4 + k) * 512: 16384 + (blk * 4 + k + 1) * 512] for k in range(4)] for blk in range(4)]
        Ptk = [[Tok() for _ in range(4)] for _ in range(4)]
        Zp = [[B16[:, 24576 + (i * 4 + k) * 512: 24576 + (i * 4 + k + 1) * 512] for k in range(4)] for i in range(2)]
        Ztok = [Tok() for _ in range(2)]
        xTt = [B16[:, 28672 + i * 128: 28672 + (i + 1) * 128] for i in range(8)]; xTtok = [Tok() for _ in range(8)]
        u_sb = B16[:, 29696:38912]; utok = Tok()
        dsk = B32[:, 8192:8704]; dtok = Tok()
        B.dma("sp", v3(u_sb, NCH, 512), a16[:, 0:512].rearrange("(c p) n -> p c n", p=128), reads=[a16_tok], writes=[utok])
        B.dma("sp", dsk, s5_d[l].partition_broadcast(128), writes=[dtok])
        for c in range(NCH):
            dve(lambda e, c=c: e.tensor_tensor(out=yacc[:, c * 512:(c + 1) * 512], in0=u_sb[:, c * 512:(c + 1) * 512], in1=dsk, op=ALU.mult), [utok, dtok], ytok[c])
            pi = 6 + c % 2
            psb = PS[pi][:].bitcast(BF16)
            for blk in range(4):
                pe(lambda e, psb=psb, blk=blk, c=c: e.transpose(out=psb[:, blk * 128:(blk + 1) * 128], in_=u_sb[:, c * 512 + blk * 128: c * 512 + (blk + 1) * 128], identity=ident_b[:]), [utok, ctok], [PST[pi]])
            act(lambda e, psb=psb, c=c: e.activation(out=v3(uT, 4, T)[:, :, c * 128:(c + 1) * 128], in_=v3(psb[:, 0:512], 4, 128), func=AF.Copy), [PST[pi]], [uTtok])
        B.barrier()
        for d in range(2):
            S = [B32[:, 8192 + i * 2048: 8192 + (i + 1) * 2048] for i in range(5)]
            stok = Tok()
            lrd, th, ph, sinv, cosv = S
            kint = B32[:, 18432:20480].bitcast(I32)
            dtb = misc[:, 0:32]
            ntp = misc[:, 32:33]
            tp = tpos[:, d:d + 1]
            B.dma("sp", lrd, lam_re[l, d].rearrange("g p -> (g p)").partition_broadcast(128), writes=[stok])
            B.dma("sp", th, lam_im[l, d].rearrange("g p -> (g p)").partition_broadcast(128), writes=[stok])
            B.dma("sp", dtb, log_step[l, d].partition_broadcast(128), writes=[stok])
            act(lambda e: e.activation(out=dtb, in_=dtb, func=AF.Exp), [stok], [stok])
            dve(lambda e: e.tensor_scalar(out=ntp, in0=tp, scalar1=-1.0, scalar2=None, op0=ALU.mult), [ctok, stok], [stok])
            dtb3 = dtb[:, :, None].broadcast_to([128, 32, 64])
            dve(lambda e: e.tensor_scalar(out=lrd, in0=lrd, scalar1=-1e-4, scalar2=None, op0=ALU.min), [stok], [stok])
            dve(lambda e: e.tensor_tensor(out=v3(lrd, 32, 64), in0=v3(lrd, 32, 64), in1=dtb3, op=ALU.mult), [stok], [stok])
            dve(lambda e: e.tensor_tensor(out=v3(th, 32, 64), in0=v3(th, 32, 64), in1=dtb3, op=ALU.mult), [stok], [stok])
            dve(lambda e: e.tensor_scalar(out=ph, in0=th, scalar1=tp, scalar2=1.0 / TWO_PI, op0=ALU.mult, op1=ALU.mult), [stok, ctok], [stok])
            range_reduce_sincos(ph, kint, sinv, cosv, stok)
            act(lambda e: e.activation(out=th, in_=lrd, func=AF.Exp, scale=tp), [stok, ctok], [stok])
            dve(lambda e: e.reciprocal(out=ph, in_=th), [stok], [stok])
            dve(lambda e: e.tensor_tensor(out=tab[2], in0=th, in1=cosv, op=ALU.mult), [stok], [tabtok])
            dve(lambda e: e.tensor_tensor(out=tab[3], in0=th, in1=sinv, op=ALU.mult), [stok], [tabtok])
            dve(lambda e: e.tensor_tensor(out=tab[0], in0=ph, in1=cosv, op=ALU.mult), [stok], [tabtok])
            dve(lambda e: e.scalar_tensor_tensor(out=tab[1], in0=ph, scalar=-1.0, in1=sinv, op0=ALU.mult, op1=ALU.mult), [stok], [tabtok])
            B.barrier()
            bre = B32[0:64, 8192:8704]; bim = B32[0:64, 8704:9216]; bbr = B32[0:64, 9216:9728]; bbi = B32[0:64, 9728:10240]
            t1 = B32[0:64, 10240:10752]; t2 = B32[0:64, 10752:11264]
            sm = [B32[0:64, 11264 + i * 32: 11264 + (i + 1) * 32] for i in range(12)]
            smi = B32[0:64, 11776:11808].bitcast(I32)
            btk = Tok()
            B.dma("sp", v3(bre, 32, 16), s5b_re[l, d].rearrange("g p n -> p g n"), writes=[btk])
            B.dma("sp", v3(bim, 32, 16), s5b_im[l, d].rearrange("g p n -> p g n"), writes=[btk])
            lr, li, dt2, mag, phs, sn, cs, are, aim, rden, cr, ci = sm
            B.dma("sp", lr, lam_re[l, d].rearrange("g p -> p g"), writes=[btk], allow_slow_non_contiguous=True)
            B.dma("sp", li, lam_im[l, d].rearrange("g p -> p g"), writes=[btk], allow_slow_non_contiguous=True)
            B.dma("sp", dt2, log_step[l, d].partition_broadcast(64), writes=[btk])
            act(lambda e: e.activation(out=dt2, in_=dt2, func=AF.Exp), [btk], [btk])
            dve(lambda e: e.tensor_scalar(out=lr, in0=lr, scalar1=-1e-4, scalar2=None, op0=ALU.min), [btk], [btk])
            dve(lambda e: e.tensor_tensor(out=mag, in0=lr, in1=dt2, op=ALU.mult), [btk], [btk])
            act(lambda e: e.activation(out=mag, in_=mag, func=AF.Exp), [btk], [btk])
            dve(lambda e: e.scalar_tensor_tensor(out=phs, in0=li, scalar=1.0 / TWO_PI, in1=dt2, op0=ALU.mult, op1=ALU.mult), [btk], [btk])
            range_reduce_sincos(phs, smi, sn, cs, btk)
            dve(lambda e: e.tensor_tensor(out=are, in0=mag, in1=cs, op=ALU.mult), [btk], [btk])
            dve(lambda e: e.tensor_scalar(out=are, in0=are, scalar1=-1.0, scalar2=None, op0=ALU.add), [btk], [btk])
            dve(lambda e: e.tensor_tensor(out=aim, in0=mag, in1=sn, op=ALU.mult), [btk], [btk])
            dve(lambda e: e.tensor_tensor(out=rden, in0=lr, in1=lr, op=ALU.mult), [btk], [btk])
            dve(lambda e: e.tensor_tensor(out=cr, in0=li, in1=li, op=ALU.mult), [btk], [btk])
            dve(lambda e: e.tensor_tensor(out=rden, in0=rden, in1=cr, op=ALU.add), [btk], [btk])
            dve(lambda e: e.reciprocal(out=rden, in_=rden), [btk], [btk])
            dve(lambda e: e.tensor_tensor(out=cr, in0=lr, in1=rden, op=ALU.mult), [btk], [btk])
            dve(lambda e: e.scalar_tensor_tensor(out=ci, in0=li, scalar=-1.0, in1=rden, op0=ALU.mult, op1=ALU.mult), [btk], [btk])
            dve(lambda e: e.tensor_tensor(out=mag, in0=are, in1=cr, op=ALU.mult), [btk], [btk])
            dve(lambda e: e.tensor_tensor(out=sn, in0=aim, in1=ci, op=ALU.mult), [btk], [btk])
            dve(lambda e: e.tensor_tensor(out=mag, in0=mag, in1=sn, op=ALU.subtract), [btk], [btk])
            dve(lambda e: e.tensor_tensor(out=phs, in0=are, in1=ci, op=ALU.mult), [btk], [btk])
            dve(lambda e: e.tensor_tensor(out=sn, in0=aim, in1=cr, op=ALU.mult), [btk], [btk])
            dve(lambda e: e.tensor_tensor(out=phs, in0=phs, in1=sn, op=ALU.add), [btk], [btk])
            nr3 = mag[:, :, None].broadcast_to([64, 32, 16]); ni3 = phs[:, :, None].broadcast_to([64, 32, 16])
            dve(lambda e: e.tensor_tensor(out=v3(t1, 32, 16), in0=v3(bre, 32, 16), in1=nr3, op=ALU.mult), [btk], [btk])
            dve(lambda e: e.tensor_tensor(out=v3(t2, 32, 16), in0=v3(bim, 32, 16), in1=ni3, op=ALU.mult), [btk], [btk])
            dve(lambda e: e.tensor_tensor(out=bbr, in0=t1, in1=t2, op=ALU.subtract), [btk], [btk])
            dve(lambda e: e.tensor_tensor(out=v3(t1, 32, 16), in0=v3(bim, 32, 16), in1=nr3, op=ALU.mult), [btk], [btk])
            dve(lambda e: e.tensor_tensor(out=v3(t2, 32, 16), in0=v3(bre, 32, 16), in1=ni3, op=ALU.mult), [btk], [btk])
            dve(lambda e: e.tensor_tensor(out=bbi, in0=t1, in1=t2, op=ALU.add), [btk], [btk])
            for nm_, ap_ in (("lr", lr), ("li", li), ("dt2", dt2), ("cs", cs), ("are", are), ("aim", aim), ("rden", rden), ("cr", cr), ("ci", ci)):
                dump("s_%s%d" % (nm_, d), ap_, [btk])
            dump("s_bbr%d" % d, bbr, [btk]); dump("s_bbi%d" % d, bbi, [btk]); dump("s_nr%d" % d, mag, [btk]); dump("s_ni%d" % d, phs, [btk])
            m8 = mask8[:, :, None].broadcast_to([128, 8, 64])
            n = 0
            for blk in range(4):
                for ri, bb in enumerate((bbr, bbi)):
                    pi = 6 + n % 2; n += 1
                    pe(lambda e, pi=pi, bb=bb, blk=blk: e.transpose(out=PS[pi][:, 0:64], in_=bb[:, blk * 128:(blk + 1) * 128], identity=ident_f[0:64, 0:64]), [btk, ctok], [PST[pi]])
                    dst = v3(Bblk, 4, 1024)[:, blk, ri * 512:(ri + 1) * 512].rearrange("p (g q) -> p g q", g=8)
                    dve(lambda e, pi=pi, dst=dst: e.tensor_tensor(out=dst, in0=PS[pi][:, None, 0:64].broadcast_to([128, 8, 64]), in1=m8, op=ALU.mult), [PST[pi], ctok], [Btok])
            cnat = [B32[:, 12288 + i * 64: 12288 + (i + 1) * 64] for i in range(8)]
            cntk = Tok()
            for ri, csrc in enumerate((s5c_re, s5c_im)):
                for blk in range(4):
                    B.dma("sp", cnat[ri * 4 + blk], csrc[l, d, blk * 8:(blk + 1) * 8].rearrange("g n p -> (g n) p"), writes=[cntk])
            xm = [B16[:, 26624 + i * 128: 26624 + (i + 1) * 128] for i in range(4)]; xmtok = [Tok() for _ in range(4)]
            n = 0
            for blk in range(4):
                for q in range(4):
                    for ri in range(2):
                        xi = n % 4; pi = 6 + n % 2; n += 1
                        for g2 in range(2):
                            mk = mask8[:, 2 * q + g2: 2 * q + g2 + 1]
                            dve(lambda e, xi=xi, g2=g2, ri=ri, blk=blk, mk=mk: e.tensor_scalar(out=xm[xi][:, g2 * 64:(g2 + 1) * 64], in0=cnat[ri * 4 + blk], scalar1=mk, scalar2=(-1.0 if ri else 1.0), op0=ALU.mult, op1=ALU.mult),
                                [cntk, ctok], [xmtok[xi]])
                        psb = PS[pi][:].bitcast(BF16)
                        pe(lambda e, psb=psb, xi=xi: e.transpose(out=psb[:, 0:128], in_=xm[xi], identity=ident_b[:]), [xmtok[xi], ctok], [PST[pi]])
                        ci_ = (blk * 4 + q) * 2 + ri
                        act(lambda e, psb=psb, ci_=ci_: e.activation(out=Cmat[:, ci_ * 128:(ci_ + 1) * 128], in_=psb[:, 0:128], func=AF.Copy), [PST[pi]], [Ctok])
            B.barrier()
            for i_ in range(4):
                dump("s_tab%d_%d" % (i_, d), tab[i_], [tabtok])
            dump("s_Bblk%d" % d, Bblk, [Btok])
            dump("s_Cmat%d" % d, Cmat, [Ctok])
            mark("s5c%d" % d)
            U = ufwd_b if d == 0 else ubwd_b; NU = nufwd_b if d == 0 else nubwd_b
            SEL = self_b if d == 0 else selb_b; NSEL = nself_b if d == 0 else nselb_b
            xn = 0
            X6 = [Tok() for _ in range(4)]; Y7 = [Tok() for _ in range(4)]
            Zt = [[Tok() for _ in range(4)] for _ in range(2)]
            for step, c in enumerate(ORD[d]):
                for blk in range(4):
                    zi = blk % 2
                    pr, pim = (0, 1) if zi == 0 else (2, 3)
                    cols = slice(blk * 512, (blk + 1) * 512)
                    lhs_u = v3(uT, 4, T)[:, blk, c * 128:(c + 1) * 128]
                    pe(lambda e, pr=pr, lhs_u=lhs_u, blk=blk: e.matmul(PS[pr][:], lhsT=lhs_u, rhs=v3(Bblk, 4, 1024)[:, blk, 0:512], start=True, stop=True), [uTtok, Btok], [PST[pr]])
                    pe(lambda e, pim=pim, lhs_u=lhs_u, blk=blk: e.matmul(PS[pim][:], lhsT=lhs_u, rhs=v3(Bblk, 4, 1024)[:, blk, 512:1024], start=True, stop=True), [uTtok, Btok], [PST[pim]])
                    Z = Zp[zi]
                    dve(lambda e, Z=Z, pr=pr, cols=cols: e.tensor_tensor(out=Z[0], in0=PS[pr][:], in1=tab[0][:, cols], op=ALU.mult), [PST[pr], tabtok], [Zt[zi][0]])
                    dve(lambda e, Z=Z, pim=pim, cols=cols: e.tensor_tensor(out=Z[1], in0=PS[pim][:], in1=tab[1][:, cols], op=ALU.mult), [PST[pim], tabtok], [Zt[zi][1]])
                    dve(lambda e, Z=Z, pim=pim, cols=cols: e.tensor_tensor(out=Z[2], in0=PS[pim][:], in1=tab[0][:, cols], op=ALU.mult), [PST[pim], tabtok], [Zt[zi][2]])
                    dve(lambda e, Z=Z, pr=pr, cols=cols: e.tensor_tensor(out=Z[3], in0=PS[pr][:], in1=tab[1][:, cols], op=ALU.mult), [PST[pr], tabtok], [Zt[zi][3]])
                    P = Pp[blk]
                    first = (step == 0)
                    pe(lambda e, Z=Z: e.matmul(PS[4][:], lhsT=U[:], rhs=Z[0], start=True, stop=False), [Zt[zi][0], ctok], [PST[4]])
                    pe(lambda e, Z=Z, first=first: e.matmul(PS[4][:], lhsT=NU[:], rhs=Z[1], start=False, stop=first), [Zt[zi][1], ctok], [PST[4]])
                    if not first:
                        pe(lambda e, P=P: e.matmul(PS[4][:], lhsT=SEL[:], rhs=P[0], start=False, stop=False), [Ptk[blk][0], ctok], [PST[4]])
                        pe(lambda e, P=P: e.matmul(PS[4][:], lhsT=NSEL[:], rhs=P[1], start=False, stop=True), [Ptk[blk][1], ctok], [PST[4]])
                    pe(lambda e, Z=Z: e.matmul(PS[5][:], lhsT=U[:], rhs=Z[2], start=True, stop=False), [Zt[zi][2], ctok], [PST[5]])
                    pe(lambda e, Z=Z, first=first: e.matmul(PS[5][:], lhsT=U[:], rhs=Z[3], start=False, stop=first), [Zt[zi][3], ctok], [PST[5]])
                    if not first:
                        pe(lambda e, P=P: e.matmul(PS[5][:], lhsT=SEL[:], rhs=P[2], start=False, stop=False), [Ptk[blk][2], ctok], [PST[5]])
                        pe(lambda e, P=P: e.matmul(PS[5][:], lhsT=SEL[:], rhs=P[3], start=False, stop=True), [Ptk[blk][3], ctok], [PST[5]])
                    dve(lambda e, P=P, cols=cols: e.tensor_tensor(out=P[0], in0=PS[4][:], in1=tab[2][:, cols], op=ALU.mult), [PST[4], tabtok], [Ptk[blk][0]])
                    dve(lambda e, P=P, cols=cols: e.tensor_tensor(out=P[1], in0=PS[5][:], in1=tab[3][:, cols], op=ALU.mult), [PST[5], tabtok], [Ptk[blk][1]])
                    dve(lambda e, P=P, cols=cols: e.tensor_tensor(out=P[2], in0=PS[5][:], in1=tab[2][:, cols], op=ALU.mult), [PST[5], tabtok], [Ptk[blk][2]])
                    dve(lambda e, P=P, cols=cols: e.tensor_tensor(out=P[3], in0=PS[4][:], in1=tab[3][:, cols], op=ALU.mult), [PST[4], tabtok], [Ptk[blk][3]])
                    for q in range(4):
                        qs = slice(q * 128, (q + 1) * 128)
                        xs_ = []
                        for ri in range(2):
                            xi = xn % 8; xn += 1
                            pslot = PS[6][:, (xi % 4) * 128:((xi % 4) + 1) * 128]
                            a0, a1 = (P[0], P[1]) if ri == 0 else (P[2], P[3])
                            idn = nident_b if ri == 0 else ident_b
                            pe(lambda e, pslot=pslot, a0=a0, qs=qs: e.matmul(pslot, lhsT=a0[:, qs], rhs=ident_b[:], start=True, stop=False), Ptk[blk] + [ctok], [X6[xi % 4]])
                            pe(lambda e, pslot=pslot, a1=a1, qs=qs, idn=idn: e.matmul(pslot, lhsT=a1[:, qs], rhs=idn[:], start=False, stop=True), Ptk[blk] + [ctok], [X6[xi % 4]])
                            act(lambda e, pslot=pslot, xi=xi: e.activation(out=xTt[xi], in_=pslot, func=AF.Copy), [X6[xi % 4]], [xTtok[xi]])
                            xs_.append(xi)
                        yslot = PS[7][:, (blk % 4) * 128:((blk % 4) + 1) * 128]
                        for ri in range(2):
                            ci_ = (blk * 4 + q) * 2 + ri
                            xi = xs_[ri]
                            pe(lambda e, yslot=yslot, xi=xi, ci_=ci_, q=q, ri=ri: e.matmul(yslot, lhsT=xTt[xi], rhs=Cmat[:, ci_ * 128:(ci_ + 1) * 128], start=(q == 0 and ri == 0), stop=(q == 3 and ri == 1)),
                               [xTtok[xi], Ctok], [Y7[blk]])
                    ysl = yacc[:, c * 512 + blk * 128: c * 512 + (blk + 1) * 128]
                    dve(lambda e, ysl=ysl, yslot=yslot: e.tensor_tensor(out=ysl, in0=yslot, in1=ysl, op=ALU.add), [Y7[blk], ytok[c][blk]], [ytok[c][blk]])
            B.barrier()
        mark("s5d")
        dump("s_yacc", yacc, [t_ for r_ in ytok for t_ in r_])
        mark("s5d")
        gyT = uT; gtok = Tok()
        wg = B16[:, 0:4096]; wgtok = Tok()
        bgl = B32[:, 2048:3072]; bgtok = Tok()
        B.dma("sp", bgl, b_glu[l].partition_broadcast(128), writes=[bgtok])
        load_wblock(w_glu[l], 0, 1024, 4, wg, wgtok)
        gs = [B32[:, 8192 + i * 512: 8192 + (i + 1) * 512] for i in range(4)]; gstok = [Tok() for _ in range(2)]
        gb = [B16[:, 24576 + i * 512: 24576 + (i + 1) * 512] for i in range(2)]; gbtok = [Tok() for _ in range(2)]
        GC = 2.0 * math.sqrt(2.0 / math.pi)
        for c in range(NCH):
            bi = c % 2
            y = yacc[:, c * 512:(c + 1) * 512]; t = gs[bi * 2]; sg = gs[bi * 2 + 1]
            dve(lambda e, y=y, t=t: e.tensor_tensor(out=t, in0=y, in1=y, op=ALU.mult), ytok[c], [gstok[bi]])
            dve(lambda e, t=t: e.tensor_scalar(out=t, in0=t, scalar1=0.044715, scalar2=1.0, op0=ALU.mult, op1=ALU.add), [gstok[bi]], [gstok[bi]])
            dve(lambda e, y=y, t=t: e.tensor_tensor(out=t, in0=t, in1=y, op=ALU.mult), [gstok[bi]] + ytok[c], [gstok[bi]])
            act(lambda e, t=t, sg=sg: e.activation(out=sg, in_=t, func=AF.Sigmoid, scale=GC), [gstok[bi]], [gstok[bi]])
            dve(lambda e, y=y, sg=sg, bi=bi: e.tensor_tensor(out=gb[bi], in0=y, in1=sg, op=ALU.mult), [gstok[bi]] + ytok[c], [gbtok[bi]])
            pi = 6 + c % 2
            psb = PS[pi][:].bitcast(BF16)
            for kc in range(4):
                pe(lambda e, psb=psb, kc=kc, bi=bi: e.transpose(out=psb[:, kc * 128:(kc + 1) * 128], in_=gb[bi][:, kc * 128:(kc + 1) * 128], identity=ident_b[:]), [gbtok[bi], ctok], [PST[pi]])
            act(lambda e, psb=psb, c=c: e.activation(out=v3(gyT, 4, T)[:, :, c * 128:(c + 1) * 128], in_=v3(psb[:, 0:512], 4, 128), func=AF.Copy), [PST[pi]], [gtok])
        zs = [B32[:, 10240 + i * 512: 10240 + (i + 1) * 512] for i in range(4)]; zstok = [Tok() for _ in range(2)]
        zo = [B16[:, 25600 + i * 512: 25600 + (i + 1) * 512] for i in range(2)]; zotok = [Tok() for _ in range(2)]
        for c in range(NCH):
            bi = c % 2
            for half in range(2):
                pi = half + 2 * bi
                for kc in range(4):
                    pe(lambda e, pi=pi, kc=kc, c=c, half=half: e.matmul(PS[pi][:], lhsT=v3(gyT, 4, T)[:, kc, c * 128:(c + 1) * 128], rhs=wg[:, kc * 1024 + half * 512: kc * 1024 + (half + 1) * 512], start=(kc == 0), stop=(kc == 3)),
                       [gtok, wgtok], [PST[pi]])
            va = zs[bi * 2]; gt = zs[bi * 2 + 1]
            dve(lambda e, va=va, bi=bi: e.tensor_tensor(out=va, in0=PS[2 * bi][:], in1=bgl[:, 0:512], op=ALU.add), [PST[2 * bi], bgtok], [zstok[bi]])
            dve(lambda e, gt=gt, bi=bi: e.tensor_tensor(out=gt, in0=PS[2 * bi + 1][:], in1=bgl[:, 512:1024], op=ALU.add), [PST[2 * bi + 1], bgtok], [zstok[bi]])
            act(lambda e, gt=gt: e.activation(out=gt, in_=gt, func=AF.Sigmoid), [zstok[bi]], [zstok[bi]])
            dve(lambda e, va=va, gt=gt, bi=bi: e.tensor_tensor(out=zo[bi], in0=va, in1=gt, op=ALU.mult), [zstok[bi]], [zotok[bi]])
            B.dma("sp", ymix[c * 128:(c + 1) * 128, 0:512], zo[bi], reads=[zotok[bi]], writes=[ymix_tok])
        B.barrier()

    def phase_gla(l):
        gt = [A32[:, i * 432:(i + 1) * 432] for i in range(6)]
        LF, II, Bc, colfac, rowfac, cdec = gt
        biasF = A32[:, 2592:2616]; biasI = A32[:, 2616:2640]
        gk = Tok()
        v24 = lambda ap: v3(ap, NCH, 24)
        dve(lambda e: e.memset(biasI, 0.0), [], [gk])
        dve(lambda e: e.memset(LF, 0.0), [], [gk])
        dve(lambda e: e.memset(II, 0.0), [], [gk])
        B.dma("sp", v3(biasF, 2, 12)[:, :, 0:6], ret_logit[l].partition_broadcast(128), writes=[gk])
        B.dma("sp", v3(biasF, 2, 12)[:, :, 6:12], fg_b[l].partition_broadcast(128), writes=[gk])
        B.dma("sp", v3(biasI, 2, 12)[:, :, 6:12], ig_b[l].partition_broadcast(128), writes=[gk])
        for d in range(2):
            dve(lambda e, d=d: e.tensor_copy(out=v24(LF)[:, :, d * 12 + 6: d * 12 + 12], in_=gates_sb[:, :, d * 12 + 6: d * 12 + 12]), [gates_tok, gk], [gk])
            dve(lambda e, d=d: e.tensor_copy(out=v24(II)[:, :, d * 12 + 6: d * 12 + 12], in_=gates_sb[:, :, d * 12: d * 12 + 6]), [gates_tok, gk], [gk])
        dve(lambda e: e.tensor_tensor(out=v24(LF), in0=v24(LF), in1=biasF[:, None, :].broadcast_to([128, NCH, 24]), op=ALU.add), [gk], [gk])
        dve(lambda e: e.tensor_tensor(out=v24(II), in0=v24(II), in1=biasI[:, None, :].broadcast_to([128, NCH, 24]), op=ALU.add), [gk], [gk])
        act(lambda e: e.activation(out=LF, in_=LF, func=AF.Exp, scale=-1.0), [gk], [gk])
        act(lambda e: e.activation(out=LF, in_=LF, func=AF.Ln, bias=1.0), [gk], [gk])
        dve(lambda e: e.tensor_scalar(out=LF, in0=LF, scalar1=-1.0, scalar2=None, op0=ALU.mult), [gk], [gk])
        for d in range(2):
            Uf = ufwd_f if d == 0 else ubwd_f
            rhs = v24(LF)[:, :, d * 12:(d + 1) * 12]
            pe(lambda e, Uf=Uf, rhs=rhs: e.matmul(PS[0][:, 0:216], lhsT=Uf[:], rhs=rhs, start=True, stop=True), [gk, ctok], [PST[0]])
            pe(lambda e, rhs=rhs: e.matmul(PS[1][:, 0:216], lhsT=ones_f[:], rhs=rhs, start=True, stop=True), [gk, ctok], [PST[1]])
            act(lambda e, d=d: e.activation(out=v24(Bc)[:, :, d * 12:(d + 1) * 12], in_=v3(PS[0][:, 0:216], NCH, 12), func=AF.Copy), [PST[0]], [gk])
            act(lambda e, d=d: e.activation(out=v24(cdec)[:, :, d * 12:(d + 1) * 12], in_=v3(PS[1][:, 0:216], NCH, 12), func=AF.Exp), [PST[1]], [gk])
        act(lambda e: e.activation(out=rowfac, in_=Bc, func=AF.Exp), [gk], [gk])
        dve(lambda e: e.tensor_tensor(out=colfac, in0=II, in1=Bc, op=ALU.subtract), [gk], [gk])
        act(lambda e: e.activation(out=colfac, in_=colfac, func=AF.Exp, bias=float(math.log(128.0 ** -0.5))), [gk], [gk])
        for nm, ap_ in (("g_LF", LF), ("g_II", II), ("g_Bc", Bc), ("g_colfac", colfac), ("g_rowfac", rowfac), ("g_cdec", cdec)):
            dump(nm, ap_, [gk])
        o16 = 5376
        def a16v(i):
            return A16[:, o16 + i * 2304: o16 + (i + 1) * 2304]
        qs, ks, vs, gs_, qr, kr, kc0, kc1, qT, kT0, kT1 = [a16v(i) for i in range(11)]
        vaug = A16[:, 30720:33060]
        sTm = [A16[:, 33060 + i * 128: 33060 + (i + 1) * 128] for i in range(4)]; sTtok = [Tok() for _ in range(4)]
        Cbf = [A16[:, 33572 + i * 130: 33572 + (i + 1) * 130] for i in range(2)]
        Oacc = B32[:, 0:2304]; F1 = B32[:, 2304:4608]; F2 = B32[:, 4608:6912]; F3 = B32[:, 6912:9216]
        ybf = B16[:, 18432:20736]
        Cst = [B32[:, 10368 + i * 130: 10368 + (i + 1) * 130] for i in range(2)]
        wn = B32[:, 10752:10880]
        st = B32[:, 10880:10880 + 128]
        tiny = B32[:, 11008:11008 + 64]
        lsets = [[qs, ks, vs, gs_], [B16[:, 22272 + i * 2304: 22272 + (i + 1) * 2304] for i in range(4)]]
        wns = [wn, B32[:, 15744:15872]]
        rtmp = {"dve": B32[:, 15872:18176], "pool": B32[:, 18176:20480]}
        lk = [Tok(), Tok()]; pk = Tok(); ck = [Tok(), Tok()]; okc = [Tok() for _ in range(NCH)]; tk = [Tok(), Tok()]; fk = Tok(); yk = Tok()
        rk = {"dve": Tok(), "pool": Tok()}

        def g_loads(hh):
            is_ml = hh >= 6; h = hh % 6; s_ = hh % 2
            base = 3584 if is_ml else 512
            for i, off in enumerate((0, 768, 1536, 2304)):
                c0 = base + off + h * 128
                B.dma("sp" if i % 2 else "act", v3(lsets[s_][i], NCH, 128), a16[:, c0:c0 + 128].rearrange("(c p) n -> p c n", p=128), reads=[a16_tok], writes=[lk[s_]])
            B.dma("sp", wns[s_], (ml_nw if is_ml else ret_nw)[l, h * 128:(h + 1) * 128].partition_broadcast(128), writes=[lk[s_]])

        def g_head(hh):
            is_ml = hh >= 6; h = hh % 6; s_ = hh % 2
            q_in, k_in, v_in, g_in = lsets[s_]
            hk = pk
            if not is_ml:
                for (src, dst, eng) in ((q_in, qr, "dve"), (k_in, kr, "pool")):
                    s1 = v3(src, NCH, 128)[:, :, 0:64]; s2 = v3(src, NCH, 128)[:, :, 64:128]
                    d1 = v3(dst, NCH, 128)[:, :, 0:64]; d2 = v3(dst, NCH, 128)[:, :, 64:128]
                    f1 = v3(rtmp[eng][:, 0:1152], NCH, 64); f2 = v3(rtmp[eng][:, 1152:2304], NCH, 64)
                    rke = rk[eng]
                    opf = lambda fn, rd, wr, eng=eng: B.op(eng, fn, reads=rd, writes=wr)
                    opf(lambda e: e.tensor_tensor(out=f1, in0=s1, in1=cos_t[:], op=ALU.mult), [lk[s_], ctok], [rke])
                    opf(lambda e: e.tensor_tensor(out=f2, in0=s2, in1=sin_t[:], op=ALU.mult), [lk[s_], ctok], [rke])
                    opf(lambda e: e.tensor_tensor(out=d1, in0=f1, in1=f2, op=ALU.subtract), [rke], [hk])
                    opf(lambda e: e.tensor_tensor(out=f1, in0=s1, in1=sin_t[:], op=ALU.mult), [lk[s_], ctok], [rke])
                    opf(lambda e: e.tensor_tensor(out=f2, in0=s2, in1=cos_t[:], op=ALU.mult), [lk[s_], ctok], [rke])
                    opf(lambda e: e.tensor_tensor(out=d2, in0=f1, in1=f2, op=ALU.add), [rke], [hk])
                qq, kk = qr, kr
            else:
                qq, kk = q_in, k_in
            for d, kcd in enumerate((kc0, kc1)):
                col = d * 12 + hh
                dve(lambda e: e.tensor_tensor(out=v3(kcd, NCH, 128), in0=v3(kk, NCH, 128), in1=v24(colfac)[:, :, col:col + 1].broadcast_to([128, NCH, 128]), op=ALU.mult), [hk, lk[s_], gk], [hk])
            act(lambda e: e.activation(out=v3(vaug, NCH, 130)[:, :, 0:128], in_=v3(v_in, NCH, 128), func=AF.Copy), [lk[s_]], [hk])
            pool(lambda e: e.memset(v3(vaug, NCH, 130)[:, :, 128:130], 1.0), [], [hk])
            n = 0
            for (src, dst) in ((qq, qT), (kc0, kT0), (kc1, kT1)):
                for g4 in range(0, NCH, 4):
                    cnt4 = min(4, NCH - g4)
                    pi = 6 + n % 2; n += 1
                    psb = PS[pi][:].bitcast(BF16)
                    for j in range(cnt4):
                        c = g4 + j
                        pe(lambda e: e.transpose(out=psb[:, j * 128:(j + 1) * 128], in_=src[:, c * 128:(c + 1) * 128], identity=ident_b[:]), [hk, lk[s_], ctok], [PST[pi]])
                    act(lambda e: e.activation(out=dst[:, g4 * 128:(g4 + cnt4) * 128], in_=psb[:, 0:cnt4 * 128], func=AF.Copy), [PST[pi]], [hk])
            dve(lambda e: e.memset(Oacc, 0.0), [], okc)
            for d in range(2):
                pool(lambda e: e.memset(Cbf[d], 0.0), [], [ck[d]])
            sn_ = 0
            for step in range(NCH):
                for d in range(2):
                    c = ORD[d][step]
                    col = d * 12 + hh
                    kT = kT0 if d == 0 else kT1; kcd = kc0 if d == 0 else kc1
                    msk = ufwd_f if d == 0 else ubwd_f
                    cs_ = slice(c * 128, (c + 1) * 128)
                    pS, pO, pC = d, 2 + d, 4 + d
                    pe(lambda e: e.matmul(PS[pS][:, 0:128], lhsT=kT[:, cs_], rhs=qT[:, cs_], start=True, stop=True), [hk], [PST[pS]])
                    si = sn_ % 4; sn_ += 1
                    dve(lambda e: e.tensor_tensor(out=sTm[si], in0=PS[pS][:, 0:128], in1=msk[:], op=ALU.mult), [PST[pS], ctok], [sTtok[si]])
                    va = v3(vaug, NCH, 130)[:, c, :]
                    pe(lambda e: e.matmul(PS[pO][:, 0:130], lhsT=sTm[si], rhs=va, start=True, stop=False), [sTtok[si], hk], [PST[pO]])
                    pe(lambda e: e.matmul(PS[pO][:, 0:130], lhsT=qT[:, cs_], rhs=Cbf[d], start=False, stop=True), [hk, ck[d]], [PST[pO]])
                    rf = v24(rowfac)[:, c, col:col + 1]
                    oslice = Oacc[:, c * 128:(c + 1) * 128]
                    if is_ml:
                        t1_ = tiny[:, d * 4:d * 4 + 1]; t2_ = tiny[:, d * 4 + 1:d * 4 + 2]
                        act(lambda e: e.activation(out=t1_, in_=PS[pO][:, 128:129], func=AF.Abs, scale=rf), [PST[pO], gk], [tk[d]])
                        dve(lambda e: e.tensor_scalar(out=t1_, in0=t1_, scalar1=1.0, scalar2=None, op0=ALU.max), [tk[d]], [tk[d]])
                        dve(lambda e: e.reciprocal(out=t1_, in_=t1_), [tk[d]], [tk[d]])
                        dve(lambda e: e.tensor_tensor(out=t2_, in0=t1_, in1=rf, op=ALU.mult), [tk[d], gk], [tk[d]])
                        rr = t2_
                    else:
                        rr = rf
                    dve(lambda e: e.scalar_tensor_tensor(out=oslice, in0=PS[pO][:, 0:128], scalar=rr, in1=oslice, op0=ALU.mult, op1=ALU.add), [PST[pO], tk[d], gk, okc[c]], [okc[c]])
                    if step < NCH - 1:
                        pe(lambda e: e.matmul(PS[pC][:, 0:130], lhsT=kcd[:, cs_], rhs=va, start=True, stop=True), [hk], [PST[pC]])
                        cd = v24(cdec)[:, c, col:col + 1]
                        if step == 0:
                            dve(lambda e: e.tensor_copy(out=Cst[d], in_=PS[pC][:, 0:130]), [PST[pC], ck[d]], [ck[d]])
                        else:
                            cprev = v24(cdec)[:, ORD[d][step - 1], col:col + 1]
                            dve(lambda e: e.scalar_tensor_tensor(out=Cst[d], in0=Cst[d], scalar=cprev, in1=PS[pC][:, 0:130], op0=ALU.mult, op1=ALU.add), [PST[pC], ck[d], gk], [ck[d]])
                        act(lambda e: e.activation(out=Cbf[d], in_=Cst[d], func=AF.Copy, scale=cd), [ck[d], gk], [ck[d]])
            dump("g_O%d" % hh, Oacc, okc)
            O3 = v3(Oacc, NCH, 128)
            mean = st[:, 0:18]; ssq = st[:, 18:36]
            if not is_ml:
                dve(lambda e: e.tensor_reduce(out=mean, in_=O3, axis=AX.X, op=ALU.add), okc, [fk])
                dve(lambda e: e.scalar_tensor_tensor(out=O3, in0=mean[:, :, None].broadcast_to([128, NCH, 128]), scalar=-1.0 / 128.0, in1=O3, op0=ALU.mult, op1=ALU.add), [fk] + okc, okc)
            act(lambda e: e.activation(out=F1, in_=Oacc, func=AF.Square), okc, [fk])
            dve(lambda e: e.tensor_reduce(out=ssq, in_=v3(F1, NCH, 128), axis=AX.X, op=ALU.add), [fk], [fk])
            var_ = st[:, 36:54]; sd_ = st[:, 54:72]; rstd_ = st[:, 72:90]
            dve(lambda e: e.tensor_scalar(out=var_, in0=ssq, scalar1=1.0 / 128.0, scalar2=EPS, op0=ALU.mult, op1=ALU.add), [fk], [fk])
            act(lambda e: e.activation(out=sd_, in_=var_, func=AF.Sqrt), [fk], [fk])
            dve(lambda e: e.reciprocal(out=rstd_, in_=sd_), [fk], [fk])
            act(lambda e: e.activation(out=F2, in_=g_in, func=(AF.Sigmoid if is_ml else AF.Silu)), [lk[s_]], [fk])
            pool(lambda e: e.tensor_tensor(out=v3(F2, NCH, 128), in0=v3(F2, NCH, 128), in1=wns[s_][:, None, :].broadcast_to([128, NCH, 128]), op=ALU.mult), [fk, lk[s_]], [fk])
            dve(lambda e: e.tensor_tensor(out=v3(F3, NCH, 128), in0=O3, in1=rstd_[:, :, None].broadcast_to([128, NCH, 128]), op=ALU.mult), [fk] + okc, [fk])
            dve(lambda e: e.tensor_tensor(out=ybf, in0=F3, in1=F2, op=ALU.mult), [fk], [yk])
            yc0 = (1280 if is_ml else 512) + h * 128
            B.dma("sp", ymix[:, yc0:yc0 + 128].rearrange("(c p) n -> p c n", p=128), v3(ybf, NCH, 128), reads=[yk], writes=[ymix_tok])

        g_loads(0)
        for hh in range(12):
            if hh + 1 < 12:
                g_loads(hh + 1)
            g_head(hh)
        B.barrier()


    gates_d = dscr("gates_d", [128, NCH * 24])
    IN_BLOCKS = [(cb * 512, 512) for cb in range(13)] + [(6656, 24)]
    OUT_BLOCKS = [(cb * 512, 512) for cb in range(4)]
    try:
        setup()
        phase_mod()
        mark("mod")
        for l in range(DEPTH):
            last = (l == DEPTH - 1)
            src = xh if l == 0 else resB
            stok = None if l == 0 else res_tok["B"]
            prep_norm_mod(l)
            phase_norm_T(src, 0, range(NCH))
            B.barrier()
            proj_tok(w_in[l], IN_BLOCKS, range(NCH), ep_inproj)
            B.barrier()
            if "gates_d" in debug and l == 0:
                B.dma("sp", gates_d, gates_sb[:].rearrange("p a b -> p (a b)"), reads=[gates_tok], writes=[Tok()])
            mark("inproj%d" % l)
            phase_s5(l)
            mark("s5%d" % l)
            phase_gla(l)
            B.barrier()
            mark("mix%d" % l)
            chunks = range(2, NCH) if last else range(NCH)
            load_gate(l, 2)
            phase_plain_T(ymix, chunks)
            B.barrier()
            proj_tok(w_out[l], OUT_BLOCKS, chunks, make_ep_resid(src, stok, resA, res_tok["A"]))
            B.barrier()
            mark("outproj%d" % l)
            phase_norm_T(resA, 1, chunks)
            B.barrier()
            phase_ffn1(w_ff1[l], chunks)
            B.barrier()
            mark("ffn1%d" % l)
            load_gate(l, 5)
            phase_ffn2(w_ff2[l], chunks, resA, res_tok["A"], resB, res_tok["B"])
            B.barrier()
            mark("layer%d" % l)
        phase_final(resB, res_tok["B"])
    except _Stop:
        pass
    B.barrier()
    print("instr counts", {e: len(B.ops[e]) for e in B.engs}, flush=True)
    B.replay()
    return nc, es, dbg


def _consts():
    idx = np.arange(128)
    c = {}
    c["k_ident"] = np.eye(128, dtype=np.float32)
    c["k_ufwd"] = (idx[:, None] <= idx[None, :]).astype(np.float32)
    c["k_ubwd"] = (idx[:, None] >= idx[None, :]).astype(np.float32)
    c["k_self"] = np.zeros((128, 128), np.float32); c["k_self"][127, :] = 1.0
    c["k_selb"] = np.zeros((128, 128), np.float32); c["k_selb"][0, :] = 1.0
    t = np.arange(SEQ)
    rows = (t // 64).astype(np.float32); cols = (t % 64).astype(np.float32)
    inv = (10000.0 ** (-np.arange(32, dtype=np.float32) / 32.0)).astype(np.float32)
    ang = np.concatenate([rows[:, None] * inv, cols[:, None] * inv], axis=-1).astype(np.float32)
    cos = np.ones((T, 64), np.float32); sin = np.zeros((T, 64), np.float32)
    cos[CTX:] = np.cos(ang); sin[CTX:] = np.sin(ang)
    c["k_cos"] = np.ascontiguousarray(cos.reshape(NCH, 128, 64).transpose(1, 0, 2))
    c["k_sin"] = np.ascontiguousarray(sin.reshape(NCH, 128, 64).transpose(1, 0, 2))
    c["k_mask8"] = (idx[:, None] // 16 == np.arange(8)[None, :]).astype(np.float32)
    c["k_tpos"] = np.stack([idx + 1.0, 128.0 - idx], axis=1).astype(np.float32)
    return c


_PROG = {}


def kernel(**inputs):
    if "p" not in _PROG:
        _PROG["p"] = build_program()
    nc, es, _ = _PROG["p"]
    f32 = lambda a: np.ascontiguousarray(np.asarray(a, dtype=np.float32))
    x = f32(inputs["x"]); ctx = f32(inputs["ctx"]); c = f32(inputs["c"]); c_ctx = f32(inputs["c_ctx"])
    shared = {k: f32(inputs[k]) for k in ("w_mod", "b_mod", "norm1_w", "norm2_w", "w_in", "w_out", "s5_lam_re", "s5_lam_im", "s5_log_step",
                                          "s5_b_re", "s5_b_im", "s5_c_re", "s5_c_im", "s5_d", "s5_w_glu", "s5_b_glu", "ret_decay_logit",
                                          "ret_norm_w", "mlstm_igate_b", "mlstm_fgate_b", "mlstm_norm_w", "w_ff1", "w_ff2", "norm_f_w")}
    shared.update(_consts())
    in_maps = []
    for core in range(8):
        b = core % NB
        m = dict(shared)
        m["xh"] = np.ascontiguousarray(np.concatenate([ctx[b], x[b]], axis=0))
        m["cc"] = np.ascontiguousarray(np.stack([c[b], c_ctx], axis=0))
        in_maps.append(m)
    res = run_bass_kernel_spmd(nc, in_maps, core_ids=list(range(8)))
    outs = [np.asarray(res.results[b]["out"], dtype=np.float32) for b in range(NB)]
    return np.stack(outs, axis=0)
```

```python
import math
from contextlib import ExitStack
import numpy as np
import ml_dtypes
import concourse.bass as bass
import concourse.mybir as mybir
from concourse.bass_utils import run_bass_kernel_spmd

F32 = mybir.dt.float32
BF16 = mybir.dt.bfloat16
I32 = mybir.dt.int32
AF = mybir.ActivationFunctionType
ALU = mybir.AluOpType
AX = mybir.AxisListType

D = 2048
NB = 4
SEQ = 2048
CTX = 256
T = SEQ + CTX
NCH = T // 128
DEPTH = 2
INW = 6680
DFF = 8192
EPS = 1e-6
NDMASEM = 12
STRICT = True


class Tok:
    __slots__ = ("w", "r")

    def __init__(self):
        self.w = None
        self.r = []


class _Rec:
    def __init__(self):
        self.call = None

    def __getattr__(self, name):
        def f(*a, **k):
            self.call = (name, a, k)
            return self
        return f


class Builder:
    def __init__(self, nc, es):
        self.nc = nc
        self.es = es
        self.engs = ["pe", "dve", "act", "pool", "sp"]
        self.ops = {e: [] for e in self.engs}
        self.seq = {e: 0 for e in self.engs}
        self.waited = {e: {} for e in self.engs}
        self.sems = {}
        for e in ["pe", "dve", "act", "pool"]:
            self.sems[("e", e)] = es.enter_context(nc.semaphore("p_" + e))
        self.dcount = {}
        self.drr = {"sp": 0, "pool": 0, "act": 0}
        for q in ["sp", "pool", "act"]:
            for i in range(NDMASEM):
                self.sems[("d", q, i)] = es.enter_context(nc.semaphore("d_%s%d" % (q, i)))
                self.dcount[("d", q, i)] = 0
        self.final = []

    def _need(self, eng, deps):
        out = []
        for (k, v, raw) in deps:
            if k == ("e", eng):
                if eng == "pe":
                    continue
                if eng != "pool" and not (STRICT and raw):
                    continue
            if self.waited[eng].get(k, 0) >= v:
                continue
            self.waited[eng][k] = v
            out.append((k, v))
        return out

    def _deps(self, reads, writes):
        deps = []
        for t in reads:
            if t.w is not None:
                deps.append((t.w[0], t.w[1], True))
        for t in writes:
            if t.w is not None:
                deps.append((t.w[0], t.w[1], False))
            deps.extend((r[0], r[1], False) for r in t.r)
        return deps

    def op(self, eng, fn, reads=(), writes=()):
        deps = self._deps(reads, writes)
        waits = self._need(eng, deps)
        self.seq[eng] += 1
        done = (("e", eng), self.seq[eng])
        rec = _Rec()
        fn(rec)
        call = rec.call
        fn = lambda e, call=call: getattr(e, call[0])(*call[1], **call[2])
        self.ops[eng].append((waits, fn, done))
        for t in reads:
            t.r.append(done)
        for t in writes:
            t.w = done
            t.r = []
        return done

    def dma(self, q, out, in_, reads=(), writes=(), **kw):
        i = self.drr[q]
        self.drr[q] = (i + 1) % NDMASEM
        k = ("d", q, i)
        deps = self._deps(reads, writes)
        if self.dcount[k] > 0:
            deps.append((k, self.dcount[k], False))
        waits = self._need(q, deps)
        self.dcount[k] += 16
        done = (k, self.dcount[k])
        fn = lambda e, out=out, in_=in_, kw=kw: e.dma_start(out=out, in_=in_, **kw)
        self.ops[q].append((waits, fn, done))
        for t in reads:
            t.r.append(done)
        for t in writes:
            t.w = done
            t.r = []
        return done

    def barrier(self):
        allk = [(("e", e), self.seq[e]) for e in ["pe", "dve", "act", "pool"] if self.seq[e] > 0]
        allk += [(k, v) for k, v in self.dcount.items() if v > 0]
        for e in self.engs:
            waits = self._need(e, [(k, v, True) for (k, v) in allk])
            if waits:
                self.ops[e].append((waits, None, None))

    def check_deadlock(self):
        vals = {k: 0 for k in self.sems}
        pos = {e: 0 for e in self.engs}
        progress = True
        while progress:
            progress = False
            for e in self.engs:
                ops = self.ops[e]
                while pos[e] < len(ops):
                    waits, fn, done = ops[pos[e]]
                    if any(vals[k] < v for (k, v) in waits):
                        break
                    if done is not None:
                        vals[done[0]] += 1 if done[0][0] == "e" else 16
                        assert vals[done[0]] == done[1], (e, pos[e], done, vals[done[0]])
                    pos[e] += 1
                    progress = True
        stuck = {e: (pos[e], len(self.ops[e])) for e in self.engs if pos[e] < len(self.ops[e])}
        if stuck:
            for e in stuck:
                waits, fn, done = self.ops[e][pos[e]]
                print("DEADLOCK", e, pos[e], [(k, v, vals[k]) for (k, v) in waits], flush=True)
            raise RuntimeError("deadlock in semaphore protocol: %s" % stuck)

    def replay(self):
        self.check_deadlock()
        nc = self.nc
        block = self.es.enter_context(nc.Block())
        hw = {"pe": block.tensor, "dve": block.vector, "act": block.scalar, "pool": block.gpsimd, "sp": block.sync}
        for e in self.engs:
            ops = self.ops[e]

            def body(eng, ops=ops):
                for (waits, fn, done) in ops:
                    for (k, v) in waits:
                        eng.wait_ge(self.sems[k], v)
                    if fn is None:
                        continue
                    inst = fn(eng)
                    if done[0][0] == "e":
                        inst.then_inc(self.sems[done[0]], 1)
                    else:
                        inst.then_inc(self.sems[done[0]], 16)

            hw[e](body)


class _Stop(Exception):
    pass


def build_program(debug=(), upto=None):
    nc = bass.Bass("TRN2", target_bir_lowering=False)
    es = ExitStack()
    B = Builder(nc, es)
    dbg = {}

    def din(name, shape, dt=F32):
        return nc.dram_tensor(name, list(shape), dt, kind="ExternalInput").ap()

    def dscr(name, shape, dt=F32):
        kind = "ExternalOutput" if name in debug else "Internal"
        ap = nc.dram_tensor(name, list(shape), dt, kind=kind).ap()
        if name in debug:
            dbg[name] = ap
        return ap

    def mark(name):
        if upto == name:
            raise _Stop()

    def dump(name, ap, toks):
        if name not in debug or name in dbg:
            return
        t = nc.dram_tensor(name, list(ap.shape), ap.dtype, kind="ExternalOutput").ap()
        dbg[name] = t
        B.dma("sp", t, ap, reads=toks, writes=[Tok()])

    def sb(name, shape, dt=F32):
        return es.enter_context(nc.sbuf_tensor(name, list(shape), dt))

    xh = din("xh", [T, D])
    cc = din("cc", [2, D])
    w_mod = din("w_mod", [DEPTH, D, 6 * D])
    b_mod = din("b_mod", [DEPTH, 6 * D])
    norm1_w = din("norm1_w", [DEPTH, D])
    norm2_w = din("norm2_w", [DEPTH, D])
    w_in = din("w_in", [DEPTH, D, INW])
    w_out = din("w_out", [DEPTH, D, D])
    lam_re = din("s5_lam_re", [DEPTH, 2, 32, 64])
    lam_im = din("s5_lam_im", [DEPTH, 2, 32, 64])
    log_step = din("s5_log_step", [DEPTH, 2, 32])
    s5b_re = din("s5_b_re", [DEPTH, 2, 32, 64, 16])
    s5b_im = din("s5_b_im", [DEPTH, 2, 32, 64, 16])
    s5c_re = din("s5_c_re", [DEPTH, 2, 32, 16, 64])
    s5c_im = din("s5_c_im", [DEPTH, 2, 32, 16, 64])
    s5_d = din("s5_d", [DEPTH, 512])
    w_glu = din("s5_w_glu", [DEPTH, 512, 1024])
    b_glu = din("s5_b_glu", [DEPTH, 1024])
    ret_logit = din("ret_decay_logit", [DEPTH, 2, 6])
    ret_nw = din("ret_norm_w", [DEPTH, 768])
    ig_b = din("mlstm_igate_b", [DEPTH, 2, 6])
    fg_b = din("mlstm_fgate_b", [DEPTH, 2, 6])
    ml_nw = din("mlstm_norm_w", [DEPTH, 768])
    w_ff1 = din("w_ff1", [DEPTH, D, DFF])
    w_ff2 = din("w_ff2", [DEPTH, DFF, D])
    norm_f = din("norm_f_w", [D])
    k_ident = din("k_ident", [128, 128])
    k_ufwd = din("k_ufwd", [128, 128])
    k_ubwd = din("k_ubwd", [128, 128])
    k_self = din("k_self", [128, 128])
    k_selb = din("k_selb", [128, 128])
    k_cos = din("k_cos", [128, NCH, 64])
    k_sin = din("k_sin", [128, NCH, 64])
    k_mask8 = din("k_mask8", [128, 8])
    k_tpos = din("k_tpos", [128, 2])
    out = nc.dram_tensor("out", [SEQ, D], F32, kind="ExternalOutput").ap()

    modd = dscr("modd", [DEPTH, 2, 6 * D])
    a16 = dscr("a16", [T, 6656], BF16)
    ymix = dscr("ymix", [T, D], BF16)
    resA = dscr("resA", [T, D])
    resB = dscr("resB", [T, D])
    h1d = dscr("h1d", [NCH, 128, 64, 128], BF16)

    PS = [es.enter_context(nc.psum_tensor("ps%d" % i, [128, 512], F32)) for i in range(8)]
    PST = [Tok() for _ in range(8)]

    ident_f = sb("ident_f", [128, 128]); ident_b = sb("ident_b", [128, 128], BF16)
    nident_b = sb("nident_b", [128, 128], BF16)
    ufwd_f = sb("ufwd_f", [128, 128]); ubwd_f = sb("ubwd_f", [128, 128])
    ufwd_b = sb("ufwd_b", [128, 128], BF16); ubwd_b = sb("ubwd_b", [128, 128], BF16)
    nufwd_b = sb("nufwd_b", [128, 128], BF16); nubwd_b = sb("nubwd_b", [128, 128], BF16)
    self_b = sb("self_b", [128, 128], BF16); selb_b = sb("selb_b", [128, 128], BF16)
    nself_b = sb("nself_b", [128, 128], BF16); nselb_b = sb("nselb_b", [128, 128], BF16)
    ones_f = sb("ones_f", [128, 128])
    mask8 = sb("mask8", [128, 8]); tpos = sb("tpos", [128, 2])
    cos_t = sb("cos_t", [128, NCH, 64]); sin_t = sb("sin_t", [128, NCH, 64])
    ctok = Tok()
    actT = sb("actT", [128, 16, T], BF16)
    actT_tok = [Tok() for _ in range(NCH)]
    arenaB = sb("arenaB", [128, 20480])
    arenaB_bf = arenaB[:].bitcast(BF16)
    gates_sb = sb("gates_sb", [128, NCH, 24]); gates_tok = Tok()
    small = sb("small", [128, 256]);

    def setup():
        stg = arenaB
        loads = [(k_ident, 0), (k_ufwd, 128), (k_ubwd, 256), (k_self, 384), (k_selb, 512)]
        for ap, o in loads:
            B.dma("sp", stg[:, o:o + 128], ap, writes=[ctok])
        B.dma("sp", mask8[:], k_mask8, writes=[ctok])
        B.dma("sp", tpos[:], k_tpos, writes=[ctok])
        B.dma("sp", cos_t[:], k_cos, writes=[ctok])
        B.dma("sp", sin_t[:], k_sin, writes=[ctok])
        cp = lambda o, i: B.op("dve", lambda e, o=o, i=i: e.tensor_copy(out=o, in_=i), reads=[ctok], writes=[ctok])
        ng = lambda o, i: B.op("dve", lambda e, o=o, i=i: e.tensor_scalar(out=o, in0=i, scalar1=-1.0, scalar2=None, op0=ALU.mult), reads=[ctok], writes=[ctok])
        cp(ident_f[:], stg[:, 0:128]); cp(ident_b[:], stg[:, 0:128]); ng(nident_b[:], stg[:, 0:128])
        cp(ufwd_f[:], stg[:, 128:256]); cp(ufwd_b[:], stg[:, 128:256]); ng(nufwd_b[:], stg[:, 128:256])
        cp(ubwd_f[:], stg[:, 256:384]); cp(ubwd_b[:], stg[:, 256:384]); ng(nubwd_b[:], stg[:, 256:384])
        cp(self_b[:], stg[:, 384:512]); ng(nself_b[:], stg[:, 384:512])
        cp(selb_b[:], stg[:, 512:640]); ng(nselb_b[:], stg[:, 512:640])
        B.op("dve", lambda e: e.memset(ones_f[:], 1.0), writes=[ctok])
        B.barrier()

    def phase_mod():
        cT = sb("cT", [128, 16, 2]); sT = sb("sT", [128, 16, 2]); t_c = Tok()
        for s in range(2):
            B.dma("sp", cT[:, :, s], cc[s].rearrange("(c p) -> p c", p=128), writes=[t_c], allow_slow_non_contiguous=True)
        B.op("act", lambda e: e.activation(out=sT[:], in_=cT[:], func=AF.Silu), reads=[t_c], writes=[t_c])
        NWS = 8
        wst = [arenaB[:, i * 2048:(i + 1) * 2048] for i in range(NWS)]
        wtok = [Tok() for _ in range(NWS)]
        bst = [arenaB[0:2, 18432 + i * 512: 18432 + (i + 1) * 512] for i in range(2)]
        btok = [Tok() for _ in range(2)]
        ost = [arenaB[0:2, 19456 + i * 512: 19456 + (i + 1) * 512] for i in range(2)]
        otok = [Tok() for _ in range(2)]
        n = 0
        for l in range(DEPTH):
            for cb in range(24):
                bi = cb % 2
                B.dma("sp", bst[bi], b_mod[l, cb * 512:(cb + 1) * 512].partition_broadcast(2), writes=[btok[bi]])
                pt = PST[cb % 2]; ps = PS[cb % 2]
                for kg in range(4):
                    wi = n % NWS; n += 1
                    src = w_mod[l, kg * 512:(kg + 1) * 512, cb * 512:(cb + 1) * 512].rearrange("(k p) n -> p k n", p=128)
                    B.dma("act" if kg % 2 else "sp", wst[wi].rearrange("p (k n) -> p k n", k=4), src, writes=[wtok[wi]])
                    for k4 in range(4):
                        kc = kg * 4 + k4
                        B.op("pe", lambda e, ps=ps, kc=kc, wi=wi, k4=k4: e.matmul(ps[0:2, :], lhsT=sT[:, kc, :], rhs=wst[wi][:, k4 * 512:(k4 + 1) * 512], start=(kc == 0), stop=(kc == 15)),
                             reads=[t_c, wtok[wi]], writes=[pt])
                B.op("dve", lambda e, ps=ps, bi=bi: e.tensor_tensor(out=ost[bi], in0=ps[0:2, :], in1=bst[bi], op=ALU.add), reads=[pt, btok[bi]], writes=[otok[bi]])
                B.dma("sp", modd[l, :, cb * 512:(cb + 1) * 512], ost[bi], reads=[otok[bi]], writes=[modtok])
        B.barrier()

    modtok = Tok()

    def load_vec16(dst, src, tok):
        B.dma("sp", dst, src.rearrange("(c p) -> p c", p=128), reads=[modtok], writes=[tok], allow_slow_non_contiguous=True)

    gsh = sb("gsh", [128, 2, 2, 2, 16]); gsh_tok = Tok()
    gate_bc = sb("gate_bc", [128, 2, D]); gate_tok = Tok()

    def prep_norm_mod(l):
        nw = small[:, 0:32].rearrange("p (a c) -> p a c", a=2); ntok = Tok()
        load_vec16(nw[:, 0, :], norm1_w[l], ntok); load_vec16(nw[:, 1, :], norm2_w[l], ntok)
        tmp = small[:, 32:160].rearrange("p (a s g c) -> p a s g c", a=2, s=2, g=2)
        for a in range(2):
            for s in range(2):
                load_vec16(tmp[:, a, s, 1, :], modd[l, s, (3 * a) * D:(3 * a + 1) * D], ntok)
                load_vec16(tmp[:, a, s, 0, :], modd[l, s, (3 * a + 1) * D:(3 * a + 2) * D], ntok)
        for a in range(2):
            for s in range(2):
                B.op("dve", lambda e, a=a, s=s: e.scalar_tensor_tensor(out=gsh[:, a, s, 0, :], in0=tmp[:, a, s, 0, :], scalar=1.0, in1=nw[:, a, :], op0=ALU.add, op1=ALU.mult),
                     reads=[ntok], writes=[gsh_tok])
                B.op("dve", lambda e, a=a, s=s: e.tensor_copy(out=gsh[:, a, s, 1, :], in_=tmp[:, a, s, 1, :]), reads=[ntok], writes=[gsh_tok])

    def load_gate(l, which):
        for s in range(2):
            B.dma("sp", gate_bc[:, s, :], modd[l, s, which * D:(which + 1) * D].partition_broadcast(128), reads=[modtok], writes=[gate_tok])

    def phase_norm_T(src, a_idx, chunks):
        xin = [arenaB[:, i * 2048:(i + 1) * 2048] for i in range(2)]; xtok = [Tok() for _ in range(2)]
        xs = [arenaB_bf[:, 8192 + i * 2048: 8192 + (i + 1) * 2048] for i in range(2)]; xstok = [Tok() for _ in range(2)]
        junk = arenaB_bf[:, 12288:14336]; jtok = Tok()
        st = small[:, 160:176]; sttok = [Tok() for _ in range(2)]
        for n, tc in enumerate(chunks):
            bi = n % 2
            s = 1 if tc < 2 else 0
            B.dma("sp", xin[bi], src[tc * 128:(tc + 1) * 128, :], writes=[xtok[bi]])
            ss = st[:, bi * 4:bi * 4 + 1]; rs = st[:, bi * 4 + 1:bi * 4 + 2]
            B.op("act", lambda e, bi=bi, ss=ss: e.activation(out=junk, in_=xin[bi], func=AF.Square, accum_out=ss), reads=[xtok[bi]], writes=[jtok, sttok[bi]])
            B.op("dve", lambda e, ss=ss, rs=rs: e.tensor_scalar(out=rs, in0=ss, scalar1=1.0 / D, scalar2=EPS, op0=ALU.mult, op1=ALU.add), reads=[sttok[bi]], writes=[sttok[bi]])
            B.op("act", lambda e, rs=rs: e.activation(out=rs, in_=rs, func=AF.Sqrt), reads=[sttok[bi]], writes=[sttok[bi]])
            B.op("dve", lambda e, rs=rs: e.reciprocal(out=rs, in_=rs), reads=[sttok[bi]], writes=[sttok[bi]])
            B.op("act", lambda e, bi=bi, rs=rs: e.activation(out=xs[bi], in_=xin[bi], func=AF.Copy, scale=rs), reads=[xtok[bi], sttok[bi]], writes=[xstok[bi]])
            for q in range(4):
                pi = 4 + (n * 4 + q) % 4
                psb = PS[pi][:].bitcast(BF16)
                for j in range(4):
                    kc = q * 4 + j
                    B.op("pe", lambda e, psb=psb, j=j, kc=kc, bi=bi: e.transpose(out=psb[:, j * 128:(j + 1) * 128], in_=xs[bi][:, kc * 128:(kc + 1) * 128], identity=ident_b[:]),
                         reads=[xstok[bi], ctok], writes=[PST[pi]])
                for j in range(4):
                    kc = q * 4 + j
                    B.op("dve", lambda e, psb=psb, j=j, kc=kc, tc=tc, s=s: e.tensor_scalar(out=actT[:, kc, tc * 128:(tc + 1) * 128], in0=psb[:, j * 128:(j + 1) * 128],
                                                                                 scalar1=gsh[:, a_idx, s, 0, kc:kc + 1], scalar2=gsh[:, a_idx, s, 1, kc:kc + 1], op0=ALU.mult, op1=ALU.add),
                         reads=[PST[pi], gsh_tok], writes=[actT_tok[tc]])

    def phase_plain_T(src, chunks):
        xs = [arenaB_bf[:, i * 2048:(i + 1) * 2048] for i in range(2)]; xstok = [Tok() for _ in range(2)]
        for n, tc in enumerate(chunks):
            bi = n % 2
            B.dma("sp", xs[bi], src[tc * 128:(tc + 1) * 128, :], writes=[xstok[bi]])
            for q in range(4):
                pi = 4 + (n * 4 + q) % 4
                psb = PS[pi][:].bitcast(BF16)
                for j in range(4):
                    kc = q * 4 + j
                    B.op("pe", lambda e, psb=psb, j=j, kc=kc, bi=bi: e.transpose(out=psb[:, j * 128:(j + 1) * 128], in_=xs[bi][:, kc * 128:(kc + 1) * 128], identity=ident_b[:]),
                         reads=[xstok[bi], ctok], writes=[PST[pi]])
                B.op("act", lambda e, psb=psb, q=q, tc=tc: e.activation(out=actT[:, q * 4:(q + 1) * 4, tc * 128:(tc + 1) * 128], in_=psb[:, 0:512].rearrange("p (j t) -> p j t", j=4), func=AF.Copy),
                     reads=[PST[pi]], writes=[actT_tok[tc]])

    WB_OFF = 16384
    wstage = [arenaB[:, 4096 + i * 2048: 4096 + (i + 1) * 2048] for i in range(2)]; wstok = [Tok() for _ in range(2)]
    wcnt = [0]

    def load_wblock(wsrc, c0, w, KC, dst, dtok, stage=None, stok_=None, queues=("act", "sp")):
        stage = stage or wstage; stok_ = stok_ or wstok
        g = max(1, 2048 // w)
        for k0 in range(0, KC, g):
            kk = min(g, KC - k0)
            si = wcnt[0] % 2; wcnt[0] += 1
            src = wsrc[k0 * 128:(k0 + kk) * 128, c0:c0 + w].rearrange("(k p) n -> p k n", p=128)
            B.dma(queues[si % len(queues)], stage[si][:, 0:kk * w].rearrange("p (k n) -> p k n", k=kk), src, writes=[stok_[si]])
            B.op("pool", lambda e, si=si, k0=k0, kk=kk: e.tensor_copy(out=dst[:, k0 * w:(k0 + kk) * w], in_=stage[si][:, 0:kk * w]), reads=[stok_[si]], writes=[dtok])

    wbuf = [arenaB_bf[:, 16384 + i * 8192: 16384 + (i + 1) * 8192] for i in range(2)]; wbtok = [Tok() for _ in range(2)]
    ostg_f = [arenaB[:, 16384 + i * 512: 16384 + (i + 1) * 512] for i in range(4)]; ostok = [Tok() for _ in range(4)]
    ostg2_f = [arenaB[:, 18432 + i * 512: 18432 + (i + 1) * 512] for i in range(4)]; os2tok = [Tok() for _ in range(4)]
    cnt = {"ps": 0, "o": 0}

    def proj_tok(wsrc, col_blocks, chunks, epilogue):
        load_wblock(wsrc, col_blocks[0][0], col_blocks[0][1], 16, wbuf[0], wbtok[0])
        for cbi, (c0, w) in enumerate(col_blocks):
            bi = cbi % 2
            if cbi + 1 < len(col_blocks):
                load_wblock(wsrc, col_blocks[cbi + 1][0], col_blocks[cbi + 1][1], 16, wbuf[1 - bi], wbtok[1 - bi])
            for tc in chunks:
                pi = cnt["ps"] % 4; cnt["ps"] += 1
                for kc in range(16):
                    B.op("pe", lambda e, pi=pi, kc=kc, tc=tc, bi=bi, w=w: e.matmul(PS[pi][:, 0:w], lhsT=actT[:, kc, tc * 128:(tc + 1) * 128], rhs=wbuf[bi][:, kc * w:(kc + 1) * w], start=(kc == 0), stop=(kc == 15)),
                         reads=[actT_tok[tc], wbtok[bi]], writes=[PST[pi]])
                epilogue(tc, c0, w, pi)

    def ep_inproj(tc, c0, w, pi):
        if c0 >= 6656:
            B.op("act", lambda e: e.activation(out=gates_sb[:, tc, :], in_=PS[pi][:, 0:24], func=AF.Copy), reads=[PST[pi]], writes=[gates_tok])
            return
        oi = cnt["o"] % 4; cnt["o"] += 1
        ob = ostg_f[oi].bitcast(BF16)[:, 0:512]
        B.op("act", lambda e: e.activation(out=ob, in_=PS[pi][:, 0:512], func=AF.Copy), reads=[PST[pi]], writes=[ostok[oi]])
        B.dma("sp", a16[tc * 128:(tc + 1) * 128, c0:c0 + 512], ob, reads=[ostok[oi]], writes=[a16_tok])

    a16_tok = Tok(); ymix_tok = Tok(); res_tok = {"A": Tok(), "B": Tok()}; h1_tok = Tok()

    def make_ep_resid(rsrc, rsrc_tok, rdst, rdst_tok):
        def ep(tc, c0, w, pi):
            s = 1 if tc < 2 else 0
            oi = cnt["o"] % 4; cnt["o"] += 1
            xo = ostg2_f[oi][:, 0:w]; tm = ostg_f[oi][:, 0:w]
            B.dma("act", xo, rsrc[tc * 128:(tc + 1) * 128, c0:c0 + w], reads=[rsrc_tok] if rsrc_tok else [], writes=[os2tok[oi]])
            B.op("dve", lambda e: e.tensor_tensor(out=tm, in0=PS[pi][:, 0:w], in1=gate_bc[:, s, c0:c0 + w], op=ALU.mult), reads=[PST[pi], gate_tok], writes=[ostok[oi]])
            B.op("pool", lambda e: e.tensor_tensor(out=tm, in0=tm, in1=xo, op=ALU.add), reads=[ostok[oi], os2tok[oi]], writes=[ostok[oi]])
            B.dma("sp", rdst[tc * 128:(tc + 1) * 128, c0:c0 + w], tm, reads=[ostok[oi]], writes=[rdst_tok])
        return ep

    def phase_ffn1(wsrc, chunks):
        blocks = []
        cl = list(chunks)
        for i in range(0, len(cl), 4):
            blocks.append(cl[i:i + 4])
        load_wblock(wsrc, 0, 512, 16, wbuf[0], wbtok[0])
        for cb in range(16):
            bi = cb % 2
            if cb + 1 < 16:
                load_wblock(wsrc, (cb + 1) * 512, 512, 16, wbuf[1 - bi], wbtok[1 - bi])
            for fs in range(4):
                kcf = cb * 4 + fs
                for blk in blocks:
                    t0 = blk[0] * 128; N = len(blk) * 128
                    pi = cnt["ps"] % 4; cnt["ps"] += 1
                    for kc in range(16):
                        B.op("pe", lambda e, pi=pi, kc=kc, bi=bi, fs=fs, t0=t0, N=N: e.matmul(PS[pi][:, 0:N], lhsT=wbuf[bi][:, kc * 512 + fs * 128: kc * 512 + (fs + 1) * 128], rhs=actT[:, kc, t0:t0 + N], start=(kc == 0), stop=(kc == 15)),
                             reads=[actT_tok[t] for t in blk] + [wbtok[bi]], writes=[PST[pi]])
                    oi = cnt["o"] % 4; cnt["o"] += 1
                    r = ostg2_f[oi][:, 0:N]; hb = ostg_f[oi].bitcast(BF16)[:, 0:N]
                    B.op("act", lambda e, pi=pi, r=r, N=N: e.activation(out=r, in_=PS[pi][:, 0:N], func=AF.Relu), reads=[PST[pi]], writes=[os2tok[oi]])
                    B.op("dve", lambda e, r=r, hb=hb: e.tensor_tensor(out=hb, in0=r, in1=r, op=ALU.mult), reads=[os2tok[oi]], writes=[ostok[oi]])
                    B.dma("sp", h1d[blk[0]:blk[0] + len(blk), :, kcf, :].rearrange("t p j -> p t j"), hb.rearrange("p (t j) -> p t j", j=128), reads=[ostok[oi]], writes=[h1_tok])

    def phase_ffn2(wsrc, chunks, rsrc, rsrc_tok, rdst, rdst_tok):
        actv = actT[:].rearrange("p a b -> p (a b)")
        wb2 = [actv[:, i * 16384:(i + 1) * 16384] for i in range(2)]; wb2tok = [Tok() for _ in range(2)]
        acc = arenaB[:, 0:9216]; acctok = [Tok() for _ in range(NCH)]
        stg = [arenaB[:, 9216 + i * 2048: 9216 + (i + 1) * 2048] for i in range(2)]; stgtok = [Tok() for _ in range(2)]
        hst = [arenaB_bf[:, 26624 + i * 4096: 26624 + (i + 1) * 4096] for i in range(2)] + [actv[:, 32768:36864]]; hstok = [Tok() for _ in range(3)]
        o1 = [arenaB[:, 17408 + i * 512: 17408 + (i + 1) * 512] for i in range(2)]; o1tok = [Tok() for _ in range(2)]
        o2 = [arenaB[:, 18432 + i * 512: 18432 + (i + 1) * 512] for i in range(2)]; o2tok = [Tok() for _ in range(2)]
        units = [(cb, half) for cb in range(4) for half in range(2)]
        def ld(u):
            cb, half = units[u]
            load_wblock(wsrc[half * 4096:(half + 1) * 4096], cb * 512, 512, 32, wb2[u % 2], wb2tok[u % 2], stage=stg, stok_=stgtok, queues=("act",))
        ld(0)
        cl = list(chunks)
        seq = [(u, cb, half, tc) for u, (cb, half) in enumerate(units) for tc in cl]

        def issue_h(i):
            u, cb, half, tc = seq[i]
            hi = i % 3
            B.dma("sp", hst[hi], h1d[tc][:, half * 32:(half + 1) * 32, :].rearrange("p k j -> p (k j)"), reads=[h1_tok], writes=[hstok[hi]])

        issue_h(0)
        if len(seq) > 1:
            issue_h(1)
        on = 0
        for i, (u, cb, half, tc) in enumerate(seq):
            bi = u % 2
            if tc == cl[0] and u + 1 < len(units):
                ld(u + 1)
            if i + 2 < len(seq):
                issue_h(i + 2)
            hi = i % 3
            c0 = cb * 512
            if half == 1:
                oi = on % 2; on += 1
                B.dma("sp", o2[oi], rsrc[tc * 128:(tc + 1) * 128, c0:c0 + 512], reads=[rsrc_tok] if rsrc_tok else [], writes=[o2tok[oi]])
            pi = cnt["ps"] % 4; cnt["ps"] += 1
            for kc in range(32):
                B.op("pe", lambda e, pi=pi, kc=kc, hi=hi, bi=bi: e.matmul(PS[pi][:], lhsT=hst[hi][:, kc * 128:(kc + 1) * 128], rhs=wb2[bi][:, kc * 512:(kc + 1) * 512], start=(kc == 0), stop=(kc == 31)),
                     reads=[hstok[hi], wb2tok[bi]], writes=[PST[pi]])
            asl = acc[:, tc * 512:(tc + 1) * 512]
            if half == 0:
                B.op("dve", lambda e, pi=pi, asl=asl: e.tensor_copy(out=asl, in_=PS[pi][:]), reads=[PST[pi]], writes=[acctok[tc]])
            else:
                s_ = 1 if tc < 2 else 0
                B.op("dve", lambda e, pi=pi, asl=asl, oi=oi: e.tensor_tensor(out=o1[oi], in0=PS[pi][:], in1=asl, op=ALU.add), reads=[PST[pi], acctok[tc]], writes=[o1tok[oi]])
                B.op("dve", lambda e, oi=oi, s_=s_, c0=c0: e.tensor_tensor(out=o1[oi], in0=o1[oi], in1=gate_bc[:, s_, c0:c0 + 512], op=ALU.mult), reads=[o1tok[oi], gate_tok], writes=[o1tok[oi]])
                B.op("pool", lambda e, oi=oi: e.tensor_tensor(out=o1[oi], in0=o1[oi], in1=o2[oi], op=ALU.add), reads=[o1tok[oi], o2tok[oi]], writes=[o1tok[oi]])
                B.dma("sp", rdst[tc * 128:(tc + 1) * 128, c0:c0 + 512], o1[oi], reads=[o1tok[oi]], writes=[rdst_tok])

    def phase_final(src, src_tok):
        nf = gate_bc[:, 0, :]
        B.dma("sp", nf, norm_f.partition_broadcast(128), writes=[gate_tok])
        xin = [arenaB[:, i * 2048:(i + 1) * 2048] for i in range(2)]; xtok = [Tok() for _ in range(2)]
        yo = [arenaB[:, 4096 + i * 2048: 4096 + (i + 1) * 2048] for i in range(2)]; ytok = [Tok() for _ in range(2)]
        junk = arenaB_bf[:, 16384:18432]; jtok = Tok()
        st = small[:, 160:176]; sttok = [Tok() for _ in range(2)]
        outtok = Tok()
        for n, tc in enumerate(range(2, NCH)):
            bi = n % 2
            B.dma("sp", xin[bi], src[tc * 128:(tc + 1) * 128, :], reads=[src_tok], writes=[xtok[bi]])
            ss = st[:, bi * 4:bi * 4 + 1]; rs = st[:, bi * 4 + 1:bi * 4 + 2]
            B.op("act", lambda e, bi=bi, ss=ss: e.activation(out=junk, in_=xin[bi], func=AF.Square, accum_out=ss), reads=[xtok[bi]], writes=[jtok, sttok[bi]])
            B.op("dve", lambda e, ss=ss, rs=rs: e.tensor_scalar(out=rs, in0=ss, scalar1=1.0 / D, scalar2=EPS, op0=ALU.mult, op1=ALU.add), reads=[sttok[bi]], writes=[sttok[bi]])
            B.op("act", lambda e, rs=rs: e.activation(out=rs, in_=rs, func=AF.Sqrt), reads=[sttok[bi]], writes=[sttok[bi]])
            B.op("dve", lambda e, rs=rs: e.reciprocal(out=rs, in_=rs), reads=[sttok[bi]], writes=[sttok[bi]])
            B.op("act", lambda e, bi=bi, rs=rs: e.activation(out=yo[bi], in_=xin[bi], func=AF.Copy, scale=rs), reads=[xtok[bi], sttok[bi]], writes=[ytok[bi]])
            B.op("dve", lambda e, bi=bi: e.tensor_tensor(out=yo[bi], in0=yo[bi], in1=nf, op=ALU.mult), reads=[ytok[bi], gate_tok], writes=[ytok[bi]])
            d = B.dma("sp", out[(tc - 2) * 128:(tc - 1) * 128, :], yo[bi], reads=[ytok[bi]], writes=[outtok])

    A32 = actT[:].rearrange("p a b -> p (a b)").bitcast(F32)
    A16 = actT[:].rearrange("p a b -> p (a b)")
    B32 = arenaB; B16 = arenaB_bf
    ORD = [list(range(NCH)), [1, 0] + list(range(NCH - 1, 1, -1))]
    TWO_PI = 2.0 * math.pi

    def v3(ap, a, b):
        return ap.rearrange("p (a b) -> p a b", a=a, b=b)

    def dve(fn, reads, writes):
        return B.op("dve", fn, reads=reads, writes=writes)

    def act(fn, reads, writes):
        return B.op("act", fn, reads=reads, writes=writes)

    def pool(fn, reads, writes):
        return B.op("pool", fn, reads=reads, writes=writes)

    def pe(fn, reads, writes):
        return B.op("pe", fn, reads=reads, writes=writes)

    def range_reduce_sincos(ph, kint, sinv, cosv, tk, shape_note=None):
        dve(lambda e: e.tensor_copy(out=kint, in_=ph), [tk], [tk])
        dve(lambda e: e.tensor_copy(out=cosv, in_=kint), [tk], [tk])
        dve(lambda e: e.tensor_tensor(out=ph, in0=ph, in1=cosv, op=ALU.subtract), [tk], [tk])
        act(lambda e: e.activation(out=sinv, in_=ph, func=AF.Sin, scale=TWO_PI), [tk], [tk])
        dve(lambda e: e.tensor_scalar(out=ph, in0=ph, scalar1=0.25, scalar2=None, op0=ALU.add), [tk], [tk])
        dve(lambda e: e.tensor_scalar(out=cosv, in0=ph, scalar1=0.5, scalar2=None, op0=ALU.is_gt), [tk], [tk])
        dve(lambda e: e.tensor_tensor(out=ph, in0=ph, in1=cosv, op=ALU.subtract), [tk], [tk])
        act(lambda e: e.activation(out=cosv, in_=ph, func=AF.Sin, scale=TWO_PI), [tk], [tk])

    def phase_s5(l):
        yacc = A32[:, 0:9216]; ytok = [[Tok() for _ in range(4)] for _ in range(NCH)]
        uT = A16[:, 18432:27648]; uTtok = Tok()
        Bblk = A16[:, 27648:31744]; Btok = Tok()
        Cmat = A16[:, 31744:35840]; Ctok = Tok()
        misc = A32[:, 17920:18432]
        tab = [B32[:, i * 2048:(i + 1) * 2048] for i in range(4)]; tabtok = Tok()
        Pp = [[B16[:, 16384 + (blk * 4 + k) * 512: 16384 + (blk * 4 + k + 1) * 512] for k in range(4)] for blk in range(4)]
        Ptk = [[Tok() for _ in range(4)] for _ in range(4)]
        Zp = [[B16[:, 24576 + (i * 4 + k) * 512: 24576 + (i * 4 + k + 1) * 512] for k in range(4)] for i in range(2)]
        Ztok = [Tok() for _ in range(2)]
        xTt = [B16[:, 28672 + i * 128: 28672 + (i + 1) * 128] for i in range(8)]; xTtok = [Tok() for _ in range(8)]
        u_sb = B16[:, 29696:38912]; utok = Tok()
        dsk = B32[:, 8192:8704]; dtok = Tok()
        B.dma("sp", v3(u_sb, NCH, 512), a16[:, 0:512].rearrange("(c p) n -> p c n", p=128), reads=[a16_tok], writes=[utok])
        B.dma("sp", dsk, s5_d[l].partition_broadcast(128), writes=[dtok])
        for c in range(NCH):
            dve(lambda e, c=c: e.tensor_tensor(out=yacc[:, c * 512:(c + 1) * 512], in0=u_sb[:, c * 512:(c + 1) * 512], in1=dsk, op=ALU.mult), [utok, dtok], ytok[c])
            pi = 6 + c % 2
            psb = PS[pi][:].bitcast(BF16)
            for blk in range(4):
                pe(lambda e, psb=psb, blk=blk, c=c: e.transpose(out=psb[:, blk * 128:(blk + 1) * 128], in_=u_sb[:, c * 512 + blk * 128: c * 512 + (blk + 1) * 128], identity=ident_b[:]), [utok, ctok], [PST[pi]])
            act(lambda e, psb=psb, c=c: e.activation(out=v3(uT, 4, T)[:, :, c * 128:(c + 1) * 128], in_=v3(psb[:, 0:512], 4, 128), func=AF.Copy), [PST[pi]], [uTtok])
        B.barrier()
        for d in range(2):
            S = [B32[:, 8192 + i * 2048: 8192 + (i + 1) * 2048] for i in range(5)]
            stok = Tok()
            lrd, th, ph, sinv, cosv = S
            kint = B32[:, 18432:20480].bitcast(I32)
            dtb = misc[:, 0:32]
            ntp = misc[:, 32:33]
            tp = tpos[:, d:d + 1]
            B.dma("sp", lrd, lam_re[l, d].rearrange("g p -> (g p)").partition_broadcast(128), writes=[stok])
            B.dma("sp", th, lam_im[l, d].rearrange("g p -> (g p)").partition_broadcast(128), writes=[stok])
            B.dma("sp", dtb, log_step[l, d].partition_broadcast(128), writes=[stok])
            act(lambda e: e.activation(out=dtb, in_=dtb, func=AF.Exp), [stok], [stok])
            dve(lambda e: e.tensor_scalar(out=ntp, in0=tp, scalar1=-1.0, scalar2=None, op0=ALU.mult), [ctok, stok], [stok])
            dtb3 = dtb[:, :, None].broadcast_to([128, 32, 64])
            dve(lambda e: e.tensor_scalar(out=lrd, in0=lrd, scalar1=-1e-4, scalar2=None, op0=ALU.min), [stok], [stok])
            dve(lambda e: e.tensor_tensor(out=v3(lrd, 32, 64), in0=v3(lrd, 32, 64), in1=dtb3, op=ALU.mult), [stok], [stok])
            dve(lambda e: e.tensor_tensor(out=v3(th, 32, 64), in0=v3(th, 32, 64), in1=dtb3, op=ALU.mult), [stok], [stok])
            dve(lambda e: e.tensor_scalar(out=ph, in0=th, scalar1=tp, scalar2=1.0 / TWO_PI, op0=ALU.mult, op1=ALU.mult), [stok, ctok], [stok])
            range_reduce_sincos(ph, kint, sinv, cosv, stok)
            act(lambda e: e.activation(out=th, in_=lrd, func=AF.Exp, scale=tp), [stok, ctok], [stok])
            dve(lambda e: e.reciprocal(out=ph, in_=th), [stok], [stok])
            dve(lambda e: e.tensor_tensor(out=tab[2], in0=th, in1=cosv, op=ALU.mult), [stok], [tabtok])
            dve(lambda e: e.tensor_tensor(out=tab[3], in0=th, in1=sinv, op=ALU.mult), [stok], [tabtok])
            dve(lambda e: e.tensor_tensor(out=tab[0], in0=ph, in1=cosv, op=ALU.mult), [stok], [tabtok])
            dve(lambda e: e.scalar_tensor_tensor(out=tab[1], in0=ph, scalar=-1.0, in1=sinv, op0=ALU.mult, op1=ALU.mult), [stok], [tabtok])
            B.barrier()
            bre = B32[0:64, 8192:8704]; bim = B32[0:64, 8704:9216]; bbr = B32[0:64, 9216:9728]; bbi = B32[0:64, 9728:10240]
            t1 = B32[0:64, 10240:10752]; t2 = B32[0:64, 10752:11264]
            sm = [B32[0:64, 11264 + i * 32: 11264 + (i + 1) * 32] for i in range(12)]
            smi = B32[0:64, 11776:11808].bitcast(I32)
            btk = Tok()
            B.dma("sp", v3(bre, 32, 16), s5b_re[l, d].rearrange("g p n -> p g n"), writes=[btk])
            B.dma("sp", v3(bim, 32, 16), s5b_im[l, d].rearrange("g p n -> p g n"), writes=[btk])
            lr, li, dt2, mag, phs, sn, cs, are, aim, rden, cr, ci = sm
            B.dma("sp", lr, lam_re[l, d].rearrange("g p -> p g"), writes=[btk], allow_slow_non_contiguous=True)
            B.dma("sp", li, lam_im[l, d].rearrange("g p -> p g"), writes=[btk], allow_slow_non_contiguous=True)
            B.dma("sp", dt2, log_step[l, d].partition_broadcast(64), writes=[btk])
            act(lambda e: e.activation(out=dt2, in_=dt2, func=AF.Exp), [btk], [btk])
            dve(lambda e: e.tensor_scalar(out=lr, in0=lr, scalar1=-1e-4, scalar2=None, op0=ALU.min), [btk], [btk])
            dve(lambda e: e.tensor_tensor(out=mag, in0=lr, in1=dt2, op=ALU.mult), [btk], [btk])
            act(lambda e: e.activation(out=mag, in_=mag, func=AF.Exp), [btk], [btk])
            dve(lambda e: e.scalar_tensor_tensor(out=phs, in0=li, scalar=1.0 / TWO_PI, in1=dt2, op0=ALU.mult, op1=ALU.mult), [btk], [btk])
            range_reduce_sincos(phs, smi, sn, cs, btk)
            dve(lambda e: e.tensor_tensor(out=are, in0=mag, in1=cs, op=ALU.mult), [btk], [btk])
            dve(lambda e: e.tensor_scalar(out=are, in0=are, scalar1=-1.0, scalar2=None, op0=ALU.add), [btk], [btk])
            dve(lambda e: e.tensor_tensor(out=aim, in0=mag, in1=sn, op=ALU.mult), [btk], [btk])
            dve(lambda e: e.tensor_tensor(out=rden, in0=lr, in1=lr, op=ALU.mult), [btk], [btk])
            dve(lambda e: e.tensor_tensor(out=cr, in0=li, in1=li, op=ALU.mult), [btk], [btk])
            dve(lambda e: e.tensor_tensor(out=rden, in0=rden, in1=cr, op=ALU.add), [btk], [btk])
            dve(lambda e: e.reciprocal(out=rden, in_=rden), [btk], [btk])
            dve(lambda e: e.tensor_tensor(out=cr, in0=lr, in1=rden, op=ALU.mult), [btk], [btk])
            dve(lambda e: e.scalar_tensor_tensor(out=ci, in0=li, scalar=-1.0, in1=rden, op0=ALU.mult, op1=ALU.mult), [btk], [btk])
            dve(lambda e: e.tensor_tensor(out=mag, in0=are, in1=cr, op=ALU.mult), [btk], [btk])
            dve(lambda e: e.tensor_tensor(out=sn, in0=aim, in1=ci, op=ALU.mult), [btk], [btk])
            dve(lambda e: e.tensor_tensor(out=mag, in0=mag, in1=sn, op=ALU.subtract), [btk], [btk])
            dve(lambda e: e.tensor_tensor(out=phs, in0=are, in1=ci, op=ALU.mult), [btk], [btk])
            dve(lambda e: e.tensor_tensor(out=sn, in0=aim, in1=cr, op=ALU.mult), [btk], [btk])
            dve(lambda e: e.tensor_tensor(out=phs, in0=phs, in1=sn, op=ALU.add), [btk], [btk])
            nr3 = mag[:, :, None].broadcast_to([64, 32, 16]); ni3 = phs[:, :, None].broadcast_to([64, 32, 16])
            dve(lambda e: e.tensor_tensor(out=v3(t1, 32, 16), in0=v3(bre, 32, 16), in1=nr3, op=ALU.mult), [btk], [btk])
            dve(lambda e: e.tensor_tensor(out=v3(t2, 32, 16), in0=v3(bim, 32, 16), in1=ni3, op=ALU.mult), [btk], [btk])
            dve(lambda e: e.tensor_tensor(out=bbr, in0=t1, in1=t2, op=ALU.subtract), [btk], [btk])
            dve(lambda e: e.tensor_tensor(out=v3(t1, 32, 16), in0=v3(bim, 32, 16), in1=nr3, op=ALU.mult), [btk], [btk])
            dve(lambda e: e.tensor_tensor(out=v3(t2, 32, 16), in0=v3(bre, 32, 16), in1=ni3, op=ALU.mult), [btk], [btk])
            dve(lambda e: e.tensor_tensor(out=bbi, in0=t1, in1=t2, op=ALU.add), [btk], [btk])
            for nm_, ap_ in (("lr", lr), ("li", li), ("dt2", dt2), ("cs", cs), ("are", are), ("aim", aim), ("rden", rden), ("cr", cr), ("ci", ci)):
                dump("s_%s%d" % (nm_, d), ap_, [btk])
            dump("s_bbr%d" % d, bbr, [btk]); dump("s_bbi%d" % d, bbi, [btk]); dump("s_nr%d" % d, mag, [btk]); dump("s_ni%d" % d, phs, [btk])
            m8 = mask8[:, :, None].broadcast_to([128, 8, 64])
            n = 0
            for blk in range(4):
                for ri, bb in enumerate((bbr, bbi)):
                    pi = 6 + n % 2; n += 1
                    pe(lambda e, pi=pi, bb=bb, blk=blk: e.transpose(out=PS[pi][:, 0:64], in_=bb[:, blk * 128:(blk + 1) * 128], identity=ident_f[0:64, 0:64]), [btk, ctok], [PST[pi]])
                    dst = v3(Bblk, 4, 1024)[:, blk, ri * 512:(ri + 1) * 512].rearrange("p (g q) -> p g q", g=8)
                    dve(lambda e, pi=pi, dst=dst: e.tensor_tensor(out=dst, in0=PS[pi][:, None, 0:64].broadcast_to([128, 8, 64]), in1=m8, op=ALU.mult), [PST[pi], ctok], [Btok])
            cnat = [B32[:, 12288 + i * 64: 12288 + (i + 1) * 64] for i in range(8)]
            cntk = Tok()
            for ri, csrc in enumerate((s5c_re, s5c_im)):
                for blk in range(4):
                    B.dma("sp", cnat[ri * 4 + blk], csrc[l, d, blk * 8:(blk + 1) * 8].rearrange("g n p -> (g n) p"), writes=[cntk])
            xm = [B16[:, 26624 + i * 128: 26624 + (i + 1) * 128] for i in range(4)]; xmtok = [Tok() for _ in range(4)]
            n = 0
            for blk in range(4):
                for q in range(4):
                    for ri in range(2):
                        xi = n % 4; pi = 6 + n % 2; n += 1
                        for g2 in range(2):
                            mk = mask8[:, 2 * q + g2: 2 * q + g2 + 1]
                            dve(lambda e, xi=xi, g2=g2, ri=ri, blk=blk, mk=mk: e.tensor_scalar(out=xm[xi][:, g2 * 64:(g2 + 1) * 64], in0=cnat[ri * 4 + blk], scalar1=mk, scalar2=(-1.0 if ri else 1.0), op0=ALU.mult, op1=ALU.mult),
                                [cntk, ctok], [xmtok[xi]])
                        psb = PS[pi][:].bitcast(BF16)
                        pe(lambda e, psb=psb, xi=xi: e.transpose(out=psb[:, 0:128], in_=xm[xi], identity=ident_b[:]), [xmtok[xi], ctok], [PST[pi]])
                        ci_ = (blk * 4 + q) * 2 + ri
                        act(lambda e, psb=psb, ci_=ci_: e.activation(out=Cmat[:, ci_ * 128:(ci_ + 1) * 128], in_=psb[:, 0:128], func=AF.Copy), [PST[pi]], [Ctok])
            B.barrier()
            for i_ in range(4):
                dump("s_tab%d_%d" % (i_, d), tab[i_], [tabtok])
            dump("s_Bblk%d" % d, Bblk, [Btok])
            dump("s_Cmat%d" % d, Cmat, [Ctok])
            mark("s5c%d" % d)
            U = ufwd_b if d == 0 else ubwd_b; NU = nufwd_b if d == 0 else nubwd_b
            SEL = self_b if d == 0 else selb_b; NSEL = nself_b if d == 0 else nselb_b
            xn = 0
            X6 = [Tok() for _ in range(4)]; Y7 = [Tok() for _ in range(4)]
            Zt = [[Tok() for _ in range(4)] for _ in range(2)]
            for step, c in enumerate(ORD[d]):
                for blk in range(4):
                    zi = blk % 2
                    pr, pim = (0, 1) if zi == 0 else (2, 3)
                    cols = slice(blk * 512, (blk + 1) * 512)
                    lhs_u = v3(uT, 4, T)[:, blk, c * 128:(c + 1) * 128]
                    pe(lambda e, pr=pr, lhs_u=lhs_u, blk=blk: e.matmul(PS[pr][:], lhsT=lhs_u, rhs=v3(Bblk, 4, 1024)[:, blk, 0:512], start=True, stop=True), [uTtok, Btok], [PST[pr]])
                    pe(lambda e, pim=pim, lhs_u=lhs_u, blk=blk: e.matmul(PS[pim][:], lhsT=lhs_u, rhs=v3(Bblk, 4, 1024)[:, blk, 512:1024], start=True, stop=True), [uTtok, Btok], [PST[pim]])
                    Z = Zp[zi]
                    dve(lambda e, Z=Z, pr=pr, cols=cols: e.tensor_tensor(out=Z[0], in0=PS[pr][:], in1=tab[0][:, cols], op=ALU.mult), [PST[pr], tabtok], [Zt[zi][0]])
                    dve(lambda e, Z=Z, pim=pim, cols=cols: e.tensor_tensor(out=Z[1], in0=PS[pim][:], in1=tab[1][:, cols], op=ALU.mult), [PST[pim], tabtok], [Zt[zi][1]])
                    dve(lambda e, Z=Z, pim=pim, cols=cols: e.tensor_tensor(out=Z[2], in0=PS[pim][:], in1=tab[0][:, cols], op=ALU.mult), [PST[pim], tabtok], [Zt[zi][2]])
                    dve(lambda e, Z=Z, pr=pr, cols=cols: e.tensor_tensor(out=Z[3], in0=PS[pr][:], in1=tab[1][:, cols], op=ALU.mult), [PST[pr], tabtok], [Zt[zi][3]])
                    P = Pp[blk]
                    first = (step == 0)
                    pe(lambda e, Z=Z: e.matmul(PS[4][:], lhsT=U[:], rhs=Z[0], start=True, stop=False), [Zt[zi][0], ctok], [PST[4]])
                    pe(lambda e, Z=Z, first=first: e.matmul(PS[4][:], lhsT=NU[:], rhs=Z[1], start=False, stop=first), [Zt[zi][1], ctok], [PST[4]])
                    if not first:
                        pe(lambda e, P=P: e.matmul(PS[4][:], lhsT=SEL[:], rhs=P[0], start=False, stop=False), [Ptk[blk][0], ctok], [PST[4]])
                        pe(lambda e, P=P: e.matmul(PS[4][:], lhsT=NSEL[:], rhs=P[1], start=False, stop=True), [Ptk[blk][1], ctok], [PST[4]])
                    pe(lambda e, Z=Z: e.matmul(PS[5][:], lhsT=U[:], rhs=Z[2], start=True, stop=False), [Zt[zi][2], ctok], [PST[5]])
                    pe(lambda e, Z=Z, first=first: e.matmul(PS[5][:], lhsT=U[:], rhs=Z[3], start=False, stop=first), [Zt[zi][3], ctok], [PST[5]])
                    if not first:
                        pe(lambda e, P=P: e.matmul(PS[5][:], lhsT=SEL[:], rhs=P[2], start=False, stop=False), [Ptk[blk][2], ctok], [PST[5]])
                        pe(lambda e, P=P: e.matmul(PS[5][:], lhsT=SEL[:], rhs=P[3], start=False, stop=True), [Ptk[blk][3], ctok], [PST[5]])
                    dve(lambda e, P=P, cols=cols: e.tensor_tensor(out=P[0], in0=PS[4][:], in1=tab[2][:, cols], op=ALU.mult), [PST[4], tabtok], [Ptk[blk][0]])
                    dve(lambda e, P=P, cols=cols: e.tensor_tensor(out=P[1], in0=PS[5][:], in1=tab[3][:, cols], op=ALU.mult), [PST[5], tabtok], [Ptk[blk][1]])
                    dve(lambda e, P=P, cols=cols: e.tensor_tensor(out=P[2], in0=PS[5][:], in1=tab[2][:, cols], op=ALU.mult), [PST[5], tabtok], [Ptk[blk][2]])
                    dve(lambda e, P=P, cols=cols: e.tensor_tensor(out=P[3], in0=PS[4][:], in1=tab[3][:, cols], op=ALU.mult), [PST[4], tabtok], [Ptk[blk][3]])
                    for q in range(4):
                        qs = slice(q * 128, (q + 1) * 128)
                        xs_ = []
                        for ri in range(2):
                            xi = xn % 8; xn += 1
                            pslot = PS[6][:, (xi % 4) * 128:((xi % 4) + 1) * 128]
                            a0, a1 = (P[0], P[1]) if ri == 0 else (P[2], P[3])
                            idn = nident_b if ri == 0 else ident_b
                            pe(lambda e, pslot=pslot, a0=a0, qs=qs: e.matmul(pslot, lhsT=a0[:, qs], rhs=ident_b[:], start=True, stop=False), Ptk[blk] + [ctok], [X6[xi % 4]])
                            pe(lambda e, pslot=pslot, a1=a1, qs=qs, idn=idn: e.matmul(pslot, lhsT=a1[:, qs], rhs=idn[:], start=False, stop=True), Ptk[blk] + [ctok], [X6[xi % 4]])
                            act(lambda e, pslot=pslot, xi=xi: e.activation(out=xTt[xi], in_=pslot, func=AF.Copy), [X6[xi % 4]], [xTtok[xi]])
                            xs_.append(xi)
                        yslot = PS[7][:, (blk % 4) * 128:((blk % 4) + 1) * 128]
                        for ri in range(2):
                            ci_ = (blk * 4 + q) * 2 + ri
                            xi = xs_[ri]
                            pe(lambda e, yslot=yslot, xi=xi, ci_=ci_, q=q, ri=ri: e.matmul(yslot, lhsT=xTt[xi], rhs=Cmat[:, ci_ * 128:(ci_ + 1) * 128], start=(q == 0 and ri == 0), stop=(q == 3 and ri == 1)),
                               [xTtok[xi], Ctok], [Y7[blk]])
                    ysl = yacc[:, c * 512 + blk * 128: c * 512 + (blk + 1) * 128]
                    dve(lambda e, ysl=ysl, yslot=yslot: e.tensor_tensor(out=ysl, in0=yslot, in1=ysl, op=ALU.add), [Y7[blk], ytok[c][blk]], [ytok[c][blk]])
            B.barrier()
        mark("s5d")
        dump("s_yacc", yacc, [t_ for r_ in ytok for t_ in r_])
        mark("s5d")
        gyT = uT; gtok = Tok()
        wg = B16[:, 0:4096]; wgtok = Tok()
        bgl = B32[:, 2048:3072]; bgtok = Tok()
        B.dma("sp", bgl, b_glu[l].partition_broadcast(128), writes=[bgtok])
        load_wblock(w_glu[l], 0, 1024, 4, wg, wgtok)
        gs = [B32[:, 8192 + i * 512: 8192 + (i + 1) * 512] for i in range(4)]; gstok = [Tok() for _ in range(2)]
        gb = [B16[:, 24576 + i * 512: 24576 + (i + 1) * 512] for i in range(2)]; gbtok = [Tok() for _ in range(2)]
        GC = 2.0 * math.sqrt(2.0 / math.pi)
        for c in range(NCH):
            bi = c % 2
            y = yacc[:, c * 512:(c + 1) * 512]; t = gs[bi * 2]; sg = gs[bi * 2 + 1]
            dve(lambda e, y=y, t=t: e.tensor_tensor(out=t, in0=y, in1=y, op=ALU.mult), ytok[c], [gstok[bi]])
            dve(lambda e, t=t: e.tensor_scalar(out=t, in0=t, scalar1=0.044715, scalar2=1.0, op0=ALU.mult, op1=ALU.add), [gstok[bi]], [gstok[bi]])
            dve(lambda e, y=y, t=t: e.tensor_tensor(out=t, in0=t, in1=y, op=ALU.mult), [gstok[bi]] + ytok[c], [gstok[bi]])
            act(lambda e, t=t, sg=sg: e.activation(out=sg, in_=t, func=AF.Sigmoid, scale=GC), [gstok[bi]], [gstok[bi]])
            dve(lambda e, y=y, sg=sg, bi=bi: e.tensor_tensor(out=gb[bi], in0=y, in1=sg, op=ALU.mult), [gstok[bi]] + ytok[c], [gbtok[bi]])
            pi = 6 + c % 2
            psb = PS[pi][:].bitcast(BF16)
            for kc in range(4):
                pe(lambda e, psb=psb, kc=kc, bi=bi: e.transpose(out=psb[:, kc * 128:(kc + 1) * 128], in_=gb[bi][:, kc * 128:(kc + 1) * 128], identity=ident_b[:]), [gbtok[bi], ctok], [PST[pi]])
            act(lambda e, psb=psb, c=c: e.activation(out=v3(gyT, 4, T)[:, :, c * 128:(c + 1) * 128], in_=v3(psb[:, 0:512], 4, 128), func=AF.Copy), [PST[pi]], [gtok])
        zs = [B32[:, 10240 + i * 512: 10240 + (i + 1) * 512] for i in range(4)]; zstok = [Tok() for _ in range(2)]
        zo = [B16[:, 25600 + i * 512: 25600 + (i + 1) * 512] for i in range(2)]; zotok = [Tok() for _ in range(2)]
        for c in range(NCH):
            bi = c % 2
            for half in range(2):
                pi = half + 2 * bi
                for kc in range(4):
                    pe(lambda e, pi=pi, kc=kc, c=c, half=half: e.matmul(PS[pi][:], lhsT=v3(gyT, 4, T)[:, kc, c * 128:(c + 1) * 128], rhs=wg[:, kc * 1024 + half * 512: kc * 1024 + (half + 1) * 512], start=(kc == 0), stop=(kc == 3)),
                       [gtok, wgtok], [PST[pi]])
            va = zs[bi * 2]; gt = zs[bi * 2 + 1]
            dve(lambda e, va=va, bi=bi: e.tensor_tensor(out=va, in0=PS[2 * bi][:], in1=bgl[:, 0:512], op=ALU.add), [PST[2 * bi], bgtok], [zstok[bi]])
            dve(lambda e, gt=gt, bi=bi: e.tensor_tensor(out=gt, in0=PS[2 * bi + 1][:], in1=bgl[:, 512:1024], op=ALU.add), [PST[2 * bi + 1], bgtok], [zstok[bi]])
            act(lambda e, gt=gt: e.activation(out=gt, in_=gt, func=AF.Sigmoid), [zstok[bi]], [zstok[bi]])
            dve(lambda e, va=va, gt=gt, bi=bi: e.tensor_tensor(out=zo[bi], in0=va, in1=gt, op=ALU.mult), [zstok[bi]], [zotok[bi]])
            B.dma("sp", ymix[c * 128:(c + 1) * 128, 0:512], zo[bi], reads=[zotok[bi]], writes=[ymix_tok])
        B.barrier()

    def phase_gla(l):
        gt = [A32[:, i * 432:(i + 1) * 432] for i in range(6)]
        LF, II, Bc, colfac, rowfac, cdec = gt
        biasF = A32[:, 2592:2616]; biasI = A32[:, 2616:2640]
        gk = Tok()
        v24 = lambda ap: v3(ap, NCH, 24)
        dve(lambda e: e.memset(biasI, 0.0), [], [gk])
        dve(lambda e: e.memset(LF, 0.0), [], [gk])
        dve(lambda e: e.memset(II, 0.0), [], [gk])
        B.dma("sp", v3(biasF, 2, 12)[:, :, 0:6], ret_logit[l].partition_broadcast(128), writes=[gk])
        B.dma("sp", v3(biasF, 2, 12)[:, :, 6:12], fg_b[l].partition_broadcast(128), writes=[gk])
        B.dma("sp", v3(biasI, 2, 12)[:, :, 6:12], ig_b[l].partition_broadcast(128), writes=[gk])
        for d in range(2):
            dve(lambda e, d=d: e.tensor_copy(out=v24(LF)[:, :, d * 12 + 6: d * 12 + 12], in_=gates_sb[:, :, d * 12 + 6: d * 12 + 12]), [gates_tok, gk], [gk])
            dve(lambda e, d=d: e.tensor_copy(out=v24(II)[:, :, d * 12 + 6: d * 12 + 12], in_=gates_sb[:, :, d * 12: d * 12 + 6]), [gates_tok, gk], [gk])
        dve(lambda e: e.tensor_tensor(out=v24(LF), in0=v24(LF), in1=biasF[:, None, :].broadcast_to([128, NCH, 24]), op=ALU.add), [gk], [gk])
        dve(lambda e: e.tensor_tensor(out=v24(II), in0=v24(II), in1=biasI[:, None, :].broadcast_to([128, NCH, 24]), op=ALU.add), [gk], [gk])
        act(lambda e: e.activation(out=LF, in_=LF, func=AF.Exp, scale=-1.0), [gk], [gk])
        act(lambda e: e.activation(out=LF, in_=LF, func=AF.Ln, bias=1.0), [gk], [gk])
        dve(lambda e: e.tensor_scalar(out=LF, in0=LF, scalar1=-1.0, scalar2=None, op0=ALU.mult), [gk], [gk])
        for d in range(2):
            Uf = ufwd_f if d == 0 else ubwd_f
            rhs = v24(LF)[:, :, d * 12:(d + 1) * 12]
            pe(lambda e, Uf=Uf, rhs=rhs: e.matmul(PS[0][:, 0:216], lhsT=Uf[:], rhs=rhs, start=True, stop=True), [gk, ctok], [PST[0]])
            pe(lambda e, rhs=rhs: e.matmul(PS[1][:, 0:216], lhsT=ones_f[:], rhs=rhs, start=True, stop=True), [gk, ctok], [PST[1]])
            act(lambda e, d=d: e.activation(out=v24(Bc)[:, :, d * 12:(d + 1) * 12], in_=v3(PS[0][:, 0:216], NCH, 12), func=AF.Copy), [PST[0]], [gk])
            act(lambda e, d=d: e.activation(out=v24(cdec)[:, :, d * 12:(d + 1) * 12], in_=v3(PS[1][:, 0:216], NCH, 12), func=AF.Exp), [PST[1]], [gk])
        act(lambda e: e.activation(out=rowfac, in_=Bc, func=AF.Exp), [gk], [gk])
        dve(lambda e: e.tensor_tensor(out=colfac, in0=II, in1=Bc, op=ALU.subtract), [gk], [gk])
        act(lambda e: e.activation(out=colfac, in_=colfac, func=AF.Exp, bias=float(math.log(128.0 ** -0.5))), [gk], [gk])
        for nm, ap_ in (("g_LF", LF), ("g_II", II), ("g_Bc", Bc), ("g_colfac", colfac), ("g_rowfac", rowfac), ("g_cdec", cdec)):
            dump(nm, ap_, [gk])
        o16 = 5376
        def a16v(i):
            return A16[:, o16 + i * 2304: o16 + (i + 1) * 2304]
        qs, ks, vs, gs_, qr, kr, kc0, kc1, qT, kT0, kT1 = [a16v(i) for i in range(11)]
        vaug = A16[:, 30720:33060]
        sTm = [A16[:, 33060 + i * 128: 33060 + (i + 1) * 128] for i in range(4)]; sTtok = [Tok() for _ in range(4)]
        Cbf = [A16[:, 33572 + i * 130: 33572 + (i + 1) * 130] for i in range(2)]
        Oacc = B32[:, 0:2304]; F1 = B32[:, 2304:4608]; F2 = B32[:, 4608:6912]; F3 = B32[:, 6912:9216]
        ybf = B16[:, 18432:20736]
        Cst = [B32[:, 10368 + i * 130: 10368 + (i + 1) * 130] for i in range(2)]
        wn = B32[:, 10752:10880]
        st = B32[:, 10880:10880 + 128]
        tiny = B32[:, 11008:11008 + 64]
        lsets = [[qs, ks, vs, gs_], [B16[:, 22272 + i * 2304: 22272 + (i + 1) * 2304] for i in range(4)]]
        wns = [wn, B32[:, 15744:15872]]
        rtmp = {"dve": B32[:, 15872:18176], "pool": B32[:, 18176:20480]}
        lk = [Tok(), Tok()]; pk = Tok(); ck = [Tok(), Tok()]; okc = [Tok() for _ in range(NCH)]; tk = [Tok(), Tok()]; fk = Tok(); yk = Tok()
        rk = {"dve": Tok(), "pool": Tok()}

        def g_loads(hh):
            is_ml = hh >= 6; h = hh % 6; s_ = hh % 2
            base = 3584 if is_ml else 512
            for i, off in enumerate((0, 768, 1536, 2304)):
                c0 = base + off + h * 128
                B.dma("sp" if i % 2 else "act", v3(lsets[s_][i], NCH, 128), a16[:, c0:c0 + 128].rearrange("(c p) n -> p c n", p=128), reads=[a16_tok], writes=[lk[s_]])
            B.dma("sp", wns[s_], (ml_nw if is_ml else ret_nw)[l, h * 128:(h + 1) * 128].partition_broadcast(128), writes=[lk[s_]])

        def g_head(hh):
            is_ml = hh >= 6; h = hh % 6; s_ = hh % 2
            q_in, k_in, v_in, g_in = lsets[s_]
            hk = pk
            if not is_ml:
                for (src, dst, eng) in ((q_in, qr, "dve"), (k_in, kr, "pool")):
                    s1 = v3(src, NCH, 128)[:, :, 0:64]; s2 = v3(src, NCH, 128)[:, :, 64:128]
                    d1 = v3(dst, NCH, 128)[:, :, 0:64]; d2 = v3(dst, NCH, 128)[:, :, 64:128]
                    f1 = v3(rtmp[eng][:, 0:1152], NCH, 64); f2 = v3(rtmp[eng][:, 1152:2304], NCH, 64)
                    rke = rk[eng]
                    opf = lambda fn, rd, wr, eng=eng: B.op(eng, fn, reads=rd, writes=wr)
                    opf(lambda e: e.tensor_tensor(out=f1, in0=s1, in1=cos_t[:], op=ALU.mult), [lk[s_], ctok], [rke])
                    opf(lambda e: e.tensor_tensor(out=f2, in0=s2, in1=sin_t[:], op=ALU.mult), [lk[s_], ctok], [rke])
                    opf(lambda e: e.tensor_tensor(out=d1, in0=f1, in1=f2, op=ALU.subtract), [rke], [hk])
                    opf(lambda e: e.tensor_tensor(out=f1, in0=s1, in1=sin_t[:], op=ALU.mult), [lk[s_], ctok], [rke])
                    opf(lambda e: e.tensor_tensor(out=f2, in0=s2, in1=cos_t[:], op=ALU.mult), [lk[s_], ctok], [rke])
                    opf(lambda e: e.tensor_tensor(out=d2, in0=f1, in1=f2, op=ALU.add), [rke], [hk])
                qq, kk = qr, kr
            else:
                qq, kk = q_in, k_in
            for d, kcd in enumerate((kc0, kc1)):
                col = d * 12 + hh
                dve(lambda e: e.tensor_tensor(out=v3(kcd, NCH, 128), in0=v3(kk, NCH, 128), in1=v24(colfac)[:, :, col:col + 1].broadcast_to([128, NCH, 128]), op=ALU.mult), [hk, lk[s_], gk], [hk])
            act(lambda e: e.activation(out=v3(vaug, NCH, 130)[:, :, 0:128], in_=v3(v_in, NCH, 128), func=AF.Copy), [lk[s_]], [hk])
            pool(lambda e: e.memset(v3(vaug, NCH, 130)[:, :, 128:130], 1.0), [], [hk])
            n = 0
            for (src, dst) in ((qq, qT), (kc0, kT0), (kc1, kT1)):
                for g4 in range(0, NCH, 4):
                    cnt4 = min(4, NCH - g4)
                    pi = 6 + n % 2; n += 1
                    psb = PS[pi][:].bitcast(BF16)
                    for j in range(cnt4):
                        c = g4 + j
                        pe(lambda e: e.transpose(out=psb[:, j * 128:(j + 1) * 128], in_=src[:, c * 128:(c + 1) * 128], identity=ident_b[:]), [hk, lk[s_], ctok], [PST[pi]])
                    act(lambda e: e.activation(out=dst[:, g4 * 128:(g4 + cnt4) * 128], in_=psb[:, 0:cnt4 * 128], func=AF.Copy), [PST[pi]], [hk])
            dve(lambda e: e.memset(Oacc, 0.0), [], okc)
            for d in range(2):
                pool(lambda e: e.memset(Cbf[d], 0.0), [], [ck[d]])
            sn_ = 0
            for step in range(NCH):
                for d in range(2):
                    c = ORD[d][step]
                    col = d * 12 + hh
                    kT = kT0 if d == 0 else kT1; kcd = kc0 if d == 0 else kc1
                    msk = ufwd_f if d == 0 else ubwd_f
                    cs_ = slice(c * 128, (c + 1) * 128)
                    pS, pO, pC = d, 2 + d, 4 + d
                    pe(lambda e: e.matmul(PS[pS][:, 0:128], lhsT=kT[:, cs_], rhs=qT[:, cs_], start=True, stop=True), [hk], [PST[pS]])
                    si = sn_ % 4; sn_ += 1
                    dve(lambda e: e.tensor_tensor(out=sTm[si], in0=PS[pS][:, 0:128], in1=msk[:], op=ALU.mult), [PST[pS], ctok], [sTtok[si]])
                    va = v3(vaug, NCH, 130)[:, c, :]
                    pe(lambda e: e.matmul(PS[pO][:, 0:130], lhsT=sTm[si], rhs=va, start=True, stop=False), [sTtok[si], hk], [PST[pO]])
                    pe(lambda e: e.matmul(PS[pO][:, 0:130], lhsT=qT[:, cs_], rhs=Cbf[d], start=False, stop=True), [hk, ck[d]], [PST[pO]])
                    rf = v24(rowfac)[:, c, col:col + 1]
                    oslice = Oacc[:, c * 128:(c + 1) * 128]
                    if is_ml:
                        t1_ = tiny[:, d * 4:d * 4 + 1]; t2_ = tiny[:, d * 4 + 1:d * 4 + 2]
                        act(lambda e: e.activation(out=t1_, in_=PS[pO][:, 128:129], func=AF.Abs, scale=rf), [PST[pO], gk], [tk[d]])
                        dve(lambda e: e.tensor_scalar(out=t1_, in0=t1_, scalar1=1.0, scalar2=None, op0=ALU.max), [tk[d]], [tk[d]])
                        dve(lambda e: e.reciprocal(out=t1_, in_=t1_), [tk[d]], [tk[d]])
                        dve(lambda e: e.tensor_tensor(out=t2_, in0=t1_, in1=rf, op=ALU.mult), [tk[d], gk], [tk[d]])
                        rr = t2_
                    else:
                        rr = rf
                    dve(lambda e: e.scalar_tensor_tensor(out=oslice, in0=PS[pO][:, 0:128], scalar=rr, in1=oslice, op0=ALU.mult, op1=ALU.add), [PST[pO], tk[d], gk, okc[c]], [okc[c]])
                    if step < NCH - 1:
                        pe(lambda e: e.matmul(PS[pC][:, 0:130], lhsT=kcd[:, cs_], rhs=va, start=True, stop=True), [hk], [PST[pC]])
                        cd = v24(cdec)[:, c, col:col + 1]
                        if step == 0:
                            dve(lambda e: e.tensor_copy(out=Cst[d], in_=PS[pC][:, 0:130]), [PST[pC], ck[d]], [ck[d]])
                        else:
                            cprev = v24(cdec)[:, ORD[d][step - 1], col:col + 1]
                            dve(lambda e: e.scalar_tensor_tensor(out=Cst[d], in0=Cst[d], scalar=cprev, in1=PS[pC][:, 0:130], op0=ALU.mult, op1=ALU.add), [PST[pC], ck[d], gk], [ck[d]])
                        act(lambda e: e.activation(out=Cbf[d], in_=Cst[d], func=AF.Copy, scale=cd), [ck[d], gk], [ck[d]])
            dump("g_O%d" % hh, Oacc, okc)
            O3 = v3(Oacc, NCH, 128)
            mean = st[:, 0:18]; ssq = st[:, 18:36]
            if not is_ml:
                dve(lambda e: e.tensor_reduce(out=mean, in_=O3, axis=AX.X, op=ALU.add), okc, [fk])
                dve(lambda e: e.scalar_tensor_tensor(out=O3, in0=mean[:, :, None].broadcast_to([128, NCH, 128]), scalar=-1.0 / 128.0, in1=O3, op0=ALU.mult, op1=ALU.add), [fk] + okc, okc)
            act(lambda e: e.activation(out=F1, in_=Oacc, func=AF.Square), okc, [fk])
            dve(lambda e: e.tensor_reduce(out=ssq, in_=v3(F1, NCH, 128), axis=AX.X, op=ALU.add), [fk], [fk])
            var_ = st[:, 36:54]; sd_ = st[:, 54:72]; rstd_ = st[:, 72:90]
            dve(lambda e: e.tensor_scalar(out=var_, in0=ssq, scalar1=1.0 / 128.0, scalar2=EPS, op0=ALU.mult, op1=ALU.add), [fk], [fk])
            act(lambda e: e.activation(out=sd_, in_=var_, func=AF.Sqrt), [fk], [fk])
            dve(lambda e: e.reciprocal(out=rstd_, in_=sd_), [fk], [fk])
            act(lambda e: e.activation(out=F2, in_=g_in, func=(AF.Sigmoid if is_ml else AF.Silu)), [lk[s_]], [fk])
            pool(lambda e: e.tensor_tensor(out=v3(F2, NCH, 128), in0=v3(F2, NCH, 128), in1=wns[s_][:, None, :].broadcast_to([128, NCH, 128]), op=ALU.mult), [fk, lk[s_]], [fk])
            dve(lambda e: e.tensor_tensor(out=v3(F3, NCH, 128), in0=O3, in1=rstd_[:, :, None].broadcast_to([128, NCH, 128]), op=ALU.mult), [fk] + okc, [fk])
            dve(lambda e: e.tensor_tensor(out=ybf, in0=F3, in1=F2, op=ALU.mult), [fk], [yk])
            yc0 = (1280 if is_ml else 512) + h * 128
            B.dma("sp", ymix[:, yc0:yc0 + 128].rearrange("(c p) n -> p c n", p=128), v3(ybf, NCH, 128), reads=[yk], writes=[ymix_tok])

        g_loads(0)
        for hh in range(12):
            if hh + 1 < 12:
                g_loads(hh + 1)
            g_head(hh)
        B.barrier()


    gates_d = dscr("gates_d", [128, NCH * 24])
    IN_BLOCKS = [(cb * 512, 512) for cb in range(13)] + [(6656, 24)]
    OUT_BLOCKS = [(cb * 512, 512) for cb in range(4)]
    try:
        setup()
        phase_mod()
        mark("mod")
        for l in range(DEPTH):
            last = (l == DEPTH - 1)
            src = xh if l == 0 else resB
            stok = None if l == 0 else res_tok["B"]
            prep_norm_mod(l)
            phase_norm_T(src, 0, range(NCH))
            B.barrier()
            proj_tok(w_in[l], IN_BLOCKS, range(NCH), ep_inproj)
            B.barrier()
            if "gates_d" in debug and l == 0:
                B.dma("sp", gates_d, gates_sb[:].rearrange("p a b -> p (a b)"), reads=[gates_tok], writes=[Tok()])
            mark("inproj%d" % l)
            phase_s5(l)
            mark("s5%d" % l)
            phase_gla(l)
            B.barrier()
            mark("mix%d" % l)
            chunks = range(2, NCH) if last else range(NCH)
            load_gate(l, 2)
            phase_plain_T(ymix, chunks)
            B.barrier()
            proj_tok(w_out[l], OUT_BLOCKS, chunks, make_ep_resid(src, stok, resA, res_tok["A"]))
            B.barrier()
            mark("outproj%d" % l)
            phase_norm_T(resA, 1, chunks)
            B.barrier()
            phase_ffn1(w_ff1[l], chunks)
            B.barrier()
            mark("ffn1%d" % l)
            load_gate(l, 5)
            phase_ffn2(w_ff2[l], chunks, resA, res_tok["A"], resB, res_tok["B"])
            B.barrier()
            mark("layer%d" % l)
        phase_final(resB, res_tok["B"])
    except _Stop:
        pass
    B.barrier()
    print("instr counts", {e: len(B.ops[e]) for e in B.engs}, flush=True)
    B.replay()
    return nc, es, dbg


def _consts():
    idx = np.arange(128)
    c = {}
    c["k_ident"] = np.eye(128, dtype=np.float32)
    c["k_ufwd"] = (idx[:, None] <= idx[None, :]).astype(np.float32)
    c["k_ubwd"] = (idx[:, None] >= idx[None, :]).astype(np.float32)
    c["k_self"] = np.zeros((128, 128), np.float32); c["k_self"][127, :] = 1.0
    c["k_selb"] = np.zeros((128, 128), np.float32); c["k_selb"][0, :] = 1.0
    t = np.arange(SEQ)
    rows = (t // 64).astype(np.float32); cols = (t % 64).astype(np.float32)
    inv = (10000.0 ** (-np.arange(32, dtype=np.float32) / 32.0)).astype(np.float32)
    ang = np.concatenate([rows[:, None] * inv, cols[:, None] * inv], axis=-1).astype(np.float32)
    cos = np.ones((T, 64), np.float32); sin = np.zeros((T, 64), np.float32)
    cos[CTX:] = np.cos(ang); sin[CTX:] = np.sin(ang)
    c["k_cos"] = np.ascontiguousarray(cos.reshape(NCH, 128, 64).transpose(1, 0, 2))
    c["k_sin"] = np.ascontiguousarray(sin.reshape(NCH, 128, 64).transpose(1, 0, 2))
    c["k_mask8"] = (idx[:, None] // 16 == np.arange(8)[None, :]).astype(np.float32)
    c["k_tpos"] = np.stack([idx + 1.0, 128.0 - idx], axis=1).astype(np.float32)
    return c


_PROG = {}


def kernel(**inputs):
    if "p" not in _PROG:
        _PROG["p"] = build_program()
    nc, es, _ = _PROG["p"]
    f32 = lambda a: np.ascontiguousarray(np.asarray(a, dtype=np.float32))
    x = f32(inputs["x"]); ctx = f32(inputs["ctx"]); c = f32(inputs["c"]); c_ctx = f32(inputs["c_ctx"])
    shared = {k: f32(inputs[k]) for k in ("w_mod", "b_mod", "norm1_w", "norm2_w", "w_in", "w_out", "s5_lam_re", "s5_lam_im", "s5_log_step",
                                          "s5_b_re", "s5_b_im", "s5_c_re", "s5_c_im", "s5_d", "s5_w_glu", "s5_b_glu", "ret_decay_logit",
                                          "ret_norm_w", "mlstm_igate_b", "mlstm_fgate_b", "mlstm_norm_w", "w_ff1", "w_ff2", "norm_f_w")}
    shared.update(_consts())
    in_maps = []
    for core in range(8):
        b = core % NB
        m = dict(shared)
        m["xh"] = np.ascontiguousarray(np.concatenate([ctx[b], x[b]], axis=0))
        m["cc"] = np.ascontiguousarray(np.stack([c[b], c_ctx], axis=0))
        in_maps.append(m)
    res = run_bass_kernel_spmd(nc, in_maps, core_ids=list(range(8)))
    outs = [np.asarray(res.results[b]["out"], dtype=np.float32) for b in range(NB)]
    return np.stack(outs, axis=0)
```
